# Optimizing a Trainium2 kernel written in Bass

```python
import math
import jax
import jax.numpy as jnp
from jax import lax
import numpy as np

D_MODEL = 2048
BATCH = 32
SEQ = 256
DEPTH = 2
DEC_BATCH = 2
DEC_SEQ = 4096
PAST_LEN = 512

GRID_W = 64
N_EVEN = (DEPTH + 1) // 2
N_ODD = DEPTH // 2
NORM_EPS = 1e-6

A_HEAD_DIM = 64
A_WIDTH = D_MODEL // 2
A_HEADS = A_WIDTH // A_HEAD_DIM
A_DECAY_RANK = 64
A_ICLR_RANK = 64
A_GATE_RANK = 128
A_COLS = 3 * A_WIDTH + 2 * A_DECAY_RANK + 2 * A_ICLR_RANK + A_GATE_RANK
A_GN_EPS = 64e-5

B_HEAD_DIM = 128
B_WIDTH = D_MODEL // 2
B_Q_HEADS = B_WIDTH // B_HEAD_DIM
B_KV_HEADS = B_Q_HEADS // 4
B_GROUP = B_Q_HEADS // B_KV_HEADS
B_KV_WIDTH = B_KV_HEADS * B_HEAD_DIM
B_COLS = B_WIDTH + 2 * B_KV_WIDTH
ROPE_THETA = 10000.0
Q_BLOCK = 128

C_WIDTH = D_MODEL
HYENA_ORDER = 2
FILT_EMB = 33
FILT_BANDS = (FILT_EMB - 1) // 2
FILT_HIDDEN = 64
DECAY_TARGET = 1e-2
FAST_DECAY_PCT = 0.3
SLOW_DECAY_PCT = 1.5

PEER_HEADS = 8
PEER_NKEYS = 128
PEER_EXPERTS = PEER_NKEYS * PEER_NKEYS
PEER_QDIM = 256
PEER_TOPK = 16
PEER_CHUNK = 128

kernel_name = "hybrid_diffusion_rwkv7_gqa_hyena_peer_step"

F32 = jnp.float32


def _rmsnorm(x, g):
    xf = x.astype(F32)
    y = xf * lax.rsqrt(jnp.mean(xf * xf, axis=-1, keepdims=True) + NORM_EPS)
    return y * g.astype(F32)


def _short_conv3(x, w):
    xp = jnp.pad(x, ((0, 0), (1, 1), (0, 0)))
    return xp[:, :-2] * w[0] + xp[:, 1:-1] * w[1] + xp[:, 2:] * w[2]


def _axial_rope(x):
    L = x.shape[1]
    n_rows = L // GRID_W
    row = jnp.repeat(jnp.arange(n_rows, dtype=F32), GRID_W)
    col = jnp.tile(jnp.arange(GRID_W, dtype=F32), n_rows)
    half = x.shape[-1] // 2
    nf = half // 2
    inv = ROPE_THETA ** (-jnp.arange(nf, dtype=F32) / nf)
    shape = (1, L) + (1,) * (x.ndim - 3) + (nf,)
    xf = x.astype(F32)

    def rot(xh, pos):
        ang = (pos[:, None] * inv[None, :]).reshape(shape)
        cs, sn = jnp.cos(ang), jnp.sin(ang)
        x1, x2 = xh[..., :nf], xh[..., nf:]
        return jnp.concatenate([x1 * cs - x2 * sn, x1 * sn + x2 * cs], axis=-1)

    return jnp.concatenate([rot(xf[..., :half], row), rot(xf[..., half:], col)], axis=-1)


def _attend(q, k, v):
    Bn, Lq, Hkv, G, Dh = q.shape
    nb = Lq // Q_BLOCK
    qb = jnp.moveaxis(q.astype(F32).reshape(Bn, nb, Q_BLOCK, Hkv, G, Dh), 1, 0)
    kf, vf = k.astype(F32), v.astype(F32)
    scale = Dh ** -0.5

    def one(qblk):
        s = jnp.einsum('bqhgd,bkhd->bhgqk', qblk, kf) * scale
        p = jax.nn.softmax(s, axis=-1)
        return jnp.einsum('bhgqk,bkhd->bqhgd', p, vf)

    o = lax.map(one, qb)
    return jnp.moveaxis(o, 0, 1).reshape(Bn, Lq, Hkv * G * Dh)


def _bidir_delta_scan(s0, r, w, kk, kt, a, v):
    def both(t):
        return jnp.stack([t, jnp.flip(t, 1)], axis=2)

    def rev1(t):
        return jnp.stack([t[:, :, 0], jnp.flip(t[:, :, 1], 1)], axis=2)

    xs = tuple(jnp.moveaxis(t.astype(F32), 1, 0)
               for t in (both(r), rev1(w), both(kk), rev1(kt), rev1(a), both(v)))

    def step(S, inp):
        r_t, w_t, kk_t, kt_t, a_t, v_t = inp
        sa = jnp.einsum('bdhvk,bdhk->bdhv', S, kk_t)
        S = (S * w_t[..., None, :] - sa[..., None] * (kk_t * a_t)[..., None, :]
             + v_t[..., None] * kt_t[..., None, :])
        return S, jnp.einsum('bdhvk,bdhk->bdhv', S, r_t)

    s_fin, ys = lax.scan(step, s0.astype(F32), xs)
    ys = jnp.moveaxis(ys, 0, 1)
    return ys[:, :, 0] + jnp.flip(ys[:, :, 1], 1), s_fin


def _rwkv7_mixer(za, s0, conv, w0, wu, a0, au, gu, k_k, k_a, r_k, ln_w, ln_b):
    Bn, L, _ = za.shape
    W, Rw, Ra = A_WIDTH, A_DECAY_RANK, A_ICLR_RANK
    za = _short_conv3(za.astype(F32), conv)
    r = za[..., :W]
    k = za[..., W:2 * W]
    v = za[..., 2 * W:3 * W]
    o = 3 * W
    wd = za[..., o:o + 2 * Rw].reshape(Bn, L, 2, Rw)
    o += 2 * Rw
    ad = za[..., o:o + 2 * Ra].reshape(Bn, L, 2, Ra)
    o += 2 * Ra
    gd = za[..., o:]
    w_log = -jax.nn.softplus(-(w0 + jnp.einsum('bldr,drc->bldc', jnp.tanh(wd), wu))) - 0.5
    w = jnp.exp(-jnp.exp(w_log))
    a = jax.nn.sigmoid(a0 + jnp.einsum('bldr,drc->bldc', ad, au))
    g = jax.nn.sigmoid(gd) @ gu
    hs = (Bn, L, A_HEADS, A_HEAD_DIM)
    dh = (Bn, L, 2, A_HEADS, A_HEAD_DIM)
    kk = (k * k_k).reshape(hs)
    kk = kk * lax.rsqrt(jnp.sum(kk * kk, axis=-1, keepdims=True) + 1e-12)
    kt = k[:, :, None, :] * (1.0 + (a - 1.0) * k_a)
    r_h, k_h, v_h = r.reshape(hs), k.reshape(hs), v.reshape(hs)
    y, s_fin = _bidir_delta_scan(s0, r_h, w.reshape(dh), kk, kt.reshape(dh), a.reshape(dh), v_h)
    mu = jnp.mean(y, axis=-1, keepdims=True)
    var = jnp.mean(jnp.square(y - mu), axis=-1, keepdims=True)
    yn = ((y - mu) * lax.rsqrt(var + A_GN_EPS)).reshape(Bn, L, W) * ln_w + ln_b
    bonus = (jnp.sum(r_h * k_h * r_k, axis=-1, keepdims=True) * v_h).reshape(Bn, L, W)
    return (yn + bonus) * g, s_fin


def _even_mixer(h, P, j, latent, cache_k, cache_v, state_a):
    Bn, L, _ = h.shape
    z = h @ P['even_w_in'][j]
    za, zb = z[..., :A_COLS], z[..., A_COLS:]
    q = _rmsnorm(zb[..., :B_WIDTH].reshape(Bn, L, B_KV_HEADS, B_GROUP, B_HEAD_DIM), P['even_b_qnorm'][j])
    k = _rmsnorm(zb[..., B_WIDTH:B_WIDTH + B_KV_WIDTH].reshape(Bn, L, B_KV_HEADS, B_HEAD_DIM), P['even_b_knorm'][j])
    v = zb[..., B_WIDTH + B_KV_WIDTH:].reshape(Bn, L, B_KV_HEADS, B_HEAD_DIM).astype(F32)
    if latent:
        q = _axial_rope(q)
        keys = jnp.concatenate([_axial_rope(k), cache_k[:, j].astype(F32)], axis=1)
        vals = jnp.concatenate([v, cache_v[:, j].astype(F32)], axis=1)
        s0 = state_a[:, j]
    else:
        keys, vals = k, v
        s0 = jnp.zeros((Bn, 2, A_HEADS, A_HEAD_DIM, A_HEAD_DIM), F32)
    y_b = _attend(q, keys, vals)
    y_a, s_fin = _rwkv7_mixer(za, s0, P['even_a_conv'][j], P['even_a_w0'][j], P['even_a_wu'][j],
                              P['even_a_a0'][j], P['even_a_au'][j], P['even_a_gu'][j], P['even_a_kk'][j],
                              P['even_a_ka'][j], P['even_a_rk'][j], P['even_a_ln_w'][j], P['even_a_ln_b'][j])
    y = jnp.concatenate([y_a, y_b], axis=-1) @ P['even_w_out'][j]
    return y, k, v, s_fin


def _hyena_filters(L, fw1, fb1, freq, fw2, fb2, fw3):
    t = jnp.linspace(0.0, 1.0, L, dtype=F32)[:, None]
    wpos = 2.0 * math.pi * jnp.arange(L, dtype=F32)[:, None] / L
    f = jnp.linspace(1e-4, FILT_BANDS - 1, FILT_BANDS, dtype=F32)[None, :]
    z = jnp.concatenate([t, jnp.cos(f * wpos), -jnp.sin(f * wpos)], axis=-1)
    fr = freq.astype(F32)
    hdn = jnp.sin(fr * (z @ fw1 + fb1))
    hdn = jnp.sin(fr * (hdn @ fw2 + fb2))
    filt = (hdn @ fw3).reshape(L, HYENA_ORDER, 2, C_WIDTH)
    deltas = jnp.abs(jnp.linspace(math.log(DECAY_TARGET) / SLOW_DECAY_PCT,
                                  math.log(DECAY_TARGET) / FAST_DECAY_PCT, C_WIDTH, dtype=F32))
    filt = filt * jnp.exp(-t * deltas[None, :])[:, None, None, :]
    return filt * lax.rsqrt(jnp.sum(filt * filt, axis=(0, 2), keepdims=True) + 1e-12)


def _bidir_long_conv(u, h_fwd, h_bwd):
    L = u.shape[1]
    f = jnp.concatenate([h_fwd, jnp.zeros((1, h_fwd.shape[1]), F32), jnp.flip(h_bwd[1:], 0)], axis=0)
    U = jnp.fft.rfft(u.astype(F32), n=2 * L, axis=1)
    Fh = jnp.fft.rfft(f, n=2 * L, axis=0)
    return jnp.fft.irfft(U * Fh[None], n=2 * L, axis=1)[:, :L]


def _hyena_mixer(h, w_in, conv, conv_b, fw1, fb1, freq, fw2, fb2, fw3, bias, w_out):
    L = h.shape[1]
    u = _short_conv3(h @ w_in, conv) + conv_b
    x1, x2, v = u[..., :C_WIDTH], u[..., C_WIDTH:2 * C_WIDTH], u[..., 2 * C_WIDTH:]
    filt = _hyena_filters(L, fw1, fb1, freq, fw2, fb2, fw3)
    z = v
    for n, gate in enumerate((x1, x2)):
        z = gate * (_bidir_long_conv(z, filt[:, n, 0], filt[:, n, 1]) + bias[n] * z)
    return z @ w_out


def _peer(h, wq, sub_keys, u_tab, v_tab):
    Bn, L, D = h.shape
    T = Bn * L
    x = h.reshape(T, D)
    q = (x @ wq).astype(F32).reshape(T, PEER_HEADS, 2, PEER_QDIM // 2)
    s = jnp.einsum('thcd,hcnd->thcn', q, sub_keys.astype(F32))
    s1, i1 = lax.top_k(s[:, :, 0], PEER_TOPK)
    s2, i2 = lax.top_k(s[:, :, 1], PEER_TOPK)
    cand = (s1[..., :, None] + s2[..., None, :]).reshape(T, PEER_HEADS, PEER_TOPK * PEER_TOPK)
    top, ci = lax.top_k(cand, PEER_TOPK)
    e = (jnp.take_along_axis(i1, ci // PEER_TOPK, axis=-1) * PEER_NKEYS
         + jnp.take_along_axis(i2, ci % PEER_TOPK, axis=-1))
    gate = jax.nn.softmax(top, axis=-1)
    nc = T // PEER_CHUNK

    def chunk(args):
        xc, ec, gc = args
        u = jnp.take(u_tab, ec, axis=0)
        act = jax.nn.gelu(jnp.einsum('chkd,cd->chk', u, xc), approximate=False)
        vv = jnp.take(v_tab, ec, axis=0)
        return jnp.einsum('chk,chkd->cd', gc * act, vv)

    out = lax.map(chunk, (x.reshape(nc, PEER_CHUNK, D),
                          e.reshape(nc, PEER_CHUNK, PEER_HEADS, PEER_TOPK),
                          gate.reshape(nc, PEER_CHUNK, PEER_HEADS, PEER_TOPK)))
    return out.reshape(Bn, L, D)


def _trunk(x, cond, P, latent, cache_k, cache_v, state_a):
    new_k, new_v, new_s = [], [], []
    for layer in range(DEPTH):
        j = layer // 2
        m = jax.nn.silu(cond.astype(F32)) @ P['mod_w'][layer] + P['mod_b'][layer]
        if m.ndim == 2:
            m = m[:, None, :]
        sh1, sc1, gt1, sh2, sc2, gt2 = jnp.split(m, 6, axis=-1)
        hm = _rmsnorm(x, P['norm1'][layer]) * (1.0 + sc1) + sh1
        if layer % 2 == 0:
            mix, k_c, v_c, s_c = _even_mixer(hm, P, j, latent, cache_k, cache_v, state_a)
            if not latent:
                new_k.append(k_c)
                new_v.append(v_c)
                new_s.append(s_c)
        else:
            mix = _hyena_mixer(hm, P['odd_w_in'][j], P['odd_c_conv'][j], P['odd_c_conv_b'][j],
                               P['odd_c_fw1'][j], P['odd_c_fb1'][j], P['odd_c_freq'][j],
                               P['odd_c_fw2'][j], P['odd_c_fb2'][j], P['odd_c_fw3'][j],
                               P['odd_c_bias'][j], P['odd_w_out'][j])
        x = x + gt1 * mix
        hm = _rmsnorm(x, P['norm2'][layer]) * (1.0 + sc2) + sh2
        x = x + gt2 * _peer(hm, P['peer_wq'][layer], P['peer_keys'][layer], P['peer_u'][layer], P['peer_v'][layer])
    return x, new_k, new_v, new_s


def setup_inputs(seed: int = 0) -> dict:
    key = jax.random.key(seed)
    ks = iter(jax.random.split(key, 48))

    def nrm(shape, scale):
        return jax.random.normal(next(ks), shape, F32) * scale

    centre = jnp.array([0.0, 1.0, 0.0], F32)[:, None]
    D = D_MODEL
    return {
        'x_prompt': nrm((BATCH, SEQ, D), 1.0),
        'x_sample': nrm((DEC_BATCH, DEC_SEQ, D), 1.0),
        'cache_b_k': nrm((DEC_BATCH, N_EVEN, PAST_LEN, B_KV_HEADS, B_HEAD_DIM), 1.0),
        'cache_b_v': nrm((DEC_BATCH, N_EVEN, PAST_LEN, B_KV_HEADS, B_HEAD_DIM), 1.0),
        'state_a': nrm((DEC_BATCH, N_EVEN, 2, A_HEADS, A_HEAD_DIM, A_HEAD_DIM), 0.3),
        'c': nrm((DEC_BATCH, D), 1.0),
        'c_ctx': nrm((D,), 1.0),
        'mod_w': nrm((DEPTH, D, 6 * D), 0.5 * D ** -0.5),
        'mod_b': nrm((DEPTH, 6 * D), 0.02),
        'norm1': 1.0 + nrm((DEPTH, D), 0.02),
        'norm2': 1.0 + nrm((DEPTH, D), 0.02),
        'even_w_in': nrm((N_EVEN, D, A_COLS + B_COLS), D ** -0.5),
        'even_a_conv': nrm((N_EVEN, 3, A_COLS), 0.2) + centre,
        'even_a_w0': nrm((N_EVEN, 2, A_WIDTH), 1.0) - 2.0,
        'even_a_wu': nrm((N_EVEN, 2, A_DECAY_RANK, A_WIDTH), 0.1),
        'even_a_a0': nrm((N_EVEN, 2, A_WIDTH), 0.5),
        'even_a_au': nrm((N_EVEN, 2, A_ICLR_RANK, A_WIDTH), 0.1),
        'even_a_gu': nrm((N_EVEN, A_GATE_RANK, A_WIDTH), A_GATE_RANK ** -0.5),
        'even_a_kk': 1.0 + nrm((N_EVEN, A_WIDTH), 0.1),
        'even_a_ka': 1.0 + nrm((N_EVEN, A_WIDTH), 0.1),
        'even_a_rk': nrm((N_EVEN, A_HEADS, A_HEAD_DIM), 0.1),
        'even_a_ln_w': 1.0 + nrm((N_EVEN, A_WIDTH), 0.02),
        'even_a_ln_b': nrm((N_EVEN, A_WIDTH), 0.02),
        'even_b_qnorm': 1.0 + nrm((N_EVEN, B_HEAD_DIM), 0.02),
        'even_b_knorm': 1.0 + nrm((N_EVEN, B_HEAD_DIM), 0.02),
        'even_w_out': nrm((N_EVEN, A_WIDTH + B_WIDTH, D), (A_WIDTH + B_WIDTH) ** -0.5),
        'odd_w_in': nrm((N_ODD, D, 3 * C_WIDTH), D ** -0.5),
        'odd_c_conv': nrm((N_ODD, 3, 3 * C_WIDTH), 0.2) + centre,
        'odd_c_conv_b': nrm((N_ODD, 3 * C_WIDTH), 0.02),
        'odd_c_fw1': nrm((N_ODD, FILT_EMB, FILT_HIDDEN), FILT_EMB ** -0.5),
        'odd_c_fb1': nrm((N_ODD, FILT_HIDDEN), 0.1),
        'odd_c_freq': 1.0 + nrm((N_ODD, FILT_HIDDEN), 0.1),
        'odd_c_fw2': nrm((N_ODD, FILT_HIDDEN, FILT_HIDDEN), FILT_HIDDEN ** -0.5),
        'odd_c_fb2': nrm((N_ODD, FILT_HIDDEN), 0.1),
        'odd_c_fw3': nrm((N_ODD, FILT_HIDDEN, HYENA_ORDER * 2 * C_WIDTH), FILT_HIDDEN ** -0.5),
        'odd_c_bias': nrm((N_ODD, HYENA_ORDER, C_WIDTH), 0.5),
        'odd_w_out': nrm((N_ODD, C_WIDTH, D), C_WIDTH ** -0.5),
        'peer_wq': nrm((DEPTH, D, PEER_HEADS * PEER_QDIM), D ** -0.5),
        'peer_keys': nrm((DEPTH, PEER_HEADS, 2, PEER_NKEYS, PEER_QDIM // 2), (PEER_QDIM // 2) ** -0.5),
        'peer_u': nrm((DEPTH, PEER_EXPERTS, D), D ** -0.5),
        'peer_v': nrm((DEPTH, PEER_EXPERTS, D), PEER_HEADS ** -0.5),
    }


def reference(x_prompt, x_sample, cache_b_k, cache_b_v, state_a, c, c_ctx, mod_w, mod_b, norm1, norm2,
              even_w_in, even_a_conv, even_a_w0, even_a_wu, even_a_a0, even_a_au, even_a_gu, even_a_kk,
              even_a_ka, even_a_rk, even_a_ln_w, even_a_ln_b, even_b_qnorm, even_b_knorm, even_w_out,
              odd_w_in, odd_c_conv, odd_c_conv_b, odd_c_fw1, odd_c_fb1, odd_c_freq, odd_c_fw2, odd_c_fb2,
              odd_c_fw3, odd_c_bias, odd_w_out, peer_wq, peer_keys, peer_u, peer_v):
    P = {
        'mod_w': mod_w, 'mod_b': mod_b, 'norm1': norm1, 'norm2': norm2,
        'even_w_in': even_w_in, 'even_a_conv': even_a_conv, 'even_a_w0': even_a_w0, 'even_a_wu': even_a_wu,
        'even_a_a0': even_a_a0, 'even_a_au': even_a_au, 'even_a_gu': even_a_gu, 'even_a_kk': even_a_kk,
        'even_a_ka': even_a_ka, 'even_a_rk': even_a_rk, 'even_a_ln_w': even_a_ln_w, 'even_a_ln_b': even_a_ln_b,
        'even_b_qnorm': even_b_qnorm, 'even_b_knorm': even_b_knorm, 'even_w_out': even_w_out,
        'odd_w_in': odd_w_in, 'odd_c_conv': odd_c_conv, 'odd_c_conv_b': odd_c_conv_b, 'odd_c_fw1': odd_c_fw1,
        'odd_c_fb1': odd_c_fb1, 'odd_c_freq': odd_c_freq, 'odd_c_fw2': odd_c_fw2, 'odd_c_fb2': odd_c_fb2,
        'odd_c_fw3': odd_c_fw3, 'odd_c_bias': odd_c_bias, 'odd_w_out': odd_w_out,
        'peer_wq': peer_wq, 'peer_keys': peer_keys, 'peer_u': peer_u, 'peer_v': peer_v,
    }
    y_prompt, nk, nv, ns = _trunk(x_prompt, c_ctx, P, False, None, None, None)
    y_sample, _, _, _ = _trunk(x_sample, c, P, True, cache_b_k, cache_b_v, state_a)
    new_cache_b_k = jnp.stack(nk, axis=1)
    new_cache_b_v = jnp.stack(nv, axis=1)
    new_state_a = jnp.stack(ns, axis=1)
    return (y_prompt, y_sample, new_cache_b_k, new_cache_b_v, new_state_a)
```

```python
from contextlib import ExitStack
import math
import numpy as np
import concourse.bass as bass
import concourse.mybir as mybir
from concourse.bass_utils import run_bass_kernel_spmd

F32 = mybir.dt.float32
BF16 = mybir.dt.bfloat16
I32 = mybir.dt.int32
U32 = mybir.dt.uint32
AF = mybir.ActivationFunctionType
ALU = mybir.AluOpType
AX = mybir.AxisListType

D = 2048
NC = 8
COMPUTE = ("tensor", "vector", "scalar", "gpsimd")
NDMA_SLOTS = 12
SEM_LIMIT = 30000
DEBUG_OUT = set()
DEBUG_RES = {}


class Prog:
    def __init__(self, nc, stack):
        self.nc = nc
        self.stack = stack
        self.ops = {e: [] for e in COMPUTE + ("sync",)}
        self.nsem = 0
        self.csem = {e: self._newsem("c_" + e) for e in COMPUTE}
        self.ccnt = {e: 0 for e in COMPUTE}
        self.dslots = {}
        self.dnext = {}
        for q in ("sync", "gpsimd"):
            self.dslots[q] = [[self._newsem("d_%s%d" % (q, i)), 0] for i in range(NDMA_SLOTS)]
            self.dnext[q] = 0
        self.last_w = {}
        self.readers = {}
        self.waited = {e: {} for e in self.ops}
        self.out_events = []
        self.all_events = {}

    def _newsem(self, name):
        self.nsem += 1
        return self.stack.enter_context(self.nc.semaphore("%s_%d" % (name, self.nsem)))

    def _deps(self, reads, writes):
        deps = []
        for k in reads:
            if k in self.last_w:
                deps.append(self.last_w[k])
        for k in writes:
            if k in self.last_w:
                deps.append(self.last_w[k])
            deps.extend(self.readers.get(k, ()))
        return deps

    def _emit_waits(self, eng, deps):
        need = {}
        for (s, v) in deps:
            if v > need.get(id(s), (s, 0))[1]:
                need[id(s)] = (s, v)
        w = self.waited[eng]
        for sid, (s, v) in need.items():
            if w.get(sid, 0) >= v:
                continue
            w[sid] = v
            self.ops[eng].append(("wait", s, v))

    def _commit(self, ev, reads, writes):
        self.all_events[id(ev[0])] = ev
        for k in reads:
            self.readers.setdefault(k, []).append(ev)
        for k in writes:
            self.last_w[k] = ev
            self.readers[k] = []

    def op(self, eng, fn, reads=(), writes=()):
        deps = self._deps(reads, writes)
        self._emit_waits(eng, deps)
        if self.ccnt[eng] >= SEM_LIMIT:
            self.csem[eng] = self._newsem("c_" + eng)
            self.ccnt[eng] = 0
        self.ccnt[eng] += 1
        ev = (self.csem[eng], self.ccnt[eng])
        self.ops[eng].append(("op", fn, ev[0], 1))
        self._commit(ev, reads, writes)
        return ev

    def dma(self, fn, reads=(), writes=(), q="sync", is_output=False):
        deps = self._deps(reads, writes)
        slots = self.dslots[q]
        i = self.dnext[q]
        self.dnext[q] = (i + 1) % len(slots)
        slot = slots[i]
        if slot[1] > 0:
            deps.append((slot[0], slot[1]))
        self._emit_waits(q, deps)
        if slot[1] + 16 > SEM_LIMIT:
            slot[0] = self._newsem("d_" + q)
            slot[1] = 0
        slot[1] += 16
        ev = (slot[0], slot[1])
        self.ops[q].append(("op", fn, ev[0], 16))
        self._commit(ev, reads, writes)
        if is_output:
            self.out_events.append(ev)
        return ev

    def barrier(self):
        evs = list(self.all_events.values())
        for e in self.ops:
            self._emit_waits(e, evs)
        self.last_w = {}
        self.readers = {}

    def emit(self):
        nc = self.nc
        self._emit_waits("sync", list(self.all_events.values()))
        with nc.Block() as block:
            def run(engname):
                def body(eng):
                    for item in self.ops[engname]:
                        if item[0] == "wait":
                            eng.wait_ge(item[1], item[2])
                        else:
                            item[1](eng).then_inc(item[2], item[3])
                return body
            block.sync(run("sync"))
            block.tensor(run("tensor"))
            block.vector(run("vector"))
            block.scalar(run("scalar"))
            block.gpsimd(run("gpsimd"))


class Arena:
    def __init__(self, t, size):
        self.t = t
        self.size = size
        self.off = 0

    def reset(self):
        self.off = 0

    def f32(self, n):
        o = self.off
        self.off += n
        assert self.off <= self.size, ("arena overflow", self.off, self.size)
        return o

    def v(self, off, dims, p0=0, np_=128):
        return bass.AP(self.t, p0 * self.size + off, [[self.size, np_]] + [list(d) for d in dims])

    def bf(self, off, n):
        return self.t[:, off:off + n].bitcast(BF16)


def bc_rows(dram_t, elem_off, n, nparts=128):
    return bass.AP(dram_t, elem_off, [[0, nparts], [1, n]])


def rsqrt_ops(P, ap, mul, add, key):
    P.op("vector", lambda e: e.tensor_scalar(ap, ap, mul, add, ALU.mult, ALU.add), reads=[key], writes=[key])
    P.op("scalar", lambda e: e.activation(ap, ap, AF.Sqrt), reads=[key], writes=[key])
    P.op("vector", lambda e: e.reciprocal(ap, ap), reads=[key], writes=[key])

def build_program(mode="full"):
    nc = bass.Bass("TRN2", target_bir_lowering=False)
    TP, TS = 1024, 4096
    dt_in = {}

    def inp(name, shape, dt=F32):
        dt_in[name] = nc.dram_tensor(name, list(shape), dt, kind="ExternalInput")
        return dt_in[name]

    def outp(name, shape, dt=F32):
        return nc.dram_tensor(name, list(shape), dt, kind="ExternalOutput")

    def scr(name, shape, dt=F32):
        if name in DEBUG_OUT:
            return nc.dram_tensor(name, list(shape), dt, kind="ExternalOutput")
        return nc.dram_tensor(name, list(shape), dt)

    xg = [inp("xp", [TP, D]), inp("xs", [TS, D])]
    ck = inp("ck", [512, 256]); cv = inp("cv", [512, 256]); st_in = inp("st", [2, 16, 64, 64])
    cond = inp("cond", [2, D])
    mod_w = inp("mod_w", [2, D, 6 * D]); mod_b = inp("mod_b", [2, 6 * D])
    norm1 = inp("norm1", [2, D]); norm2 = inp("norm2", [2, D])
    even_w_in = inp("even_w_in", [D, 4992])
    a_conv = inp("even_a_conv", [3, 3456])
    a_w0 = inp("even_a_w0", [2, 1024]); a_wu = inp("even_a_wu", [128, 1024])
    a_a0 = inp("even_a_a0", [2, 1024]); a_au = inp("even_a_au", [128, 1024])
    a_gu = inp("even_a_gu", [128, 1024])
    a_kk = inp("even_a_kk", [1024]); a_ka = inp("even_a_ka", [1024]); a_rk = inp("even_a_rk", [1024])
    a_lnw = inp("even_a_ln_w", [1024]); a_lnb = inp("even_a_ln_b", [1024])
    b_qn = inp("even_b_qnorm", [128]); b_kn = inp("even_b_knorm", [128])
    ident_d = inp("ident", [128, 128])
    ropec = inp("ropec", [4096, 64]); ropes = inp("ropes", [4096, 64])
    even_w_out = inp("even_w_out", [D, D])
    odd_w_in = inp("odd_w_in", [D, 6144]); odd_w_out = inp("odd_w_out", [D, D])
    c_conv = inp("odd_c_conv", [3, 6144]); c_convb = inp("odd_c_conv_b", [6144])
    c_fw1 = inp("odd_c_fw1", [33, 64]); c_fb1 = inp("odd_c_fb1", [64]); c_freq = inp("odd_c_freq", [64])
    c_fw2 = inp("odd_c_fw2", [64, 64]); c_fb2 = inp("odd_c_fb2", [64]); c_fw3 = inp("odd_c_fw3", [64, 8192])
    c_bias = inp("odd_c_bias", [2, 2048])
    zposT = [inp("zposT_%d" % li, [33, L_]) for li, L_ in enumerate((256, 4096))]
    tneg = [inp("tneg_%d" % li, [L_]) for li, L_ in enumerate((256, 4096))]
    wkv = [inp("wk_%d" % li, [L_]) for li, L_ in enumerate((256, 4096))]
    deltas = inp("deltas", [2048])
    kconst = inp("kconst", [3, 128, 256], U32)
    qidx = inp("qidx", [1024, 1], I32)
    dftc = [inp("dftc_%d" % li, [L_, L_], BF16) for li, L_ in enumerate((256, 4096))]
    dfts = [inp("dfts_%d" % li, [L_, L_], BF16) for li, L_ in enumerate((256, 4096))]
    peer_wq = inp("peer_wq", [2, D, D]); peer_keys = inp("peer_keys", [2, 8, 2, 128, 128])
    peer_u = [inp("peer_u0", [16384, D]), inp("peer_u1", [16384, D])]
    peer_v = [inp("peer_v0", [16384, D]), inp("peer_v1", [16384, D])]

    yp = outp("yp", [TP, D]); ys = outp("ys", [1024, D])
    nk = outp("nk", [TP, 256]); nv = outp("nv", [TP, 256]); ns = outp("ns", [4, 2, 16, 64, 64])

    modv = scr("modv", [2, 2, 6 * D])
    zg = [scr("zp", [TP, 4992]), scr("zs", [TS, 4992])]
    SQ = ["kk", "w0", "w1", "b0", "b1", "kt0", "kt1", "r"]
    sq = [{n: scr("sq_%s_%d" % (n, g), [T, 1024]) for n in SQ} for g, T in enumerate((TP, TS))]
    vT = [scr("vT_%d" % g, [8, 128, T]) for g, T in enumerate((TP, TS))]
    yT = [[scr("yT_%d_%d" % (g, d), [8, 128, T]) for d in range(2)] for g, T in enumerate((TP, TS))]
    gsc = [scr("g_%d" % g, [T, 1024]) for g, T in enumerate((TP, TS))]
    bon = [scr("bon_%d" % g, [T, 1024]) for g, T in enumerate((TP, TS))]
    zero_d = scr("zero_d", [128, 64])
    ycatT = [scr("ycatT_%d" % g, [16, 128, T], BF16) for g, T in enumerate((TP, TS))]
    qT_d = [scr("qT_%d" % g, [8, 128, T], BF16) for g, T in enumerate((TP, TS))]
    kT_d = [scr("kT_0", [2, 128, TP], BF16), scr("kT_1", [2, 128, TS + 512], BF16)]
    V_d = [scr("V_0", [TP, 256], BF16), scr("V_1", [TS + 512, 256], BF16)]
    x1 = [scr("x1_%d" % g, [T, D]) for g, T in enumerate((TP, TS))]
    if mode == "hytest":
        x2 = [inp("x2_%d" % g, [T, D]) for g, T in enumerate((TP, TS))]
    else:
        x2 = [scr("x2_%d" % g, [T, D]) for g, T in enumerate((TP, TS))]
    x3 = [scr("x3_%d" % g, [T, D]) for g, T in enumerate((TP, TS))]
    x3q = scr("x3q", [1024, D])
    uz = [scr("uz_%d" % g, [T, 6144]) for g, T in enumerate((TP, TS))]
    ug = [scr("ug_%d" % g, [T, 4096]) for g, T in enumerate((TP, TS))]
    zb16 = [[scr("zb_%d_%d" % (g, i), [T, 2048], BF16) for i in range(3)] for g, T in enumerate((TP, TS))]
    hzT = [scr("hzT_%d" % g, [16, 128, T], BF16) for g, T in enumerate((TP, TS))]
    LL = (256, 4096)
    HS = [scr("HS_%d" % li, [L_, 4, 2048], BF16) for li, L_ in enumerate(LL)]
    FS = [scr("FS_%d" % li, [2, 2, L_, 2048]) for li, L_ in enumerate(LL)]
    nsc = [scr("nsc_%d" % li, [2, 2048]) for li in range(2)]
    hm2 = [scr("hm2_%d" % g, [T, D]) for g, T in enumerate((TP, TS))]
    pq = [scr("pq_%d" % g, [T, D]) for g, T in enumerate((TP, TS))]

    with ExitStack() as st:
        st.enter_context(nc.allow_non_contiguous_dma(reason="layout"))
        P = Prog(nc, st)
        ASZ = 47616
        A = Arena(st.enter_context(nc.sbuf_tensor("arena", [128, ASZ], F32)), ASZ)
        psb = [st.enter_context(nc.psum_tensor("ps%d" % i, [128, 512], F32)) for i in range(6)]
        psT = st.enter_context(nc.psum_tensor("psT", [128, 2048], BF16))
        cons = st.enter_context(nc.sbuf_tensor("cons", [128, 128 + 64], F32))
        consb = st.enter_context(nc.sbuf_tensor("consb", [128, 128], BF16))
        ident = cons[:, 0:128]
        identb = consb[:, 0:128]
        P.dma(lambda e: e.dma_start(out=ident, in_=ident_d.ap()), writes=["ident"])
        P.op("vector", lambda e: e.tensor_copy(identb, ident), reads=["ident"], writes=["identb"])
        P.op("vector", lambda e: e.memset(cons[:, 128:192], 0.0), writes=["zeros"])
        P.dma(lambda e: e.dma_start(out=zero_d.ap(), in_=cons[:, 128:192]), reads=["zeros"], writes=["zero_d"])

        def stage_mod():
            A.reset()
            o_c = A.f32(32); o_s = A.f32(32); o_mb = A.f32(6 * D)
            o_w = [A.f32(16 * 512), A.f32(16 * 512)]
            cT = A.v(o_c, [[1, 32]]); sT = A.v(o_s, [[1, 32]])
            for gg in range(2):
                P.dma(lambda e, gg=gg: e.dma_start(out=A.v(o_c + gg * 16, [[1, 16]]),
                                                   in_=cond.ap()[gg].rearrange("(kc p) -> p kc", p=128)), writes=["cT"])
            P.op("scalar", lambda e: e.activation(sT, cT, AF.Silu), reads=["cT"], writes=["sT"])
            mb = A.v(o_mb, [[1, 6 * D]], np_=2)
            for layer in range(2):
                P.dma(lambda e, layer=layer: e.dma_start(out=mb, in_=bc_rows(mod_b, layer * 6 * D, 6 * D, 2)), writes=["mb"])
                for ch in range(24):
                    k = ch % 2
                    wt = A.v(o_w[k], [[512, 16], [1, 512]])
                    P.dma(lambda e, layer=layer, ch=ch, wt=wt: e.dma_start(
                        out=wt, in_=mod_w.ap()[layer, :, ch * 512:(ch + 1) * 512].rearrange("(kc p) n -> p kc n", p=128)),
                        writes=["mw%d" % k])
                    pb = psb[ch % 2]
                    for kc in range(16):
                        P.op("tensor", lambda e, kc=kc, k=k, pb=pb: e.matmul(
                            pb[0:2, :], A.v(o_s + kc, [[16, 2]]), A.v(o_w[k] + kc * 512, [[1, 512]]),
                            start=(kc == 0), stop=(kc == 15)),
                            reads=["sT", "mw%d" % k], writes=["psm%d" % (ch % 2)])
                    P.op("vector", lambda e, ch=ch, pb=pb: e.tensor_tensor(
                        A.v(o_mb + ch * 512, [[1, 512]], np_=2), A.v(o_mb + ch * 512, [[1, 512]], np_=2), pb[0:2, :], ALU.add),
                        reads=["psm%d" % (ch % 2), "mb"], writes=["mb"])
                for so in (D, 4 * D):
                    P.op("vector", lambda e, so=so: e.tensor_scalar_add(
                        A.v(o_mb + so, [[1, D]], np_=2), A.v(o_mb + so, [[1, D]], np_=2), 1.0), reads=["mb"], writes=["mb"])
                P.dma(lambda e, layer=layer: e.dma_start(out=modv.ap()[layer], in_=mb), reads=["mb"], writes=["modv"])
            P.barrier()

        def stage_norm_gemm(g, T, x_ap, layer, which, normw, W_ap, N, out_t, hm_out=None, srcT=None, resid=None):
            A.reset()
            sc_off = (1 if which == 1 else 4) * D
            sh_off = (0 if which == 1 else 3) * D
            o_G = A.f32(D); o_SH = A.f32(D); o_h = A.f32(D)
            o_x = [A.f32(D), A.f32(D)]
            o_w = A.f32(16 * 512)
            o_ot = [A.f32(512), A.f32(512)]
            o_xr = [A.f32(512), A.f32(512)]
            o_ss = A.f32(8)
            o_hb = A.f32(D // 2)
            o_hT = A.f32(16 * 1024 // 2)
            o_wb = [A.f32(16 * 512 // 2), A.f32(16 * 512 // 2)]
            Gb = A.v(o_G, [[1, D]]); SHb = A.v(o_SH, [[1, D]]); hh = A.v(o_h, [[1, D]])
            hb = A.bf(o_hb, D // 2)
            hT = A.bf(o_hT, 16 * 1024 // 2).rearrange("p (kc t) -> p kc t", kc=16)
            if srcT is None:
                P.dma(lambda e: e.dma_start(out=Gb, in_=bc_rows(normw, layer * D, D)), writes=["Gb"])
                P.dma(lambda e: e.dma_start(out=hh, in_=bc_rows(modv, (layer * 2 + g) * 6 * D + sc_off, D)), reads=["modv"], writes=["hh"])
                P.dma(lambda e: e.dma_start(out=SHb, in_=bc_rows(modv, (layer * 2 + g) * 6 * D + sh_off, D)), reads=["modv"], writes=["SHb"])
                P.op("vector", lambda e: e.tensor_tensor(Gb, Gb, hh, ALU.mult), reads=["Gb", "hh"], writes=["Gb"])
            if resid is not None:
                P.dma(lambda e: e.dma_start(out=SHb, in_=bc_rows(modv, (layer * 2 + g) * 6 * D + resid[1], D)), reads=["modv"], writes=["SHb"])
            nchunks = (N + 511) // 512
            for blk in range(T // 1024):
                if srcT is not None:
                    P.dma(lambda e, blk=blk: e.dma_start(out=hT, in_=srcT.ap()[:, :, blk * 1024:(blk + 1) * 1024].rearrange("kc p t -> p kc t")),
                          reads=[srcT.name], writes=["hT"])
                for tt in range(8 if srcT is None else 0):
                    t0 = blk * 1024 + tt * 128
                    k = tt % 2
                    xt = A.v(o_x[k], [[1, D]])
                    P.dma(lambda e, xt=xt, t0=t0: e.dma_start(out=xt, in_=x_ap[t0:t0 + 128, :]), writes=["x%d" % k])
                    ss = A.v(o_ss + k, [[1, 1]]); rs = A.v(o_ss + 2 + k, [[1, 1]])
                    P.op("scalar", lambda e, xt=xt, ss=ss: e.activation(hh, xt, AF.Square, accum_out=ss),
                         reads=["x%d" % k], writes=["hh", "ss%d" % k])
                    P.op("vector", lambda e, ss=ss, rs=rs: e.tensor_copy(rs, ss), reads=["ss%d" % k], writes=["rs%d" % k])
                    rsqrt_ops(P, rs, 1.0 / D, 1e-6, "rs%d" % k)
                    P.op("vector", lambda e, xt=xt, rs=rs: e.scalar_tensor_tensor(hh, xt, rs, Gb, ALU.mult, ALU.mult),
                         reads=["x%d" % k, "rs%d" % k, "Gb"], writes=["hh"])
                    if hm_out is not None:
                        P.op("gpsimd", lambda e: e.tensor_tensor(hh, hh, SHb, ALU.add), reads=["hh", "SHb"], writes=["hh"])
                        P.dma(lambda e, t0=t0: e.dma_start(out=hm_out.ap()[t0:t0 + 128, :], in_=hh), reads=["hh"], writes=[hm_out.name])
                        P.op("scalar", lambda e: e.activation(hb, hh, AF.Copy), reads=["hh"], writes=["hb"])
                    else:
                        P.op("gpsimd", lambda e: e.tensor_tensor(hb, hh, SHb, ALU.add), reads=["hh", "SHb"], writes=["hb"])
                    for kc in range(16):
                        P.op("tensor", lambda e, kc=kc: e.transpose(psT[:, kc * 128:(kc + 1) * 128], hb[:, kc * 128:(kc + 1) * 128], identb),
                             reads=["hb", "identb"], writes=["psT"])
                    P.op("scalar", lambda e, tt=tt: e.activation(hT[:, :, tt * 128:(tt + 1) * 128],
                                                                 psT[:, :].rearrange("p (kc t) -> p kc t", kc=16), AF.Copy),
                         reads=["psT"], writes=["hT"])
                for ch in range(nchunks):
                    n0 = ch * 512
                    nw = min(512, N - n0)
                    kb = ch % 2
                    wt = A.v(o_w, [[512, 16], [1, nw]])
                    wb = A.bf(o_wb[kb], 16 * 512 // 2).rearrange("p (kc n) -> p kc n", kc=16)
                    P.dma(lambda e, wt=wt, n0=n0, nw=nw: e.dma_start(
                        out=wt, in_=W_ap[:, n0:n0 + nw].rearrange("(kc p) n -> p kc n", p=128)), writes=["wt"])
                    P.op("gpsimd" if ch % 2 else "scalar",
                         (lambda e, wt=wt, wb=wb, nw=nw: e.tensor_copy(wb[:, :, 0:nw], wt)) if ch % 2 else
                         (lambda e, wt=wt, wb=wb, nw=nw: e.activation(wb[:, :, 0:nw], wt, AF.Copy)),
                         reads=["wt"], writes=["wb%d" % kb])
                    for tt in range(8):
                        t0 = blk * 1024 + tt * 128
                        pi = (ch * 8 + tt) % 4
                        pb = psb[pi]
                        for kc in range(16):
                            P.op("tensor", lambda e, kc=kc, tt=tt, pb=pb, wb=wb, nw=nw: e.matmul(
                                pb[:, 0:nw], hT[:, kc, tt * 128:(tt + 1) * 128], wb[:, kc, 0:nw],
                                start=(kc == 0), stop=(kc == 15)),
                                reads=["hT", "wb%d" % kb], writes=["pg%d" % pi])
                        ko = tt % 2
                        ot = A.v(o_ot[ko], [[1, nw]])
                        if resid is None:
                            P.op("vector", lambda e, ot=ot, pb=pb, nw=nw: e.tensor_copy(ot, pb[:, 0:nw]),
                                 reads=["pg%d" % pi], writes=["ot%d" % ko])
                        else:
                            xr = A.v(o_xr[ko], [[1, nw]])
                            P.dma(lambda e, xr=xr, t0=t0, n0=n0, nw=nw: e.dma_start(out=xr, in_=resid[0][t0:t0 + 128, n0:n0 + nw]),
                                  writes=["xr%d" % ko])
                            P.op("vector", lambda e, ot=ot, pb=pb, nw=nw, n0=n0: e.tensor_tensor(ot, pb[:, 0:nw], A.v(o_SH + n0, [[1, nw]]), ALU.mult),
                                 reads=["pg%d" % pi, "SHb"], writes=["ot%d" % ko])
                            P.op("gpsimd", lambda e, ot=ot, xr=xr: e.tensor_tensor(ot, ot, xr, ALU.add),
                                 reads=["ot%d" % ko, "xr%d" % ko], writes=["ot%d" % ko])
                        P.dma(lambda e, ot=ot, t0=t0, n0=n0, nw=nw: e.dma_start(out=out_t.ap()[t0:t0 + 128, n0:n0 + nw], in_=ot),
                              reads=["ot%d" % ko], writes=[out_t.name])
            P.barrier()

        def sq_store(g, nm, t0, tile_ap, rd, wr):
            off = tile_ap.offset
            for hh in range(2):
                src = bass.AP(A.t, off + hh * 64, [[A.size, 128], [128, 8], [1, 64]])
                dst = bass.AP(sq[g][nm], t0 * 1024 + hh * 512, [[1024, 128], [64, 8], [1, 64]])
                P.dma(lambda e, src=src, dst=dst: e.dma_start(out=dst, in_=src), reads=rd, writes=wr)

        def stage_even_prep(g, T, L):
            A.reset()
            z = zg[g]
            o_cw = A.f32(3 * 3456)
            o_vec = A.f32(9 * 1024)
            o_za = A.f32(3456)
            o_ld = [A.f32(3456), A.f32(3456)]
            o_d = [A.f32(1024) for _ in range(4)]
            o_kk = A.f32(1024); o_tmp = A.f32(1024); o_g = A.f32(1024); o_bon = A.f32(1024)
            o_sm = A.f32(64)
            o_lr = A.f32(3 * 128 // 2); o_lrT = A.f32(3 * 128 // 2)
            o_lw = A.f32(3 * 1024 // 2); o_lwf = A.f32(1024)
            o_vt = A.f32(1024)
            CW = A.v(o_cw, [[3456, 3], [1, 3456]])
            P.dma(lambda e: e.dma_start(out=CW, in_=bass.AP(a_conv, 0, [[0, 128], [3456, 3], [1, 3456]])), writes=["CW"])
            vecsrc = [(a_w0, 0), (a_w0, 1024), (a_a0, 0), (a_a0, 1024), (a_kk, 0), (a_ka, 0), (a_rk, 0)]
            for i, (t_, off) in enumerate(vecsrc):
                P.dma(lambda e, i=i, t_=t_, off=off: e.dma_start(out=A.v(o_vec + i * 1024, [[1, 1024]]), in_=bc_rows(t_, off, 1024)),
                      writes=["vec"])
            vec = lambda i: A.v(o_vec + i * 1024, [[1, 1024]])
            lw = A.bf(o_lw, 3 * 1024 // 2).rearrange("p (j n) -> p j n", j=3)
            for j, t_ in enumerate((a_wu, a_au, a_gu)):
                lwf = A.v(o_lwf, [[1, 1024]])
                P.dma(lambda e, t_=t_, lwf=lwf: e.dma_start(out=lwf, in_=t_.ap()), writes=["lwf"])
                P.op("vector", lambda e, j=j, lwf=lwf: e.tensor_copy(lw[:, j, :], lwf), reads=["lwf"], writes=["lw"])
            lr = A.bf(o_lr, 3 * 128 // 2).rearrange("p (j n) -> p j n", j=3)
            lrT = A.bf(o_lrT, 3 * 128 // 2).rearrange("p (j n) -> p j n", j=3)
            ZA = A.v(o_za, [[1, 3456]])
            r_ = A.v(o_za, [[1, 1024]]); k_ = A.v(o_za + 1024, [[1, 1024]]); v_ = A.v(o_za + 2048, [[1, 1024]])
            KK = A.v(o_kk, [[1, 1024]]); TMP = A.v(o_tmp, [[1, 1024]]); G = A.v(o_g, [[1, 1024]]); BON = A.v(o_bon, [[1, 1024]])
            h3 = lambda o: A.v(o, [[64, 16], [1, 64]])
            hb3 = lambda o: A.v(o, [[1, 16], [0, 64]])
            tiles_per_seq = L // 128
            for ti in range(T // 128):
                t0 = ti * 128
                first = (ti % tiles_per_seq == 0)
                last = (ti % tiles_per_seq == tiles_per_seq - 1)
                ld = A.v(o_ld[0], [[1, 3456]])
                P.dma(lambda e, ld=ld, t0=t0: e.dma_start(out=ld, in_=z.ap()[t0:t0 + 128, 0:3456]), reads=[z.name], writes=["ld0"])
                P.op("vector", lambda e, ld=ld: e.tensor_tensor(ZA, ld, A.v(o_cw + 3456, [[1, 3456]]), ALU.mult),
                     reads=["ld0", "CW"], writes=["ZA"])
                ld = A.v(o_ld[1], [[1, 3456]])
                if first:
                    P.op("gpsimd", lambda e, ld=ld: e.memset(ld, 0.0), writes=["ld1"])
                    P.dma(lambda e, t0=t0: e.dma_start(out=A.v(o_ld[1], [[1, 3456]], p0=1, np_=127), in_=z.ap()[t0:t0 + 127, 0:3456]),
                          reads=[z.name], writes=["ld1"])
                else:
                    P.dma(lambda e, ld=ld, t0=t0: e.dma_start(out=ld, in_=z.ap()[t0 - 1:t0 + 127, 0:3456]), reads=[z.name], writes=["ld1"])
                P.op("gpsimd", lambda e, ld=ld: e.tensor_tensor(ld, ld, A.v(o_cw, [[1, 3456]]), ALU.mult), reads=["ld1", "CW"], writes=["ld1"])
                P.op("vector", lambda e, ld=ld: e.tensor_tensor(ZA, ZA, ld, ALU.add), reads=["ld1", "ZA"], writes=["ZA"])
                ld = A.v(o_ld[0], [[1, 3456]])
                if last:
                    P.op("gpsimd", lambda e, ld=ld: e.memset(ld, 0.0), reads=[], writes=["ld0"])
                    P.dma(lambda e, t0=t0: e.dma_start(out=A.v(o_ld[0], [[1, 3456]], p0=0, np_=127), in_=z.ap()[t0 + 1:t0 + 128, 0:3456]),
                          reads=[z.name], writes=["ld0"])
                else:
                    P.dma(lambda e, ld=ld, t0=t0: e.dma_start(out=ld, in_=z.ap()[t0 + 1:t0 + 129, 0:3456]), reads=[z.name], writes=["ld0"])
                P.op("gpsimd", lambda e, ld=ld: e.tensor_tensor(ld, ld, A.v(o_cw + 2 * 3456, [[1, 3456]]), ALU.mult), reads=["ld0", "CW"], writes=["ld0"])
                P.op("vector", lambda e, ld=ld: e.tensor_tensor(ZA, ZA, ld, ALU.add), reads=["ld0", "ZA"], writes=["ZA"])
                sq_store(g, "r", t0, r_, ["ZA"], ["sq_r"])
                P.op("scalar", lambda e: e.activation(lr[:, 0, :], A.v(o_za + 3072, [[1, 128]]), AF.Tanh), reads=["ZA"], writes=["lr"])
                P.op("scalar", lambda e: e.activation(lr[:, 1, :], A.v(o_za + 3200, [[1, 128]]), AF.Copy), reads=["ZA"], writes=["lr"])
                P.op("scalar", lambda e: e.activation(lr[:, 2, :], A.v(o_za + 3328, [[1, 128]]), AF.Sigmoid), reads=["ZA"], writes=["lr"])
                for j in range(3):
                    P.op("tensor", lambda e, j=j: e.transpose(psT[:, j * 128:(j + 1) * 128], lr[:, j, :], identb), reads=["lr", "identb"], writes=["psT"])
                P.op("vector", lambda e: e.tensor_copy(lrT, psT[:, 0:384].rearrange("p (j n) -> p j n", j=3)), reads=["psT"], writes=["lrT"])
                for hf in range(2):
                    P.op("tensor", lambda e, hf=hf: e.matmul(psb[hf][:, :], lrT[:, 2, :], lw[:, 2, hf * 512:(hf + 1) * 512], start=True, stop=True),
                         reads=["lrT", "lw"], writes=["pe%d" % hf])
                    P.op("scalar", lambda e, hf=hf: e.activation(A.v(o_g + hf * 512, [[1, 512]]), psb[hf][:, :], AF.Copy),
                         reads=["pe%d" % hf], writes=["G"])
                P.dma(lambda e, t0=t0: e.dma_start(out=gsc[g].ap()[t0:t0 + 128, :], in_=G), reads=["G"], writes=["gsc"])
                P.op("vector", lambda e: e.tensor_tensor(KK, k_, vec(4), ALU.mult), reads=["ZA", "vec"], writes=["KK"])
                P.op("gpsimd", lambda e: e.tensor_tensor(TMP, KK, KK, ALU.mult), reads=["KK"], writes=["TMP"])
                P.op("vector", lambda e: e.tensor_reduce(A.v(o_sm, [[1, 16]]), h3(o_tmp), AX.X, ALU.add), reads=["TMP"], writes=["sm"])
                rsqrt_ops(P, A.v(o_sm, [[1, 16]]), 1.0, 1e-12, "sm")
                P.op("vector", lambda e: e.tensor_tensor(h3(o_kk), h3(o_kk), hb3(o_sm), ALU.mult), reads=["sm", "KK"], writes=["KK"])
                sq_store(g, "kk", t0, KK, ["KK"], ["sq_kk"])
                P.op("gpsimd", lambda e: e.tensor_tensor(TMP, r_, k_, ALU.mult), reads=["ZA"], writes=["TMP"])
                P.op("gpsimd", lambda e: e.tensor_tensor(TMP, TMP, vec(6), ALU.mult), reads=["TMP", "vec"], writes=["TMP"])
                P.op("vector", lambda e: e.tensor_reduce(A.v(o_sm + 16, [[1, 16]]), h3(o_tmp), AX.X, ALU.add), reads=["TMP"], writes=["sm2"])
                P.op("vector", lambda e: e.tensor_tensor(h3(o_bon), h3(o_za + 2048), hb3(o_sm + 16), ALU.mult), reads=["sm2", "ZA"], writes=["BON"])
                P.dma(lambda e, t0=t0: e.dma_start(out=bon[g].ap()[t0:t0 + 128, :], in_=BON), reads=["BON"], writes=["bon"])
                vt = A.v(o_vt, [[128, 8], [1, 128]])
                for hf in range(2):
                    for j in range(4):
                        c = hf * 4 + j
                        P.op("tensor", lambda e, c=c, hf=hf, j=j: e.transpose(psb[2 + hf][:, j * 128:(j + 1) * 128],
                                                                             A.v(o_za + 2048 + c * 128, [[1, 128]]), ident),
                             reads=["ZA", "ident"], writes=["pv%d" % hf])
                    P.op("scalar", lambda e, hf=hf: e.activation(A.v(o_vt + hf * 512, [[1, 512]]), psb[2 + hf][:, :], AF.Copy),
                         reads=["pv%d" % hf], writes=["vt"])
                P.dma(lambda e, t0=t0, vt=vt: e.dma_start(out=vT[g].ap()[:, :, t0:t0 + 128].rearrange("c p t -> p c t"), in_=vt),
                      reads=["vt"], writes=["vT"])
                for d in range(2):
                    Wd, Ad, KTd, Bd = (A.v(o, [[1, 1024]]) for o in o_d)
                    for hf in range(2):
                        P.op("tensor", lambda e, d=d, hf=hf: e.matmul(psb[hf][:, :], lrT[64 * d:64 * d + 64, 0, :],
                                                                      lw[64 * d:64 * d + 64, 0, hf * 512:(hf + 1) * 512], start=True, stop=True),
                             reads=["lrT", "lw"], writes=["pe%d" % hf])
                        P.op("vector", lambda e, d=d, hf=hf: e.tensor_tensor(A.v(o_d[0] + hf * 512, [[1, 512]]), psb[hf][:, :],
                                                                            A.v(o_vec + d * 1024 + hf * 512, [[1, 512]]), ALU.add),
                             reads=["pe%d" % hf, "vec"], writes=["Wd"])
                    P.op("scalar", lambda e, Wd=Wd: e.activation(Wd, Wd, AF.Sigmoid), reads=["Wd"], writes=["Wd"])
                    P.op("scalar", lambda e, Wd=Wd: e.activation(Wd, Wd, AF.Exp, scale=-math.exp(-0.5)), reads=["Wd"], writes=["Wd"])
                    sq_store(g, "w%d" % d, t0, Wd, ["Wd"], ["sq_w"])
                    for hf in range(2):
                        P.op("tensor", lambda e, d=d, hf=hf: e.matmul(psb[hf][:, :], lrT[64 * d:64 * d + 64, 1, :],
                                                                      lw[64 * d:64 * d + 64, 1, hf * 512:(hf + 1) * 512], start=True, stop=True),
                             reads=["lrT", "lw"], writes=["pe%d" % hf])
                        P.op("vector", lambda e, d=d, hf=hf: e.tensor_tensor(A.v(o_d[1] + hf * 512, [[1, 512]]), psb[hf][:, :],
                                                                            A.v(o_vec + (2 + d) * 1024 + hf * 512, [[1, 512]]), ALU.add),
                             reads=["pe%d" % hf, "vec"], writes=["Ad"])
                    P.op("scalar", lambda e, Ad=Ad: e.activation(Ad, Ad, AF.Sigmoid), reads=["Ad"], writes=["Ad"])
                    P.op("vector", lambda e, Ad=Ad, KTd=KTd: e.scalar_tensor_tensor(KTd, Ad, -1.0, vec(5), ALU.add, ALU.mult),
                         reads=["Ad", "vec"], writes=["KTd"])
                    P.op("vector", lambda e, KTd=KTd: e.scalar_tensor_tensor(KTd, KTd, 1.0, k_, ALU.add, ALU.mult),
                         reads=["KTd", "ZA"], writes=["KTd"])
                    sq_store(g, "kt%d" % d, t0, KTd, ["KTd"], ["sq_kt"])
                    P.op("gpsimd", lambda e, Ad=Ad, Bd=Bd: e.tensor_tensor(Bd, KK, Ad, ALU.mult), reads=["KK", "Ad"], writes=["Bd"])
                    sq_store(g, "b%d" % d, t0, Bd, ["Bd"], ["sq_b"])
            P.barrier()

        def stage_scan(g, seq, L, d, s0_ap=None, sfin_ap=None):
            A.reset()
            SB = 4
            o_S = A.f32(512); o_tmp = A.f32(512); o_sa = A.f32(8)
            o_X = [A.f32(5 * SB * 512), A.f32(5 * SB * 512)]
            VB = 128
            o_V = [A.f32(8 * VB), A.f32(8 * VB)]
            o_Y = [A.f32(8 * VB), A.f32(8 * VB)]
            S = A.v(o_S, [[1, 512]]); TMP = A.v(o_tmp, [[1, 512]])
            S3 = A.v(o_S, [[64, 8], [1, 64]]); TMP3 = A.v(o_tmp, [[64, 8], [1, 64]])
            sa = A.v(o_sa, [[1, 8]]); sab = A.v(o_sa, [[1, 8], [0, 64]])
            base = seq * L
            for hh in range(2):
                dst = A.v(o_S, [[64, 8], [1, 64]], p0=64 * hh, np_=64)
                if s0_ap is not None:
                    src = bass.AP(s0_ap, (d * 16 + hh) * 4096, [[64, 64], [2 * 4096, 8], [1, 64]])
                    P.dma(lambda e, dst=dst, src=src: e.dma_start(out=dst, in_=src), writes=["S"])
                else:
                    src = bass.AP(zero_d, 0, [[64, 64], [0, 8], [1, 64]])
                    P.dma(lambda e, dst=dst, src=src: e.dma_start(out=dst, in_=src), reads=["zero_d"], writes=["S"])
            names = ["kk", "w%d" % d, "b%d" % d, "kt%d" % d, "r"]
            for blk in range(L // SB):
                kx = blk % 2
                if d == 0:
                    tok0 = blk * SB
                else:
                    tok0 = L - (blk + 1) * SB
                for qi, nm in enumerate(names):
                    for hh in range(2):
                        dst = A.v(o_X[kx] + qi * SB * 512, [[512, SB], [1, 512]], p0=64 * hh, np_=64)
                        src = bass.AP(sq[g][nm], (base + tok0) * 1024 + hh * 512, [[0, 64], [1024, SB], [1, 512]])
                        P.dma(lambda e, dst=dst, src=src: e.dma_start(out=dst, in_=src), reads=["sq"], writes=["X%d" % kx],
                              q="gpsimd" if (qi % 2) else "sync")
                if (blk * SB) % VB == 0:
                    vb = (blk * SB) // VB
                    kv = vb % 2
                    vtok0 = vb * VB if d == 0 else L - (vb + 1) * VB
                    dst = A.v(o_V[kv], [[VB, 8], [1, VB]])
                    src = bass.AP(vT[g], base + vtok0, [[vT[g].shape[2], 128], [128 * vT[g].shape[2], 8], [1, VB]])
                    P.dma(lambda e, dst=dst, src=src: e.dma_start(out=dst, in_=src), reads=["vT"], writes=["V%d" % kv])
                for j in range(SB):
                    step = blk * SB + j
                    jj = j if d == 0 else SB - 1 - j
                    vb = step // VB
                    kv = vb % 2
                    sv = step % VB
                    vcol = sv if d == 0 else VB - 1 - sv
                    X = (lambda xs: (lambda qi: xs[qi]))([A.v(o_X[kx] + qi * SB * 512 + jj * 512, [[64, 8], [1, 64]]) for qi in range(5)])
                    vbc = A.v(o_V[kv] + vcol, [[VB, 8], [0, 64]])
                    ycol = A.v(o_Y[kv] + vcol, [[VB, 8]])
                    rk = ["X%d" % kx]
                    P.op("vector", lambda e, X=X: e.tensor_tensor(TMP3, S3, X(0), ALU.mult), reads=["S"] + rk, writes=["TMP"])
                    P.op("vector", lambda e: e.tensor_reduce(sa, TMP3, AX.X, ALU.add), reads=["TMP"], writes=["sa"])
                    P.op("vector", lambda e, X=X: e.tensor_tensor(S3, S3, X(1), ALU.mult), reads=["S"] + rk, writes=["S"])
                    P.op("vector", lambda e, X=X: e.tensor_tensor(TMP3, X(2), sab, ALU.mult), reads=["sa"] + rk, writes=["TMP"])
                    P.op("vector", lambda e: e.tensor_tensor(S3, S3, TMP3, ALU.subtract), reads=["S", "TMP"], writes=["S"])
                    P.op("vector", lambda e, X=X, vbc=vbc: e.tensor_tensor(TMP3, X(3), vbc, ALU.mult), reads=["V%d" % kv] + rk, writes=["TMP"])
                    P.op("vector", lambda e: e.tensor_tensor(S3, S3, TMP3, ALU.add), reads=["S", "TMP"], writes=["S"])
                    P.op("vector", lambda e, X=X: e.tensor_tensor(TMP3, S3, X(4), ALU.mult), reads=["S"] + rk, writes=["TMP"])
                    P.op("vector", lambda e, ycol=ycol: e.tensor_reduce(ycol, TMP3, AX.X, ALU.add), reads=["TMP"], writes=["Y%d" % kv])
                    if sv == VB - 1:
                        ytok0 = vb * VB if d == 0 else L - (vb + 1) * VB
                        srcy = A.v(o_Y[kv], [[VB, 8], [1, VB]])
                        dsty = bass.AP(yT[g][d], base + ytok0, [[yT[g][d].shape[2], 128], [128 * yT[g][d].shape[2], 8], [1, VB]])
                        P.dma(lambda e, srcy=srcy, dsty=dsty: e.dma_start(out=dsty, in_=srcy), reads=["Y%d" % kv], writes=["yT"])
            if sfin_ap is not None:
                for hh in range(2):
                    srcS = A.v(o_S, [[64, 8], [1, 64]], p0=64 * hh, np_=64)
                    dstS = bass.AP(sfin_ap, ((seq * 2 + d) * 16 + hh) * 4096, [[64, 64], [2 * 4096, 8], [1, 64]])
                    P.dma(lambda e, srcS=srcS, dstS=dstS: e.dma_start(out=dstS, in_=srcS), reads=["S"], writes=["ns"], is_output=True)
            P.barrier()

        def stage_rwkv_post(g, T):
            A.reset()
            o_y0 = A.f32(1024); o_y1 = A.f32(1024); o_Y = A.f32(1024); o_YC = A.f32(1024); o_SQ = A.f32(1024)
            o_bon = A.f32(1024); o_g = A.f32(1024); o_lnw = A.f32(1024); o_lnb = A.f32(1024); o_sm = A.f32(64)
            o_yb = A.f32(512); o_yT = A.f32(512)
            LNW = A.v(o_lnw, [[1, 1024]]); LNB = A.v(o_lnb, [[1, 1024]])
            P.dma(lambda e: e.dma_start(out=LNW, in_=bc_rows(a_lnw, 0, 1024)), writes=["LNW"])
            P.dma(lambda e: e.dma_start(out=LNB, in_=bc_rows(a_lnb, 0, 1024)), writes=["LNB"])
            Y = A.v(o_Y, [[1, 1024]]); YC = A.v(o_YC, [[1, 1024]]); SQ = A.v(o_SQ, [[1, 1024]])
            BON = A.v(o_bon, [[1, 1024]]); G = A.v(o_g, [[1, 1024]])
            h3 = lambda o: A.v(o, [[64, 16], [1, 64]])
            hb3 = lambda o: A.v(o, [[1, 16], [0, 64]])
            yb = A.bf(o_yb, 512)
            yT8 = A.bf(o_yT, 512).rearrange("p (c t) -> p c t", c=8)
            for ti in range(T // 128):
                t0 = ti * 128
                y0 = A.v(o_y0, [[128, 8], [1, 128]]); y1 = A.v(o_y1, [[128, 8], [1, 128]])
                for d, yy in ((0, y0), (1, y1)):
                    src = bass.AP(yT[g][d], t0, [[T, 128], [128 * T, 8], [1, 128]])
                    P.dma(lambda e, yy=yy, src=src: e.dma_start(out=yy, in_=src), reads=["yT"], writes=["y%d" % d])
                P.op("vector", lambda e: e.tensor_tensor(A.v(o_y0, [[1, 1024]]), A.v(o_y0, [[1, 1024]]), A.v(o_y1, [[1, 1024]]), ALU.add),
                     reads=["y0", "y1"], writes=["y0"])
                for c in range(8):
                    P.op("tensor", lambda e, c=c: e.transpose(psb[c // 4][:, (c % 4) * 128:(c % 4 + 1) * 128], A.v(o_y0 + c * 128, [[1, 128]]), ident),
                         reads=["y0", "ident"], writes=["pp%d" % (c // 4)])
                for hf in range(2):
                    P.op("scalar", lambda e, hf=hf: e.activation(A.v(o_Y + hf * 512, [[1, 512]]), psb[hf][:, :], AF.Copy), reads=["pp%d" % hf], writes=["Y"])
                P.op("vector", lambda e: e.tensor_reduce(A.v(o_sm, [[1, 16]]), h3(o_Y), AX.X, ALU.add), reads=["Y"], writes=["mu"])
                P.op("vector", lambda e: e.tensor_scalar_mul(A.v(o_sm, [[1, 16]]), A.v(o_sm, [[1, 16]]), 1.0 / 64), reads=["mu"], writes=["mu"])
                P.op("vector", lambda e: e.tensor_tensor(h3(o_YC), h3(o_Y), hb3(o_sm), ALU.subtract), reads=["Y", "mu"], writes=["YC"])
                P.op("gpsimd", lambda e: e.tensor_tensor(SQ, YC, YC, ALU.mult), reads=["YC"], writes=["SQ"])
                P.op("vector", lambda e: e.tensor_reduce(A.v(o_sm + 16, [[1, 16]]), h3(o_SQ), AX.X, ALU.add), reads=["SQ"], writes=["var"])
                rsqrt_ops(P, A.v(o_sm + 16, [[1, 16]]), 1.0 / 64, 64e-5, "var")
                P.op("vector", lambda e: e.tensor_tensor(h3(o_YC), h3(o_YC), hb3(o_sm + 16), ALU.mult), reads=["YC", "var"], writes=["YC"])
                P.op("gpsimd", lambda e: e.tensor_tensor(YC, YC, LNW, ALU.mult), reads=["YC", "LNW"], writes=["YC"])
                P.op("vector", lambda e: e.tensor_tensor(YC, YC, LNB, ALU.add), reads=["YC", "LNB"], writes=["YC"])
                P.dma(lambda e, t0=t0: e.dma_start(out=BON, in_=bon[g].ap()[t0:t0 + 128, :]), reads=["bon"], writes=["BON"])
                P.dma(lambda e, t0=t0: e.dma_start(out=G, in_=gsc[g].ap()[t0:t0 + 128, :]), reads=["gsc"], writes=["G"])
                P.op("gpsimd", lambda e: e.tensor_tensor(YC, YC, BON, ALU.add), reads=["YC", "BON"], writes=["YC"])
                P.op("vector", lambda e: e.tensor_tensor(yb, YC, G, ALU.mult), reads=["YC", "G"], writes=["yb"])
                for c in range(8):
                    P.op("tensor", lambda e, c=c: e.transpose(psT[:, c * 128:(c + 1) * 128], yb[:, c * 128:(c + 1) * 128], identb),
                         reads=["yb", "identb"], writes=["psT"])
                P.op("scalar", lambda e: e.activation(yT8, psT[:, 0:1024].rearrange("p (c t) -> p c t", c=8), AF.Copy), reads=["psT"], writes=["yT8"])
                P.dma(lambda e, t0=t0: e.dma_start(out=ycatT[g].ap()[0:8, :, t0:t0 + 128].rearrange("c p t -> p c t"), in_=yT8),
                      reads=["yT8"], writes=[ycatT[g].name])
            P.barrier()

        def stage_attn_prep(g, T, L, latent):
            A.reset()
            o_z = A.f32(1536); o_sq = A.f32(1280); o_qr = A.f32(1280); o_sm = A.f32(16)
            o_qn = A.f32(128); o_kn = A.f32(128); o_cs = A.f32(128); o_t = [A.f32(640) for _ in range(4)]
            o_qb = A.f32(640); o_qT = A.f32(640); o_vb = A.f32(128)
            QN = A.v(o_qn, [[1, 128]]); KN = A.v(o_kn, [[1, 128]])
            P.dma(lambda e: e.dma_start(out=QN, in_=bc_rows(b_qn, 0, 128)), writes=["QN"])
            P.dma(lambda e: e.dma_start(out=KN, in_=bc_rows(b_kn, 0, 128)), writes=["KN"])
            qb = A.bf(o_qb, 640)
            qT = A.bf(o_qT, 640).rearrange("p (h t) -> p h t", h=10)
            vb = A.bf(o_vb, 128)
            Lk = kT_d[g].shape[2] // (T // L)
            ntile = T // 128 + (4 if latent else 0)
            for ti in range(ntile):
                t0 = ti * 128
                cache = ti >= T // 128
                seq = t0 // L
                tin = t0 % L
                if not cache:
                    P.dma(lambda e, t0=t0: e.dma_start(out=A.v(o_z, [[1, 1536]]), in_=zg[g].ap()[t0:t0 + 128, 3456:4992]), reads=[zg[g].name], writes=["zq"])
                    P.op("gpsimd", lambda e: e.tensor_tensor(A.v(o_sq, [[1, 1280]]), A.v(o_z, [[1, 1280]]), A.v(o_z, [[1, 1280]]), ALU.mult), reads=["zq"], writes=["sqq"])
                    P.op("vector", lambda e: e.tensor_reduce(A.v(o_sm, [[1, 10]]), A.v(o_sq, [[128, 10], [1, 128]]), AX.X, ALU.add), reads=["sqq"], writes=["sm"])
                    rsqrt_ops(P, A.v(o_sm, [[1, 10]]), 1.0 / 128, 1e-6, "sm")
                    P.op("vector", lambda e: e.tensor_tensor(A.v(o_z, [[128, 10], [1, 128]]), A.v(o_z, [[128, 10], [1, 128]]), A.v(o_sm, [[1, 10], [0, 128]]), ALU.mult),
                         reads=["zq", "sm"], writes=["zq"])
                    P.op("vector", lambda e: e.tensor_tensor(A.v(o_z, [[128, 8], [1, 128]]), A.v(o_z, [[128, 8], [1, 128]]), A.v(o_qn, [[0, 8], [1, 128]]), ALU.mult),
                         reads=["zq", "QN"], writes=["zq"])
                    P.op("vector", lambda e: e.tensor_tensor(A.v(o_z + 1024, [[128, 2], [1, 128]]), A.v(o_z + 1024, [[128, 2], [1, 128]]), A.v(o_kn, [[0, 2], [1, 128]]), ALU.mult),
                         reads=["zq", "KN"], writes=["zq"])
                    if latent:
                        P.dma(lambda e, tin=tin: e.dma_start(out=A.v(o_cs, [[1, 64]]), in_=ropec.ap()[tin:tin + 128, :]), writes=["cs"])
                        P.dma(lambda e, tin=tin: e.dma_start(out=A.v(o_cs + 64, [[1, 64]]), in_=ropes.ap()[tin:tin + 128, :]), writes=["cs"])
                        X1 = A.v(o_z, [[128, 10], [64, 2], [1, 32]]); X2 = A.v(o_z + 32, [[128, 10], [64, 2], [1, 32]])
                        O1 = A.v(o_qr, [[128, 10], [64, 2], [1, 32]]); O2 = A.v(o_qr + 32, [[128, 10], [64, 2], [1, 32]])
                        Cb = A.v(o_cs, [[0, 10], [32, 2], [1, 32]]); Sb = A.v(o_cs + 64, [[0, 10], [32, 2], [1, 32]])
                        Tt = [A.v(o, [[64, 10], [32, 2], [1, 32]]) for o in o_t]
                        P.op("vector", lambda e: e.tensor_tensor(Tt[0], X1, Cb, ALU.mult), reads=["zq", "cs"], writes=["t0"])
                        P.op("gpsimd", lambda e: e.tensor_tensor(Tt[1], X2, Sb, ALU.mult), reads=["zq", "cs"], writes=["t1"])
                        P.op("vector", lambda e: e.tensor_tensor(Tt[2], X1, Sb, ALU.mult), reads=["zq", "cs"], writes=["t2"])
                        P.op("gpsimd", lambda e: e.tensor_tensor(Tt[3], X2, Cb, ALU.mult), reads=["zq", "cs"], writes=["t3"])
                        P.op("vector", lambda e: e.tensor_tensor(O1, Tt[0], Tt[1], ALU.subtract), reads=["t0", "t1"], writes=["qr"])
                        P.op("vector", lambda e: e.tensor_tensor(O2, Tt[2], Tt[3], ALU.add), reads=["t2", "t3"], writes=["qr"])
                        P.op("scalar", lambda e: e.activation(qb, A.v(o_qr, [[1, 1280]]), AF.Copy), reads=["qr"], writes=["qb"])
                    else:
                        P.op("scalar", lambda e: e.activation(qb, A.v(o_z, [[1, 1280]]), AF.Copy), reads=["zq"], writes=["qb"])
                    P.op("gpsimd", lambda e: e.tensor_copy(vb, A.v(o_z + 1280, [[1, 256]])), reads=["zq"], writes=["vb"])
                    h0 = 0
                    krow = seq * Lk + tin
                else:
                    c0 = (ti - T // 128) * 128
                    P.dma(lambda e, c0=c0: e.dma_start(out=A.v(o_z + 1024, [[1, 256]]), in_=ck.ap()[c0:c0 + 128, :]), writes=["zq"])
                    P.dma(lambda e, c0=c0: e.dma_start(out=A.v(o_z + 1280, [[1, 256]]), in_=cv.ap()[c0:c0 + 128, :]), writes=["zq"])
                    P.op("scalar", lambda e: e.activation(qb[:, 1024:1280], A.v(o_z + 1024, [[1, 256]]), AF.Copy), reads=["zq"], writes=["qb"])
                    P.op("gpsimd", lambda e: e.tensor_copy(vb, A.v(o_z + 1280, [[1, 256]])), reads=["zq"], writes=["vb"])
                    h0 = 8
                    krow = L + c0
                for h in range(h0, 10):
                    P.op("tensor", lambda e, h=h: e.transpose(psT[:, h * 128:(h + 1) * 128], qb[:, h * 128:(h + 1) * 128], identb),
                         reads=["qb", "identb"], writes=["psT"])
                P.op("vector", lambda e, h0=h0: e.tensor_copy(qT[:, h0:10, :], psT[:, h0 * 128:1280].rearrange("p (h t) -> p h t", h=10 - h0)),
                     reads=["psT"], writes=["qT"])
                if not cache:
                    P.dma(lambda e, t0=t0: e.dma_start(out=qT_d[g].ap()[:, :, t0:t0 + 128].rearrange("h p t -> p h t"), in_=qT[:, 0:8, :]),
                          reads=["qT"], writes=["qT_d"])
                P.dma(lambda e, krow=krow: e.dma_start(out=kT_d[g].ap()[:, :, krow:krow + 128].rearrange("h p t -> p h t"), in_=qT[:, 8:10, :]),
                      reads=["qT"], writes=["kT_d"])
                P.dma(lambda e, krow=krow: e.dma_start(out=V_d[g].ap()[krow:krow + 128, :], in_=vb), reads=["vb"], writes=["V_d"])
            P.barrier()

        def stage_attn(g, T, seq, L, Lk):
            A.reset()
            nkt = Lk // 128
            NQ = min(512, L)
            o_kT = A.f32(2 * Lk // 2); o_V = A.f32(nkt * 256 // 2)
            o_q = [A.f32(NQ // 2), A.f32(NQ // 2)]
            o_E = [A.f32(NQ // 2), A.f32(NQ // 2)]
            o_rec = A.f32(NQ); o_ob = [A.f32(NQ // 2), A.f32(NQ // 2)]
            o_one = A.f32(64)
            KT = A.bf(o_kT, 2 * Lk // 2).rearrange("p (h t) -> p h t", h=2)
            VV = A.bf(o_V, nkt * 256 // 2).rearrange("p (k n) -> p k n", k=nkt)
            ones = A.bf(o_one, 64)
            P.op("vector", lambda e: e.memset(ones, 1.0), writes=["ones"])
            P.dma(lambda e: e.dma_start(out=KT, in_=kT_d[g].ap()[:, :, seq * Lk:(seq + 1) * Lk].rearrange("h p t -> p h t")),
                  reads=["kT_d"], writes=["KT"])
            P.dma(lambda e: e.dma_start(out=VV, in_=V_d[g].ap()[seq * Lk:(seq + 1) * Lk, :].rearrange("(k p) n -> p k n", p=128)),
                  reads=["V_d"], writes=["VV"])
            scale = 128 ** -0.5
            it = 0
            for h in range(8):
                kv = h // 4
                for qb_ in range(L // NQ):
                    q0 = seq * L + qb_ * NQ
                    kq = (h * (L // NQ) + qb_) % 2
                    qt = A.bf(o_q[kq], NQ // 2)
                    P.dma(lambda e, qt=qt, h=h, q0=q0: e.dma_start(out=qt, in_=qT_d[g].ap()[h, :, q0:q0 + NQ]), reads=["qT_d"], writes=["q%d" % kq])
                    for kt in range(nkt):
                        ke = it % 2
                        it += 1
                        Et = A.bf(o_E[ke], NQ // 2)
                        P.op("tensor", lambda e, kt=kt, kv=kv, qt=qt, ke=ke: e.matmul(psb[ke][:, 0:NQ], KT[:, kv, kt * 128:(kt + 1) * 128], qt, start=True, stop=True),
                             reads=["KT", "q%d" % kq], writes=["pS%d" % ke])
                        P.op("scalar", lambda e, Et=Et, ke=ke: e.activation(Et, psb[ke][:, 0:NQ], AF.Exp, scale=scale),
                             reads=["pS%d" % ke], writes=["E%d" % ke])
                        P.op("tensor", lambda e, kt=kt, kv=kv, Et=Et: e.matmul(psb[2][:, 0:NQ], VV[:, kt, kv * 128:(kv + 1) * 128], Et,
                                                                               start=(kt == 0), stop=(kt == nkt - 1)),
                             reads=["VV", "E%d" % ke], writes=["pO"])
                        P.op("tensor", lambda e, kt=kt, Et=Et: e.matmul(psb[3][:, 0:NQ], ones, Et, start=(kt == 0), stop=(kt == nkt - 1)),
                             reads=["ones", "E%d" % ke], writes=["pD"])
                    rec = A.v(o_rec, [[1, NQ]])
                    ob = A.bf(o_ob[kq], NQ // 2)
                    P.op("vector", lambda e, rec=rec: e.reciprocal(rec, psb[3][:, 0:NQ]), reads=["pD"], writes=["rec"])
                    P.op("vector", lambda e, rec=rec, ob=ob: e.tensor_tensor(ob, psb[2][:, 0:NQ], rec, ALU.mult), reads=["pO", "rec"], writes=["ob%d" % kq])
                    P.dma(lambda e, ob=ob, h=h, q0=q0: e.dma_start(out=ycatT[g].ap()[8 + h, :, q0:q0 + NQ], in_=ob),
                          reads=["ob%d" % kq], writes=[ycatT[g].name])
            P.barrier()

        def stage_peer(g, T, layer, q_t, hm_t, xres_ap, out_ap, out_key, is_output=False):
            A.reset()
            NB = 4
            o_kn = A.f32(2048); o_kT = A.f32(2048); o_q = A.f32(2048); o_qT = A.f32(2048); o_S = A.f32(2048)
            o_X = A.f32(2048); o_ACC = A.f32(2048); o_jk = A.f32(2048); o_GT = A.f32(2048); o_xr = A.f32(2048)
            o_R = [A.f32(2048) for _ in range(NB)]
            o_V1 = A.f32(256); o_IDX = A.f32(256); o_IDF = A.f32(256); o_CAND = A.f32(256); o_EID = A.f32(256); o_CW = A.f32(256)
            o_W = A.f32(128); o_T = A.f32(128); o_E = A.f32(128); o_EI = A.f32(128); o_GATE = A.f32(128); o_DOT = A.f32(128); o_WG = A.f32(128)
            o_sm = A.f32(16)
            o_MK = A.f32(256); o_FF = A.f32(256); o_CD = A.f32(256)
            u32v = lambda off, dims: bass.AP(A.t, off, [[A.size, 128]] + [list(d_) for d_ in dims]).bitcast(U32)
            P.dma(lambda e: e.dma_start(out=u32v(o_MK, [[1, 256]]), in_=kconst.ap()[0]), writes=["MK"])
            P.dma(lambda e: e.dma_start(out=u32v(o_FF, [[1, 256]]), in_=kconst.ap()[1]), writes=["FF"])
            P.dma(lambda e: e.dma_start(out=u32v(o_CD, [[1, 256]]), in_=kconst.ap()[2]), writes=["CD"])
            ut = peer_u[layer]; vt_ = peer_v[layer]
            GT = A.v(o_GT, [[1, 2048]])
            P.dma(lambda e: e.dma_start(out=GT, in_=bc_rows(modv, (layer * 2 + g) * 6 * D + 5 * D, D)), reads=["modv"], writes=["GT"])
            P.dma(lambda e: e.dma_start(out=A.v(o_kn, [[128, 16], [1, 128]]), in_=peer_keys.ap()[layer].rearrange("h c n d -> n (h c) d")), writes=["kn"])
            for hc in range(16):
                P.op("tensor", lambda e, hc=hc: e.transpose(psb[hc // 4][:, (hc % 4) * 128:(hc % 4 + 1) * 128], A.v(o_kn + hc * 128, [[1, 128]]), ident),
                     reads=["kn", "ident"], writes=["pk%d" % (hc // 4)])
            for b4 in range(4):
                P.op("scalar", lambda e, b4=b4: e.activation(A.v(o_kT + b4 * 512, [[1, 512]]), psb[b4][:, :], AF.Copy), reads=["pk%d" % b4], writes=["kT"])
            EIu = bass.AP(A.t, o_EI, [[A.size, 128], [1, 128]]).bitcast(I32)
            IDXu = bass.AP(A.t, o_IDX, [[A.size, 128], [1, 256]]).bitcast(U32)
            for ti in range(T // 128):
                t0 = ti * 128
                P.dma(lambda e, t0=t0: e.dma_start(out=A.v(o_q, [[1, 2048]]), in_=q_t.ap()[t0:t0 + 128, :]), reads=[q_t.name], writes=["q"])
                P.dma(lambda e, t0=t0: e.dma_start(out=A.v(o_X, [[1, 2048]]), in_=hm_t.ap()[t0:t0 + 128, :]), reads=[hm_t.name], writes=["X"])
                P.dma(lambda e, t0=t0: e.dma_start(out=A.v(o_xr, [[1, 2048]]), in_=xres_ap[t0:t0 + 128, :]), writes=["xr"])
                for hc in range(16):
                    P.op("tensor", lambda e, hc=hc: e.transpose(psb[hc // 4][:, (hc % 4) * 128:(hc % 4 + 1) * 128], A.v(o_q + hc * 128, [[1, 128]]), ident),
                         reads=["q", "ident"], writes=["pk%d" % (hc // 4)])
                for b4 in range(4):
                    P.op("scalar", lambda e, b4=b4: e.activation(A.v(o_qT + b4 * 512, [[1, 512]]), psb[b4][:, :], AF.Copy), reads=["pk%d" % b4], writes=["qT"])
                for hc in range(16):
                    P.op("tensor", lambda e, hc=hc: e.matmul(psb[hc // 4][:, (hc % 4) * 128:(hc % 4 + 1) * 128], A.v(o_qT + hc * 128, [[1, 128]]),
                                                             A.v(o_kT + hc * 128, [[1, 128]]), start=True, stop=True),
                         reads=["qT", "kT"], writes=["pk%d" % (hc // 4)])
                for b4 in range(4):
                    P.op("scalar", lambda e, b4=b4: e.activation(A.v(o_S + b4 * 512, [[1, 512]]), psb[b4][:, :], AF.Copy), reads=["pk%d" % b4], writes=["S"])
                P.op("vector", lambda e: e.tensor_scalar_add(A.v(o_S, [[1, 2048]]), A.v(o_S, [[1, 2048]]), 64.0), reads=["S"], writes=["S"])
                P.op("vector", lambda e: e.tensor_tensor(u32v(o_S, [[128, 16], [1, 128]]), u32v(o_S, [[128, 16], [1, 128]]), u32v(o_MK, [[0, 16], [1, 128]]), ALU.bitwise_and),
                     reads=["S", "MK"], writes=["S"])
                P.op("vector", lambda e: e.tensor_tensor(u32v(o_S, [[128, 16], [1, 128]]), u32v(o_S, [[128, 16], [1, 128]]), u32v(o_CD, [[0, 16], [1, 128]]), ALU.bitwise_or),
                     reads=["S", "CD"], writes=["S"])
                Wk = A.v(o_W, [[1, 128]])
                for hc in range(16):
                    Sh = A.v(o_S + hc * 128, [[1, 128]])
                    va = A.v(o_V1 + hc * 16, [[1, 8]]); vb_ = A.v(o_V1 + hc * 16 + 8, [[1, 8]])
                    P.op("vector", lambda e, Sh=Sh, va=va: e.max(out=va, in_=Sh), reads=["S"], writes=["V1"])
                    P.op("vector", lambda e, Sh=Sh, va=va: e.match_replace(out=Wk, in_to_replace=va, in_values=Sh, imm_value=-1e30), reads=["S", "V1"], writes=["Wk"])
                    P.op("vector", lambda e, vb_=vb_: e.max(out=vb_, in_=Wk), reads=["Wk"], writes=["V1"])
                P.op("vector", lambda e: e.tensor_tensor(u32v(o_IDX, [[1, 256]]), u32v(o_V1, [[1, 256]]), u32v(o_FF, [[1, 256]]), ALU.bitwise_and), reads=["V1", "FF"], writes=["IDX"])
                P.op("vector", lambda e: e.tensor_copy(A.v(o_IDF, [[1, 256]]), u32v(o_IDX, [[1, 256]])), reads=["IDX"], writes=["IDF"])
                P.op("vector", lambda e: e.tensor_scalar(A.v(o_IDF, [[1, 256]]), A.v(o_IDF, [[1, 256]]), -1.0, 255.0, ALU.mult, ALU.add), reads=["IDF"], writes=["IDF"])
                P.op("vector", lambda e: e.memset(A.v(o_E, [[1, 128]]), 0.0), writes=["E"])
                CAND = A.v(o_CAND, [[1, 256]]); EID = A.v(o_EID, [[1, 256]]); CW = A.v(o_CW, [[1, 256]])
                for h in range(8):
                    c3 = A.v(o_CAND, [[16, 16], [1, 16]]); e3 = A.v(o_EID, [[16, 16], [1, 16]])
                    v1a = A.v(o_V1 + (2 * h) * 16, [[1, 16], [0, 16]]); v2b = A.v(o_V1 + (2 * h + 1) * 16, [[0, 16], [1, 16]])
                    i1a = A.v(o_IDF + (2 * h) * 16, [[1, 16], [0, 16]]); i2b = A.v(o_IDF + (2 * h + 1) * 16, [[0, 16], [1, 16]])
                    P.op("vector", lambda e, c3=c3, v1a=v1a, v2b=v2b: e.tensor_tensor(c3, v1a, v2b, ALU.add), reads=["V1"], writes=["CAND"])
                    P.op("vector", lambda e: e.tensor_tensor(u32v(o_CAND, [[1, 256]]), u32v(o_CAND, [[1, 256]]), u32v(o_MK, [[1, 256]]), ALU.bitwise_and), reads=["CAND", "MK"], writes=["CAND"])
                    P.op("vector", lambda e: e.tensor_tensor(u32v(o_CAND, [[1, 256]]), u32v(o_CAND, [[1, 256]]), u32v(o_CD, [[1, 256]]), ALU.bitwise_or), reads=["CAND", "CD"], writes=["CAND"])
                    P.op("vector", lambda e, e3=e3, i1a=i1a, i2b=i2b: e.scalar_tensor_tensor(e3, i1a, 128.0, i2b, ALU.mult, ALU.add), reads=["IDF"], writes=["EID"])
                    ta = A.v(o_T + h * 16, [[1, 8]]); tb = A.v(o_T + h * 16 + 8, [[1, 8]])
                    P.op("vector", lambda e, ta=ta: e.max(out=ta, in_=CAND), reads=["CAND"], writes=["T"])
                    P.op("vector", lambda e, ta=ta: e.match_replace(out=CW, in_to_replace=ta, in_values=CAND, imm_value=-1e30), reads=["CAND", "T"], writes=["CW"])
                    P.op("vector", lambda e, tb=tb: e.max(out=tb, in_=CW), reads=["CW"], writes=["T"])
                    for k in range(16):
                        P.op("vector", lambda e, h=h, k=k: e.scalar_tensor_tensor(CW, CAND, A.v(o_T + h * 16 + k, [[1, 1]]), EID, ALU.is_equal, ALU.mult,
                                                                                   accum_out=A.v(o_E + h * 16 + k, [[1, 1]])),
                             reads=["CAND", "T", "EID", "E"], writes=["CW", "E"])
                P.op("vector", lambda e: e.tensor_copy(EIu, A.v(o_E, [[1, 128]])), reads=["E"], writes=["EI"])
                P.op("vector", lambda e: e.tensor_tensor(A.v(o_GATE, [[16, 8], [1, 16]]), A.v(o_T, [[16, 8], [1, 16]]), A.v(o_T, [[16, 8], [0, 16]]), ALU.subtract),
                     reads=["T"], writes=["GATE"])
                P.op("scalar", lambda e: e.activation(A.v(o_GATE, [[1, 128]]), A.v(o_GATE, [[1, 128]]), AF.Exp), reads=["GATE"], writes=["GATE"])
                P.op("vector", lambda e: e.tensor_reduce(A.v(o_sm, [[1, 8]]), A.v(o_GATE, [[16, 8], [1, 16]]), AX.X, ALU.add), reads=["GATE"], writes=["sm"])
                P.op("vector", lambda e: e.reciprocal(A.v(o_sm, [[1, 8]]), A.v(o_sm, [[1, 8]])), reads=["sm"], writes=["sm"])
                P.op("vector", lambda e: e.tensor_tensor(A.v(o_GATE, [[16, 8], [1, 16]]), A.v(o_GATE, [[16, 8], [1, 16]]), A.v(o_sm, [[1, 8], [0, 16]]), ALU.mult),
                     reads=["GATE", "sm"], writes=["GATE"])
                P.op("vector", lambda e: e.memset(A.v(o_DOT, [[1, 128]]), 0.0), writes=["DOT"])
                gi = 0
                for hk in range(128):
                    kb = gi % NB; gi += 1
                    Rr = A.v(o_R[kb], [[1, 2048]])
                    P.dma(lambda e, Rr=Rr, hk=hk: e.indirect_dma_start(out=Rr, out_offset=None, in_=ut.ap(),
                                                                     in_offset=bass.IndirectOffsetOnAxis(ap=EIu[:, hk:hk + 1], axis=0)),
                          reads=["EI"], writes=["R%d" % kb], q="gpsimd")
                    P.op("vector", lambda e, Rr=Rr, hk=hk: e.scalar_tensor_tensor(A.v(o_jk, [[1, 2048]]), Rr, 1.0, A.v(o_X, [[1, 2048]]), ALU.mult, ALU.mult,
                                                                                 accum_out=A.v(o_DOT + hk, [[1, 1]])),
                         reads=["R%d" % kb, "X", "DOT"], writes=["jk", "DOT"])
                P.op("scalar", lambda e: e.activation(A.v(o_WG, [[1, 128]]), A.v(o_DOT, [[1, 128]]), AF.Gelu), reads=["DOT"], writes=["WG"])
                P.op("vector", lambda e: e.tensor_tensor(A.v(o_WG, [[1, 128]]), A.v(o_WG, [[1, 128]]), A.v(o_GATE, [[1, 128]]), ALU.mult), reads=["WG", "GATE"], writes=["WG"])
                P.op("gpsimd", lambda e: e.memset(A.v(o_ACC, [[1, 2048]]), 0.0), writes=["ACC"])
                for hk in range(128):
                    kb = gi % NB; gi += 1
                    Rr = A.v(o_R[kb], [[1, 2048]])
                    P.dma(lambda e, Rr=Rr, hk=hk: e.indirect_dma_start(out=Rr, out_offset=None, in_=vt_.ap(),
                                                                     in_offset=bass.IndirectOffsetOnAxis(ap=EIu[:, hk:hk + 1], axis=0)),
                          reads=["EI"], writes=["R%d" % kb], q="gpsimd")
                    P.op("vector", lambda e, Rr=Rr, hk=hk: e.scalar_tensor_tensor(A.v(o_ACC, [[1, 2048]]), Rr, A.v(o_WG + hk, [[1, 1]]), A.v(o_ACC, [[1, 2048]]),
                                                                                 ALU.mult, ALU.add),
                         reads=["R%d" % kb, "WG", "ACC"], writes=["ACC"])
                P.op("vector", lambda e: e.tensor_tensor(A.v(o_ACC, [[1, 2048]]), A.v(o_ACC, [[1, 2048]]), GT, ALU.mult), reads=["ACC", "GT"], writes=["ACC"])
                P.op("gpsimd", lambda e: e.tensor_tensor(A.v(o_ACC, [[1, 2048]]), A.v(o_ACC, [[1, 2048]]), A.v(o_xr, [[1, 2048]]), ALU.add), reads=["ACC", "xr"], writes=["ACC"])
                P.dma(lambda e, t0=t0: e.dma_start(out=out_ap[t0:t0 + 128, :], in_=A.v(o_ACC, [[1, 2048]])), reads=["ACC"], writes=[out_key], is_output=is_output)
            P.barrier()

        def stage_hy_conv3(g, T, L):
            A.reset()
            o_cw = A.f32(3 * 2048); o_cb = A.f32(2048); o_acc = A.f32(2048)
            o_ld = [A.f32(2048), A.f32(2048)]
            o_vb = A.f32(1024)
            vb = A.bf(o_vb, 1024)
            tps = L // 128
            for j in range(3):
                P.dma(lambda e, j=j: e.dma_start(out=A.v(o_cw, [[2048, 3], [1, 2048]]),
                                                 in_=bass.AP(c_conv, j * 2048, [[0, 128], [6144, 3], [1, 2048]])), writes=["CW"])
                P.dma(lambda e, j=j: e.dma_start(out=A.v(o_cb, [[1, 2048]]), in_=bc_rows(c_convb, j * 2048, 2048)), writes=["CB"])
                for ti in range(T // 128):
                    t0 = ti * 128
                    first = (ti % tps == 0); last = (ti % tps == tps - 1)
                    ACC = A.v(o_acc, [[1, 2048]])
                    c0 = j * 2048
                    ld = A.v(o_ld[0], [[1, 2048]])
                    P.dma(lambda e, ld=ld, t0=t0, c0=c0: e.dma_start(out=ld, in_=uz[g].ap()[t0:t0 + 128, c0:c0 + 2048]), reads=[uz[g].name], writes=["ld0"])
                    P.op("vector", lambda e, ld=ld: e.tensor_tensor(ACC, ld, A.v(o_cw + 2048, [[1, 2048]]), ALU.mult), reads=["ld0", "CW"], writes=["ACC"])
                    ld = A.v(o_ld[1], [[1, 2048]])
                    if first:
                        P.op("gpsimd", lambda e, ld=ld: e.memset(ld, 0.0), writes=["ld1"])
                        P.dma(lambda e, t0=t0, c0=c0: e.dma_start(out=A.v(o_ld[1], [[1, 2048]], p0=1, np_=127), in_=uz[g].ap()[t0:t0 + 127, c0:c0 + 2048]),
                              reads=[uz[g].name], writes=["ld1"])
                    else:
                        P.dma(lambda e, ld=ld, t0=t0, c0=c0: e.dma_start(out=ld, in_=uz[g].ap()[t0 - 1:t0 + 127, c0:c0 + 2048]), reads=[uz[g].name], writes=["ld1"])
                    P.op("gpsimd", lambda e, ld=ld: e.tensor_tensor(ld, ld, A.v(o_cw, [[1, 2048]]), ALU.mult), reads=["ld1", "CW"], writes=["ld1"])
                    P.op("vector", lambda e, ld=ld: e.tensor_tensor(ACC, ACC, ld, ALU.add), reads=["ld1", "ACC"], writes=["ACC"])
                    ld = A.v(o_ld[0], [[1, 2048]])
                    if last:
                        P.op("gpsimd", lambda e, ld=ld: e.memset(ld, 0.0), writes=["ld0"])
                        P.dma(lambda e, t0=t0, c0=c0: e.dma_start(out=A.v(o_ld[0], [[1, 2048]], p0=0, np_=127), in_=uz[g].ap()[t0 + 1:t0 + 128, c0:c0 + 2048]),
                              reads=[uz[g].name], writes=["ld0"])
                    else:
                        P.dma(lambda e, ld=ld, t0=t0, c0=c0: e.dma_start(out=ld, in_=uz[g].ap()[t0 + 1:t0 + 129, c0:c0 + 2048]), reads=[uz[g].name], writes=["ld0"])
                    P.op("gpsimd", lambda e, ld=ld: e.tensor_tensor(ld, ld, A.v(o_cw + 4096, [[1, 2048]]), ALU.mult), reads=["ld0", "CW"], writes=["ld0"])
                    P.op("vector", lambda e, ld=ld: e.tensor_tensor(ACC, ACC, ld, ALU.add), reads=["ld0", "ACC"], writes=["ACC"])
                    if j < 2:
                        P.op("vector", lambda e: e.tensor_tensor(ACC, ACC, A.v(o_cb, [[1, 2048]]), ALU.add), reads=["ACC", "CB"], writes=["ACC"])
                        P.dma(lambda e, t0=t0, c0=c0: e.dma_start(out=ug[g].ap()[t0:t0 + 128, c0:c0 + 2048], in_=ACC), reads=["ACC"], writes=[ug[g].name])
                    else:
                        P.op("vector", lambda e: e.tensor_tensor(vb, ACC, A.v(o_cb, [[1, 2048]]), ALU.add), reads=["ACC", "CB"], writes=["vb"])
                        P.dma(lambda e, t0=t0: e.dma_start(out=zb16[g][0].ap()[t0:t0 + 128, :], in_=vb), reads=["vb"], writes=[zb16[g][0].name])
            P.barrier()

        def sin_wrapped(dst, arg, m1, m2, key, okey):
            PI = math.pi
            for _ in range(2):
                P.op("vector", lambda e: e.tensor_scalar(m1, arg, -PI, 2 * PI, ALU.is_lt, ALU.mult), reads=[key], writes=[key + "m1"])
                P.op("vector", lambda e: e.tensor_scalar(m2, arg, PI, 2 * PI, ALU.is_gt, ALU.mult), reads=[key], writes=[key + "m2"])
                P.op("vector", lambda e: e.tensor_tensor(arg, arg, m1, ALU.add), reads=[key, key + "m1"], writes=[key])
                P.op("vector", lambda e: e.tensor_tensor(arg, arg, m2, ALU.subtract), reads=[key, key + "m2"], writes=[key])
            P.op("scalar", lambda e: e.activation(dst, arg, AF.Sin), reads=[key], writes=[okey])

        def stage_hy_filters(li, L):
            A.reset()
            NT = L // 128
            o_zT = A.f32(L); o_fw3 = A.f32(8192); o_h1 = A.f32(L); o_h2 = A.f32(L)
            o_FT = [A.f32(2048) for _ in range(4)]; o_SS = [A.f32(2048), A.f32(2048)]
            o_WIN = A.f32(2048); o_DEL = A.f32(2048); o_sq = A.f32(512)
            o_fw1 = A.f32(64); o_fw2 = A.f32(64); o_col = A.f32(8); o_arg = A.f32(512); o_m1 = A.f32(512); o_m2 = A.f32(512)
            o_tn = A.f32(NT); o_one = A.f32(1); o_row = A.f32(4096); o_hs = A.f32(1024)
            hsb = A.bf(o_hs, 1024)
            zT = A.v(o_zT, [[1, L]], np_=33)
            P.dma(lambda e: e.dma_start(out=zT, in_=zposT[li].ap()), writes=["zT"])
            P.dma(lambda e: e.dma_start(out=A.v(o_fw1, [[1, 64]], np_=33), in_=c_fw1.ap()), writes=["fw1"])
            P.dma(lambda e: e.dma_start(out=A.v(o_fw2, [[1, 64]], np_=64), in_=c_fw2.ap()), writes=["fw2"])
            P.dma(lambda e: e.dma_start(out=A.v(o_fw3, [[1, 8192]], np_=64), in_=c_fw3.ap()), writes=["fw3"])
            for i, t_ in enumerate((c_fb1, c_freq, c_fb2)):
                P.dma(lambda e, i=i, t_=t_: e.dma_start(out=A.v(o_col + i, [[1, 1]], np_=64), in_=bass.AP(t_, 0, [[1, 64], [1, 1]])), writes=["col"])
            P.dma(lambda e: e.dma_start(out=A.v(o_DEL, [[1, 2048]]), in_=bc_rows(deltas, 0, 2048)), writes=["DEL"])
            P.dma(lambda e: e.dma_start(out=A.v(o_tn, [[1, NT]]), in_=tneg[li].ap().rearrange("(n p) -> p n", p=128)), writes=["tn"])
            P.op("vector", lambda e: e.memset(A.v(o_one, [[1, 1]]), 1.0), writes=["one"])
            for n in range(2):
                P.op("vector", lambda e, n=n: e.memset(A.v(o_SS[n], [[1, 2048]]), 0.0), writes=["SS%d" % n])
            nb = max(1, L // 512)
            bw = min(512, L)
            for layer_i, (o_src, o_dst, o_w, kk, bcol, skey, wkey, okey) in enumerate(
                    ((o_zT, o_h1, o_fw1, 33, 0, "zT", "fw1", "h1"), (o_h1, o_h2, o_fw2, 64, 2, "h1", "fw2", "h2"))):
                for b in range(nb):
                    P.op("tensor", lambda e, b=b, o_src=o_src, o_w=o_w, kk=kk: e.matmul(psb[b % 2][0:64, 0:bw], A.v(o_w, [[1, 64]], np_=kk),
                                                                                      A.v(o_src + b * bw, [[1, bw]], np_=kk), start=True, stop=True),
                         reads=[skey, wkey], writes=["pf%d" % (b % 2)])
                    arg = A.v(o_arg, [[1, bw]], np_=64)
                    P.op("vector", lambda e, b=b, arg=arg, bcol=bcol: e.tensor_scalar(arg, psb[b % 2][0:64, 0:bw], A.v(o_col + bcol, [[1, 1]], np_=64),
                                                                                     A.v(o_col + 1, [[1, 1]], np_=64), ALU.add, ALU.mult),
                         reads=["pf%d" % (b % 2), "col"], writes=["arg"])
                    sin_wrapped(A.v(o_dst + b * bw, [[1, bw]], np_=64), arg, A.v(o_m1, [[1, bw]], np_=64), A.v(o_m2, [[1, bw]], np_=64), "arg", okey)
            for lt in range(NT):
                P.op("scalar", lambda e, lt=lt: e.activation(A.v(o_WIN, [[1, 2048]]), A.v(o_DEL, [[1, 2048]]), AF.Exp, scale=A.v(o_tn + lt, [[1, 1]])),
                     reads=["DEL", "tn"], writes=["WIN"])
                for cc in range(16):
                    nd = cc // 4; cch = cc % 4
                    P.op("tensor", lambda e, lt=lt, cc=cc: e.matmul(psb[cc % 4][:, :], A.v(o_h2 + lt * 128, [[1, 128]], np_=64),
                                                                    A.v(o_fw3 + cc * 512, [[1, 512]], np_=64), start=True, stop=True),
                         reads=["h2", "fw3"], writes=["pf%d" % (cc % 4)])
                    ft = A.v(o_FT[nd] + cch * 512, [[1, 512]])
                    P.op("vector", lambda e, cc=cc, ft=ft, cch=cch: e.tensor_tensor(ft, psb[cc % 4][:, :], A.v(o_WIN + cch * 512, [[1, 512]]), ALU.mult),
                         reads=["pf%d" % (cc % 4), "WIN"], writes=["FT%d" % nd])
                    P.op("gpsimd", lambda e, ft=ft: e.tensor_tensor(A.v(o_sq, [[1, 512]]), ft, ft, ALU.mult), reads=["FT%d" % nd], writes=["sq"])
                    ssv = A.v(o_SS[nd // 2] + cch * 512, [[1, 512]])
                    P.op("vector", lambda e, ssv=ssv: e.tensor_tensor(ssv, ssv, A.v(o_sq, [[1, 512]]), ALU.add), reads=["sq", "SS%d" % (nd // 2)], writes=["SS%d" % (nd // 2)])
                for n in range(2):
                    hf = A.v(o_FT[2 * n], [[1, 2048]]); hbk = A.v(o_FT[2 * n + 1], [[1, 2048]])
                    if lt == 0:
                        P.op("vector", lambda e, n=n: e.memset(A.v(o_FT[2 * n + 1], [[1, 2048]], np_=1), 0.0), reads=["FT%d" % (2 * n + 1)], writes=["FT%d" % (2 * n + 1)])
                    P.op("vector", lambda e, hf=hf, hbk=hbk: e.tensor_tensor(hsb[:, 0:2048], hf, hbk, ALU.add), reads=["FT%d" % (2 * n), "FT%d" % (2 * n + 1)], writes=["hs"])
                    P.dma(lambda e, lt=lt, n=n: e.dma_start(out=HS[li].ap()[lt * 128:(lt + 1) * 128, 2 * n, :], in_=hsb[:, 0:2048]), reads=["hs"], writes=["HS"])
                    P.op("vector", lambda e, hf=hf, hbk=hbk: e.tensor_tensor(hsb[:, 0:2048], hbk, hf, ALU.subtract), reads=["FT%d" % (2 * n), "FT%d" % (2 * n + 1)], writes=["hs"])
                    P.dma(lambda e, lt=lt, n=n: e.dma_start(out=HS[li].ap()[lt * 128:(lt + 1) * 128, 2 * n + 1, :], in_=hsb[:, 0:2048]), reads=["hs"], writes=["HS"])
            for n in range(2):
                for cch in range(4):
                    P.op("tensor", lambda e, n=n, cch=cch: e.matmul(psb[cch][0:1, :], A.v(o_one, [[1, 1]]), A.v(o_SS[n] + cch * 512, [[1, 512]]), start=True, stop=True),
                         reads=["one", "SS%d" % n], writes=["pf%d" % cch])
                    P.op("vector", lambda e, n=n, cch=cch: e.tensor_copy(A.v(o_row + n * 2048 + cch * 512, [[1, 512]], np_=1), psb[cch][0:1, :]),
                         reads=["pf%d" % cch], writes=["row"])
            rsqrt_ops(P, A.v(o_row, [[1, 4096]], np_=1), 1.0, 1e-12, "row")
            P.dma(lambda e: e.dma_start(out=bass.AP(nsc[li], 0, [[4096, 1], [1, 4096]]), in_=A.v(o_row, [[1, 4096]], np_=1)),
                  reads=["row"], writes=["nsc"])
            P.barrier()

        def hy_forward(li, L, Zc, o_mat, kt, data_keys, tag):
            NT = L // 128
            kb = kt % 2
            Cm = A.bf(o_mat[kb][0], NT * 64).rearrange("p (t k) -> p t k", t=NT)
            Sm = A.bf(o_mat[kb][1], NT * 64).rearrange("p (t k) -> p t k", t=NT)
            P.dma(lambda e: e.dma_start(out=Cm, in_=dftc[li].ap()[:, kt * 128:(kt + 1) * 128].rearrange("(t p) k -> p t k", p=128)), writes=["Cm%d" % kb])
            P.dma(lambda e: e.dma_start(out=Sm, in_=dfts[li].ap()[:, kt * 128:(kt + 1) * 128].rearrange("(t p) k -> p t k", p=128)), writes=["Sm%d" % kb])
            Zcos, Zsin = Zc if isinstance(Zc, tuple) else (Zc, Zc)
            pa = psb[2 * kb]; pbk = psb[2 * kb + 1]
            for tt in range(NT):
                P.op("tensor", lambda e, tt=tt: e.matmul(pa[:, :], Cm[:, tt, :], Zcos[:, tt, :], start=(tt == 0), stop=(tt == NT - 1)),
                     reads=["Cm%d" % kb] + data_keys, writes=["pA%d" % kb])
            for tt in range(NT):
                P.op("tensor", lambda e, tt=tt: e.matmul(pbk[:, :], Sm[:, tt, :], Zsin[:, tt, :], start=(tt == 0), stop=(tt == NT - 1)),
                     reads=["Sm%d" % kb] + data_keys, writes=["pB%d" % kb])
            return pa, pbk, kb

        def stage_hy_fspec(li, L):
            A.reset()
            NT = L // 128
            o_hs = A.f32(NT * 256); o_hd = A.f32(NT * 256)
            o_mat = [[A.f32(NT * 64), A.f32(NT * 64)], [A.f32(NT * 64), A.f32(NT * 64)]]
            o_wk = A.f32(NT); o_o = [A.f32(512), A.f32(512)]
            P.dma(lambda e: e.dma_start(out=A.v(o_wk, [[1, NT]]), in_=wkv[li].ap().rearrange("(n p) -> p n", p=128)), writes=["wk"])
            for n in range(2):
                for cch in range(4):
                    hsT = A.bf(o_hs, NT * 256).rearrange("p (t c) -> p t c", t=NT)
                    hdT = A.bf(o_hd, NT * 256).rearrange("p (t c) -> p t c", t=NT)
                    P.dma(lambda e, n=n, cch=cch, hsT=hsT: e.dma_start(out=hsT, in_=HS[li].ap()[:, 2 * n, cch * 512:(cch + 1) * 512].rearrange("(t p) c -> p t c", p=128)),
                          reads=["HS"], writes=["hsT"])
                    P.dma(lambda e, n=n, cch=cch, hdT=hdT: e.dma_start(out=hdT, in_=HS[li].ap()[:, 2 * n + 1, cch * 512:(cch + 1) * 512].rearrange("(t p) c -> p t c", p=128)),
                          reads=["HS"], writes=["hdT"])
                    for kt in range(NT):
                        pa, pbk, kb = hy_forward(li, L, (hsT, hdT), o_mat, kt, ["hsT", "hdT"], "f")
                        for j, (pp, key) in enumerate(((pa, "pA%d" % kb), (pbk, "pB%d" % kb))):
                            ot = A.v(o_o[j], [[1, 512]])
                            P.op("vector", lambda e, ot=ot, pp=pp, kt=kt: e.tensor_scalar(ot, pp[:, :], A.v(o_wk + kt, [[1, 1]]), None, ALU.mult),
                                 reads=[key, "wk"], writes=["o%d" % j])
                            P.dma(lambda e, ot=ot, n=n, j=j, kt=kt, cch=cch: e.dma_start(out=FS[li].ap()[n, j, kt * 128:(kt + 1) * 128, cch * 512:(cch + 1) * 512], in_=ot),
                                  reads=["o%d" % j], writes=["FS"])
            P.barrier()

        def stage_hy_conv(g, T, li, L, seq, n, final):
            A.reset()
            NT = L // 128
            base = seq * L
            o_Z = A.f32(NT * 256); o_YC = A.f32(NT * 256); o_YS = A.f32(NT * 256)
            o_mat = [[A.f32(NT * 64), A.f32(NT * 64)], [A.f32(NT * 64), A.f32(NT * 64)]]
            o_F = [[A.f32(512), A.f32(512)], [A.f32(512), A.f32(512)]]
            o_t = [A.f32(512) for _ in range(4)]
            o_gt = [A.f32(512), A.f32(512)]; o_ns = A.f32(512); o_bs = A.f32(512)
            o_zn = [A.f32(256), A.f32(256)]; o_zT = A.f32(256)
            zin = zb16[g][n]; zout = zb16[g][n + 1]
            for cch in range(4):
                c0 = cch * 512
                Z = A.bf(o_Z, NT * 256).rearrange("p (t c) -> p t c", t=NT)
                YC = A.bf(o_YC, NT * 256).rearrange("p (t c) -> p t c", t=NT)
                YS = A.bf(o_YS, NT * 256).rearrange("p (t c) -> p t c", t=NT)
                P.dma(lambda e, c0=c0, Z=Z: e.dma_start(out=Z, in_=zin.ap()[base:base + L, c0:c0 + 512].rearrange("(t p) c -> p t c", p=128)),
                      reads=[zin.name], writes=["Z"])
                P.dma(lambda e, c0=c0: e.dma_start(out=A.v(o_ns, [[1, 512]]), in_=bc_rows(nsc[li], n * 2048 + c0, 512)), reads=["nsc"], writes=["NS"])
                P.dma(lambda e, c0=c0: e.dma_start(out=A.v(o_bs, [[1, 512]]), in_=bc_rows(c_bias, n * 2048 + c0, 512)), writes=["BS"])
                for kt in range(NT):
                    pa, pbk, kb = hy_forward(li, L, Z, o_mat, kt, ["Z"], "c")
                    Fc = A.v(o_F[kb][0], [[1, 512]]); Fs = A.v(o_F[kb][1], [[1, 512]])
                    P.dma(lambda e, Fc=Fc, kt=kt, c0=c0: e.dma_start(out=Fc, in_=FS[li].ap()[n, 0, kt * 128:(kt + 1) * 128, c0:c0 + 512]), reads=["FS"], writes=["Fc%d" % kb])
                    P.dma(lambda e, Fs=Fs, kt=kt, c0=c0: e.dma_start(out=Fs, in_=FS[li].ap()[n, 1, kt * 128:(kt + 1) * 128, c0:c0 + 512]), reads=["FS"], writes=["Fs%d" % kb])
                    t = [A.v(o, [[1, 512]]) for o in o_t]
                    P.op("vector", lambda e, t=t, pa=pa, Fc=Fc: e.tensor_tensor(t[0], pa[:, :], Fc, ALU.mult), reads=["pA%d" % kb, "Fc%d" % kb], writes=["t0"])
                    P.op("vector", lambda e, t=t, pbk=pbk, Fs=Fs: e.tensor_tensor(t[1], pbk[:, :], Fs, ALU.mult), reads=["pB%d" % kb, "Fs%d" % kb], writes=["t1"])
                    P.op("vector", lambda e, t=t, pbk=pbk, Fc=Fc: e.tensor_tensor(t[2], pbk[:, :], Fc, ALU.mult), reads=["pB%d" % kb, "Fc%d" % kb], writes=["t2"])
                    P.op("vector", lambda e, t=t, pa=pa, Fs=Fs: e.tensor_tensor(t[3], pa[:, :], Fs, ALU.mult), reads=["pA%d" % kb, "Fs%d" % kb], writes=["t3"])
                    P.op("gpsimd", lambda e, t=t, kt=kt, YC=YC: e.tensor_tensor(YC[:, kt, :], t[0], t[1], ALU.add), reads=["t0", "t1"], writes=["YC"])
                    P.op("gpsimd", lambda e, t=t, kt=kt, YS=YS: e.tensor_tensor(YS[:, kt, :], t[2], t[3], ALU.subtract), reads=["t2", "t3"], writes=["YS"])
                for tt in range(NT):
                    kb = tt % 2
                    Cm = A.bf(o_mat[kb][0], NT * 64).rearrange("p (t k) -> p t k", t=NT)
                    Sm = A.bf(o_mat[kb][1], NT * 64).rearrange("p (t k) -> p t k", t=NT)
                    P.dma(lambda e, Cm=Cm, tt=tt: e.dma_start(out=Cm, in_=dftc[li].ap()[:, tt * 128:(tt + 1) * 128].rearrange("(t p) k -> p t k", p=128)), writes=["Cm%d" % kb])
                    P.dma(lambda e, Sm=Sm, tt=tt: e.dma_start(out=Sm, in_=dfts[li].ap()[:, tt * 128:(tt + 1) * 128].rearrange("(t p) k -> p t k", p=128)), writes=["Sm%d" % kb])
                    py = psb[4 + kb]
                    for kt in range(NT):
                        P.op("tensor", lambda e, kt=kt, Cm=Cm, YC=YC, py=py: e.matmul(py[:, :], Cm[:, kt, :], YC[:, kt, :], start=(kt == 0), stop=False),
                             reads=["Cm%d" % kb, "YC"], writes=["pY%d" % kb])
                    for kt in range(NT):
                        P.op("tensor", lambda e, kt=kt, Sm=Sm, YS=YS, py=py: e.matmul(py[:, :], Sm[:, kt, :], YS[:, kt, :], start=False, stop=(kt == NT - 1)),
                             reads=["Sm%d" % kb, "YS"], writes=["pY%d" % kb])
                    gt_ = A.v(o_gt[kb], [[1, 512]])
                    t0 = base + tt * 128
                    P.dma(lambda e, gt_=gt_, t0=t0, c0=c0: e.dma_start(out=gt_, in_=ug[g].ap()[t0:t0 + 128, n * 2048 + c0:n * 2048 + c0 + 512]), reads=[ug[g].name], writes=["gt%d" % kb])
                    ta = A.v(o_t[0], [[1, 512]]); tb = A.v(o_t[1], [[1, 512]])
                    P.op("vector", lambda e, ta=ta, py=py: e.tensor_tensor(ta, py[:, :], A.v(o_ns, [[1, 512]]), ALU.mult), reads=["pY%d" % kb, "NS"], writes=["t0"])
                    P.op("gpsimd", lambda e, tb=tb, tt=tt, Z=Z: e.tensor_tensor(tb, Z[:, tt, :], A.v(o_bs, [[1, 512]]), ALU.mult), reads=["Z", "BS"], writes=["t1"])
                    P.op("vector", lambda e, ta=ta, tb=tb: e.tensor_tensor(ta, ta, tb, ALU.add), reads=["t0", "t1"], writes=["t0"])
                    zn = A.bf(o_zn[kb], 256)
                    P.op("vector", lambda e, ta=ta, gt_=gt_, zn=zn: e.tensor_tensor(zn, ta, gt_, ALU.mult), reads=["t0", "gt%d" % kb], writes=["zn%d" % kb])
                    if not final:
                        P.dma(lambda e, zn=zn, t0=t0, c0=c0: e.dma_start(out=zout.ap()[t0:t0 + 128, c0:c0 + 512], in_=zn), reads=["zn%d" % kb], writes=[zout.name])
                    else:
                        zT4 = A.bf(o_zT, 256).rearrange("p (c t) -> p c t", c=4)
                        for c4 in range(4):
                            P.op("tensor", lambda e, c4=c4, zn=zn: e.transpose(psT[:, c4 * 128:(c4 + 1) * 128], zn[:, c4 * 128:(c4 + 1) * 128], identb),
                                 reads=["zn%d" % kb, "identb"], writes=["psT"])
                        P.op("scalar", lambda e, zT4=zT4: e.activation(zT4, psT[:, 0:512].rearrange("p (c t) -> p c t", c=4), AF.Copy), reads=["psT"], writes=["zT4"])
                        P.dma(lambda e, zT4=zT4, t0=t0, cch=cch: e.dma_start(out=hzT[g].ap()[cch * 4:(cch + 1) * 4, :, t0:t0 + 128].rearrange("c p t -> p c t"), in_=zT4),
                              reads=["zT4"], writes=[hzT[g].name])
            P.barrier()

        def stage_gather_rows(src_t, dst_t, n):
            A.reset()
            o_i = A.f32(8); o_r = [A.f32(2048), A.f32(2048)]
            idx = bass.AP(A.t, o_i, [[A.size, 128], [1, 8]]).bitcast(I32)
            P.dma(lambda e: e.dma_start(out=idx, in_=qidx.ap().rearrange("(n p) o -> p (n o)", p=128)), writes=["qi"])
            for ti in range(n // 128):
                k = ti % 2
                Rr = A.v(o_r[k], [[1, 2048]])
                P.dma(lambda e, Rr=Rr, ti=ti: e.indirect_dma_start(out=Rr, out_offset=None, in_=src_t.ap(),
                                                                 in_offset=bass.IndirectOffsetOnAxis(ap=idx[:, ti:ti + 1], axis=0)),
                      reads=["qi"], writes=["gr%d" % k], q="gpsimd")
                P.dma(lambda e, Rr=Rr, ti=ti: e.dma_start(out=dst_t.ap()[ti * 128:(ti + 1) * 128, :], in_=Rr), reads=["gr%d" % k], writes=[dst_t.name])
            P.barrier()

        def stage_kv_out():
            A.reset()
            o_z = A.f32(512); o_t = A.f32(256); o_s = A.f32(8); o_kn = A.f32(128)
            KN = A.v(o_kn, [[1, 128]])
            P.dma(lambda e: e.dma_start(out=KN, in_=bc_rows(b_kn, 0, 128)), writes=["KN"])
            for ti in range(TP // 128):
                t0 = ti * 128
                zt = A.v(o_z, [[1, 512]])
                P.dma(lambda e, t0=t0: e.dma_start(out=zt, in_=zg[0].ap()[t0:t0 + 128, 4480:4992]), reads=["zp"], writes=["zt"])
                P.dma(lambda e, t0=t0: e.dma_start(out=nv.ap()[t0:t0 + 128, :], in_=A.v(o_z + 256, [[1, 256]])), reads=["zt"], writes=["nv"], is_output=True)
                P.op("vector", lambda e: e.tensor_tensor(A.v(o_t, [[1, 256]]), A.v(o_z, [[1, 256]]), A.v(o_z, [[1, 256]]), ALU.mult), reads=["zt"], writes=["t"])
                P.op("vector", lambda e: e.tensor_reduce(A.v(o_s, [[1, 2]]), A.v(o_t, [[128, 2], [1, 128]]), AX.X, ALU.add), reads=["t"], writes=["s"])
                rsqrt_ops(P, A.v(o_s, [[1, 2]]), 1.0 / 128, 1e-6, "s")
                P.op("vector", lambda e: e.tensor_tensor(A.v(o_t, [[128, 2], [1, 128]]), A.v(o_z, [[128, 2], [1, 128]]), A.v(o_s, [[1, 2], [0, 128]]), ALU.mult),
                     reads=["s", "zt"], writes=["t"])
                P.op("vector", lambda e: e.tensor_tensor(A.v(o_t, [[128, 2], [1, 128]]), A.v(o_t, [[128, 2], [1, 128]]), A.v(o_kn, [[0, 2], [1, 128]]), ALU.mult),
                     reads=["t", "KN"], writes=["t"])
                P.dma(lambda e, t0=t0: e.dma_start(out=nk.ap()[t0:t0 + 128, :], in_=A.v(o_t, [[1, 256]])), reads=["t"], writes=["nk"], is_output=True)
            P.barrier()

        GROUPS = ((0, TP, 256), (1, TS, 4096))

        def layer1(g, T, L, out_ap, out_key):
            li = 0 if L == 256 else 1
            stage_norm_gemm(g, T, x2[g].ap(), 1, 1, norm1, odd_w_in.ap(), 6144, uz[g])
            stage_hy_conv3(g, T, L)
            for n in range(2):
                for seq in range(T // L):
                    stage_hy_conv(g, T, li, L, seq, n, n == 1)
            stage_norm_gemm(g, T, None, 1, 1, None, odd_w_out.ap(), D, x3[g], srcT=hzT[g], resid=(x2[g].ap(), 2 * D))
            if mode == "hytest":
                return
            if g == 1:
                stage_gather_rows(x3[1], x3q, 1024)
                stage_norm_gemm(g, 1024, x3q.ap(), 1, 2, norm2, peer_wq.ap()[1], D, pq[g], hm_out=hm2[g])
                stage_peer(g, 1024, 1, pq[g], hm2[g], x3q.ap(), out_ap, out_key, is_output=True)
                return
            stage_norm_gemm(g, T, x3[g].ap(), 1, 2, norm2, peer_wq.ap()[1], D, pq[g], hm_out=hm2[g])
            stage_peer(g, T, 1, pq[g], hm2[g], x3[g].ap(), out_ap, out_key, is_output=True)

        stage_mod()
        for li, L_ in enumerate((256, 4096)):
            stage_hy_filters(li, L_)
            stage_hy_fspec(li, L_)
        if mode == "full":
            for g, T, L in GROUPS:
                latent = (g == 1)
                stage_norm_gemm(g, T, xg[g].ap(), 0, 1, norm1, even_w_in.ap(), 4992, zg[g])
                if g == 0:
                    stage_kv_out()
                stage_even_prep(g, T, L)
                for seq in range(T // L):
                    for d in range(2):
                        stage_scan(g, seq, L, d, st_in if latent else None, None if latent else ns)
                stage_rwkv_post(g, T)
                stage_attn_prep(g, T, L, latent)
                for seq in range(T // L):
                    stage_attn(g, T, seq, L, L + (512 if latent else 0))
                stage_norm_gemm(g, T, None, 0, 1, None, even_w_out.ap(), D, x1[g], srcT=ycatT[g], resid=(xg[g].ap(), 2 * D))
                stage_norm_gemm(g, T, x1[g].ap(), 0, 2, norm2, peer_wq.ap()[0], D, pq[g], hm_out=hm2[g])
                stage_peer(g, T, 0, pq[g], hm2[g], x1[g].ap(), x2[g].ap(), x2[g].name)
        layer1(0, TP, 256, yp.ap(), "yp")
        layer1(1, TS, 4096, ys.ap(), "ys")
        P.emit()
    return nc


_CACHE = {}


def _rope_tables():
    pos = np.arange(4096)
    row = (pos // 64).astype(np.float32); col = (pos % 64).astype(np.float32)
    inv = (np.float32(10000.0) ** (-np.arange(32, dtype=np.float32) / np.float32(32))).astype(np.float32)
    ar = (row[:, None] * inv[None, :]).astype(np.float32); ac = (col[:, None] * inv[None, :]).astype(np.float32)
    cs = np.concatenate([np.cos(ar), np.cos(ac)], 1).astype(np.float32)
    sn = np.concatenate([np.sin(ar), np.sin(ac)], 1).astype(np.float32)
    return cs, sn


def _hyena_consts():
    import ml_dtypes
    out = {}
    for li, L in enumerate((256, 4096)):
        t = np.linspace(0.0, 1.0, L, dtype=np.float32)[:, None]
        wpos = (np.float32(2.0 * math.pi) * np.arange(L, dtype=np.float32)[:, None] / np.float32(L)).astype(np.float32)
        fb = np.linspace(1e-4, 15, 16, dtype=np.float32)[None, :]
        z = np.concatenate([t, np.cos(fb * wpos), -np.sin(fb * wpos)], axis=-1).astype(np.float32)
        out["zposT_%d" % li] = np.ascontiguousarray(z.T)
        out["tneg_%d" % li] = np.ascontiguousarray(-t[:, 0])
        Nf = 2 * L - 1
        wk = np.full((L,), 2.0 / Nf, np.float32); wk[0] = 1.0 / Nf
        out["wk_%d" % li] = wk
        idx = np.arange(L, dtype=np.int64)
        ang = (2.0 * np.pi / Nf) * ((idx[:, None] * idx[None, :]) % Nf).astype(np.float64)
        out["dftc_%d" % li] = np.cos(ang).astype(ml_dtypes.bfloat16)
        out["dfts_%d" % li] = np.sin(ang).astype(ml_dtypes.bfloat16)
    out["deltas"] = np.abs(np.linspace(math.log(1e-2) / 1.5, math.log(1e-2) / 0.3, 2048, dtype=np.float32)).astype(np.float32)
    return out


def _shared_inputs(inputs):
    f = lambda a: np.ascontiguousarray(np.asarray(a, dtype=np.float32))
    sh = {
        "mod_w": f(inputs["mod_w"]), "mod_b": f(inputs["mod_b"]),
        "norm1": f(inputs["norm1"]), "norm2": f(inputs["norm2"]),
        "even_w_in": f(inputs["even_w_in"][0]),
        "even_a_conv": f(inputs["even_a_conv"][0]),
        "even_a_w0": f(inputs["even_a_w0"][0]), "even_a_wu": f(inputs["even_a_wu"][0]).reshape(128, 1024),
        "even_a_a0": f(inputs["even_a_a0"][0]), "even_a_au": f(inputs["even_a_au"][0]).reshape(128, 1024),
        "even_a_gu": f(inputs["even_a_gu"][0]),
        "even_a_kk": f(inputs["even_a_kk"][0]), "even_a_ka": f(inputs["even_a_ka"][0]),
        "even_a_rk": f(inputs["even_a_rk"][0]).reshape(1024),
        "even_a_ln_w": f(inputs["even_a_ln_w"][0]), "even_a_ln_b": f(inputs["even_a_ln_b"][0]),
        "even_b_qnorm": f(inputs["even_b_qnorm"][0]), "even_b_knorm": f(inputs["even_b_knorm"][0]),
        "ident": np.eye(128, dtype=np.float32),
        "ropec": _rope_tables()[0], "ropes": _rope_tables()[1],
        "even_w_out": f(inputs["even_w_out"][0]),
        "odd_w_in": f(inputs["odd_w_in"][0]), "odd_w_out": f(inputs["odd_w_out"][0]),
        "odd_c_conv": f(inputs["odd_c_conv"][0]), "odd_c_conv_b": f(inputs["odd_c_conv_b"][0]),
        "odd_c_fw1": f(inputs["odd_c_fw1"][0]), "odd_c_fb1": f(inputs["odd_c_fb1"][0]), "odd_c_freq": f(inputs["odd_c_freq"][0]),
        "odd_c_fw2": f(inputs["odd_c_fw2"][0]), "odd_c_fb2": f(inputs["odd_c_fb2"][0]), "odd_c_fw3": f(inputs["odd_c_fw3"][0]),
        "odd_c_bias": f(inputs["odd_c_bias"][0]),
        "peer_wq": f(inputs["peer_wq"]), "peer_keys": f(inputs["peer_keys"]),
        "peer_u0": f(inputs["peer_u"][0]), "peer_u1": f(inputs["peer_u"][1]),
        "peer_v0": f(inputs["peer_v"][0]), "peer_v1": f(inputs["peer_v"][1]),
    }
    sh.update(_hyena_consts())
    kc = np.zeros((3, 128, 256), np.uint32)
    kc[0] = 0xFFFFFF00
    kc[1] = 0xFF
    kc[2] = (255 - (np.arange(256) % 256)).astype(np.uint32)[None, :]
    sh["kconst"] = kc
    return sh


def kernel(**inputs):
    f = lambda a: np.ascontiguousarray(np.asarray(a, dtype=np.float32))
    if "nc" not in _CACHE:
        _CACHE["nc"] = build_program()
    nc = _CACHE["nc"]
    x_prompt = f(inputs["x_prompt"]); x_sample = f(inputs["x_sample"])
    shared = _shared_inputs(inputs)
    in_maps = []
    for c in range(NC):
        b = c // 4
        m = dict(shared)
        m["xp"] = x_prompt[4 * c:4 * c + 4].reshape(1024, D)
        m["xs"] = x_sample[b]
        m["ck"] = f(inputs["cache_b_k"][b, 0]).reshape(512, 256)
        m["cv"] = f(inputs["cache_b_v"][b, 0]).reshape(512, 256)
        m["st"] = f(inputs["state_a"][b, 0])
        m["cond"] = np.stack([f(inputs["c_ctx"]), f(inputs["c"][b])], 0)
        m["qidx"] = (np.arange(1024, dtype=np.int32) + 1024 * (c % 4)).reshape(1024, 1)
        in_maps.append(m)
    res = run_bass_kernel_spmd(nc, in_maps, core_ids=list(range(NC)))
    R = res.results
    if DEBUG_OUT:
        DEBUG_RES['R'] = R
    y_prompt = np.concatenate([R[c]["yp"].reshape(4, 256, D) for c in range(NC)], 0)
    y_sample = np.stack([np.concatenate([R[4 * b + q]["ys"] for q in range(4)], 0) for b in range(2)], 0)
    new_k = np.concatenate([R[c]["nk"].reshape(4, 1, 256, 2, 128) for c in range(NC)], 0)
    new_v = np.concatenate([R[c]["nv"].reshape(4, 1, 256, 2, 128) for c in range(NC)], 0)
    new_s = np.concatenate([R[c]["ns"].reshape(4, 1, 2, 16, 64, 64) for c in range(NC)], 0)
    return (y_prompt.astype(np.float32), y_sample.astype(np.float32), new_k.astype(np.float32),
            new_v.astype(np.float32), new_s.astype(np.float32))
```

```python
from contextlib import ExitStack
import math
import numpy as np
import concourse.bass as bass
import concourse.mybir as mybir
from concourse.bass_utils import run_bass_kernel_spmd

F32 = mybir.dt.float32
BF16 = mybir.dt.bfloat16
I32 = mybir.dt.int32
U32 = mybir.dt.uint32
AF = mybir.ActivationFunctionType
ALU = mybir.AluOpType
AX = mybir.AxisListType

D = 2048
NC = 8
COMPUTE = ("tensor", "vector", "scalar", "gpsimd")
NDMA_SLOTS = 12
SEM_LIMIT = 30000
DEBUG_OUT = set()
UNTRACKED = {"sq", "sq_r", "sq_kk", "sq_w", "sq_kt", "sq_b", "gsc", "bon", "vT", "yT", "nv", "nk", "qT_d", "kT_d", "V_d",
             "HS", "FS", "ns", "yp", "ys", "modv", "nsc", "zero_d"}
SCAN_DVE_INORDER = True
DEBUG_RES = {}


class Prog:
    def __init__(self, nc, stack):
        self.nc = nc
        self.stack = stack
        self.ops = {e: [] for e in COMPUTE + ("sync",)}
        self.nsem = 0
        self.csem = {e: self._newsem("c_" + e) for e in COMPUTE}
        self.ccnt = {e: 0 for e in COMPUTE}
        self.dslots = {}
        self.dnext = {}
        for q in ("sync", "gpsimd"):
            self.dslots[q] = [[self._newsem("d_%s%d" % (q, i)), 0] for i in range(NDMA_SLOTS)]
            self.dnext[q] = 0
        self.last_w = {}
        self.readers = {}
        self.waited = {e: {} for e in self.ops}
        self.out_events = []
        self.all_events = {}
        self.no_self = {"tensor"}
        self.own = {e: {id(self.csem[e])} for e in COMPUTE}

    def _newsem(self, name):
        self.nsem += 1
        return self.stack.enter_context(self.nc.semaphore("%s_%d" % (name, self.nsem)))

    def _deps(self, reads, writes):
        reads = [k for k in reads if k not in UNTRACKED]
        writes = [k for k in writes if k not in UNTRACKED]
        deps = []
        for k in reads:
            if k in self.last_w:
                deps.append(self.last_w[k])
        for k in writes:
            if k in self.last_w:
                deps.append(self.last_w[k])
            deps.extend(self.readers.get(k, ()))
        return deps

    def _emit_waits(self, eng, deps):
        need = {}
        for (s, v) in deps:
            if v > need.get(id(s), (s, 0))[1]:
                need[id(s)] = (s, v)
        w = self.waited[eng]
        skip = self.own.get(eng, ()) if eng in self.no_self else ()
        for sid, (s, v) in need.items():
            if w.get(sid, 0) >= v or sid in skip:
                continue
            w[sid] = v
            self.ops[eng].append(("wait", s, v))

    def _commit(self, ev, reads, writes):
        self.all_events[id(ev[0])] = ev
        reads = [k for k in reads if k not in UNTRACKED]
        writes = [k for k in writes if k not in UNTRACKED]
        for k in reads:
            self.readers.setdefault(k, []).append(ev)
        for k in writes:
            self.last_w[k] = ev
            self.readers[k] = []

    def op(self, eng, fn, reads=(), writes=()):
        deps = self._deps(reads, writes)
        self._emit_waits(eng, deps)
        if self.ccnt[eng] >= SEM_LIMIT:
            self.csem[eng] = self._newsem("c_" + eng)
            self.own[eng].add(id(self.csem[eng]))
            self.ccnt[eng] = 0
        self.ccnt[eng] += 1
        ev = (self.csem[eng], self.ccnt[eng])
        self.ops[eng].append(("op", fn, ev[0], 1))
        self._commit(ev, reads, writes)
        return ev

    def dma(self, fn, reads=(), writes=(), q="sync", is_output=False):
        deps = self._deps(reads, writes)
        slots = self.dslots[q]
        i = self.dnext[q]
        self.dnext[q] = (i + 1) % len(slots)
        slot = slots[i]
        if slot[1] > 0:
            deps.append((slot[0], slot[1]))
        self._emit_waits(q, deps)
        if slot[1] + 16 > SEM_LIMIT:
            slot[0] = self._newsem("d_" + q)
            slot[1] = 0
        slot[1] += 16
        ev = (slot[0], slot[1])
        self.ops[q].append(("op", fn, ev[0], 16))
        self._commit(ev, reads, writes)
        if is_output:
            self.out_events.append(ev)
        return ev

    def barrier(self):
        evs = list(self.all_events.values())
        saved = self.no_self
        self.no_self = set()
        for e in self.ops:
            self._emit_waits(e, evs)
        self.no_self = saved
        self.last_w = {}
        self.readers = {}

    def emit(self):
        nc = self.nc
        self._emit_waits("sync", list(self.all_events.values()))
        with nc.Block() as block:
            def run(engname):
                def body(eng):
                    for item in self.ops[engname]:
                        if item[0] == "wait":
                            eng.wait_ge(item[1], item[2])
                        else:
                            item[1](eng).then_inc(item[2], item[3])
                return body
            block.sync(run("sync"))
            block.tensor(run("tensor"))
            block.vector(run("vector"))
            block.scalar(run("scalar"))
            block.gpsimd(run("gpsimd"))


class Arena:
    def __init__(self, t, size):
        self.t = t
        self.size = size
        self.off = 0

    def reset(self):
        self.off = 0

    def f32(self, n):
        o = self.off
        self.off += n
        assert self.off <= self.size, ("arena overflow", self.off, self.size)
        return o

    def v(self, off, dims, p0=0, np_=128):
        return bass.AP(self.t, p0 * self.size + off, [[self.size, np_]] + [list(d) for d in dims])

    def bf(self, off, n):
        return self.t[:, off:off + n].bitcast(BF16)


def bc_rows(dram_t, elem_off, n, nparts=128):
    return bass.AP(dram_t, elem_off, [[0, nparts], [1, n]])


def rsqrt_ops(P, ap, mul, add, key):
    P.op("vector", lambda e: e.tensor_scalar(ap, ap, mul, add, ALU.mult, ALU.add), reads=[key], writes=[key])
    P.op("scalar", lambda e: e.activation(ap, ap, AF.Sqrt), reads=[key], writes=[key])
    P.op("vector", lambda e: e.reciprocal(ap, ap), reads=[key], writes=[key])

def build_program(mode="full"):
    nc = bass.Bass("TRN2", target_bir_lowering=False)
    TP, TS = 1024, 4096
    dt_in = {}

    def inp(name, shape, dt=F32):
        dt_in[name] = nc.dram_tensor(name, list(shape), dt, kind="ExternalInput")
        return dt_in[name]

    def outp(name, shape, dt=F32):
        return nc.dram_tensor(name, list(shape), dt, kind="ExternalOutput")

    def scr(name, shape, dt=F32):
        UNTRACKED.add(name)
        if name in DEBUG_OUT:
            return nc.dram_tensor(name, list(shape), dt, kind="ExternalOutput")
        return nc.dram_tensor(name, list(shape), dt)

    xg = [inp("xp", [TP, D]), inp("xs", [TS, D])]
    ck = inp("ck", [512, 256]); cv = inp("cv", [512, 256]); st_in = inp("st", [2, 16, 64, 64])
    cond = inp("cond", [2, D])
    mod_w = inp("mod_w", [2, D, 6 * D]); mod_b = inp("mod_b", [2, 6 * D])
    norm1 = inp("norm1", [2, D]); norm2 = inp("norm2", [2, D])
    even_w_in = inp("even_w_in", [D, 4992])
    a_conv = inp("even_a_conv", [3, 3456])
    a_w0 = inp("even_a_w0", [2, 1024]); a_wu = inp("even_a_wu", [128, 1024])
    a_a0 = inp("even_a_a0", [2, 1024]); a_au = inp("even_a_au", [128, 1024])
    a_gu = inp("even_a_gu", [128, 1024])
    a_kk = inp("even_a_kk", [1024]); a_ka = inp("even_a_ka", [1024]); a_rk = inp("even_a_rk", [1024])
    a_lnw = inp("even_a_ln_w", [1024]); a_lnb = inp("even_a_ln_b", [1024])
    b_qn = inp("even_b_qnorm", [128]); b_kn = inp("even_b_knorm", [128])
    ident_d = inp("ident", [128, 128])
    ropec = inp("ropec", [4096, 64]); ropes = inp("ropes", [4096, 64])
    even_w_out = inp("even_w_out", [D, D])
    odd_w_in = inp("odd_w_in", [D, 6144]); odd_w_out = inp("odd_w_out", [D, D])
    c_conv = inp("odd_c_conv", [3, 6144]); c_convb = inp("odd_c_conv_b", [6144])
    c_fw1 = inp("odd_c_fw1", [33, 64]); c_fb1 = inp("odd_c_fb1", [64]); c_freq = inp("odd_c_freq", [64])
    c_fw2 = inp("odd_c_fw2", [64, 64]); c_fb2 = inp("odd_c_fb2", [64]); c_fw3 = inp("odd_c_fw3", [64, 8192])
    c_bias = inp("odd_c_bias", [2, 2048])
    zposT = [inp("zposT_%d" % li, [33, L_]) for li, L_ in enumerate((256, 4096))]
    tneg = [inp("tneg_%d" % li, [L_]) for li, L_ in enumerate((256, 4096))]
    wkv = [inp("wk_%d" % li, [L_]) for li, L_ in enumerate((256, 4096))]
    deltas = inp("deltas", [2048])
    kconst = inp("kconst", [3, 128, 256], U32)
    qidx = inp("qidx", [1024, 1], I32)
    dftc = [inp("dftc_%d" % li, [L_, L_], BF16) for li, L_ in enumerate((256, 4096))]
    dfts = [inp("dfts_%d" % li, [L_, L_], BF16) for li, L_ in enumerate((256, 4096))]
    peer_wq = inp("peer_wq", [2, D, D]); peer_keys = inp("peer_keys", [2, 8, 2, 128, 128])
    peer_u = [inp("peer_u0", [16384, D]), inp("peer_u1", [16384, D])]
    peer_v = [inp("peer_v0", [16384, D]), inp("peer_v1", [16384, D])]

    yp = outp("yp", [TP, D]); ys = outp("ys", [1024, D])
    nk = outp("nk", [TP, 256]); nv = outp("nv", [TP, 256]); ns = outp("ns", [4, 2, 16, 64, 64])

    modv = scr("modv", [2, 2, 6 * D])
    zg = [scr("zp", [TP, 4992]), scr("zs", [TS, 4992])]
    SQ = ["kk", "w0", "w1", "b0", "b1", "kt0", "kt1", "r"]
    sq = [{n: scr("sq_%s_%d" % (n, g), [T, 1024]) for n in SQ} for g, T in enumerate((TP, TS))]
    vT = [scr("vT_%d" % g, [8, 128, T]) for g, T in enumerate((TP, TS))]
    yT = [[scr("yT_%d_%d" % (g, d), [8, 128, T]) for d in range(2)] for g, T in enumerate((TP, TS))]
    gsc = [scr("g_%d" % g, [T, 1024]) for g, T in enumerate((TP, TS))]
    bon = [scr("bon_%d" % g, [T, 1024]) for g, T in enumerate((TP, TS))]
    zero_d = scr("zero_d", [128, 64])
    ycatT = [scr("ycatT_%d" % g, [16, 128, T], BF16) for g, T in enumerate((TP, TS))]
    qT_d = [scr("qT_%d" % g, [8, 128, T], BF16) for g, T in enumerate((TP, TS))]
    kT_d = [scr("kT_0", [2, 128, TP], BF16), scr("kT_1", [2, 128, TS + 512], BF16)]
    V_d = [scr("V_0", [TP, 256], BF16), scr("V_1", [TS + 512, 256], BF16)]
    x1 = [scr("x1_%d" % g, [T, D]) for g, T in enumerate((TP, TS))]
    if mode == "hytest":
        x2 = [inp("x2_%d" % g, [T, D]) for g, T in enumerate((TP, TS))]
    else:
        x2 = [scr("x2_%d" % g, [T, D]) for g, T in enumerate((TP, TS))]
    x3 = [scr("x3_%d" % g, [T, D]) for g, T in enumerate((TP, TS))]
    x3q = scr("x3q", [1024, D])
    uz = [scr("uz_%d" % g, [T, 6144]) for g, T in enumerate((TP, TS))]
    ug = [scr("ug_%d" % g, [T, 4096]) for g, T in enumerate((TP, TS))]
    zb16 = [[scr("zb_%d_%d" % (g, i), [T, 2048], BF16) for i in range(3)] for g, T in enumerate((TP, TS))]
    hzT = [scr("hzT_%d" % g, [16, 128, T], BF16) for g, T in enumerate((TP, TS))]
    LL = (256, 4096)
    HS = [scr("HS_%d" % li, [L_, 4, 2048], BF16) for li, L_ in enumerate(LL)]
    FS = [scr("FS_%d" % li, [2, 2, L_, 2048]) for li, L_ in enumerate(LL)]
    nsc = [scr("nsc_%d" % li, [2, 2048]) for li in range(2)]
    hm2 = [scr("hm2_%d" % g, [T, D]) for g, T in enumerate((TP, TS))]
    pq = [scr("pq_%d" % g, [T, D]) for g, T in enumerate((TP, TS))]

    with ExitStack() as st:
        st.enter_context(nc.allow_non_contiguous_dma(reason="layout"))
        P = Prog(nc, st)
        ASZ = 47616
        A = Arena(st.enter_context(nc.sbuf_tensor("arena", [128, ASZ], F32)), ASZ)
        psb = [st.enter_context(nc.psum_tensor("ps%d" % i, [128, 512], F32)) for i in range(6)]
        psT = st.enter_context(nc.psum_tensor("psT", [128, 2048], BF16))
        cons = st.enter_context(nc.sbuf_tensor("cons", [128, 128 + 64], F32))
        consb = st.enter_context(nc.sbuf_tensor("consb", [128, 128], BF16))
        ident = cons[:, 0:128]
        identb = consb[:, 0:128]
        P.dma(lambda e: e.dma_start(out=ident, in_=ident_d.ap()), writes=["ident"])
        P.op("vector", lambda e: e.tensor_copy(identb, ident), reads=["ident"], writes=["identb"])
        P.op("vector", lambda e: e.memset(cons[:, 128:192], 0.0), writes=["zeros"])
        P.dma(lambda e: e.dma_start(out=zero_d.ap(), in_=cons[:, 128:192]), reads=["zeros"], writes=["zero_d"])

        def stage_mod():
            A.reset()
            o_c = A.f32(32); o_s = A.f32(32); o_mb = A.f32(6 * D)
            o_w = [A.f32(16 * 512), A.f32(16 * 512)]
            cT = A.v(o_c, [[1, 32]]); sT = A.v(o_s, [[1, 32]])
            for gg in range(2):
                P.dma(lambda e, gg=gg: e.dma_start(out=A.v(o_c + gg * 16, [[1, 16]]),
                                                   in_=cond.ap()[gg].rearrange("(kc p) -> p kc", p=128)), writes=["cT"])
            P.op("scalar", lambda e: e.activation(sT, cT, AF.Silu), reads=["cT"], writes=["sT"])
            mb = A.v(o_mb, [[1, 6 * D]], np_=2)
            for layer in range(2):
                P.dma(lambda e, layer=layer: e.dma_start(out=mb, in_=bc_rows(mod_b, layer * 6 * D, 6 * D, 2)), writes=["mb"])
                for ch in range(24):
                    k = ch % 2
                    wt = A.v(o_w[k], [[512, 16], [1, 512]])
                    P.dma(lambda e, layer=layer, ch=ch, wt=wt: e.dma_start(
                        out=wt, in_=mod_w.ap()[layer, :, ch * 512:(ch + 1) * 512].rearrange("(kc p) n -> p kc n", p=128)),
                        writes=["mw%d" % k])
                    pb = psb[ch % 2]
                    for kc in range(16):
                        P.op("tensor", lambda e, kc=kc, k=k, pb=pb: e.matmul(
                            pb[0:2, :], A.v(o_s + kc, [[16, 2]]), A.v(o_w[k] + kc * 512, [[1, 512]]),
                            start=(kc == 0), stop=(kc == 15)),
                            reads=["sT", "mw%d" % k], writes=["psm%d" % (ch % 2)])
                    P.op("vector", lambda e, ch=ch, pb=pb: e.tensor_tensor(
                        A.v(o_mb + ch * 512, [[1, 512]], np_=2), A.v(o_mb + ch * 512, [[1, 512]], np_=2), pb[0:2, :], ALU.add),
                        reads=["psm%d" % (ch % 2), "mb"], writes=["mb"])
                for so in (D, 4 * D):
                    P.op("vector", lambda e, so=so: e.tensor_scalar_add(
                        A.v(o_mb + so, [[1, D]], np_=2), A.v(o_mb + so, [[1, D]], np_=2), 1.0), reads=["mb"], writes=["mb"])
                P.dma(lambda e, layer=layer: e.dma_start(out=modv.ap()[layer], in_=mb), reads=["mb"], writes=["modv"])
            P.barrier()

        def stage_norm_gemm(g, T, x_ap, layer, which, normw, W_ap, N, out_t, hm_out=None, srcT=None, resid=None):
            A.reset()
            sc_off = (1 if which == 1 else 4) * D
            sh_off = (0 if which == 1 else 3) * D
            o_G = A.f32(D); o_SH = A.f32(D); o_h = A.f32(D)
            o_x = [A.f32(D), A.f32(D)]
            o_w = A.f32(16 * 512)
            o_ot = [A.f32(512), A.f32(512)]
            o_xr = [A.f32(512), A.f32(512)]
            o_ss = A.f32(8)
            o_hb = A.f32(D // 2)
            o_hT = A.f32(16 * 1024 // 2)
            o_wb = [A.f32(16 * 512 // 2), A.f32(16 * 512 // 2)]
            Gb = A.v(o_G, [[1, D]]); SHb = A.v(o_SH, [[1, D]]); hh = A.v(o_h, [[1, D]])
            hb = A.bf(o_hb, D // 2)
            hT = A.bf(o_hT, 16 * 1024 // 2).rearrange("p (kc t) -> p kc t", kc=16)
            if srcT is None:
                P.dma(lambda e: e.dma_start(out=Gb, in_=bc_rows(normw, layer * D, D)), writes=["Gb"])
                P.dma(lambda e: e.dma_start(out=hh, in_=bc_rows(modv, (layer * 2 + g) * 6 * D + sc_off, D)), reads=["modv"], writes=["hh"])
                P.dma(lambda e: e.dma_start(out=SHb, in_=bc_rows(modv, (layer * 2 + g) * 6 * D + sh_off, D)), reads=["modv"], writes=["SHb"])
                P.op("vector", lambda e: e.tensor_tensor(Gb, Gb, hh, ALU.mult), reads=["Gb", "hh"], writes=["Gb"])
            if resid is not None:
                P.dma(lambda e: e.dma_start(out=SHb, in_=bc_rows(modv, (layer * 2 + g) * 6 * D + resid[1], D)), reads=["modv"], writes=["SHb"])
            nchunks = (N + 511) // 512
            for blk in range(T // 1024):
                if srcT is not None:
                    P.dma(lambda e, blk=blk: e.dma_start(out=hT, in_=srcT.ap()[:, :, blk * 1024:(blk + 1) * 1024].rearrange("kc p t -> p kc t")),
                          reads=[srcT.name], writes=["hT"])
                for tt in range(8 if srcT is None else 0):
                    t0 = blk * 1024 + tt * 128
                    k = tt % 2
                    xt = A.v(o_x[k], [[1, D]])
                    P.dma(lambda e, xt=xt, t0=t0: e.dma_start(out=xt, in_=x_ap[t0:t0 + 128, :]), writes=["x%d" % k])
                    ss = A.v(o_ss + k, [[1, 1]]); rs = A.v(o_ss + 2 + k, [[1, 1]])
                    P.op("scalar", lambda e, xt=xt, ss=ss: e.activation(hh, xt, AF.Square, accum_out=ss),
                         reads=["x%d" % k], writes=["hh", "ss%d" % k])
                    P.op("vector", lambda e, ss=ss, rs=rs: e.tensor_copy(rs, ss), reads=["ss%d" % k], writes=["rs%d" % k])
                    rsqrt_ops(P, rs, 1.0 / D, 1e-6, "rs%d" % k)
                    P.op("vector", lambda e, xt=xt, rs=rs: e.scalar_tensor_tensor(hh, xt, rs, Gb, ALU.mult, ALU.mult),
                         reads=["x%d" % k, "rs%d" % k, "Gb"], writes=["hh"])
                    if hm_out is not None:
                        P.op("gpsimd", lambda e: e.tensor_tensor(hh, hh, SHb, ALU.add), reads=["hh", "SHb"], writes=["hh"])
                        P.dma(lambda e, t0=t0: e.dma_start(out=hm_out.ap()[t0:t0 + 128, :], in_=hh), reads=["hh"], writes=[hm_out.name])
                        P.op("scalar", lambda e: e.activation(hb, hh, AF.Copy), reads=["hh"], writes=["hb"])
                    else:
                        P.op("gpsimd", lambda e: e.tensor_tensor(hb, hh, SHb, ALU.add), reads=["hh", "SHb"], writes=["hb"])
                    for kc in range(16):
                        P.op("tensor", lambda e, kc=kc: e.transpose(psT[:, kc * 128:(kc + 1) * 128], hb[:, kc * 128:(kc + 1) * 128], identb),
                             reads=["hb", "identb"], writes=["psT"])
                    P.op("scalar", lambda e, tt=tt: e.activation(hT[:, :, tt * 128:(tt + 1) * 128],
                                                                 psT[:, :].rearrange("p (kc t) -> p kc t", kc=16), AF.Copy),
                         reads=["psT"], writes=["hT"])
                for ch in range(nchunks):
                    n0 = ch * 512
                    nw = min(512, N - n0)
                    kb = ch % 2
                    wt = A.v(o_w, [[512, 16], [1, nw]])
                    wb = A.bf(o_wb[kb], 16 * 512 // 2).rearrange("p (kc n) -> p kc n", kc=16)
                    P.dma(lambda e, wt=wt, n0=n0, nw=nw: e.dma_start(
                        out=wt, in_=W_ap[:, n0:n0 + nw].rearrange("(kc p) n -> p kc n", p=128)), writes=["wt"])
                    P.op("gpsimd" if ch % 2 else "scalar",
                         (lambda e, wt=wt, wb=wb, nw=nw: e.tensor_copy(wb[:, :, 0:nw], wt)) if ch % 2 else
                         (lambda e, wt=wt, wb=wb, nw=nw: e.activation(wb[:, :, 0:nw], wt, AF.Copy)),
                         reads=["wt"], writes=["wb%d" % kb])
                    for tt in range(8):
                        t0 = blk * 1024 + tt * 128
                        pi = (ch * 8 + tt) % 4
                        pb = psb[pi]
                        for kc in range(16):
                            P.op("tensor", lambda e, kc=kc, tt=tt, pb=pb, wb=wb, nw=nw: e.matmul(
                                pb[:, 0:nw], hT[:, kc, tt * 128:(tt + 1) * 128], wb[:, kc, 0:nw],
                                start=(kc == 0), stop=(kc == 15)),
                                reads=["hT", "wb%d" % kb], writes=["pg%d" % pi])
                        ko = tt % 2
                        ot = A.v(o_ot[ko], [[1, nw]])
                        if resid is None:
                            P.op("vector", lambda e, ot=ot, pb=pb, nw=nw: e.tensor_copy(ot, pb[:, 0:nw]),
                                 reads=["pg%d" % pi], writes=["ot%d" % ko])
                        else:
                            xr = A.v(o_xr[ko], [[1, nw]])
                            P.dma(lambda e, xr=xr, t0=t0, n0=n0, nw=nw: e.dma_start(out=xr, in_=resid[0][t0:t0 + 128, n0:n0 + nw]),
                                  writes=["xr%d" % ko])
                            P.op("vector", lambda e, ot=ot, pb=pb, nw=nw, n0=n0: e.tensor_tensor(ot, pb[:, 0:nw], A.v(o_SH + n0, [[1, nw]]), ALU.mult),
                                 reads=["pg%d" % pi, "SHb"], writes=["ot%d" % ko])
                            P.op("gpsimd", lambda e, ot=ot, xr=xr: e.tensor_tensor(ot, ot, xr, ALU.add),
                                 reads=["ot%d" % ko, "xr%d" % ko], writes=["ot%d" % ko])
                        P.dma(lambda e, ot=ot, t0=t0, n0=n0, nw=nw: e.dma_start(out=out_t.ap()[t0:t0 + 128, n0:n0 + nw], in_=ot),
                              reads=["ot%d" % ko], writes=[out_t.name])
            P.barrier()

        def sq_store(g, nm, t0, tile_ap, rd, wr):
            off = tile_ap.offset
            for hh in range(2):
                src = bass.AP(A.t, off + hh * 64, [[A.size, 128], [128, 8], [1, 64]])
                dst = bass.AP(sq[g][nm], t0 * 1024 + hh * 512, [[1024, 128], [64, 8], [1, 64]])
                P.dma(lambda e, src=src, dst=dst: e.dma_start(out=dst, in_=src), reads=rd, writes=wr)

        def stage_even_prep(g, T, L):
            A.reset()
            z = zg[g]
            o_cw = A.f32(3 * 3456)
            o_vec = A.f32(9 * 1024)
            o_za = A.f32(3456)
            o_ld = [A.f32(3456), A.f32(3456)]
            o_d = [A.f32(1024) for _ in range(4)]
            o_kk = A.f32(1024); o_tmp = A.f32(1024); o_g = A.f32(1024); o_bon = A.f32(1024)
            o_sm = A.f32(64)
            o_lr = A.f32(3 * 128 // 2); o_lrT = A.f32(3 * 128 // 2)
            o_lw = A.f32(3 * 1024 // 2); o_lwf = A.f32(1024)
            o_vt = A.f32(1024)
            CW = A.v(o_cw, [[3456, 3], [1, 3456]])
            P.dma(lambda e: e.dma_start(out=CW, in_=bass.AP(a_conv, 0, [[0, 128], [3456, 3], [1, 3456]])), writes=["CW"])
            vecsrc = [(a_w0, 0), (a_w0, 1024), (a_a0, 0), (a_a0, 1024), (a_kk, 0), (a_ka, 0), (a_rk, 0)]
            for i, (t_, off) in enumerate(vecsrc):
                P.dma(lambda e, i=i, t_=t_, off=off: e.dma_start(out=A.v(o_vec + i * 1024, [[1, 1024]]), in_=bc_rows(t_, off, 1024)),
                      writes=["vec"])
            vec = lambda i: A.v(o_vec + i * 1024, [[1, 1024]])
            lw = A.bf(o_lw, 3 * 1024 // 2).rearrange("p (j n) -> p j n", j=3)
            for j, t_ in enumerate((a_wu, a_au, a_gu)):
                lwf = A.v(o_lwf, [[1, 1024]])
                P.dma(lambda e, t_=t_, lwf=lwf: e.dma_start(out=lwf, in_=t_.ap()), writes=["lwf"])
                P.op("vector", lambda e, j=j, lwf=lwf: e.tensor_copy(lw[:, j, :], lwf), reads=["lwf"], writes=["lw"])
            lr = A.bf(o_lr, 3 * 128 // 2).rearrange("p (j n) -> p j n", j=3)
            lrT = A.bf(o_lrT, 3 * 128 // 2).rearrange("p (j n) -> p j n", j=3)
            ZA = A.v(o_za, [[1, 3456]])
            r_ = A.v(o_za, [[1, 1024]]); k_ = A.v(o_za + 1024, [[1, 1024]]); v_ = A.v(o_za + 2048, [[1, 1024]])
            KK = A.v(o_kk, [[1, 1024]]); TMP = A.v(o_tmp, [[1, 1024]]); G = A.v(o_g, [[1, 1024]]); BON = A.v(o_bon, [[1, 1024]])
            h3 = lambda o: A.v(o, [[64, 16], [1, 64]])
            hb3 = lambda o: A.v(o, [[1, 16], [0, 64]])
            tiles_per_seq = L // 128
            for ti in range(T // 128):
                t0 = ti * 128
                first = (ti % tiles_per_seq == 0)
                last = (ti % tiles_per_seq == tiles_per_seq - 1)
                ld = A.v(o_ld[0], [[1, 3456]])
                P.dma(lambda e, ld=ld, t0=t0: e.dma_start(out=ld, in_=z.ap()[t0:t0 + 128, 0:3456]), reads=[z.name], writes=["ld0"])
                P.op("vector", lambda e, ld=ld: e.tensor_tensor(ZA, ld, A.v(o_cw + 3456, [[1, 3456]]), ALU.mult),
                     reads=["ld0", "CW"], writes=["ZA"])
                ld = A.v(o_ld[1], [[1, 3456]])
                if first:
                    P.op("gpsimd", lambda e, ld=ld: e.memset(ld, 0.0), writes=["ld1"])
                    P.dma(lambda e, t0=t0: e.dma_start(out=A.v(o_ld[1], [[1, 3456]], p0=1, np_=127), in_=z.ap()[t0:t0 + 127, 0:3456]),
                          reads=[z.name], writes=["ld1"])
                else:
                    P.dma(lambda e, ld=ld, t0=t0: e.dma_start(out=ld, in_=z.ap()[t0 - 1:t0 + 127, 0:3456]), reads=[z.name], writes=["ld1"])
                P.op("gpsimd", lambda e, ld=ld: e.tensor_tensor(ld, ld, A.v(o_cw, [[1, 3456]]), ALU.mult), reads=["ld1", "CW"], writes=["ld1"])
                P.op("vector", lambda e, ld=ld: e.tensor_tensor(ZA, ZA, ld, ALU.add), reads=["ld1", "ZA"], writes=["ZA"])
                ld = A.v(o_ld[0], [[1, 3456]])
                if last:
                    P.op("gpsimd", lambda e, ld=ld: e.memset(ld, 0.0), reads=[], writes=["ld0"])
                    P.dma(lambda e, t0=t0: e.dma_start(out=A.v(o_ld[0], [[1, 3456]], p0=0, np_=127), in_=z.ap()[t0 + 1:t0 + 128, 0:3456]),
                          reads=[z.name], writes=["ld0"])
                else:
                    P.dma(lambda e, ld=ld, t0=t0: e.dma_start(out=ld, in_=z.ap()[t0 + 1:t0 + 129, 0:3456]), reads=[z.name], writes=["ld0"])
                P.op("gpsimd", lambda e, ld=ld: e.tensor_tensor(ld, ld, A.v(o_cw + 2 * 3456, [[1, 3456]]), ALU.mult), reads=["ld0", "CW"], writes=["ld0"])
                P.op("vector", lambda e, ld=ld: e.tensor_tensor(ZA, ZA, ld, ALU.add), reads=["ld0", "ZA"], writes=["ZA"])
                sq_store(g, "r", t0, r_, ["ZA"], ["sq_r"])
                P.op("scalar", lambda e: e.activation(lr[:, 0, :], A.v(o_za + 3072, [[1, 128]]), AF.Tanh), reads=["ZA"], writes=["lr"])
                P.op("scalar", lambda e: e.activation(lr[:, 1, :], A.v(o_za + 3200, [[1, 128]]), AF.Copy), reads=["ZA"], writes=["lr"])
                P.op("scalar", lambda e: e.activation(lr[:, 2, :], A.v(o_za + 3328, [[1, 128]]), AF.Sigmoid), reads=["ZA"], writes=["lr"])
                for j in range(3):
                    P.op("tensor", lambda e, j=j: e.transpose(psT[:, j * 128:(j + 1) * 128], lr[:, j, :], identb), reads=["lr", "identb"], writes=["psT"])
                P.op("vector", lambda e: e.tensor_copy(lrT, psT[:, 0:384].rearrange("p (j n) -> p j n", j=3)), reads=["psT"], writes=["lrT"])
                for hf in range(2):
                    P.op("tensor", lambda e, hf=hf: e.matmul(psb[hf][:, :], lrT[:, 2, :], lw[:, 2, hf * 512:(hf + 1) * 512], start=True, stop=True),
                         reads=["lrT", "lw"], writes=["pe%d" % hf])
                    P.op("scalar", lambda e, hf=hf: e.activation(A.v(o_g + hf * 512, [[1, 512]]), psb[hf][:, :], AF.Copy),
                         reads=["pe%d" % hf], writes=["G"])
                P.dma(lambda e, t0=t0: e.dma_start(out=gsc[g].ap()[t0:t0 + 128, :], in_=G), reads=["G"], writes=["gsc"])
                P.op("vector", lambda e: e.tensor_tensor(KK, k_, vec(4), ALU.mult), reads=["ZA", "vec"], writes=["KK"])
                P.op("gpsimd", lambda e: e.tensor_tensor(TMP, KK, KK, ALU.mult), reads=["KK"], writes=["TMP"])
                P.op("vector", lambda e: e.tensor_reduce(A.v(o_sm, [[1, 16]]), h3(o_tmp), AX.X, ALU.add), reads=["TMP"], writes=["sm"])
                rsqrt_ops(P, A.v(o_sm, [[1, 16]]), 1.0, 1e-12, "sm")
                P.op("vector", lambda e: e.tensor_tensor(h3(o_kk), h3(o_kk), hb3(o_sm), ALU.mult), reads=["sm", "KK"], writes=["KK"])
                sq_store(g, "kk", t0, KK, ["KK"], ["sq_kk"])
                P.op("gpsimd", lambda e: e.tensor_tensor(TMP, r_, k_, ALU.mult), reads=["ZA"], writes=["TMP"])
                P.op("gpsimd", lambda e: e.tensor_tensor(TMP, TMP, vec(6), ALU.mult), reads=["TMP", "vec"], writes=["TMP"])
                P.op("vector", lambda e: e.tensor_reduce(A.v(o_sm + 16, [[1, 16]]), h3(o_tmp), AX.X, ALU.add), reads=["TMP"], writes=["sm2"])
                P.op("vector", lambda e: e.tensor_tensor(h3(o_bon), h3(o_za + 2048), hb3(o_sm + 16), ALU.mult), reads=["sm2", "ZA"], writes=["BON"])
                P.dma(lambda e, t0=t0: e.dma_start(out=bon[g].ap()[t0:t0 + 128, :], in_=BON), reads=["BON"], writes=["bon"])
                vt = A.v(o_vt, [[128, 8], [1, 128]])
                for hf in range(2):
                    for j in range(4):
                        c = hf * 4 + j
                        P.op("tensor", lambda e, c=c, hf=hf, j=j: e.transpose(psb[2 + hf][:, j * 128:(j + 1) * 128],
                                                                             A.v(o_za + 2048 + c * 128, [[1, 128]]), ident),
                             reads=["ZA", "ident"], writes=["pv%d" % hf])
                    P.op("scalar", lambda e, hf=hf: e.activation(A.v(o_vt + hf * 512, [[1, 512]]), psb[2 + hf][:, :], AF.Copy),
                         reads=["pv%d" % hf], writes=["vt"])
                P.dma(lambda e, t0=t0, vt=vt: e.dma_start(out=vT[g].ap()[:, :, t0:t0 + 128].rearrange("c p t -> p c t"), in_=vt),
                      reads=["vt"], writes=["vT"])
                for d in range(2):
                    Wd, Ad, KTd, Bd = (A.v(o, [[1, 1024]]) for o in o_d)
                    for hf in range(2):
                        P.op("tensor", lambda e, d=d, hf=hf: e.matmul(psb[hf][:, :], lrT[64 * d:64 * d + 64, 0, :],
                                                                      lw[64 * d:64 * d + 64, 0, hf * 512:(hf + 1) * 512], start=True, stop=True),
                             reads=["lrT", "lw"], writes=["pe%d" % hf])
                        P.op("vector", lambda e, d=d, hf=hf: e.tensor_tensor(A.v(o_d[0] + hf * 512, [[1, 512]]), psb[hf][:, :],
                                                                            A.v(o_vec + d * 1024 + hf * 512, [[1, 512]]), ALU.add),
                             reads=["pe%d" % hf, "vec"], writes=["Wd"])
                    P.op("scalar", lambda e, Wd=Wd: e.activation(Wd, Wd, AF.Sigmoid), reads=["Wd"], writes=["Wd"])
                    P.op("scalar", lambda e, Wd=Wd: e.activation(Wd, Wd, AF.Exp, scale=-math.exp(-0.5)), reads=["Wd"], writes=["Wd"])
                    sq_store(g, "w%d" % d, t0, Wd, ["Wd"], ["sq_w"])
                    for hf in range(2):
                        P.op("tensor", lambda e, d=d, hf=hf: e.matmul(psb[hf][:, :], lrT[64 * d:64 * d + 64, 1, :],
                                                                      lw[64 * d:64 * d + 64, 1, hf * 512:(hf + 1) * 512], start=True, stop=True),
                             reads=["lrT", "lw"], writes=["pe%d" % hf])
                        P.op("vector", lambda e, d=d, hf=hf: e.tensor_tensor(A.v(o_d[1] + hf * 512, [[1, 512]]), psb[hf][:, :],
                                                                            A.v(o_vec + (2 + d) * 1024 + hf * 512, [[1, 512]]), ALU.add),
                             reads=["pe%d" % hf, "vec"], writes=["Ad"])
                    P.op("scalar", lambda e, Ad=Ad: e.activation(Ad, Ad, AF.Sigmoid), reads=["Ad"], writes=["Ad"])
                    P.op("vector", lambda e, Ad=Ad, KTd=KTd: e.scalar_tensor_tensor(KTd, Ad, -1.0, vec(5), ALU.add, ALU.mult),
                         reads=["Ad", "vec"], writes=["KTd"])
                    P.op("vector", lambda e, KTd=KTd: e.scalar_tensor_tensor(KTd, KTd, 1.0, k_, ALU.add, ALU.mult),
                         reads=["KTd", "ZA"], writes=["KTd"])
                    sq_store(g, "kt%d" % d, t0, KTd, ["KTd"], ["sq_kt"])
                    P.op("gpsimd", lambda e, Ad=Ad, Bd=Bd: e.tensor_tensor(Bd, KK, Ad, ALU.mult), reads=["KK", "Ad"], writes=["Bd"])
                    sq_store(g, "b%d" % d, t0, Bd, ["Bd"], ["sq_b"])
            P.barrier()

        def stage_scan(g, seq, L, d, s0_ap=None, sfin_ap=None):
            A.reset()
            if SCAN_DVE_INORDER:
                P.no_self.add("vector")
            SB = 4
            o_S = A.f32(512); o_tmp = A.f32(512); o_sa = A.f32(8)
            o_X = [A.f32(5 * SB * 512), A.f32(5 * SB * 512)]
            VB = 128
            o_V = [A.f32(8 * VB), A.f32(8 * VB)]
            o_Y = [A.f32(8 * VB), A.f32(8 * VB)]
            S = A.v(o_S, [[1, 512]]); TMP = A.v(o_tmp, [[1, 512]])
            S3 = A.v(o_S, [[64, 8], [1, 64]]); TMP3 = A.v(o_tmp, [[64, 8], [1, 64]])
            sa = A.v(o_sa, [[1, 8]]); sab = A.v(o_sa, [[1, 8], [0, 64]])
            base = seq * L
            for hh in range(2):
                dst = A.v(o_S, [[64, 8], [1, 64]], p0=64 * hh, np_=64)
                if s0_ap is not None:
                    src = bass.AP(s0_ap, (d * 16 + hh) * 4096, [[64, 64], [2 * 4096, 8], [1, 64]])
                    P.dma(lambda e, dst=dst, src=src: e.dma_start(out=dst, in_=src), writes=["S"])
                else:
                    src = bass.AP(zero_d, 0, [[64, 64], [0, 8], [1, 64]])
                    P.dma(lambda e, dst=dst, src=src: e.dma_start(out=dst, in_=src), reads=["zero_d"], writes=["S"])
            names = ["kk", "w%d" % d, "b%d" % d, "kt%d" % d, "r"]
            for blk in range(L // SB):
                kx = blk % 2
                if d == 0:
                    tok0 = blk * SB
                else:
                    tok0 = L - (blk + 1) * SB
                for qi, nm in enumerate(names):
                    for hh in range(2):
                        dst = A.v(o_X[kx] + qi * SB * 512, [[512, SB], [1, 512]], p0=64 * hh, np_=64)
                        src = bass.AP(sq[g][nm], (base + tok0) * 1024 + hh * 512, [[0, 64], [1024, SB], [1, 512]])
                        P.dma(lambda e, dst=dst, src=src: e.dma_start(out=dst, in_=src), reads=["sq"], writes=["X%d_%d_%d" % (kx, qi, hh)],
                              q="gpsimd" if (qi % 2) else "sync")
                if (blk * SB) % VB == 0:
                    vb = (blk * SB) // VB
                    kv = vb % 2
                    vtok0 = vb * VB if d == 0 else L - (vb + 1) * VB
                    dst = A.v(o_V[kv], [[VB, 8], [1, VB]])
                    src = bass.AP(vT[g], base + vtok0, [[vT[g].shape[2], 128], [128 * vT[g].shape[2], 8], [1, VB]])
                    P.dma(lambda e, dst=dst, src=src: e.dma_start(out=dst, in_=src), reads=["vT"], writes=["V%d" % kv])
                for j in range(SB):
                    step = blk * SB + j
                    jj = j if d == 0 else SB - 1 - j
                    vb = step // VB
                    kv = vb % 2
                    sv = step % VB
                    vcol = sv if d == 0 else VB - 1 - sv
                    X = (lambda xs: (lambda qi: xs[qi]))([A.v(o_X[kx] + qi * SB * 512 + jj * 512, [[64, 8], [1, 64]]) for qi in range(5)])
                    vbc = A.v(o_V[kv] + vcol, [[VB, 8], [0, 64]])
                    ycol = A.v(o_Y[kv] + vcol, [[VB, 8]])
                    rk = lambda qi: ["X%d_%d_0" % (kx, qi), "X%d_%d_1" % (kx, qi)]
                    P.op("vector", lambda e, X=X: e.tensor_tensor(TMP3, S3, X(0), ALU.mult), reads=["S"] + rk(0), writes=["TMP"])
                    P.op("vector", lambda e: e.tensor_reduce(sa, TMP3, AX.X, ALU.add), reads=["TMP"], writes=["sa"])
                    P.op("vector", lambda e, X=X: e.tensor_tensor(S3, S3, X(1), ALU.mult), reads=["S"] + rk(1), writes=["S"])
                    P.op("vector", lambda e, X=X: e.tensor_tensor(TMP3, X(2), sab, ALU.mult), reads=["sa"] + rk(2), writes=["TMP"])
                    P.op("vector", lambda e: e.tensor_tensor(S3, S3, TMP3, ALU.subtract), reads=["S", "TMP"], writes=["S"])
                    P.op("vector", lambda e, X=X, vbc=vbc: e.tensor_tensor(TMP3, X(3), vbc, ALU.mult), reads=["V%d" % kv] + rk(3), writes=["TMP"])
                    P.op("vector", lambda e: e.tensor_tensor(S3, S3, TMP3, ALU.add), reads=["S", "TMP"], writes=["S"])
                    P.op("vector", lambda e, X=X: e.tensor_tensor(TMP3, S3, X(4), ALU.mult), reads=["S"] + rk(4), writes=["TMP"])
                    P.op("vector", lambda e, ycol=ycol: e.tensor_reduce(ycol, TMP3, AX.X, ALU.add), reads=["TMP"], writes=["Y%d" % kv])
                    if sv == VB - 1:
                        ytok0 = vb * VB if d == 0 else L - (vb + 1) * VB
                        srcy = A.v(o_Y[kv], [[VB, 8], [1, VB]])
                        dsty = bass.AP(yT[g][d], base + ytok0, [[yT[g][d].shape[2], 128], [128 * yT[g][d].shape[2], 8], [1, VB]])
                        P.dma(lambda e, srcy=srcy, dsty=dsty: e.dma_start(out=dsty, in_=srcy), reads=["Y%d" % kv], writes=["yT"])
            if sfin_ap is not None:
                for hh in range(2):
                    srcS = A.v(o_S, [[64, 8], [1, 64]], p0=64 * hh, np_=64)
                    dstS = bass.AP(sfin_ap, ((seq * 2 + d) * 16 + hh) * 4096, [[64, 64], [2 * 4096, 8], [1, 64]])
                    P.dma(lambda e, srcS=srcS, dstS=dstS: e.dma_start(out=dstS, in_=srcS), reads=["S"], writes=["ns"], is_output=True)
            P.no_self.discard("vector")
            P.barrier()

        def stage_rwkv_post(g, T):
            A.reset()
            o_y0 = A.f32(1024); o_y1 = A.f32(1024); o_Y = A.f32(1024); o_YC = A.f32(1024); o_SQ = A.f32(1024)
            o_bon = A.f32(1024); o_g = A.f32(1024); o_lnw = A.f32(1024); o_lnb = A.f32(1024); o_sm = A.f32(64)
            o_yb = A.f32(512); o_yT = A.f32(512)
            LNW = A.v(o_lnw, [[1, 1024]]); LNB = A.v(o_lnb, [[1, 1024]])
            P.dma(lambda e: e.dma_start(out=LNW, in_=bc_rows(a_lnw, 0, 1024)), writes=["LNW"])
            P.dma(lambda e: e.dma_start(out=LNB, in_=bc_rows(a_lnb, 0, 1024)), writes=["LNB"])
            Y = A.v(o_Y, [[1, 1024]]); YC = A.v(o_YC, [[1, 1024]]); SQ = A.v(o_SQ, [[1, 1024]])
            BON = A.v(o_bon, [[1, 1024]]); G = A.v(o_g, [[1, 1024]])
            h3 = lambda o: A.v(o, [[64, 16], [1, 64]])
            hb3 = lambda o: A.v(o, [[1, 16], [0, 64]])
            yb = A.bf(o_yb, 512)
            yT8 = A.bf(o_yT, 512).rearrange("p (c t) -> p c t", c=8)
            for ti in range(T // 128):
                t0 = ti * 128
                y0 = A.v(o_y0, [[128, 8], [1, 128]]); y1 = A.v(o_y1, [[128, 8], [1, 128]])
                for d, yy in ((0, y0), (1, y1)):
                    src = bass.AP(yT[g][d], t0, [[T, 128], [128 * T, 8], [1, 128]])
                    P.dma(lambda e, yy=yy, src=src: e.dma_start(out=yy, in_=src), reads=["yT"], writes=["y%d" % d])
                P.op("vector", lambda e: e.tensor_tensor(A.v(o_y0, [[1, 1024]]), A.v(o_y0, [[1, 1024]]), A.v(o_y1, [[1, 1024]]), ALU.add),
                     reads=["y0", "y1"], writes=["y0"])
                for c in range(8):
                    P.op("tensor", lambda e, c=c: e.transpose(psb[c // 4][:, (c % 4) * 128:(c % 4 + 1) * 128], A.v(o_y0 + c * 128, [[1, 128]]), ident),
                         reads=["y0", "ident"], writes=["pp%d" % (c // 4)])
                for hf in range(2):
                    P.op("scalar", lambda e, hf=hf: e.activation(A.v(o_Y + hf * 512, [[1, 512]]), psb[hf][:, :], AF.Copy), reads=["pp%d" % hf], writes=["Y"])
                P.op("vector", lambda e: e.tensor_reduce(A.v(o_sm, [[1, 16]]), h3(o_Y), AX.X, ALU.add), reads=["Y"], writes=["mu"])
                P.op("vector", lambda e: e.tensor_scalar_mul(A.v(o_sm, [[1, 16]]), A.v(o_sm, [[1, 16]]), 1.0 / 64), reads=["mu"], writes=["mu"])
                P.op("vector", lambda e: e.tensor_tensor(h3(o_YC), h3(o_Y), hb3(o_sm), ALU.subtract), reads=["Y", "mu"], writes=["YC"])
                P.op("gpsimd", lambda e: e.tensor_tensor(SQ, YC, YC, ALU.mult), reads=["YC"], writes=["SQ"])
                P.op("vector", lambda e: e.tensor_reduce(A.v(o_sm + 16, [[1, 16]]), h3(o_SQ), AX.X, ALU.add), reads=["SQ"], writes=["var"])
                rsqrt_ops(P, A.v(o_sm + 16, [[1, 16]]), 1.0 / 64, 64e-5, "var")
                P.op("vector", lambda e: e.tensor_tensor(h3(o_YC), h3(o_YC), hb3(o_sm + 16), ALU.mult), reads=["YC", "var"], writes=["YC"])
                P.op("gpsimd", lambda e: e.tensor_tensor(YC, YC, LNW, ALU.mult), reads=["YC", "LNW"], writes=["YC"])
                P.op("vector", lambda e: e.tensor_tensor(YC, YC, LNB, ALU.add), reads=["YC", "LNB"], writes=["YC"])
                P.dma(lambda e, t0=t0: e.dma_start(out=BON, in_=bon[g].ap()[t0:t0 + 128, :]), reads=["bon"], writes=["BON"])
                P.dma(lambda e, t0=t0: e.dma_start(out=G, in_=gsc[g].ap()[t0:t0 + 128, :]), reads=["gsc"], writes=["G"])
                P.op("gpsimd", lambda e: e.tensor_tensor(YC, YC, BON, ALU.add), reads=["YC", "BON"], writes=["YC"])
                P.op("vector", lambda e: e.tensor_tensor(yb, YC, G, ALU.mult), reads=["YC", "G"], writes=["yb"])
                for c in range(8):
                    P.op("tensor", lambda e, c=c: e.transpose(psT[:, c * 128:(c + 1) * 128], yb[:, c * 128:(c + 1) * 128], identb),
                         reads=["yb", "identb"], writes=["psT"])
                P.op("scalar", lambda e: e.activation(yT8, psT[:, 0:1024].rearrange("p (c t) -> p c t", c=8), AF.Copy), reads=["psT"], writes=["yT8"])
                P.dma(lambda e, t0=t0: e.dma_start(out=ycatT[g].ap()[0:8, :, t0:t0 + 128].rearrange("c p t -> p c t"), in_=yT8),
                      reads=["yT8"], writes=[ycatT[g].name])
            P.barrier()

        def stage_attn_prep(g, T, L, latent):
            A.reset()
            o_z = A.f32(1536); o_sq = A.f32(1280); o_qr = A.f32(1280); o_sm = A.f32(16)
            o_qn = A.f32(128); o_kn = A.f32(128); o_cs = A.f32(128); o_t = [A.f32(640) for _ in range(4)]
            o_qb = A.f32(640); o_qT = A.f32(640); o_vb = A.f32(128)
            QN = A.v(o_qn, [[1, 128]]); KN = A.v(o_kn, [[1, 128]])
            P.dma(lambda e: e.dma_start(out=QN, in_=bc_rows(b_qn, 0, 128)), writes=["QN"])
            P.dma(lambda e: e.dma_start(out=KN, in_=bc_rows(b_kn, 0, 128)), writes=["KN"])
            qb = A.bf(o_qb, 640)
            qT = A.bf(o_qT, 640).rearrange("p (h t) -> p h t", h=10)
            vb = A.bf(o_vb, 128)
            Lk = kT_d[g].shape[2] // (T // L)
            ntile = T // 128 + (4 if latent else 0)
            for ti in range(ntile):
                t0 = ti * 128
                cache = ti >= T // 128
                seq = t0 // L
                tin = t0 % L
                if not cache:
                    P.dma(lambda e, t0=t0: e.dma_start(out=A.v(o_z, [[1, 1536]]), in_=zg[g].ap()[t0:t0 + 128, 3456:4992]), reads=[zg[g].name], writes=["zq"])
                    P.op("gpsimd", lambda e: e.tensor_tensor(A.v(o_sq, [[1, 1280]]), A.v(o_z, [[1, 1280]]), A.v(o_z, [[1, 1280]]), ALU.mult), reads=["zq"], writes=["sqq"])
                    P.op("vector", lambda e: e.tensor_reduce(A.v(o_sm, [[1, 10]]), A.v(o_sq, [[128, 10], [1, 128]]), AX.X, ALU.add), reads=["sqq"], writes=["sm"])
                    rsqrt_ops(P, A.v(o_sm, [[1, 10]]), 1.0 / 128, 1e-6, "sm")
                    P.op("vector", lambda e: e.tensor_tensor(A.v(o_z, [[128, 10], [1, 128]]), A.v(o_z, [[128, 10], [1, 128]]), A.v(o_sm, [[1, 10], [0, 128]]), ALU.mult),
                         reads=["zq", "sm"], writes=["zq"])
                    P.op("vector", lambda e: e.tensor_tensor(A.v(o_z, [[128, 8], [1, 128]]), A.v(o_z, [[128, 8], [1, 128]]), A.v(o_qn, [[0, 8], [1, 128]]), ALU.mult),
                         reads=["zq", "QN"], writes=["zq"])
                    P.op("vector", lambda e: e.tensor_tensor(A.v(o_z + 1024, [[128, 2], [1, 128]]), A.v(o_z + 1024, [[128, 2], [1, 128]]), A.v(o_kn, [[0, 2], [1, 128]]), ALU.mult),
                         reads=["zq", "KN"], writes=["zq"])
                    if latent:
                        P.dma(lambda e, tin=tin: e.dma_start(out=A.v(o_cs, [[1, 64]]), in_=ropec.ap()[tin:tin + 128, :]), writes=["cs"])
                        P.dma(lambda e, tin=tin: e.dma_start(out=A.v(o_cs + 64, [[1, 64]]), in_=ropes.ap()[tin:tin + 128, :]), writes=["cs"])
                        X1 = A.v(o_z, [[128, 10], [64, 2], [1, 32]]); X2 = A.v(o_z + 32, [[128, 10], [64, 2], [1, 32]])
                        O1 = A.v(o_qr, [[128, 10], [64, 2], [1, 32]]); O2 = A.v(o_qr + 32, [[128, 10], [64, 2], [1, 32]])
                        Cb = A.v(o_cs, [[0, 10], [32, 2], [1, 32]]); Sb = A.v(o_cs + 64, [[0, 10], [32, 2], [1, 32]])
                        Tt = [A.v(o, [[64, 10], [32, 2], [1, 32]]) for o in o_t]
                        P.op("vector", lambda e: e.tensor_tensor(Tt[0], X1, Cb, ALU.mult), reads=["zq", "cs"], writes=["t0"])
                        P.op("gpsimd", lambda e: e.tensor_tensor(Tt[1], X2, Sb, ALU.mult), reads=["zq", "cs"], writes=["t1"])
                        P.op("vector", lambda e: e.tensor_tensor(Tt[2], X1, Sb, ALU.mult), reads=["zq", "cs"], writes=["t2"])
                        P.op("gpsimd", lambda e: e.tensor_tensor(Tt[3], X2, Cb, ALU.mult), reads=["zq", "cs"], writes=["t3"])
                        P.op("vector", lambda e: e.tensor_tensor(O1, Tt[0], Tt[1], ALU.subtract), reads=["t0", "t1"], writes=["qr"])
                        P.op("vector", lambda e: e.tensor_tensor(O2, Tt[2], Tt[3], ALU.add), reads=["t2", "t3"], writes=["qr"])
                        P.op("scalar", lambda e: e.activation(qb, A.v(o_qr, [[1, 1280]]), AF.Copy), reads=["qr"], writes=["qb"])
                    else:
                        P.op("scalar", lambda e: e.activation(qb, A.v(o_z, [[1, 1280]]), AF.Copy), reads=["zq"], writes=["qb"])
                    P.op("gpsimd", lambda e: e.tensor_copy(vb, A.v(o_z + 1280, [[1, 256]])), reads=["zq"], writes=["vb"])
                    h0 = 0
                    krow = seq * Lk + tin
                else:
                    c0 = (ti - T // 128) * 128
                    P.dma(lambda e, c0=c0: e.dma_start(out=A.v(o_z + 1024, [[1, 256]]), in_=ck.ap()[c0:c0 + 128, :]), writes=["zq"])
                    P.dma(lambda e, c0=c0: e.dma_start(out=A.v(o_z + 1280, [[1, 256]]), in_=cv.ap()[c0:c0 + 128, :]), writes=["zq"])
                    P.op("scalar", lambda e: e.activation(qb[:, 1024:1280], A.v(o_z + 1024, [[1, 256]]), AF.Copy), reads=["zq"], writes=["qb"])
                    P.op("gpsimd", lambda e: e.tensor_copy(vb, A.v(o_z + 1280, [[1, 256]])), reads=["zq"], writes=["vb"])
                    h0 = 8
                    krow = L + c0
                for h in range(h0, 10):
                    P.op("tensor", lambda e, h=h: e.transpose(psT[:, h * 128:(h + 1) * 128], qb[:, h * 128:(h + 1) * 128], identb),
                         reads=["qb", "identb"], writes=["psT"])
                P.op("vector", lambda e, h0=h0: e.tensor_copy(qT[:, h0:10, :], psT[:, h0 * 128:1280].rearrange("p (h t) -> p h t", h=10 - h0)),
                     reads=["psT"], writes=["qT"])
                if not cache:
                    P.dma(lambda e, t0=t0: e.dma_start(out=qT_d[g].ap()[:, :, t0:t0 + 128].rearrange("h p t -> p h t"), in_=qT[:, 0:8, :]),
                          reads=["qT"], writes=["qT_d"])
                P.dma(lambda e, krow=krow: e.dma_start(out=kT_d[g].ap()[:, :, krow:krow + 128].rearrange("h p t -> p h t"), in_=qT[:, 8:10, :]),
                      reads=["qT"], writes=["kT_d"])
                P.dma(lambda e, krow=krow: e.dma_start(out=V_d[g].ap()[krow:krow + 128, :], in_=vb), reads=["vb"], writes=["V_d"])
            P.barrier()

        def stage_attn(g, T, seq, L, Lk):
            A.reset()
            nkt = Lk // 128
            NQ = min(512, L)
            o_kT = A.f32(2 * Lk // 2); o_V = A.f32(nkt * 256 // 2)
            o_q = [A.f32(NQ // 2), A.f32(NQ // 2)]
            o_E = [A.f32(NQ // 2), A.f32(NQ // 2)]
            o_rec = A.f32(NQ); o_ob = [A.f32(NQ // 2), A.f32(NQ // 2)]
            o_one = A.f32(64)
            KT = A.bf(o_kT, 2 * Lk // 2).rearrange("p (h t) -> p h t", h=2)
            VV = A.bf(o_V, nkt * 256 // 2).rearrange("p (k n) -> p k n", k=nkt)
            ones = A.bf(o_one, 64)
            P.op("vector", lambda e: e.memset(ones, 1.0), writes=["ones"])
            P.dma(lambda e: e.dma_start(out=KT, in_=kT_d[g].ap()[:, :, seq * Lk:(seq + 1) * Lk].rearrange("h p t -> p h t")),
                  reads=["kT_d"], writes=["KT"])
            P.dma(lambda e: e.dma_start(out=VV, in_=V_d[g].ap()[seq * Lk:(seq + 1) * Lk, :].rearrange("(k p) n -> p k n", p=128)),
                  reads=["V_d"], writes=["VV"])
            scale = 128 ** -0.5
            it = 0
            for h in range(8):
                kv = h // 4
                for qb_ in range(L // NQ):
                    q0 = seq * L + qb_ * NQ
                    kq = (h * (L // NQ) + qb_) % 2
                    qt = A.bf(o_q[kq], NQ // 2)
                    P.dma(lambda e, qt=qt, h=h, q0=q0: e.dma_start(out=qt, in_=qT_d[g].ap()[h, :, q0:q0 + NQ]), reads=["qT_d"], writes=["q%d" % kq])
                    for kt in range(nkt):
                        ke = it % 2
                        it += 1
                        Et = A.bf(o_E[ke], NQ // 2)
                        P.op("tensor", lambda e, kt=kt, kv=kv, qt=qt, ke=ke: e.matmul(psb[ke][:, 0:NQ], KT[:, kv, kt * 128:(kt + 1) * 128], qt, start=True, stop=True),
                             reads=["KT", "q%d" % kq], writes=["pS%d" % ke])
                        P.op("scalar", lambda e, Et=Et, ke=ke: e.activation(Et, psb[ke][:, 0:NQ], AF.Exp, scale=scale),
                             reads=["pS%d" % ke], writes=["E%d" % ke])
                        P.op("tensor", lambda e, kt=kt, kv=kv, Et=Et: e.matmul(psb[2][:, 0:NQ], VV[:, kt, kv * 128:(kv + 1) * 128], Et,
                                                                               start=(kt == 0), stop=(kt == nkt - 1)),
                             reads=["VV", "E%d" % ke], writes=["pO"])
                        P.op("tensor", lambda e, kt=kt, Et=Et: e.matmul(psb[3][:, 0:NQ], ones, Et, start=(kt == 0), stop=(kt == nkt - 1)),
                             reads=["ones", "E%d" % ke], writes=["pD"])
                    rec = A.v(o_rec, [[1, NQ]])
                    ob = A.bf(o_ob[kq], NQ // 2)
                    P.op("vector", lambda e, rec=rec: e.reciprocal(rec, psb[3][:, 0:NQ]), reads=["pD"], writes=["rec"])
                    P.op("vector", lambda e, rec=rec, ob=ob: e.tensor_tensor(ob, psb[2][:, 0:NQ], rec, ALU.mult), reads=["pO", "rec"], writes=["ob%d" % kq])
                    P.dma(lambda e, ob=ob, h=h, q0=q0: e.dma_start(out=ycatT[g].ap()[8 + h, :, q0:q0 + NQ], in_=ob),
                          reads=["ob%d" % kq], writes=[ycatT[g].name])
            P.barrier()

        def stage_peer(g, T, layer, q_t, hm_t, xres_ap, out_ap, out_key, is_output=False):
            A.reset()
            NB = 4
            o_kn = A.f32(2048); o_kT = A.f32(2048); o_q = A.f32(2048); o_qT = A.f32(2048); o_S = A.f32(2048)
            o_X = A.f32(2048); o_ACC = A.f32(2048); o_jk = A.f32(2048); o_GT = A.f32(2048); o_xr = A.f32(2048)
            o_R = [A.f32(2048) for _ in range(NB)]
            o_V1 = A.f32(256); o_IDX = A.f32(256); o_IDF = A.f32(256); o_CAND = A.f32(256); o_EID = A.f32(256); o_CW = A.f32(256)
            o_W = A.f32(128); o_T = A.f32(128); o_E = A.f32(128); o_EI = A.f32(128); o_GATE = A.f32(128); o_DOT = A.f32(128); o_WG = A.f32(128)
            o_sm = A.f32(16)
            o_MK = A.f32(256); o_FF = A.f32(256); o_CD = A.f32(256)
            u32v = lambda off, dims: bass.AP(A.t, off, [[A.size, 128]] + [list(d_) for d_ in dims]).bitcast(U32)
            P.dma(lambda e: e.dma_start(out=u32v(o_MK, [[1, 256]]), in_=kconst.ap()[0]), writes=["MK"])
            P.dma(lambda e: e.dma_start(out=u32v(o_FF, [[1, 256]]), in_=kconst.ap()[1]), writes=["FF"])
            P.dma(lambda e: e.dma_start(out=u32v(o_CD, [[1, 256]]), in_=kconst.ap()[2]), writes=["CD"])
            ut = peer_u[layer]; vt_ = peer_v[layer]
            GT = A.v(o_GT, [[1, 2048]])
            P.dma(lambda e: e.dma_start(out=GT, in_=bc_rows(modv, (layer * 2 + g) * 6 * D + 5 * D, D)), reads=["modv"], writes=["GT"])
            P.dma(lambda e: e.dma_start(out=A.v(o_kn, [[128, 16], [1, 128]]), in_=peer_keys.ap()[layer].rearrange("h c n d -> n (h c) d")), writes=["kn"])
            for hc in range(16):
                P.op("tensor", lambda e, hc=hc: e.transpose(psb[hc // 4][:, (hc % 4) * 128:(hc % 4 + 1) * 128], A.v(o_kn + hc * 128, [[1, 128]]), ident),
                     reads=["kn", "ident"], writes=["pk%d" % (hc // 4)])
            for b4 in range(4):
                P.op("scalar", lambda e, b4=b4: e.activation(A.v(o_kT + b4 * 512, [[1, 512]]), psb[b4][:, :], AF.Copy), reads=["pk%d" % b4], writes=["kT"])
            EIu = bass.AP(A.t, o_EI, [[A.size, 128], [1, 128]]).bitcast(I32)
            IDXu = bass.AP(A.t, o_IDX, [[A.size, 128], [1, 256]]).bitcast(U32)
            for ti in range(T // 128):
                t0 = ti * 128
                P.dma(lambda e, t0=t0: e.dma_start(out=A.v(o_q, [[1, 2048]]), in_=q_t.ap()[t0:t0 + 128, :]), reads=[q_t.name], writes=["q"])
                P.dma(lambda e, t0=t0: e.dma_start(out=A.v(o_X, [[1, 2048]]), in_=hm_t.ap()[t0:t0 + 128, :]), reads=[hm_t.name], writes=["X"])
                P.dma(lambda e, t0=t0: e.dma_start(out=A.v(o_xr, [[1, 2048]]), in_=xres_ap[t0:t0 + 128, :]), writes=["xr"])
                for hc in range(16):
                    P.op("tensor", lambda e, hc=hc: e.transpose(psb[hc // 4][:, (hc % 4) * 128:(hc % 4 + 1) * 128], A.v(o_q + hc * 128, [[1, 128]]), ident),
                         reads=["q", "ident"], writes=["pk%d" % (hc // 4)])
                for b4 in range(4):
                    P.op("scalar", lambda e, b4=b4: e.activation(A.v(o_qT + b4 * 512, [[1, 512]]), psb[b4][:, :], AF.Copy), reads=["pk%d" % b4], writes=["qT"])
                for hc in range(16):
                    P.op("tensor", lambda e, hc=hc: e.matmul(psb[hc // 4][:, (hc % 4) * 128:(hc % 4 + 1) * 128], A.v(o_qT + hc * 128, [[1, 128]]),
                                                             A.v(o_kT + hc * 128, [[1, 128]]), start=True, stop=True),
                         reads=["qT", "kT"], writes=["pk%d" % (hc // 4)])
                for b4 in range(4):
                    P.op("scalar", lambda e, b4=b4: e.activation(A.v(o_S + b4 * 512, [[1, 512]]), psb[b4][:, :], AF.Copy), reads=["pk%d" % b4], writes=["S"])
                P.op("vector", lambda e: e.tensor_scalar_add(A.v(o_S, [[1, 2048]]), A.v(o_S, [[1, 2048]]), 64.0), reads=["S"], writes=["S"])
                P.op("vector", lambda e: e.tensor_tensor(u32v(o_S, [[128, 16], [1, 128]]), u32v(o_S, [[128, 16], [1, 128]]), u32v(o_MK, [[0, 16], [1, 128]]), ALU.bitwise_and),
                     reads=["S", "MK"], writes=["S"])
                P.op("vector", lambda e: e.tensor_tensor(u32v(o_S, [[128, 16], [1, 128]]), u32v(o_S, [[128, 16], [1, 128]]), u32v(o_CD, [[0, 16], [1, 128]]), ALU.bitwise_or),
                     reads=["S", "CD"], writes=["S"])
                Wk = A.v(o_W, [[1, 128]])
                for hc in range(16):
                    Sh = A.v(o_S + hc * 128, [[1, 128]])
                    va = A.v(o_V1 + hc * 16, [[1, 8]]); vb_ = A.v(o_V1 + hc * 16 + 8, [[1, 8]])
                    P.op("vector", lambda e, Sh=Sh, va=va: e.max(out=va, in_=Sh), reads=["S"], writes=["V1"])
                    P.op("vector", lambda e, Sh=Sh, va=va: e.match_replace(out=Wk, in_to_replace=va, in_values=Sh, imm_value=-1e30), reads=["S", "V1"], writes=["Wk"])
                    P.op("vector", lambda e, vb_=vb_: e.max(out=vb_, in_=Wk), reads=["Wk"], writes=["V1"])
                P.op("vector", lambda e: e.tensor_tensor(u32v(o_IDX, [[1, 256]]), u32v(o_V1, [[1, 256]]), u32v(o_FF, [[1, 256]]), ALU.bitwise_and), reads=["V1", "FF"], writes=["IDX"])
                P.op("vector", lambda e: e.tensor_copy(A.v(o_IDF, [[1, 256]]), u32v(o_IDX, [[1, 256]])), reads=["IDX"], writes=["IDF"])
                P.op("vector", lambda e: e.tensor_scalar(A.v(o_IDF, [[1, 256]]), A.v(o_IDF, [[1, 256]]), -1.0, 255.0, ALU.mult, ALU.add), reads=["IDF"], writes=["IDF"])
                P.op("vector", lambda e: e.memset(A.v(o_E, [[1, 128]]), 0.0), writes=["E"])
                CAND = A.v(o_CAND, [[1, 256]]); EID = A.v(o_EID, [[1, 256]]); CW = A.v(o_CW, [[1, 256]])
                for h in range(8):
                    c3 = A.v(o_CAND, [[16, 16], [1, 16]]); e3 = A.v(o_EID, [[16, 16], [1, 16]])
                    v1a = A.v(o_V1 + (2 * h) * 16, [[1, 16], [0, 16]]); v2b = A.v(o_V1 + (2 * h + 1) * 16, [[0, 16], [1, 16]])
                    i1a = A.v(o_IDF + (2 * h) * 16, [[1, 16], [0, 16]]); i2b = A.v(o_IDF + (2 * h + 1) * 16, [[0, 16], [1, 16]])
                    P.op("vector", lambda e, c3=c3, v1a=v1a, v2b=v2b: e.tensor_tensor(c3, v1a, v2b, ALU.add), reads=["V1"], writes=["CAND"])
                    P.op("vector", lambda e: e.tensor_tensor(u32v(o_CAND, [[1, 256]]), u32v(o_CAND, [[1, 256]]), u32v(o_MK, [[1, 256]]), ALU.bitwise_and), reads=["CAND", "MK"], writes=["CAND"])
                    P.op("vector", lambda e: e.tensor_tensor(u32v(o_CAND, [[1, 256]]), u32v(o_CAND, [[1, 256]]), u32v(o_CD, [[1, 256]]), ALU.bitwise_or), reads=["CAND", "CD"], writes=["CAND"])
                    P.op("vector", lambda e, e3=e3, i1a=i1a, i2b=i2b: e.scalar_tensor_tensor(e3, i1a, 128.0, i2b, ALU.mult, ALU.add), reads=["IDF"], writes=["EID"])
                    ta = A.v(o_T + h * 16, [[1, 8]]); tb = A.v(o_T + h * 16 + 8, [[1, 8]])
                    P.op("vector", lambda e, ta=ta: e.max(out=ta, in_=CAND), reads=["CAND"], writes=["T"])
                    P.op("vector", lambda e, ta=ta: e.match_replace(out=CW, in_to_replace=ta, in_values=CAND, imm_value=-1e30), reads=["CAND", "T"], writes=["CW"])
                    P.op("vector", lambda e, tb=tb: e.max(out=tb, in_=CW), reads=["CW"], writes=["T"])
                    for k in range(16):
                        P.op("vector", lambda e, h=h, k=k: e.scalar_tensor_tensor(CW, CAND, A.v(o_T + h * 16 + k, [[1, 1]]), EID, ALU.is_equal, ALU.mult,
                                                                                   accum_out=A.v(o_E + h * 16 + k, [[1, 1]])),
                             reads=["CAND", "T", "EID", "E"], writes=["CW", "E"])
                P.op("vector", lambda e: e.tensor_copy(EIu, A.v(o_E, [[1, 128]])), reads=["E"], writes=["EI"])
                P.op("vector", lambda e: e.tensor_tensor(A.v(o_GATE, [[16, 8], [1, 16]]), A.v(o_T, [[16, 8], [1, 16]]), A.v(o_T, [[16, 8], [0, 16]]), ALU.subtract),
                     reads=["T"], writes=["GATE"])
                P.op("scalar", lambda e: e.activation(A.v(o_GATE, [[1, 128]]), A.v(o_GATE, [[1, 128]]), AF.Exp), reads=["GATE"], writes=["GATE"])
                P.op("vector", lambda e: e.tensor_reduce(A.v(o_sm, [[1, 8]]), A.v(o_GATE, [[16, 8], [1, 16]]), AX.X, ALU.add), reads=["GATE"], writes=["sm"])
                P.op("vector", lambda e: e.reciprocal(A.v(o_sm, [[1, 8]]), A.v(o_sm, [[1, 8]])), reads=["sm"], writes=["sm"])
                P.op("vector", lambda e: e.tensor_tensor(A.v(o_GATE, [[16, 8], [1, 16]]), A.v(o_GATE, [[16, 8], [1, 16]]), A.v(o_sm, [[1, 8], [0, 16]]), ALU.mult),
                     reads=["GATE", "sm"], writes=["GATE"])
                P.op("vector", lambda e: e.memset(A.v(o_DOT, [[1, 128]]), 0.0), writes=["DOT"])
                gi = 0
                for hk in range(128):
                    kb = gi % NB; gi += 1
                    Rr = A.v(o_R[kb], [[1, 2048]])
                    P.dma(lambda e, Rr=Rr, hk=hk: e.indirect_dma_start(out=Rr, out_offset=None, in_=ut.ap(),
                                                                     in_offset=bass.IndirectOffsetOnAxis(ap=EIu[:, hk:hk + 1], axis=0)),
                          reads=["EI"], writes=["R%d" % kb], q="gpsimd")
                    P.op("vector", lambda e, Rr=Rr, hk=hk: e.scalar_tensor_tensor(A.v(o_jk, [[1, 2048]]), Rr, 1.0, A.v(o_X, [[1, 2048]]), ALU.mult, ALU.mult,
                                                                                 accum_out=A.v(o_DOT + hk, [[1, 1]])),
                         reads=["R%d" % kb, "X", "DOT"], writes=["jk", "DOT"])
                P.op("scalar", lambda e: e.activation(A.v(o_WG, [[1, 128]]), A.v(o_DOT, [[1, 128]]), AF.Gelu), reads=["DOT"], writes=["WG"])
                P.op("vector", lambda e: e.tensor_tensor(A.v(o_WG, [[1, 128]]), A.v(o_WG, [[1, 128]]), A.v(o_GATE, [[1, 128]]), ALU.mult), reads=["WG", "GATE"], writes=["WG"])
                P.op("gpsimd", lambda e: e.memset(A.v(o_ACC, [[1, 2048]]), 0.0), writes=["ACC"])
                for hk in range(128):
                    kb = gi % NB; gi += 1
                    Rr = A.v(o_R[kb], [[1, 2048]])
                    P.dma(lambda e, Rr=Rr, hk=hk: e.indirect_dma_start(out=Rr, out_offset=None, in_=vt_.ap(),
                                                                     in_offset=bass.IndirectOffsetOnAxis(ap=EIu[:, hk:hk + 1], axis=0)),
                          reads=["EI"], writes=["R%d" % kb], q="gpsimd")
                    P.op("vector", lambda e, Rr=Rr, hk=hk: e.scalar_tensor_tensor(A.v(o_ACC, [[1, 2048]]), Rr, A.v(o_WG + hk, [[1, 1]]), A.v(o_ACC, [[1, 2048]]),
                                                                                 ALU.mult, ALU.add),
                         reads=["R%d" % kb, "WG", "ACC"], writes=["ACC"])
                P.op("vector", lambda e: e.tensor_tensor(A.v(o_ACC, [[1, 2048]]), A.v(o_ACC, [[1, 2048]]), GT, ALU.mult), reads=["ACC", "GT"], writes=["ACC"])
                P.op("gpsimd", lambda e: e.tensor_tensor(A.v(o_ACC, [[1, 2048]]), A.v(o_ACC, [[1, 2048]]), A.v(o_xr, [[1, 2048]]), ALU.add), reads=["ACC", "xr"], writes=["ACC"])
                P.dma(lambda e, t0=t0: e.dma_start(out=out_ap[t0:t0 + 128, :], in_=A.v(o_ACC, [[1, 2048]])), reads=["ACC"], writes=[out_key], is_output=is_output)
            P.barrier()

        def stage_hy_conv3(g, T, L):
            A.reset()
            o_cw = A.f32(3 * 2048); o_cb = A.f32(2048); o_acc = A.f32(2048)
            o_ld = [A.f32(2048), A.f32(2048)]
            o_vb = A.f32(1024)
            vb = A.bf(o_vb, 1024)
            tps = L // 128
            for j in range(3):
                P.dma(lambda e, j=j: e.dma_start(out=A.v(o_cw, [[2048, 3], [1, 2048]]),
                                                 in_=bass.AP(c_conv, j * 2048, [[0, 128], [6144, 3], [1, 2048]])), writes=["CW"])
                P.dma(lambda e, j=j: e.dma_start(out=A.v(o_cb, [[1, 2048]]), in_=bc_rows(c_convb, j * 2048, 2048)), writes=["CB"])
                for ti in range(T // 128):
                    t0 = ti * 128
                    first = (ti % tps == 0); last = (ti % tps == tps - 1)
                    ACC = A.v(o_acc, [[1, 2048]])
                    c0 = j * 2048
                    ld = A.v(o_ld[0], [[1, 2048]])
                    P.dma(lambda e, ld=ld, t0=t0, c0=c0: e.dma_start(out=ld, in_=uz[g].ap()[t0:t0 + 128, c0:c0 + 2048]), reads=[uz[g].name], writes=["ld0"])
                    P.op("vector", lambda e, ld=ld: e.tensor_tensor(ACC, ld, A.v(o_cw + 2048, [[1, 2048]]), ALU.mult), reads=["ld0", "CW"], writes=["ACC"])
                    ld = A.v(o_ld[1], [[1, 2048]])
                    if first:
                        P.op("gpsimd", lambda e, ld=ld: e.memset(ld, 0.0), writes=["ld1"])
                        P.dma(lambda e, t0=t0, c0=c0: e.dma_start(out=A.v(o_ld[1], [[1, 2048]], p0=1, np_=127), in_=uz[g].ap()[t0:t0 + 127, c0:c0 + 2048]),
                              reads=[uz[g].name], writes=["ld1"])
                    else:
                        P.dma(lambda e, ld=ld, t0=t0, c0=c0: e.dma_start(out=ld, in_=uz[g].ap()[t0 - 1:t0 + 127, c0:c0 + 2048]), reads=[uz[g].name], writes=["ld1"])
                    P.op("gpsimd", lambda e, ld=ld: e.tensor_tensor(ld, ld, A.v(o_cw, [[1, 2048]]), ALU.mult), reads=["ld1", "CW"], writes=["ld1"])
                    P.op("vector", lambda e, ld=ld: e.tensor_tensor(ACC, ACC, ld, ALU.add), reads=["ld1", "ACC"], writes=["ACC"])
                    ld = A.v(o_ld[0], [[1, 2048]])
                    if last:
                        P.op("gpsimd", lambda e, ld=ld: e.memset(ld, 0.0), writes=["ld0"])
                        P.dma(lambda e, t0=t0, c0=c0: e.dma_start(out=A.v(o_ld[0], [[1, 2048]], p0=0, np_=127), in_=uz[g].ap()[t0 + 1:t0 + 128, c0:c0 + 2048]),
                              reads=[uz[g].name], writes=["ld0"])
                    else:
                        P.dma(lambda e, ld=ld, t0=t0, c0=c0: e.dma_start(out=ld, in_=uz[g].ap()[t0 + 1:t0 + 129, c0:c0 + 2048]), reads=[uz[g].name], writes=["ld0"])
                    P.op("gpsimd", lambda e, ld=ld: e.tensor_tensor(ld, ld, A.v(o_cw + 4096, [[1, 2048]]), ALU.mult), reads=["ld0", "CW"], writes=["ld0"])
                    P.op("vector", lambda e, ld=ld: e.tensor_tensor(ACC, ACC, ld, ALU.add), reads=["ld0", "ACC"], writes=["ACC"])
                    if j < 2:
                        P.op("vector", lambda e: e.tensor_tensor(ACC, ACC, A.v(o_cb, [[1, 2048]]), ALU.add), reads=["ACC", "CB"], writes=["ACC"])
                        P.dma(lambda e, t0=t0, c0=c0: e.dma_start(out=ug[g].ap()[t0:t0 + 128, c0:c0 + 2048], in_=ACC), reads=["ACC"], writes=[ug[g].name])
                    else:
                        P.op("vector", lambda e: e.tensor_tensor(vb, ACC, A.v(o_cb, [[1, 2048]]), ALU.add), reads=["ACC", "CB"], writes=["vb"])
                        P.dma(lambda e, t0=t0: e.dma_start(out=zb16[g][0].ap()[t0:t0 + 128, :], in_=vb), reads=["vb"], writes=[zb16[g][0].name])
            P.barrier()

        def sin_wrapped(dst, arg, m1, m2, key, okey):
            PI = math.pi
            for _ in range(2):
                P.op("vector", lambda e: e.tensor_scalar(m1, arg, -PI, 2 * PI, ALU.is_lt, ALU.mult), reads=[key], writes=[key + "m1"])
                P.op("vector", lambda e: e.tensor_scalar(m2, arg, PI, 2 * PI, ALU.is_gt, ALU.mult), reads=[key], writes=[key + "m2"])
                P.op("vector", lambda e: e.tensor_tensor(arg, arg, m1, ALU.add), reads=[key, key + "m1"], writes=[key])
                P.op("vector", lambda e: e.tensor_tensor(arg, arg, m2, ALU.subtract), reads=[key, key + "m2"], writes=[key])
            P.op("scalar", lambda e: e.activation(dst, arg, AF.Sin), reads=[key], writes=[okey])

        def stage_hy_filters(li, L):
            A.reset()
            NT = L // 128
            o_zT = A.f32(L); o_fw3 = A.f32(8192); o_h1 = A.f32(L); o_h2 = A.f32(L)
            o_FT = [A.f32(2048) for _ in range(4)]; o_SS = [A.f32(2048), A.f32(2048)]
            o_WIN = A.f32(2048); o_DEL = A.f32(2048); o_sq = A.f32(512)
            o_fw1 = A.f32(64); o_fw2 = A.f32(64); o_col = A.f32(8); o_arg = A.f32(512); o_m1 = A.f32(512); o_m2 = A.f32(512)
            o_tn = A.f32(NT); o_one = A.f32(1); o_row = A.f32(4096); o_hs = A.f32(1024)
            hsb = A.bf(o_hs, 1024)
            zT = A.v(o_zT, [[1, L]], np_=33)
            P.dma(lambda e: e.dma_start(out=zT, in_=zposT[li].ap()), writes=["zT"])
            P.dma(lambda e: e.dma_start(out=A.v(o_fw1, [[1, 64]], np_=33), in_=c_fw1.ap()), writes=["fw1"])
            P.dma(lambda e: e.dma_start(out=A.v(o_fw2, [[1, 64]], np_=64), in_=c_fw2.ap()), writes=["fw2"])
            P.dma(lambda e: e.dma_start(out=A.v(o_fw3, [[1, 8192]], np_=64), in_=c_fw3.ap()), writes=["fw3"])
            for i, t_ in enumerate((c_fb1, c_freq, c_fb2)):
                P.dma(lambda e, i=i, t_=t_: e.dma_start(out=A.v(o_col + i, [[1, 1]], np_=64), in_=bass.AP(t_, 0, [[1, 64], [1, 1]])), writes=["col"])
            P.dma(lambda e: e.dma_start(out=A.v(o_DEL, [[1, 2048]]), in_=bc_rows(deltas, 0, 2048)), writes=["DEL"])
            P.dma(lambda e: e.dma_start(out=A.v(o_tn, [[1, NT]]), in_=tneg[li].ap().rearrange("(n p) -> p n", p=128)), writes=["tn"])
            P.op("vector", lambda e: e.memset(A.v(o_one, [[1, 1]]), 1.0), writes=["one"])
            for n in range(2):
                P.op("vector", lambda e, n=n: e.memset(A.v(o_SS[n], [[1, 2048]]), 0.0), writes=["SS%d" % n])
            nb = max(1, L // 512)
            bw = min(512, L)
            for layer_i, (o_src, o_dst, o_w, kk, bcol, skey, wkey, okey) in enumerate(
                    ((o_zT, o_h1, o_fw1, 33, 0, "zT", "fw1", "h1"), (o_h1, o_h2, o_fw2, 64, 2, "h1", "fw2", "h2"))):
                for b in range(nb):
                    P.op("tensor", lambda e, b=b, o_src=o_src, o_w=o_w, kk=kk: e.matmul(psb[b % 2][0:64, 0:bw], A.v(o_w, [[1, 64]], np_=kk),
                                                                                      A.v(o_src + b * bw, [[1, bw]], np_=kk), start=True, stop=True),
                         reads=[skey, wkey], writes=["pf%d" % (b % 2)])
                    arg = A.v(o_arg, [[1, bw]], np_=64)
                    P.op("vector", lambda e, b=b, arg=arg, bcol=bcol: e.tensor_scalar(arg, psb[b % 2][0:64, 0:bw], A.v(o_col + bcol, [[1, 1]], np_=64),
                                                                                     A.v(o_col + 1, [[1, 1]], np_=64), ALU.add, ALU.mult),
                         reads=["pf%d" % (b % 2), "col"], writes=["arg"])
                    sin_wrapped(A.v(o_dst + b * bw, [[1, bw]], np_=64), arg, A.v(o_m1, [[1, bw]], np_=64), A.v(o_m2, [[1, bw]], np_=64), "arg", okey)
            for lt in range(NT):
                P.op("scalar", lambda e, lt=lt: e.activation(A.v(o_WIN, [[1, 2048]]), A.v(o_DEL, [[1, 2048]]), AF.Exp, scale=A.v(o_tn + lt, [[1, 1]])),
                     reads=["DEL", "tn"], writes=["WIN"])
                for cc in range(16):
                    nd = cc // 4; cch = cc % 4
                    P.op("tensor", lambda e, lt=lt, cc=cc: e.matmul(psb[cc % 4][:, :], A.v(o_h2 + lt * 128, [[1, 128]], np_=64),
                                                                    A.v(o_fw3 + cc * 512, [[1, 512]], np_=64), start=True, stop=True),
                         reads=["h2", "fw3"], writes=["pf%d" % (cc % 4)])
                    ft = A.v(o_FT[nd] + cch * 512, [[1, 512]])
                    P.op("vector", lambda e, cc=cc, ft=ft, cch=cch: e.tensor_tensor(ft, psb[cc % 4][:, :], A.v(o_WIN + cch * 512, [[1, 512]]), ALU.mult),
                         reads=["pf%d" % (cc % 4), "WIN"], writes=["FT%d" % nd])
                    P.op("gpsimd", lambda e, ft=ft: e.tensor_tensor(A.v(o_sq, [[1, 512]]), ft, ft, ALU.mult), reads=["FT%d" % nd], writes=["sqf"])
                    ssv = A.v(o_SS[nd // 2] + cch * 512, [[1, 512]])
                    P.op("vector", lambda e, ssv=ssv: e.tensor_tensor(ssv, ssv, A.v(o_sq, [[1, 512]]), ALU.add), reads=["sqf", "SS%d" % (nd // 2)], writes=["SS%d" % (nd // 2)])
                for n in range(2):
                    hf = A.v(o_FT[2 * n], [[1, 2048]]); hbk = A.v(o_FT[2 * n + 1], [[1, 2048]])
                    if lt == 0:
                        P.op("vector", lambda e, n=n: e.memset(A.v(o_FT[2 * n + 1], [[1, 2048]], np_=1), 0.0), reads=["FT%d" % (2 * n + 1)], writes=["FT%d" % (2 * n + 1)])
                    P.op("vector", lambda e, hf=hf, hbk=hbk: e.tensor_tensor(hsb[:, 0:2048], hf, hbk, ALU.add), reads=["FT%d" % (2 * n), "FT%d" % (2 * n + 1)], writes=["hs"])
                    P.dma(lambda e, lt=lt, n=n: e.dma_start(out=HS[li].ap()[lt * 128:(lt + 1) * 128, 2 * n, :], in_=hsb[:, 0:2048]), reads=["hs"], writes=["HS"])
                    P.op("vector", lambda e, hf=hf, hbk=hbk: e.tensor_tensor(hsb[:, 0:2048], hbk, hf, ALU.subtract), reads=["FT%d" % (2 * n), "FT%d" % (2 * n + 1)], writes=["hs"])
                    P.dma(lambda e, lt=lt, n=n: e.dma_start(out=HS[li].ap()[lt * 128:(lt + 1) * 128, 2 * n + 1, :], in_=hsb[:, 0:2048]), reads=["hs"], writes=["HS"])
            for n in range(2):
                for cch in range(4):
                    P.op("tensor", lambda e, n=n, cch=cch: e.matmul(psb[cch][0:1, :], A.v(o_one, [[1, 1]]), A.v(o_SS[n] + cch * 512, [[1, 512]]), start=True, stop=True),
                         reads=["one", "SS%d" % n], writes=["pf%d" % cch])
                    P.op("vector", lambda e, n=n, cch=cch: e.tensor_copy(A.v(o_row + n * 2048 + cch * 512, [[1, 512]], np_=1), psb[cch][0:1, :]),
                         reads=["pf%d" % cch], writes=["row"])
            rsqrt_ops(P, A.v(o_row, [[1, 4096]], np_=1), 1.0, 1e-12, "row")
            P.dma(lambda e: e.dma_start(out=bass.AP(nsc[li], 0, [[4096, 1], [1, 4096]]), in_=A.v(o_row, [[1, 4096]], np_=1)),
                  reads=["row"], writes=["nsc"])
            P.barrier()

        def hy_forward(li, L, Zc, o_mat, kt, data_keys, tag):
            NT = L // 128
            kb = kt % 2
            Cm = A.bf(o_mat[kb][0], NT * 64).rearrange("p (t k) -> p t k", t=NT)
            Sm = A.bf(o_mat[kb][1], NT * 64).rearrange("p (t k) -> p t k", t=NT)
            P.dma(lambda e: e.dma_start(out=Cm, in_=dftc[li].ap()[:, kt * 128:(kt + 1) * 128].rearrange("(t p) k -> p t k", p=128)), writes=["Cm%d" % kb])
            P.dma(lambda e: e.dma_start(out=Sm, in_=dfts[li].ap()[:, kt * 128:(kt + 1) * 128].rearrange("(t p) k -> p t k", p=128)), writes=["Sm%d" % kb])
            Zcos, Zsin = Zc if isinstance(Zc, tuple) else (Zc, Zc)
            pa = psb[2 * kb]; pbk = psb[2 * kb + 1]
            for tt in range(NT):
                P.op("tensor", lambda e, tt=tt: e.matmul(pa[:, :], Cm[:, tt, :], Zcos[:, tt, :], start=(tt == 0), stop=(tt == NT - 1)),
                     reads=["Cm%d" % kb] + data_keys, writes=["pA%d" % kb])
            for tt in range(NT):
                P.op("tensor", lambda e, tt=tt: e.matmul(pbk[:, :], Sm[:, tt, :], Zsin[:, tt, :], start=(tt == 0), stop=(tt == NT - 1)),
                     reads=["Sm%d" % kb] + data_keys, writes=["pB%d" % kb])
            return pa, pbk, kb

        def stage_hy_fspec(li, L):
            A.reset()
            NT = L // 128
            o_hs = A.f32(NT * 256); o_hd = A.f32(NT * 256)
            o_mat = [[A.f32(NT * 64), A.f32(NT * 64)], [A.f32(NT * 64), A.f32(NT * 64)]]
            o_wk = A.f32(NT); o_o = [A.f32(512), A.f32(512)]
            P.dma(lambda e: e.dma_start(out=A.v(o_wk, [[1, NT]]), in_=wkv[li].ap().rearrange("(n p) -> p n", p=128)), writes=["wk"])
            for n in range(2):
                for cch in range(4):
                    hsT = A.bf(o_hs, NT * 256).rearrange("p (t c) -> p t c", t=NT)
                    hdT = A.bf(o_hd, NT * 256).rearrange("p (t c) -> p t c", t=NT)
                    P.dma(lambda e, n=n, cch=cch, hsT=hsT: e.dma_start(out=hsT, in_=HS[li].ap()[:, 2 * n, cch * 512:(cch + 1) * 512].rearrange("(t p) c -> p t c", p=128)),
                          reads=["HS"], writes=["hsT"])
                    P.dma(lambda e, n=n, cch=cch, hdT=hdT: e.dma_start(out=hdT, in_=HS[li].ap()[:, 2 * n + 1, cch * 512:(cch + 1) * 512].rearrange("(t p) c -> p t c", p=128)),
                          reads=["HS"], writes=["hdT"])
                    for kt in range(NT):
                        pa, pbk, kb = hy_forward(li, L, (hsT, hdT), o_mat, kt, ["hsT", "hdT"], "f")
                        for j, (pp, key) in enumerate(((pa, "pA%d" % kb), (pbk, "pB%d" % kb))):
                            ot = A.v(o_o[j], [[1, 512]])
                            P.op("vector", lambda e, ot=ot, pp=pp, kt=kt: e.tensor_scalar(ot, pp[:, :], A.v(o_wk + kt, [[1, 1]]), None, ALU.mult),
                                 reads=[key, "wk"], writes=["o%d" % j])
                            P.dma(lambda e, ot=ot, n=n, j=j, kt=kt, cch=cch: e.dma_start(out=FS[li].ap()[n, j, kt * 128:(kt + 1) * 128, cch * 512:(cch + 1) * 512], in_=ot),
                                  reads=["o%d" % j], writes=["FS"])
            P.barrier()

        def stage_hy_conv(g, T, li, L, seq, n, final):
            A.reset()
            NT = L // 128
            base = seq * L
            o_Z = A.f32(NT * 256); o_YC = A.f32(NT * 256); o_YS = A.f32(NT * 256)
            o_mat = [[A.f32(NT * 64), A.f32(NT * 64)], [A.f32(NT * 64), A.f32(NT * 64)]]
            o_F = [[A.f32(512), A.f32(512)], [A.f32(512), A.f32(512)]]
            o_t = [A.f32(512) for _ in range(4)]
            o_gt = [A.f32(512), A.f32(512)]; o_ns = A.f32(512); o_bs = A.f32(512)
            o_zn = [A.f32(256), A.f32(256)]; o_zT = A.f32(256)
            zin = zb16[g][n]; zout = zb16[g][n + 1]
            for cch in range(4):
                c0 = cch * 512
                Z = A.bf(o_Z, NT * 256).rearrange("p (t c) -> p t c", t=NT)
                YC = A.bf(o_YC, NT * 256).rearrange("p (t c) -> p t c", t=NT)
                YS = A.bf(o_YS, NT * 256).rearrange("p (t c) -> p t c", t=NT)
                P.dma(lambda e, c0=c0, Z=Z: e.dma_start(out=Z, in_=zin.ap()[base:base + L, c0:c0 + 512].rearrange("(t p) c -> p t c", p=128)),
                      reads=[zin.name], writes=["Z"])
                P.dma(lambda e, c0=c0: e.dma_start(out=A.v(o_ns, [[1, 512]]), in_=bc_rows(nsc[li], n * 2048 + c0, 512)), reads=["nsc"], writes=["NS"])
                P.dma(lambda e, c0=c0: e.dma_start(out=A.v(o_bs, [[1, 512]]), in_=bc_rows(c_bias, n * 2048 + c0, 512)), writes=["BS"])
                for kt in range(NT):
                    pa, pbk, kb = hy_forward(li, L, Z, o_mat, kt, ["Z"], "c")
                    Fc = A.v(o_F[kb][0], [[1, 512]]); Fs = A.v(o_F[kb][1], [[1, 512]])
                    P.dma(lambda e, Fc=Fc, kt=kt, c0=c0: e.dma_start(out=Fc, in_=FS[li].ap()[n, 0, kt * 128:(kt + 1) * 128, c0:c0 + 512]), reads=["FS"], writes=["Fc%d" % kb])
                    P.dma(lambda e, Fs=Fs, kt=kt, c0=c0: e.dma_start(out=Fs, in_=FS[li].ap()[n, 1, kt * 128:(kt + 1) * 128, c0:c0 + 512]), reads=["FS"], writes=["Fs%d" % kb])
                    t = [A.v(o, [[1, 512]]) for o in o_t]
                    P.op("vector", lambda e, t=t, pa=pa, Fc=Fc: e.tensor_tensor(t[0], pa[:, :], Fc, ALU.mult), reads=["pA%d" % kb, "Fc%d" % kb], writes=["t0"])
                    P.op("vector", lambda e, t=t, pbk=pbk, Fs=Fs: e.tensor_tensor(t[1], pbk[:, :], Fs, ALU.mult), reads=["pB%d" % kb, "Fs%d" % kb], writes=["t1"])
                    P.op("vector", lambda e, t=t, pbk=pbk, Fc=Fc: e.tensor_tensor(t[2], pbk[:, :], Fc, ALU.mult), reads=["pB%d" % kb, "Fc%d" % kb], writes=["t2"])
                    P.op("vector", lambda e, t=t, pa=pa, Fs=Fs: e.tensor_tensor(t[3], pa[:, :], Fs, ALU.mult), reads=["pA%d" % kb, "Fs%d" % kb], writes=["t3"])
                    P.op("gpsimd", lambda e, t=t, kt=kt, YC=YC: e.tensor_tensor(YC[:, kt, :], t[0], t[1], ALU.add), reads=["t0", "t1"], writes=["YC"])
                    P.op("gpsimd", lambda e, t=t, kt=kt, YS=YS: e.tensor_tensor(YS[:, kt, :], t[2], t[3], ALU.subtract), reads=["t2", "t3"], writes=["YS"])
                for tt in range(NT):
                    kb = tt % 2
                    Cm = A.bf(o_mat[kb][0], NT * 64).rearrange("p (t k) -> p t k", t=NT)
                    Sm = A.bf(o_mat[kb][1], NT * 64).rearrange("p (t k) -> p t k", t=NT)
                    P.dma(lambda e, Cm=Cm, tt=tt: e.dma_start(out=Cm, in_=dftc[li].ap()[:, tt * 128:(tt + 1) * 128].rearrange("(t p) k -> p t k", p=128)), writes=["Cm%d" % kb])
                    P.dma(lambda e, Sm=Sm, tt=tt: e.dma_start(out=Sm, in_=dfts[li].ap()[:, tt * 128:(tt + 1) * 128].rearrange("(t p) k -> p t k", p=128)), writes=["Sm%d" % kb])
                    py = psb[4 + kb]
                    for kt in range(NT):
                        P.op("tensor", lambda e, kt=kt, Cm=Cm, YC=YC, py=py: e.matmul(py[:, :], Cm[:, kt, :], YC[:, kt, :], start=(kt == 0), stop=False),
                             reads=["Cm%d" % kb, "YC"], writes=["pY%d" % kb])
                    for kt in range(NT):
                        P.op("tensor", lambda e, kt=kt, Sm=Sm, YS=YS, py=py: e.matmul(py[:, :], Sm[:, kt, :], YS[:, kt, :], start=False, stop=(kt == NT - 1)),
                             reads=["Sm%d" % kb, "YS"], writes=["pY%d" % kb])
                    gt_ = A.v(o_gt[kb], [[1, 512]])
                    t0 = base + tt * 128
                    P.dma(lambda e, gt_=gt_, t0=t0, c0=c0: e.dma_start(out=gt_, in_=ug[g].ap()[t0:t0 + 128, n * 2048 + c0:n * 2048 + c0 + 512]), reads=[ug[g].name], writes=["gt%d" % kb])
                    ta = A.v(o_t[0], [[1, 512]]); tb = A.v(o_t[1], [[1, 512]])
                    P.op("vector", lambda e, ta=ta, py=py: e.tensor_tensor(ta, py[:, :], A.v(o_ns, [[1, 512]]), ALU.mult), reads=["pY%d" % kb, "NS"], writes=["t0"])
                    P.op("gpsimd", lambda e, tb=tb, tt=tt, Z=Z: e.tensor_tensor(tb, Z[:, tt, :], A.v(o_bs, [[1, 512]]), ALU.mult), reads=["Z", "BS"], writes=["t1"])
                    P.op("vector", lambda e, ta=ta, tb=tb: e.tensor_tensor(ta, ta, tb, ALU.add), reads=["t0", "t1"], writes=["t0"])
                    zn = A.bf(o_zn[kb], 256)
                    P.op("vector", lambda e, ta=ta, gt_=gt_, zn=zn: e.tensor_tensor(zn, ta, gt_, ALU.mult), reads=["t0", "gt%d" % kb], writes=["zn%d" % kb])
                    if not final:
                        P.dma(lambda e, zn=zn, t0=t0, c0=c0: e.dma_start(out=zout.ap()[t0:t0 + 128, c0:c0 + 512], in_=zn), reads=["zn%d" % kb], writes=[zout.name])
                    else:
                        zT4 = A.bf(o_zT, 256).rearrange("p (c t) -> p c t", c=4)
                        for c4 in range(4):
                            P.op("tensor", lambda e, c4=c4, zn=zn: e.transpose(psT[:, c4 * 128:(c4 + 1) * 128], zn[:, c4 * 128:(c4 + 1) * 128], identb),
                                 reads=["zn%d" % kb, "identb"], writes=["psT"])
                        P.op("scalar", lambda e, zT4=zT4: e.activation(zT4, psT[:, 0:512].rearrange("p (c t) -> p c t", c=4), AF.Copy), reads=["psT"], writes=["zT4"])
                        P.dma(lambda e, zT4=zT4, t0=t0, cch=cch: e.dma_start(out=hzT[g].ap()[cch * 4:(cch + 1) * 4, :, t0:t0 + 128].rearrange("c p t -> p c t"), in_=zT4),
                              reads=["zT4"], writes=[hzT[g].name])
            P.barrier()

        def stage_gather_rows(src_t, dst_t, n):
            A.reset()
            o_i = A.f32(8); o_r = [A.f32(2048), A.f32(2048)]
            idx = bass.AP(A.t, o_i, [[A.size, 128], [1, 8]]).bitcast(I32)
            P.dma(lambda e: e.dma_start(out=idx, in_=qidx.ap().rearrange("(n p) o -> p (n o)", p=128)), writes=["qi"])
            for ti in range(n // 128):
                k = ti % 2
                Rr = A.v(o_r[k], [[1, 2048]])
                P.dma(lambda e, Rr=Rr, ti=ti: e.indirect_dma_start(out=Rr, out_offset=None, in_=src_t.ap(),
                                                                 in_offset=bass.IndirectOffsetOnAxis(ap=idx[:, ti:ti + 1], axis=0)),
                      reads=["qi"], writes=["gr%d" % k], q="gpsimd")
                P.dma(lambda e, Rr=Rr, ti=ti: e.dma_start(out=dst_t.ap()[ti * 128:(ti + 1) * 128, :], in_=Rr), reads=["gr%d" % k], writes=[dst_t.name])
            P.barrier()

        def stage_kv_out():
            A.reset()
            o_z = A.f32(512); o_t = A.f32(256); o_s = A.f32(8); o_kn = A.f32(128)
            KN = A.v(o_kn, [[1, 128]])
            P.dma(lambda e: e.dma_start(out=KN, in_=bc_rows(b_kn, 0, 128)), writes=["KN"])
            for ti in range(TP // 128):
                t0 = ti * 128
                zt = A.v(o_z, [[1, 512]])
                P.dma(lambda e, t0=t0: e.dma_start(out=zt, in_=zg[0].ap()[t0:t0 + 128, 4480:4992]), reads=["zp"], writes=["zt"])
                P.dma(lambda e, t0=t0: e.dma_start(out=nv.ap()[t0:t0 + 128, :], in_=A.v(o_z + 256, [[1, 256]])), reads=["zt"], writes=["nv"], is_output=True)
                P.op("vector", lambda e: e.tensor_tensor(A.v(o_t, [[1, 256]]), A.v(o_z, [[1, 256]]), A.v(o_z, [[1, 256]]), ALU.mult), reads=["zt"], writes=["t"])
                P.op("vector", lambda e: e.tensor_reduce(A.v(o_s, [[1, 2]]), A.v(o_t, [[128, 2], [1, 128]]), AX.X, ALU.add), reads=["t"], writes=["s"])
                rsqrt_ops(P, A.v(o_s, [[1, 2]]), 1.0 / 128, 1e-6, "s")
                P.op("vector", lambda e: e.tensor_tensor(A.v(o_t, [[128, 2], [1, 128]]), A.v(o_z, [[128, 2], [1, 128]]), A.v(o_s, [[1, 2], [0, 128]]), ALU.mult),
                     reads=["s", "zt"], writes=["t"])
                P.op("vector", lambda e: e.tensor_tensor(A.v(o_t, [[128, 2], [1, 128]]), A.v(o_t, [[128, 2], [1, 128]]), A.v(o_kn, [[0, 2], [1, 128]]), ALU.mult),
                     reads=["t", "KN"], writes=["t"])
                P.dma(lambda e, t0=t0: e.dma_start(out=nk.ap()[t0:t0 + 128, :], in_=A.v(o_t, [[1, 256]])), reads=["t"], writes=["nk"], is_output=True)
            P.barrier()

        GROUPS = ((0, TP, 256), (1, TS, 4096))

        def layer1(g, T, L, out_ap, out_key):
            li = 0 if L == 256 else 1
            stage_norm_gemm(g, T, x2[g].ap(), 1, 1, norm1, odd_w_in.ap(), 6144, uz[g])
            stage_hy_conv3(g, T, L)
            for n in range(2):
                for seq in range(T // L):
                    stage_hy_conv(g, T, li, L, seq, n, n == 1)
            stage_norm_gemm(g, T, None, 1, 1, None, odd_w_out.ap(), D, x3[g], srcT=hzT[g], resid=(x2[g].ap(), 2 * D))
            if mode == "hytest":
                return
            if g == 1:
                stage_gather_rows(x3[1], x3q, 1024)
                stage_norm_gemm(g, 1024, x3q.ap(), 1, 2, norm2, peer_wq.ap()[1], D, pq[g], hm_out=hm2[g])
                stage_peer(g, 1024, 1, pq[g], hm2[g], x3q.ap(), out_ap, out_key, is_output=True)
                return
            stage_norm_gemm(g, T, x3[g].ap(), 1, 2, norm2, peer_wq.ap()[1], D, pq[g], hm_out=hm2[g])
            stage_peer(g, T, 1, pq[g], hm2[g], x3[g].ap(), out_ap, out_key, is_output=True)

        if mode == "bench_scan":
            stage_scan(0, 0, 256, 0, None, ns)
            stage_scan(0, 0, 256, 1, None, ns)
            P.emit()
            return nc
        if mode == "bench_peer":
            stage_peer(0, 256, 0, pq[0], hm2[0], xg[0].ap(), x2[0].ap(), "x2_0")
            P.emit()
            return nc
        if mode == "bench_gemm":
            stage_norm_gemm(0, TP, xg[0].ap(), 0, 1, norm1, even_w_in.ap(), 4992, zg[0])
            P.emit()
            return nc
        stage_mod()
        for li, L_ in enumerate((256, 4096)):
            stage_hy_filters(li, L_)
            stage_hy_fspec(li, L_)
        if mode == "full":
            for g, T, L in GROUPS:
                latent = (g == 1)
                stage_norm_gemm(g, T, xg[g].ap(), 0, 1, norm1, even_w_in.ap(), 4992, zg[g])
                if g == 0:
                    stage_kv_out()
                stage_even_prep(g, T, L)
                for seq in range(T // L):
                    for d in range(2):
                        stage_scan(g, seq, L, d, st_in if latent else None, None if latent else ns)
                stage_rwkv_post(g, T)
                stage_attn_prep(g, T, L, latent)
                for seq in range(T // L):
                    stage_attn(g, T, seq, L, L + (512 if latent else 0))
                stage_norm_gemm(g, T, None, 0, 1, None, even_w_out.ap(), D, x1[g], srcT=ycatT[g], resid=(xg[g].ap(), 2 * D))
                stage_norm_gemm(g, T, x1[g].ap(), 0, 2, norm2, peer_wq.ap()[0], D, pq[g], hm_out=hm2[g])
                stage_peer(g, T, 0, pq[g], hm2[g], x1[g].ap(), x2[g].ap(), x2[g].name)
        layer1(0, TP, 256, yp.ap(), "yp")
        layer1(1, TS, 4096, ys.ap(), "ys")
        P.emit()
    return nc


_CACHE = {}


def _rope_tables():
    pos = np.arange(4096)
    row = (pos // 64).astype(np.float32); col = (pos % 64).astype(np.float32)
    inv = (np.float32(10000.0) ** (-np.arange(32, dtype=np.float32) / np.float32(32))).astype(np.float32)
    ar = (row[:, None] * inv[None, :]).astype(np.float32); ac = (col[:, None] * inv[None, :]).astype(np.float32)
    cs = np.concatenate([np.cos(ar), np.cos(ac)], 1).astype(np.float32)
    sn = np.concatenate([np.sin(ar), np.sin(ac)], 1).astype(np.float32)
    return cs, sn


def _hyena_consts():
    import ml_dtypes
    out = {}
    for li, L in enumerate((256, 4096)):
        t = np.linspace(0.0, 1.0, L, dtype=np.float32)[:, None]
        wpos = (np.float32(2.0 * math.pi) * np.arange(L, dtype=np.float32)[:, None] / np.float32(L)).astype(np.float32)
        fb = np.linspace(1e-4, 15, 16, dtype=np.float32)[None, :]
        z = np.concatenate([t, np.cos(fb * wpos), -np.sin(fb * wpos)], axis=-1).astype(np.float32)
        out["zposT_%d" % li] = np.ascontiguousarray(z.T)
        out["tneg_%d" % li] = np.ascontiguousarray(-t[:, 0])
        Nf = 2 * L - 1
        wk = np.full((L,), 2.0 / Nf, np.float32); wk[0] = 1.0 / Nf
        out["wk_%d" % li] = wk
        idx = np.arange(L, dtype=np.int64)
        ang = (2.0 * np.pi / Nf) * ((idx[:, None] * idx[None, :]) % Nf).astype(np.float64)
        out["dftc_%d" % li] = np.cos(ang).astype(ml_dtypes.bfloat16)
        out["dfts_%d" % li] = np.sin(ang).astype(ml_dtypes.bfloat16)
    out["deltas"] = np.abs(np.linspace(math.log(1e-2) / 1.5, math.log(1e-2) / 0.3, 2048, dtype=np.float32)).astype(np.float32)
    return out


def _shared_inputs(inputs):
    f = lambda a: np.ascontiguousarray(np.asarray(a, dtype=np.float32))
    sh = {
        "mod_w": f(inputs["mod_w"]), "mod_b": f(inputs["mod_b"]),
        "norm1": f(inputs["norm1"]), "norm2": f(inputs["norm2"]),
        "even_w_in": f(inputs["even_w_in"][0]),
        "even_a_conv": f(inputs["even_a_conv"][0]),
        "even_a_w0": f(inputs["even_a_w0"][0]), "even_a_wu": f(inputs["even_a_wu"][0]).reshape(128, 1024),
        "even_a_a0": f(inputs["even_a_a0"][0]), "even_a_au": f(inputs["even_a_au"][0]).reshape(128, 1024),
        "even_a_gu": f(inputs["even_a_gu"][0]),
        "even_a_kk": f(inputs["even_a_kk"][0]), "even_a_ka": f(inputs["even_a_ka"][0]),
        "even_a_rk": f(inputs["even_a_rk"][0]).reshape(1024),
        "even_a_ln_w": f(inputs["even_a_ln_w"][0]), "even_a_ln_b": f(inputs["even_a_ln_b"][0]),
        "even_b_qnorm": f(inputs["even_b_qnorm"][0]), "even_b_knorm": f(inputs["even_b_knorm"][0]),
        "ident": np.eye(128, dtype=np.float32),
        "ropec": _rope_tables()[0], "ropes": _rope_tables()[1],
        "even_w_out": f(inputs["even_w_out"][0]),
        "odd_w_in": f(inputs["odd_w_in"][0]), "odd_w_out": f(inputs["odd_w_out"][0]),
        "odd_c_conv": f(inputs["odd_c_conv"][0]), "odd_c_conv_b": f(inputs["odd_c_conv_b"][0]),
        "odd_c_fw1": f(inputs["odd_c_fw1"][0]), "odd_c_fb1": f(inputs["odd_c_fb1"][0]), "odd_c_freq": f(inputs["odd_c_freq"][0]),
        "odd_c_fw2": f(inputs["odd_c_fw2"][0]), "odd_c_fb2": f(inputs["odd_c_fb2"][0]), "odd_c_fw3": f(inputs["odd_c_fw3"][0]),
        "odd_c_bias": f(inputs["odd_c_bias"][0]),
        "peer_wq": f(inputs["peer_wq"]), "peer_keys": f(inputs["peer_keys"]),
        "peer_u0": f(inputs["peer_u"][0]), "peer_u1": f(inputs["peer_u"][1]),
        "peer_v0": f(inputs["peer_v"][0]), "peer_v1": f(inputs["peer_v"][1]),
    }
    sh.update(_hyena_consts())
    kc = np.zeros((3, 128, 256), np.uint32)
    kc[0] = 0xFFFFFF00
    kc[1] = 0xFF
    kc[2] = (255 - (np.arange(256) % 256)).astype(np.uint32)[None, :]
    sh["kconst"] = kc
    return sh


def kernel(**inputs):
    f = lambda a: np.ascontiguousarray(np.asarray(a, dtype=np.float32))
    if "nc" not in _CACHE:
        _CACHE["nc"] = build_program()
    nc = _CACHE["nc"]
    x_prompt = f(inputs["x_prompt"]); x_sample = f(inputs["x_sample"])
    shared = _shared_inputs(inputs)
    in_maps = []
    for c in range(NC):
        b = c // 4
        m = dict(shared)
        m["xp"] = x_prompt[4 * c:4 * c + 4].reshape(1024, D)
        m["xs"] = x_sample[b]
        m["ck"] = f(inputs["cache_b_k"][b, 0]).reshape(512, 256)
        m["cv"] = f(inputs["cache_b_v"][b, 0]).reshape(512, 256)
        m["st"] = f(inputs["state_a"][b, 0])
        m["cond"] = np.stack([f(inputs["c_ctx"]), f(inputs["c"][b])], 0)
        m["qidx"] = (np.arange(1024, dtype=np.int32) + 1024 * (c % 4)).reshape(1024, 1)
        in_maps.append(m)
    res = run_bass_kernel_spmd(nc, in_maps, core_ids=list(range(NC)))
    R = res.results
    if DEBUG_OUT:
        DEBUG_RES['R'] = R
    y_prompt = np.concatenate([R[c]["yp"].reshape(4, 256, D) for c in range(NC)], 0)
    y_sample = np.stack([np.concatenate([R[4 * b + q]["ys"] for q in range(4)], 0) for b in range(2)], 0)
    new_k = np.concatenate([R[c]["nk"].reshape(4, 1, 256, 2, 128) for c in range(NC)], 0)
    new_v = np.concatenate([R[c]["nv"].reshape(4, 1, 256, 2, 128) for c in range(NC)], 0)
    new_s = np.concatenate([R[c]["ns"].reshape(4, 1, 2, 16, 64, 64) for c in range(NC)], 0)
    return (y_prompt.astype(np.float32), y_sample.astype(np.float32), new_k.astype(np.float32),
            new_v.astype(np.float32), new_s.astype(np.float32))
```

```python
from contextlib import ExitStack
import math
import numpy as np
import concourse.bass as bass
import concourse.mybir as mybir
from concourse.bass_utils import run_bass_kernel_spmd

F32 = mybir.dt.float32
BF16 = mybir.dt.bfloat16
I32 = mybir.dt.int32
U32 = mybir.dt.uint32
AF = mybir.ActivationFunctionType
ALU = mybir.AluOpType
AX = mybir.AxisListType

D = 2048
NC = 8
COMPUTE = ("tensor", "vector", "scalar", "gpsimd")
NDMA_SLOTS = 12
SEM_LIMIT = 30000
DEBUG_OUT = set()
UNTRACKED = {"sq", "sq_r", "sq_kk", "sq_w", "sq_kt", "sq_b", "gsc", "bon", "vT", "yT", "nv", "nk", "qT_d", "kT_d", "V_d",
             "HS", "FS", "ns", "yp", "ys", "modv", "nsc", "zero_d"}
SCAN_DVE_INORDER = True
DEBUG_RES = {}


class Prog:
    def __init__(self, nc, stack):
        self.nc = nc
        self.stack = stack
        self.ops = {e: [] for e in COMPUTE + ("sync",)}
        self.nsem = 0
        self.csem = {e: self._newsem("c_" + e) for e in COMPUTE}
        self.ccnt = {e: 0 for e in COMPUTE}
        self.dslots = {}
        self.dnext = {}
        for q in ("sync", "gpsimd"):
            self.dslots[q] = [[self._newsem("d_%s%d" % (q, i)), 0] for i in range(NDMA_SLOTS)]
            self.dnext[q] = 0
        self.last_w = {}
        self.readers = {}
        self.waited = {e: {} for e in self.ops}
        self.out_events = []
        self.all_events = {}
        self.no_self = {"tensor"}
        self.own = {e: {id(self.csem[e])} for e in COMPUTE}

    def _newsem(self, name):
        self.nsem += 1
        return self.stack.enter_context(self.nc.semaphore("%s_%d" % (name, self.nsem)))

    def _deps(self, reads, writes):
        reads = [k for k in reads if k not in UNTRACKED]
        writes = [k for k in writes if k not in UNTRACKED]
        deps = []
        for k in reads:
            if k in self.last_w:
                deps.append(self.last_w[k])
        for k in writes:
            if k in self.last_w:
                deps.append(self.last_w[k])
            deps.extend(self.readers.get(k, ()))
        return deps

    def _emit_waits(self, eng, deps):
        need = {}
        for (s, v) in deps:
            if v > need.get(id(s), (s, 0))[1]:
                need[id(s)] = (s, v)
        w = self.waited[eng]
        skip = self.own.get(eng, ()) if eng in self.no_self else ()
        for sid, (s, v) in need.items():
            if w.get(sid, 0) >= v or sid in skip:
                continue
            w[sid] = v
            self.ops[eng].append(("wait", s, v))

    def _commit(self, ev, reads, writes):
        self.all_events[id(ev[0])] = ev
        reads = [k for k in reads if k not in UNTRACKED]
        writes = [k for k in writes if k not in UNTRACKED]
        for k in reads:
            self.readers.setdefault(k, []).append(ev)
        for k in writes:
            self.last_w[k] = ev
            self.readers[k] = []

    def op(self, eng, fn, reads=(), writes=()):
        deps = self._deps(reads, writes)
        self._emit_waits(eng, deps)
        if self.ccnt[eng] >= SEM_LIMIT:
            self.csem[eng] = self._newsem("c_" + eng)
            self.own[eng].add(id(self.csem[eng]))
            self.ccnt[eng] = 0
        self.ccnt[eng] += 1
        ev = (self.csem[eng], self.ccnt[eng])
        self.ops[eng].append(("op", fn, ev[0], 1))
        self._commit(ev, reads, writes)
        return ev

    def dma(self, fn, reads=(), writes=(), q="sync", is_output=False):
        deps = self._deps(reads, writes)
        slots = self.dslots[q]
        i = self.dnext[q]
        self.dnext[q] = (i + 1) % len(slots)
        slot = slots[i]
        if slot[1] > 0:
            deps.append((slot[0], slot[1]))
        self._emit_waits(q, deps)
        if slot[1] + 16 > SEM_LIMIT:
            slot[0] = self._newsem("d_" + q)
            slot[1] = 0
        slot[1] += 16
        ev = (slot[0], slot[1])
        self.ops[q].append(("op", fn, ev[0], 16))
        self._commit(ev, reads, writes)
        if is_output:
            self.out_events.append(ev)
        return ev

    def barrier(self):
        evs = list(self.all_events.values())
        saved = self.no_self
        self.no_self = set()
        for e in self.ops:
            self._emit_waits(e, evs)
        self.no_self = saved
        self.last_w = {}
        self.readers = {}

    def emit(self):
        nc = self.nc
        self._emit_waits("sync", list(self.all_events.values()))
        with nc.Block() as block:
            def run(engname):
                def body(eng):
                    for item in self.ops[engname]:
                        if item[0] == "wait":
                            eng.wait_ge(item[1], item[2])
                        else:
                            item[1](eng).then_inc(item[2], item[3])
                return body
            block.sync(run("sync"))
            block.tensor(run("tensor"))
            block.vector(run("vector"))
            block.scalar(run("scalar"))
            block.gpsimd(run("gpsimd"))


class Arena:
    def __init__(self, t, size):
        self.t = t
        self.size = size
        self.off = 0

    def reset(self):
        self.off = 0

    def f32(self, n):
        o = self.off
        self.off += n
        assert self.off <= self.size, ("arena overflow", self.off, self.size)
        return o

    def v(self, off, dims, p0=0, np_=128):
        return bass.AP(self.t, p0 * self.size + off, [[self.size, np_]] + [list(d) for d in dims])

    def bf(self, off, n):
        return self.t[:, off:off + n].bitcast(BF16)


def bc_rows(dram_t, elem_off, n, nparts=128):
    return bass.AP(dram_t, elem_off, [[0, nparts], [1, n]])


def rsqrt_ops(P, ap, mul, add, key):
    P.op("vector", lambda e: e.tensor_scalar(ap, ap, mul, add, ALU.mult, ALU.add), reads=[key], writes=[key])
    P.op("scalar", lambda e: e.activation(ap, ap, AF.Sqrt), reads=[key], writes=[key])
    P.op("vector", lambda e: e.reciprocal(ap, ap), reads=[key], writes=[key])

def build_program(mode="full"):
    nc = bass.Bass("TRN2", target_bir_lowering=False)
    TP, TS = 1024, 4096
    dt_in = {}

    def inp(name, shape, dt=F32):
        dt_in[name] = nc.dram_tensor(name, list(shape), dt, kind="ExternalInput")
        return dt_in[name]

    def outp(name, shape, dt=F32):
        return nc.dram_tensor(name, list(shape), dt, kind="ExternalOutput")

    def scr(name, shape, dt=F32):
        UNTRACKED.add(name)
        if name in DEBUG_OUT:
            return nc.dram_tensor(name, list(shape), dt, kind="ExternalOutput")
        return nc.dram_tensor(name, list(shape), dt)

    xg = [inp("xp", [TP, D]), inp("xs", [TS, D])]
    ck = inp("ck", [512, 256]); cv = inp("cv", [512, 256]); st_in = inp("st", [2, 16, 64, 64])
    cond = inp("cond", [2, D])
    mod_w = inp("mod_w", [2, D, 6 * D]); mod_b = inp("mod_b", [2, 6 * D])
    norm1 = inp("norm1", [2, D]); norm2 = inp("norm2", [2, D])
    even_w_in = inp("even_w_in", [D, 4992])
    a_conv = inp("even_a_conv", [3, 3456])
    a_w0 = inp("even_a_w0", [2, 1024]); a_wu = inp("even_a_wu", [128, 1024])
    a_a0 = inp("even_a_a0", [2, 1024]); a_au = inp("even_a_au", [128, 1024])
    a_gu = inp("even_a_gu", [128, 1024])
    a_kk = inp("even_a_kk", [1024]); a_ka = inp("even_a_ka", [1024]); a_rk = inp("even_a_rk", [1024])
    a_lnw = inp("even_a_ln_w", [1024]); a_lnb = inp("even_a_ln_b", [1024])
    b_qn = inp("even_b_qnorm", [128]); b_kn = inp("even_b_knorm", [128])
    ident_d = inp("ident", [128, 128])
    ropec = inp("ropec", [4096, 64]); ropes = inp("ropes", [4096, 64])
    even_w_out = inp("even_w_out", [D, D])
    odd_w_in = inp("odd_w_in", [D, 6144]); odd_w_out = inp("odd_w_out", [D, D])
    c_conv = inp("odd_c_conv", [3, 6144]); c_convb = inp("odd_c_conv_b", [6144])
    c_fw1 = inp("odd_c_fw1", [33, 64]); c_fb1 = inp("odd_c_fb1", [64]); c_freq = inp("odd_c_freq", [64])
    c_fw2 = inp("odd_c_fw2", [64, 64]); c_fb2 = inp("odd_c_fb2", [64]); c_fw3 = inp("odd_c_fw3", [64, 8192])
    c_bias = inp("odd_c_bias", [2, 2048])
    zposT = [inp("zposT_%d" % li, [33, L_]) for li, L_ in enumerate((256, 4096))]
    tneg = [inp("tneg_%d" % li, [L_]) for li, L_ in enumerate((256, 4096))]
    wkv = [inp("wk_%d" % li, [L_]) for li, L_ in enumerate((256, 4096))]
    deltas = inp("deltas", [2048])
    kconst = inp("kconst", [3, 128, 256], U32)
    qidx = inp("qidx", [1024, 1], I32)
    dftc = [inp("dftc_%d" % li, [L_, L_], BF16) for li, L_ in enumerate((256, 4096))]
    dfts = [inp("dfts_%d" % li, [L_, L_], BF16) for li, L_ in enumerate((256, 4096))]
    peer_wq = inp("peer_wq", [2, D, D]); peer_keys = inp("peer_keys", [2, 8, 2, 128, 128])
    peer_u = [inp("peer_u0", [16384, D]), inp("peer_u1", [16384, D])]
    peer_v = [inp("peer_v0", [16384, D]), inp("peer_v1", [16384, D])]

    yp = outp("yp", [TP, D]); ys = outp("ys", [1024, D])
    nk = outp("nk", [TP, 256]); nv = outp("nv", [TP, 256]); ns = outp("ns", [4, 2, 16, 64, 64])

    modv = scr("modv", [2, 2, 6 * D])
    zg = [scr("zp", [TP, 4992]), scr("zs", [TS, 4992])]
    SQ = ["kk", "w0", "w1", "b0", "b1", "kt0", "kt1", "r"]
    sq = [{n: scr("sq_%s_%d" % (n, g), [T, 1024]) for n in SQ} for g, T in enumerate((TP, TS))]
    vT = [scr("vT_%d" % g, [8, 128, T]) for g, T in enumerate((TP, TS))]
    yT = [[scr("yT_%d_%d" % (g, d), [8, 128, T]) for d in range(2)] for g, T in enumerate((TP, TS))]
    gsc = [scr("g_%d" % g, [T, 1024]) for g, T in enumerate((TP, TS))]
    bon = [scr("bon_%d" % g, [T, 1024]) for g, T in enumerate((TP, TS))]
    zero_d = scr("zero_d", [128, 64])
    ycatT = [scr("ycatT_%d" % g, [16, 128, T], BF16) for g, T in enumerate((TP, TS))]
    qT_d = [scr("qT_%d" % g, [8, 128, T], BF16) for g, T in enumerate((TP, TS))]
    kT_d = [scr("kT_0", [2, 128, TP], BF16), scr("kT_1", [2, 128, TS + 512], BF16)]
    V_d = [scr("V_0", [TP, 256], BF16), scr("V_1", [TS + 512, 256], BF16)]
    x1 = [scr("x1_%d" % g, [T, D]) for g, T in enumerate((TP, TS))]
    if mode == "hytest":
        x2 = [inp("x2_%d" % g, [T, D]) for g, T in enumerate((TP, TS))]
    else:
        x2 = [scr("x2_%d" % g, [T, D]) for g, T in enumerate((TP, TS))]
    x3 = [scr("x3_%d" % g, [T, D]) for g, T in enumerate((TP, TS))]
    x3q = scr("x3q", [1024, D])
    peer_ub = [scr("peer_ub%d" % l, [16384, D], BF16) for l in range(2)]
    peer_vb = [scr("peer_vb%d" % l, [16384, D], BF16) for l in range(2)]
    uz = [scr("uz_%d" % g, [T, 6144]) for g, T in enumerate((TP, TS))]
    ug = [scr("ug_%d" % g, [T, 4096]) for g, T in enumerate((TP, TS))]
    zb16 = [[scr("zb_%d_%d" % (g, i), [T, 2048], BF16) for i in range(3)] for g, T in enumerate((TP, TS))]
    hzT = [scr("hzT_%d" % g, [16, 128, T], BF16) for g, T in enumerate((TP, TS))]
    LL = (256, 4096)
    HS = [scr("HS_%d" % li, [L_, 4, 2048], BF16) for li, L_ in enumerate(LL)]
    FS = [scr("FS_%d" % li, [2, 2, L_, 2048]) for li, L_ in enumerate(LL)]
    nsc = [scr("nsc_%d" % li, [2, 2048]) for li in range(2)]
    hm2 = [scr("hm2_%d" % g, [T, D]) for g, T in enumerate((TP, TS))]
    pq = [scr("pq_%d" % g, [T, D]) for g, T in enumerate((TP, TS))]

    with ExitStack() as st:
        st.enter_context(nc.allow_non_contiguous_dma(reason="layout"))
        P = Prog(nc, st)
        ASZ = 47616
        A = Arena(st.enter_context(nc.sbuf_tensor("arena", [128, ASZ], F32)), ASZ)
        psb = [st.enter_context(nc.psum_tensor("ps%d" % i, [128, 512], F32)) for i in range(6)]
        psT = st.enter_context(nc.psum_tensor("psT", [128, 2048], BF16))
        cons = st.enter_context(nc.sbuf_tensor("cons", [128, 128 + 64], F32))
        consb = st.enter_context(nc.sbuf_tensor("consb", [128, 128], BF16))
        ident = cons[:, 0:128]
        identb = consb[:, 0:128]
        P.dma(lambda e: e.dma_start(out=ident, in_=ident_d.ap()), writes=["ident"])
        P.op("vector", lambda e: e.tensor_copy(identb, ident), reads=["ident"], writes=["identb"])
        P.op("vector", lambda e: e.memset(cons[:, 128:192], 0.0), writes=["zeros"])
        P.dma(lambda e: e.dma_start(out=zero_d.ap(), in_=cons[:, 128:192]), reads=["zeros"], writes=["zero_d"])

        def stage_mod():
            A.reset()
            o_c = A.f32(32); o_s = A.f32(32); o_mb = A.f32(6 * D)
            o_w = [A.f32(16 * 512), A.f32(16 * 512)]
            cT = A.v(o_c, [[1, 32]]); sT = A.v(o_s, [[1, 32]])
            for gg in range(2):
                P.dma(lambda e, gg=gg: e.dma_start(out=A.v(o_c + gg * 16, [[1, 16]]),
                                                   in_=cond.ap()[gg].rearrange("(kc p) -> p kc", p=128)), writes=["cT"])
            P.op("scalar", lambda e: e.activation(sT, cT, AF.Silu), reads=["cT"], writes=["sT"])
            mb = A.v(o_mb, [[1, 6 * D]], np_=2)
            for layer in range(2):
                P.dma(lambda e, layer=layer: e.dma_start(out=mb, in_=bc_rows(mod_b, layer * 6 * D, 6 * D, 2)), writes=["mb"])
                for ch in range(24):
                    k = ch % 2
                    wt = A.v(o_w[k], [[512, 16], [1, 512]])
                    P.dma(lambda e, layer=layer, ch=ch, wt=wt: e.dma_start(
                        out=wt, in_=mod_w.ap()[layer, :, ch * 512:(ch + 1) * 512].rearrange("(kc p) n -> p kc n", p=128)),
                        writes=["mw%d" % k])
                    pb = psb[ch % 2]
                    for kc in range(16):
                        P.op("tensor", lambda e, kc=kc, k=k, pb=pb: e.matmul(
                            pb[0:2, :], A.v(o_s + kc, [[16, 2]]), A.v(o_w[k] + kc * 512, [[1, 512]]),
                            start=(kc == 0), stop=(kc == 15)),
                            reads=["sT", "mw%d" % k], writes=["psm%d" % (ch % 2)])
                    P.op("vector", lambda e, ch=ch, pb=pb: e.tensor_tensor(
                        A.v(o_mb + ch * 512, [[1, 512]], np_=2), A.v(o_mb + ch * 512, [[1, 512]], np_=2), pb[0:2, :], ALU.add),
                        reads=["psm%d" % (ch % 2), "mb"], writes=["mb"])
                for so in (D, 4 * D):
                    P.op("vector", lambda e, so=so: e.tensor_scalar_add(
                        A.v(o_mb + so, [[1, D]], np_=2), A.v(o_mb + so, [[1, D]], np_=2), 1.0), reads=["mb"], writes=["mb"])
                P.dma(lambda e, layer=layer: e.dma_start(out=modv.ap()[layer], in_=mb), reads=["mb"], writes=["modv"])
            P.barrier()

        def stage_norm_gemm(g, T, x_ap, layer, which, normw, W_ap, N, out_t, hm_out=None, srcT=None, resid=None):
            A.reset()
            sc_off = (1 if which == 1 else 4) * D
            sh_off = (0 if which == 1 else 3) * D
            o_G = A.f32(D); o_SH = A.f32(D); o_h = A.f32(D)
            o_x = [A.f32(D), A.f32(D)]
            o_w = A.f32(16 * 512)
            o_ot = [A.f32(512), A.f32(512)]
            o_xr = [A.f32(512), A.f32(512)]
            o_ss = A.f32(8)
            o_hb = A.f32(D // 2)
            o_hT = A.f32(16 * 1024 // 2)
            o_wb = [A.f32(16 * 512 // 2), A.f32(16 * 512 // 2)]
            Gb = A.v(o_G, [[1, D]]); SHb = A.v(o_SH, [[1, D]]); hh = A.v(o_h, [[1, D]])
            hb = A.bf(o_hb, D // 2)
            hT = A.bf(o_hT, 16 * 1024 // 2).rearrange("p (kc t) -> p kc t", kc=16)
            if srcT is None:
                P.dma(lambda e: e.dma_start(out=Gb, in_=bc_rows(normw, layer * D, D)), writes=["Gb"])
                P.dma(lambda e: e.dma_start(out=hh, in_=bc_rows(modv, (layer * 2 + g) * 6 * D + sc_off, D)), reads=["modv"], writes=["hh"])
                P.dma(lambda e: e.dma_start(out=SHb, in_=bc_rows(modv, (layer * 2 + g) * 6 * D + sh_off, D)), reads=["modv"], writes=["SHb"])
                P.op("vector", lambda e: e.tensor_tensor(Gb, Gb, hh, ALU.mult), reads=["Gb", "hh"], writes=["Gb"])
            if resid is not None:
                P.dma(lambda e: e.dma_start(out=SHb, in_=bc_rows(modv, (layer * 2 + g) * 6 * D + resid[1], D)), reads=["modv"], writes=["SHb"])
            nchunks = (N + 511) // 512
            for blk in range(T // 1024):
                if srcT is not None:
                    P.dma(lambda e, blk=blk: e.dma_start(out=hT, in_=srcT.ap()[:, :, blk * 1024:(blk + 1) * 1024].rearrange("kc p t -> p kc t")),
                          reads=[srcT.name], writes=["hT"])
                for tt in range(8 if srcT is None else 0):
                    t0 = blk * 1024 + tt * 128
                    k = tt % 2
                    xt = A.v(o_x[k], [[1, D]])
                    P.dma(lambda e, xt=xt, t0=t0: e.dma_start(out=xt, in_=x_ap[t0:t0 + 128, :]), writes=["x%d" % k])
                    ss = A.v(o_ss + k, [[1, 1]]); rs = A.v(o_ss + 2 + k, [[1, 1]])
                    P.op("scalar", lambda e, xt=xt, ss=ss: e.activation(hh, xt, AF.Square, accum_out=ss),
                         reads=["x%d" % k], writes=["hh", "ss%d" % k])
                    P.op("vector", lambda e, ss=ss, rs=rs: e.tensor_copy(rs, ss), reads=["ss%d" % k], writes=["rs%d" % k])
                    rsqrt_ops(P, rs, 1.0 / D, 1e-6, "rs%d" % k)
                    P.op("vector", lambda e, xt=xt, rs=rs: e.scalar_tensor_tensor(hh, xt, rs, Gb, ALU.mult, ALU.mult),
                         reads=["x%d" % k, "rs%d" % k, "Gb"], writes=["hh"])
                    if hm_out is not None:
                        P.op("gpsimd", lambda e: e.tensor_tensor(hh, hh, SHb, ALU.add), reads=["hh", "SHb"], writes=["hh"])
                        P.dma(lambda e, t0=t0: e.dma_start(out=hm_out.ap()[t0:t0 + 128, :], in_=hh), reads=["hh"], writes=[hm_out.name])
                        P.op("scalar", lambda e: e.activation(hb, hh, AF.Copy), reads=["hh"], writes=["hb"])
                    else:
                        P.op("gpsimd", lambda e: e.tensor_tensor(hb, hh, SHb, ALU.add), reads=["hh", "SHb"], writes=["hb"])
                    for kc in range(16):
                        P.op("tensor", lambda e, kc=kc: e.transpose(psT[:, kc * 128:(kc + 1) * 128], hb[:, kc * 128:(kc + 1) * 128], identb),
                             reads=["hb", "identb"], writes=["psT"])
                    P.op("scalar", lambda e, tt=tt: e.activation(hT[:, :, tt * 128:(tt + 1) * 128],
                                                                 psT[:, :].rearrange("p (kc t) -> p kc t", kc=16), AF.Copy),
                         reads=["psT"], writes=["hT"])
                for ch in range(nchunks):
                    n0 = ch * 512
                    nw = min(512, N - n0)
                    kb = ch % 2
                    wt = A.v(o_w, [[512, 16], [1, nw]])
                    wb = A.bf(o_wb[kb], 16 * 512 // 2).rearrange("p (kc n) -> p kc n", kc=16)
                    P.dma(lambda e, wt=wt, n0=n0, nw=nw: e.dma_start(
                        out=wt, in_=W_ap[:, n0:n0 + nw].rearrange("(kc p) n -> p kc n", p=128)), writes=["wt"])
                    P.op("gpsimd" if ch % 2 else "scalar",
                         (lambda e, wt=wt, wb=wb, nw=nw: e.tensor_copy(wb[:, :, 0:nw], wt)) if ch % 2 else
                         (lambda e, wt=wt, wb=wb, nw=nw: e.activation(wb[:, :, 0:nw], wt, AF.Copy)),
                         reads=["wt"], writes=["wb%d" % kb])
                    for tt in range(8):
                        t0 = blk * 1024 + tt * 128
                        pi = (ch * 8 + tt) % 4
                        pb = psb[pi]
                        for kc in range(16):
                            P.op("tensor", lambda e, kc=kc, tt=tt, pb=pb, wb=wb, nw=nw: e.matmul(
                                pb[:, 0:nw], hT[:, kc, tt * 128:(tt + 1) * 128], wb[:, kc, 0:nw],
                                start=(kc == 0), stop=(kc == 15)),
                                reads=["hT", "wb%d" % kb], writes=["pg%d" % pi])
                        ko = tt % 2
                        ot = A.v(o_ot[ko], [[1, nw]])
                        if resid is None:
                            P.op("vector", lambda e, ot=ot, pb=pb, nw=nw: e.tensor_copy(ot, pb[:, 0:nw]),
                                 reads=["pg%d" % pi], writes=["ot%d" % ko])
                        else:
                            xr = A.v(o_xr[ko], [[1, nw]])
                            P.dma(lambda e, xr=xr, t0=t0, n0=n0, nw=nw: e.dma_start(out=xr, in_=resid[0][t0:t0 + 128, n0:n0 + nw]),
                                  writes=["xr%d" % ko])
                            P.op("vector", lambda e, ot=ot, pb=pb, nw=nw, n0=n0: e.tensor_tensor(ot, pb[:, 0:nw], A.v(o_SH + n0, [[1, nw]]), ALU.mult),
                                 reads=["pg%d" % pi, "SHb"], writes=["ot%d" % ko])
                            P.op("gpsimd", lambda e, ot=ot, xr=xr: e.tensor_tensor(ot, ot, xr, ALU.add),
                                 reads=["ot%d" % ko, "xr%d" % ko], writes=["ot%d" % ko])
                        P.dma(lambda e, ot=ot, t0=t0, n0=n0, nw=nw: e.dma_start(out=out_t.ap()[t0:t0 + 128, n0:n0 + nw], in_=ot),
                              reads=["ot%d" % ko], writes=[out_t.name])
            P.barrier()

        def sq_store(g, nm, t0, tile_ap, rd, wr):
            off = tile_ap.offset
            for hh in range(2):
                src = bass.AP(A.t, off + hh * 64, [[A.size, 128], [128, 8], [1, 64]])
                dst = bass.AP(sq[g][nm], t0 * 1024 + hh * 512, [[1024, 128], [64, 8], [1, 64]])
                P.dma(lambda e, src=src, dst=dst: e.dma_start(out=dst, in_=src), reads=rd, writes=wr)

        def stage_even_prep(g, T, L):
            A.reset()
            z = zg[g]
            o_cw = A.f32(3 * 3456)
            o_vec = A.f32(9 * 1024)
            o_za = A.f32(3456)
            o_ld = [A.f32(3456), A.f32(3456)]
            o_d = [A.f32(1024) for _ in range(4)]
            o_kk = A.f32(1024); o_tmp = A.f32(1024); o_g = A.f32(1024); o_bon = A.f32(1024)
            o_sm = A.f32(64)
            o_lr = A.f32(3 * 128 // 2); o_lrT = A.f32(3 * 128 // 2)
            o_lw = A.f32(3 * 1024 // 2); o_lwf = A.f32(1024)
            o_vt = A.f32(1024)
            CW = A.v(o_cw, [[3456, 3], [1, 3456]])
            P.dma(lambda e: e.dma_start(out=CW, in_=bass.AP(a_conv, 0, [[0, 128], [3456, 3], [1, 3456]])), writes=["CW"])
            vecsrc = [(a_w0, 0), (a_w0, 1024), (a_a0, 0), (a_a0, 1024), (a_kk, 0), (a_ka, 0), (a_rk, 0)]
            for i, (t_, off) in enumerate(vecsrc):
                P.dma(lambda e, i=i, t_=t_, off=off: e.dma_start(out=A.v(o_vec + i * 1024, [[1, 1024]]), in_=bc_rows(t_, off, 1024)),
                      writes=["vec"])
            vec = lambda i: A.v(o_vec + i * 1024, [[1, 1024]])
            lw = A.bf(o_lw, 3 * 1024 // 2).rearrange("p (j n) -> p j n", j=3)
            for j, t_ in enumerate((a_wu, a_au, a_gu)):
                lwf = A.v(o_lwf, [[1, 1024]])
                P.dma(lambda e, t_=t_, lwf=lwf: e.dma_start(out=lwf, in_=t_.ap()), writes=["lwf"])
                P.op("vector", lambda e, j=j, lwf=lwf: e.tensor_copy(lw[:, j, :], lwf), reads=["lwf"], writes=["lw"])
            lr = A.bf(o_lr, 3 * 128 // 2).rearrange("p (j n) -> p j n", j=3)
            lrT = A.bf(o_lrT, 3 * 128 // 2).rearrange("p (j n) -> p j n", j=3)
            ZA = A.v(o_za, [[1, 3456]])
            r_ = A.v(o_za, [[1, 1024]]); k_ = A.v(o_za + 1024, [[1, 1024]]); v_ = A.v(o_za + 2048, [[1, 1024]])
            KK = A.v(o_kk, [[1, 1024]]); TMP = A.v(o_tmp, [[1, 1024]]); G = A.v(o_g, [[1, 1024]]); BON = A.v(o_bon, [[1, 1024]])
            h3 = lambda o: A.v(o, [[64, 16], [1, 64]])
            hb3 = lambda o: A.v(o, [[1, 16], [0, 64]])
            tiles_per_seq = L // 128
            for ti in range(T // 128):
                t0 = ti * 128
                first = (ti % tiles_per_seq == 0)
                last = (ti % tiles_per_seq == tiles_per_seq - 1)
                ld = A.v(o_ld[0], [[1, 3456]])
                P.dma(lambda e, ld=ld, t0=t0: e.dma_start(out=ld, in_=z.ap()[t0:t0 + 128, 0:3456]), reads=[z.name], writes=["ld0"])
                P.op("vector", lambda e, ld=ld: e.tensor_tensor(ZA, ld, A.v(o_cw + 3456, [[1, 3456]]), ALU.mult),
                     reads=["ld0", "CW"], writes=["ZA"])
                ld = A.v(o_ld[1], [[1, 3456]])
                if first:
                    P.op("gpsimd", lambda e, ld=ld: e.memset(ld, 0.0), writes=["ld1"])
                    P.dma(lambda e, t0=t0: e.dma_start(out=A.v(o_ld[1], [[1, 3456]], p0=1, np_=127), in_=z.ap()[t0:t0 + 127, 0:3456]),
                          reads=[z.name], writes=["ld1"])
                else:
                    P.dma(lambda e, ld=ld, t0=t0: e.dma_start(out=ld, in_=z.ap()[t0 - 1:t0 + 127, 0:3456]), reads=[z.name], writes=["ld1"])
                P.op("gpsimd", lambda e, ld=ld: e.tensor_tensor(ld, ld, A.v(o_cw, [[1, 3456]]), ALU.mult), reads=["ld1", "CW"], writes=["ld1"])
                P.op("vector", lambda e, ld=ld: e.tensor_tensor(ZA, ZA, ld, ALU.add), reads=["ld1", "ZA"], writes=["ZA"])
                ld = A.v(o_ld[0], [[1, 3456]])
                if last:
                    P.op("gpsimd", lambda e, ld=ld: e.memset(ld, 0.0), reads=[], writes=["ld0"])
                    P.dma(lambda e, t0=t0: e.dma_start(out=A.v(o_ld[0], [[1, 3456]], p0=0, np_=127), in_=z.ap()[t0 + 1:t0 + 128, 0:3456]),
                          reads=[z.name], writes=["ld0"])
                else:
                    P.dma(lambda e, ld=ld, t0=t0: e.dma_start(out=ld, in_=z.ap()[t0 + 1:t0 + 129, 0:3456]), reads=[z.name], writes=["ld0"])
                P.op("gpsimd", lambda e, ld=ld: e.tensor_tensor(ld, ld, A.v(o_cw + 2 * 3456, [[1, 3456]]), ALU.mult), reads=["ld0", "CW"], writes=["ld0"])
                P.op("vector", lambda e, ld=ld: e.tensor_tensor(ZA, ZA, ld, ALU.add), reads=["ld0", "ZA"], writes=["ZA"])
                sq_store(g, "r", t0, r_, ["ZA"], ["sq_r"])
                P.op("scalar", lambda e: e.activation(lr[:, 0, :], A.v(o_za + 3072, [[1, 128]]), AF.Tanh), reads=["ZA"], writes=["lr"])
                P.op("scalar", lambda e: e.activation(lr[:, 1, :], A.v(o_za + 3200, [[1, 128]]), AF.Copy), reads=["ZA"], writes=["lr"])
                P.op("scalar", lambda e: e.activation(lr[:, 2, :], A.v(o_za + 3328, [[1, 128]]), AF.Sigmoid), reads=["ZA"], writes=["lr"])
                for j in range(3):
                    P.op("tensor", lambda e, j=j: e.transpose(psT[:, j * 128:(j + 1) * 128], lr[:, j, :], identb), reads=["lr", "identb"], writes=["psT"])
                P.op("vector", lambda e: e.tensor_copy(lrT, psT[:, 0:384].rearrange("p (j n) -> p j n", j=3)), reads=["psT"], writes=["lrT"])
                for hf in range(2):
                    P.op("tensor", lambda e, hf=hf: e.matmul(psb[hf][:, :], lrT[:, 2, :], lw[:, 2, hf * 512:(hf + 1) * 512], start=True, stop=True),
                         reads=["lrT", "lw"], writes=["pe%d" % hf])
                    P.op("scalar", lambda e, hf=hf: e.activation(A.v(o_g + hf * 512, [[1, 512]]), psb[hf][:, :], AF.Copy),
                         reads=["pe%d" % hf], writes=["G"])
                P.dma(lambda e, t0=t0: e.dma_start(out=gsc[g].ap()[t0:t0 + 128, :], in_=G), reads=["G"], writes=["gsc"])
                P.op("vector", lambda e: e.tensor_tensor(KK, k_, vec(4), ALU.mult), reads=["ZA", "vec"], writes=["KK"])
                P.op("gpsimd", lambda e: e.tensor_tensor(TMP, KK, KK, ALU.mult), reads=["KK"], writes=["TMP"])
                P.op("vector", lambda e: e.tensor_reduce(A.v(o_sm, [[1, 16]]), h3(o_tmp), AX.X, ALU.add), reads=["TMP"], writes=["sm"])
                rsqrt_ops(P, A.v(o_sm, [[1, 16]]), 1.0, 1e-12, "sm")
                P.op("vector", lambda e: e.tensor_tensor(h3(o_kk), h3(o_kk), hb3(o_sm), ALU.mult), reads=["sm", "KK"], writes=["KK"])
                sq_store(g, "kk", t0, KK, ["KK"], ["sq_kk"])
                P.op("gpsimd", lambda e: e.tensor_tensor(TMP, r_, k_, ALU.mult), reads=["ZA"], writes=["TMP"])
                P.op("gpsimd", lambda e: e.tensor_tensor(TMP, TMP, vec(6), ALU.mult), reads=["TMP", "vec"], writes=["TMP"])
                P.op("vector", lambda e: e.tensor_reduce(A.v(o_sm + 16, [[1, 16]]), h3(o_tmp), AX.X, ALU.add), reads=["TMP"], writes=["sm2"])
                P.op("vector", lambda e: e.tensor_tensor(h3(o_bon), h3(o_za + 2048), hb3(o_sm + 16), ALU.mult), reads=["sm2", "ZA"], writes=["BON"])
                P.dma(lambda e, t0=t0: e.dma_start(out=bon[g].ap()[t0:t0 + 128, :], in_=BON), reads=["BON"], writes=["bon"])
                vt = A.v(o_vt, [[128, 8], [1, 128]])
                for hf in range(2):
                    for j in range(4):
                        c = hf * 4 + j
                        P.op("tensor", lambda e, c=c, hf=hf, j=j: e.transpose(psb[2 + hf][:, j * 128:(j + 1) * 128],
                                                                             A.v(o_za + 2048 + c * 128, [[1, 128]]), ident),
                             reads=["ZA", "ident"], writes=["pv%d" % hf])
                    P.op("scalar", lambda e, hf=hf: e.activation(A.v(o_vt + hf * 512, [[1, 512]]), psb[2 + hf][:, :], AF.Copy),
                         reads=["pv%d" % hf], writes=["vt"])
                P.dma(lambda e, t0=t0, vt=vt: e.dma_start(out=vT[g].ap()[:, :, t0:t0 + 128].rearrange("c p t -> p c t"), in_=vt),
                      reads=["vt"], writes=["vT"])
                for d in range(2):
                    Wd, Ad, KTd, Bd = (A.v(o, [[1, 1024]]) for o in o_d)
                    for hf in range(2):
                        P.op("tensor", lambda e, d=d, hf=hf: e.matmul(psb[hf][:, :], lrT[64 * d:64 * d + 64, 0, :],
                                                                      lw[64 * d:64 * d + 64, 0, hf * 512:(hf + 1) * 512], start=True, stop=True),
                             reads=["lrT", "lw"], writes=["pe%d" % hf])
                        P.op("vector", lambda e, d=d, hf=hf: e.tensor_tensor(A.v(o_d[0] + hf * 512, [[1, 512]]), psb[hf][:, :],
                                                                            A.v(o_vec + d * 1024 + hf * 512, [[1, 512]]), ALU.add),
                             reads=["pe%d" % hf, "vec"], writes=["Wd"])
                    P.op("scalar", lambda e, Wd=Wd: e.activation(Wd, Wd, AF.Sigmoid), reads=["Wd"], writes=["Wd"])
                    P.op("scalar", lambda e, Wd=Wd: e.activation(Wd, Wd, AF.Exp, scale=-math.exp(-0.5)), reads=["Wd"], writes=["Wd"])
                    sq_store(g, "w%d" % d, t0, Wd, ["Wd"], ["sq_w"])
                    for hf in range(2):
                        P.op("tensor", lambda e, d=d, hf=hf: e.matmul(psb[hf][:, :], lrT[64 * d:64 * d + 64, 1, :],
                                                                      lw[64 * d:64 * d + 64, 1, hf * 512:(hf + 1) * 512], start=True, stop=True),
                             reads=["lrT", "lw"], writes=["pe%d" % hf])
                        P.op("vector", lambda e, d=d, hf=hf: e.tensor_tensor(A.v(o_d[1] + hf * 512, [[1, 512]]), psb[hf][:, :],
                                                                            A.v(o_vec + (2 + d) * 1024 + hf * 512, [[1, 512]]), ALU.add),
                             reads=["pe%d" % hf, "vec"], writes=["Ad"])
                    P.op("scalar", lambda e, Ad=Ad: e.activation(Ad, Ad, AF.Sigmoid), reads=["Ad"], writes=["Ad"])
                    P.op("vector", lambda e, Ad=Ad, KTd=KTd: e.scalar_tensor_tensor(KTd, Ad, -1.0, vec(5), ALU.add, ALU.mult),
                         reads=["Ad", "vec"], writes=["KTd"])
                    P.op("vector", lambda e, KTd=KTd: e.scalar_tensor_tensor(KTd, KTd, 1.0, k_, ALU.add, ALU.mult),
                         reads=["KTd", "ZA"], writes=["KTd"])
                    sq_store(g, "kt%d" % d, t0, KTd, ["KTd"], ["sq_kt"])
                    P.op("gpsimd", lambda e, Ad=Ad, Bd=Bd: e.tensor_tensor(Bd, KK, Ad, ALU.mult), reads=["KK", "Ad"], writes=["Bd"])
                    sq_store(g, "b%d" % d, t0, Bd, ["Bd"], ["sq_b"])
            P.barrier()

        def stage_scan(g, seq, L, d, s0_ap=None, sfin_ap=None):
            A.reset()
            if SCAN_DVE_INORDER:
                P.no_self.add("vector")
            SB = 4
            o_S = A.f32(512); o_tmp = A.f32(512); o_sa = A.f32(8)
            o_X = [A.f32(5 * SB * 512), A.f32(5 * SB * 512)]
            VB = 128
            o_V = [A.f32(8 * VB), A.f32(8 * VB)]
            o_Y = [A.f32(8 * VB), A.f32(8 * VB)]
            S = A.v(o_S, [[1, 512]]); TMP = A.v(o_tmp, [[1, 512]])
            S3 = A.v(o_S, [[64, 8], [1, 64]]); TMP3 = A.v(o_tmp, [[64, 8], [1, 64]])
            sa = A.v(o_sa, [[1, 8]]); sab = A.v(o_sa, [[1, 8], [0, 64]])
            base = seq * L
            for hh in range(2):
                dst = A.v(o_S, [[64, 8], [1, 64]], p0=64 * hh, np_=64)
                if s0_ap is not None:
                    src = bass.AP(s0_ap, (d * 16 + hh) * 4096, [[64, 64], [2 * 4096, 8], [1, 64]])
                    P.dma(lambda e, dst=dst, src=src: e.dma_start(out=dst, in_=src), writes=["S"])
                else:
                    src = bass.AP(zero_d, 0, [[64, 64], [0, 8], [1, 64]])
                    P.dma(lambda e, dst=dst, src=src: e.dma_start(out=dst, in_=src), reads=["zero_d"], writes=["S"])
            names = ["kk", "w%d" % d, "b%d" % d, "kt%d" % d, "r"]
            for blk in range(L // SB):
                kx = blk % 2
                if d == 0:
                    tok0 = blk * SB
                else:
                    tok0 = L - (blk + 1) * SB
                for qi, nm in enumerate(names):
                    for hh in range(2):
                        dst = A.v(o_X[kx] + qi * SB * 512, [[512, SB], [1, 512]], p0=64 * hh, np_=64)
                        src = bass.AP(sq[g][nm], (base + tok0) * 1024 + hh * 512, [[0, 64], [1024, SB], [1, 512]])
                        P.dma(lambda e, dst=dst, src=src: e.dma_start(out=dst, in_=src), reads=["sq"], writes=["X%d_%d_%d" % (kx, qi, hh)],
                              q="gpsimd" if (qi % 2) else "sync")
                if (blk * SB) % VB == 0:
                    vb = (blk * SB) // VB
                    kv = vb % 2
                    vtok0 = vb * VB if d == 0 else L - (vb + 1) * VB
                    dst = A.v(o_V[kv], [[VB, 8], [1, VB]])
                    src = bass.AP(vT[g], base + vtok0, [[vT[g].shape[2], 128], [128 * vT[g].shape[2], 8], [1, VB]])
                    P.dma(lambda e, dst=dst, src=src: e.dma_start(out=dst, in_=src), reads=["vT"], writes=["V%d" % kv])
                for j in range(SB):
                    step = blk * SB + j
                    jj = j if d == 0 else SB - 1 - j
                    vb = step // VB
                    kv = vb % 2
                    sv = step % VB
                    vcol = sv if d == 0 else VB - 1 - sv
                    X = (lambda xs: (lambda qi: xs[qi]))([A.v(o_X[kx] + qi * SB * 512 + jj * 512, [[64, 8], [1, 64]]) for qi in range(5)])
                    vbc = A.v(o_V[kv] + vcol, [[VB, 8], [0, 64]])
                    ycol = A.v(o_Y[kv] + vcol, [[VB, 8]])
                    rk = lambda qi: ["X%d_%d_0" % (kx, qi), "X%d_%d_1" % (kx, qi)]
                    P.op("vector", lambda e, X=X: e.tensor_tensor(TMP3, S3, X(0), ALU.mult), reads=["S"] + rk(0), writes=["TMP"])
                    P.op("vector", lambda e: e.tensor_reduce(sa, TMP3, AX.X, ALU.add), reads=["TMP"], writes=["sa"])
                    P.op("vector", lambda e, X=X: e.tensor_tensor(S3, S3, X(1), ALU.mult), reads=["S"] + rk(1), writes=["S"])
                    P.op("vector", lambda e, X=X: e.tensor_tensor(TMP3, X(2), sab, ALU.mult), reads=["sa"] + rk(2), writes=["TMP"])
                    P.op("vector", lambda e: e.tensor_tensor(S3, S3, TMP3, ALU.subtract), reads=["S", "TMP"], writes=["S"])
                    P.op("vector", lambda e, X=X, vbc=vbc: e.tensor_tensor(TMP3, X(3), vbc, ALU.mult), reads=["V%d" % kv] + rk(3), writes=["TMP"])
                    P.op("vector", lambda e: e.tensor_tensor(S3, S3, TMP3, ALU.add), reads=["S", "TMP"], writes=["S"])
                    P.op("vector", lambda e, X=X: e.tensor_tensor(TMP3, S3, X(4), ALU.mult), reads=["S"] + rk(4), writes=["TMP"])
                    P.op("vector", lambda e, ycol=ycol: e.tensor_reduce(ycol, TMP3, AX.X, ALU.add), reads=["TMP"], writes=["Y%d" % kv])
                    if sv == VB - 1:
                        ytok0 = vb * VB if d == 0 else L - (vb + 1) * VB
                        srcy = A.v(o_Y[kv], [[VB, 8], [1, VB]])
                        dsty = bass.AP(yT[g][d], base + ytok0, [[yT[g][d].shape[2], 128], [128 * yT[g][d].shape[2], 8], [1, VB]])
                        P.dma(lambda e, srcy=srcy, dsty=dsty: e.dma_start(out=dsty, in_=srcy), reads=["Y%d" % kv], writes=["yT"])
            if sfin_ap is not None:
                for hh in range(2):
                    srcS = A.v(o_S, [[64, 8], [1, 64]], p0=64 * hh, np_=64)
                    dstS = bass.AP(sfin_ap, ((seq * 2 + d) * 16 + hh) * 4096, [[64, 64], [2 * 4096, 8], [1, 64]])
                    P.dma(lambda e, srcS=srcS, dstS=dstS: e.dma_start(out=dstS, in_=srcS), reads=["S"], writes=["ns"], is_output=True)
            P.no_self.discard("vector")
            P.barrier()

        def stage_rwkv_post(g, T):
            A.reset()
            o_y0 = A.f32(1024); o_y1 = A.f32(1024); o_Y = A.f32(1024); o_YC = A.f32(1024); o_SQ = A.f32(1024)
            o_bon = A.f32(1024); o_g = A.f32(1024); o_lnw = A.f32(1024); o_lnb = A.f32(1024); o_sm = A.f32(64)
            o_yb = A.f32(512); o_yT = A.f32(512)
            LNW = A.v(o_lnw, [[1, 1024]]); LNB = A.v(o_lnb, [[1, 1024]])
            P.dma(lambda e: e.dma_start(out=LNW, in_=bc_rows(a_lnw, 0, 1024)), writes=["LNW"])
            P.dma(lambda e: e.dma_start(out=LNB, in_=bc_rows(a_lnb, 0, 1024)), writes=["LNB"])
            Y = A.v(o_Y, [[1, 1024]]); YC = A.v(o_YC, [[1, 1024]]); SQ = A.v(o_SQ, [[1, 1024]])
            BON = A.v(o_bon, [[1, 1024]]); G = A.v(o_g, [[1, 1024]])
            h3 = lambda o: A.v(o, [[64, 16], [1, 64]])
            hb3 = lambda o: A.v(o, [[1, 16], [0, 64]])
            yb = A.bf(o_yb, 512)
            yT8 = A.bf(o_yT, 512).rearrange("p (c t) -> p c t", c=8)
            for ti in range(T // 128):
                t0 = ti * 128
                y0 = A.v(o_y0, [[128, 8], [1, 128]]); y1 = A.v(o_y1, [[128, 8], [1, 128]])
                for d, yy in ((0, y0), (1, y1)):
                    src = bass.AP(yT[g][d], t0, [[T, 128], [128 * T, 8], [1, 128]])
                    P.dma(lambda e, yy=yy, src=src: e.dma_start(out=yy, in_=src), reads=["yT"], writes=["y%d" % d])
                P.op("vector", lambda e: e.tensor_tensor(A.v(o_y0, [[1, 1024]]), A.v(o_y0, [[1, 1024]]), A.v(o_y1, [[1, 1024]]), ALU.add),
                     reads=["y0", "y1"], writes=["y0"])
                for c in range(8):
                    P.op("tensor", lambda e, c=c: e.transpose(psb[c // 4][:, (c % 4) * 128:(c % 4 + 1) * 128], A.v(o_y0 + c * 128, [[1, 128]]), ident),
                         reads=["y0", "ident"], writes=["pp%d" % (c // 4)])
                for hf in range(2):
                    P.op("scalar", lambda e, hf=hf: e.activation(A.v(o_Y + hf * 512, [[1, 512]]), psb[hf][:, :], AF.Copy), reads=["pp%d" % hf], writes=["Y"])
                P.op("vector", lambda e: e.tensor_reduce(A.v(o_sm, [[1, 16]]), h3(o_Y), AX.X, ALU.add), reads=["Y"], writes=["mu"])
                P.op("vector", lambda e: e.tensor_scalar_mul(A.v(o_sm, [[1, 16]]), A.v(o_sm, [[1, 16]]), 1.0 / 64), reads=["mu"], writes=["mu"])
                P.op("vector", lambda e: e.tensor_tensor(h3(o_YC), h3(o_Y), hb3(o_sm), ALU.subtract), reads=["Y", "mu"], writes=["YC"])
                P.op("gpsimd", lambda e: e.tensor_tensor(SQ, YC, YC, ALU.mult), reads=["YC"], writes=["SQ"])
                P.op("vector", lambda e: e.tensor_reduce(A.v(o_sm + 16, [[1, 16]]), h3(o_SQ), AX.X, ALU.add), reads=["SQ"], writes=["var"])
                rsqrt_ops(P, A.v(o_sm + 16, [[1, 16]]), 1.0 / 64, 64e-5, "var")
                P.op("vector", lambda e: e.tensor_tensor(h3(o_YC), h3(o_YC), hb3(o_sm + 16), ALU.mult), reads=["YC", "var"], writes=["YC"])
                P.op("gpsimd", lambda e: e.tensor_tensor(YC, YC, LNW, ALU.mult), reads=["YC", "LNW"], writes=["YC"])
                P.op("vector", lambda e: e.tensor_tensor(YC, YC, LNB, ALU.add), reads=["YC", "LNB"], writes=["YC"])
                P.dma(lambda e, t0=t0: e.dma_start(out=BON, in_=bon[g].ap()[t0:t0 + 128, :]), reads=["bon"], writes=["BON"])
                P.dma(lambda e, t0=t0: e.dma_start(out=G, in_=gsc[g].ap()[t0:t0 + 128, :]), reads=["gsc"], writes=["G"])
                P.op("gpsimd", lambda e: e.tensor_tensor(YC, YC, BON, ALU.add), reads=["YC", "BON"], writes=["YC"])
                P.op("vector", lambda e: e.tensor_tensor(yb, YC, G, ALU.mult), reads=["YC", "G"], writes=["yb"])
                for c in range(8):
                    P.op("tensor", lambda e, c=c: e.transpose(psT[:, c * 128:(c + 1) * 128], yb[:, c * 128:(c + 1) * 128], identb),
                         reads=["yb", "identb"], writes=["psT"])
                P.op("scalar", lambda e: e.activation(yT8, psT[:, 0:1024].rearrange("p (c t) -> p c t", c=8), AF.Copy), reads=["psT"], writes=["yT8"])
                P.dma(lambda e, t0=t0: e.dma_start(out=ycatT[g].ap()[0:8, :, t0:t0 + 128].rearrange("c p t -> p c t"), in_=yT8),
                      reads=["yT8"], writes=[ycatT[g].name])
            P.barrier()

        def stage_attn_prep(g, T, L, latent):
            A.reset()
            o_z = A.f32(1536); o_sq = A.f32(1280); o_qr = A.f32(1280); o_sm = A.f32(16)
            o_qn = A.f32(128); o_kn = A.f32(128); o_cs = A.f32(128); o_t = [A.f32(640) for _ in range(4)]
            o_qb = A.f32(640); o_qT = A.f32(640); o_vb = A.f32(128)
            QN = A.v(o_qn, [[1, 128]]); KN = A.v(o_kn, [[1, 128]])
            P.dma(lambda e: e.dma_start(out=QN, in_=bc_rows(b_qn, 0, 128)), writes=["QN"])
            P.dma(lambda e: e.dma_start(out=KN, in_=bc_rows(b_kn, 0, 128)), writes=["KN"])
            qb = A.bf(o_qb, 640)
            qT = A.bf(o_qT, 640).rearrange("p (h t) -> p h t", h=10)
            vb = A.bf(o_vb, 128)
            Lk = kT_d[g].shape[2] // (T // L)
            ntile = T // 128 + (4 if latent else 0)
            for ti in range(ntile):
                t0 = ti * 128
                cache = ti >= T // 128
                seq = t0 // L
                tin = t0 % L
                if not cache:
                    P.dma(lambda e, t0=t0: e.dma_start(out=A.v(o_z, [[1, 1536]]), in_=zg[g].ap()[t0:t0 + 128, 3456:4992]), reads=[zg[g].name], writes=["zq"])
                    P.op("gpsimd", lambda e: e.tensor_tensor(A.v(o_sq, [[1, 1280]]), A.v(o_z, [[1, 1280]]), A.v(o_z, [[1, 1280]]), ALU.mult), reads=["zq"], writes=["sqq"])
                    P.op("vector", lambda e: e.tensor_reduce(A.v(o_sm, [[1, 10]]), A.v(o_sq, [[128, 10], [1, 128]]), AX.X, ALU.add), reads=["sqq"], writes=["sm"])
                    rsqrt_ops(P, A.v(o_sm, [[1, 10]]), 1.0 / 128, 1e-6, "sm")
                    P.op("vector", lambda e: e.tensor_tensor(A.v(o_z, [[128, 10], [1, 128]]), A.v(o_z, [[128, 10], [1, 128]]), A.v(o_sm, [[1, 10], [0, 128]]), ALU.mult),
                         reads=["zq", "sm"], writes=["zq"])
                    P.op("vector", lambda e: e.tensor_tensor(A.v(o_z, [[128, 8], [1, 128]]), A.v(o_z, [[128, 8], [1, 128]]), A.v(o_qn, [[0, 8], [1, 128]]), ALU.mult),
                         reads=["zq", "QN"], writes=["zq"])
                    P.op("vector", lambda e: e.tensor_tensor(A.v(o_z + 1024, [[128, 2], [1, 128]]), A.v(o_z + 1024, [[128, 2], [1, 128]]), A.v(o_kn, [[0, 2], [1, 128]]), ALU.mult),
                         reads=["zq", "KN"], writes=["zq"])
                    if latent:
                        P.dma(lambda e, tin=tin: e.dma_start(out=A.v(o_cs, [[1, 64]]), in_=ropec.ap()[tin:tin + 128, :]), writes=["cs"])
                        P.dma(lambda e, tin=tin: e.dma_start(out=A.v(o_cs + 64, [[1, 64]]), in_=ropes.ap()[tin:tin + 128, :]), writes=["cs"])
                        X1 = A.v(o_z, [[128, 10], [64, 2], [1, 32]]); X2 = A.v(o_z + 32, [[128, 10], [64, 2], [1, 32]])
                        O1 = A.v(o_qr, [[128, 10], [64, 2], [1, 32]]); O2 = A.v(o_qr + 32, [[128, 10], [64, 2], [1, 32]])
                        Cb = A.v(o_cs, [[0, 10], [32, 2], [1, 32]]); Sb = A.v(o_cs + 64, [[0, 10], [32, 2], [1, 32]])
                        Tt = [A.v(o, [[64, 10], [32, 2], [1, 32]]) for o in o_t]
                        P.op("vector", lambda e: e.tensor_tensor(Tt[0], X1, Cb, ALU.mult), reads=["zq", "cs"], writes=["t0"])
                        P.op("gpsimd", lambda e: e.tensor_tensor(Tt[1], X2, Sb, ALU.mult), reads=["zq", "cs"], writes=["t1"])
                        P.op("vector", lambda e: e.tensor_tensor(Tt[2], X1, Sb, ALU.mult), reads=["zq", "cs"], writes=["t2"])
                        P.op("gpsimd", lambda e: e.tensor_tensor(Tt[3], X2, Cb, ALU.mult), reads=["zq", "cs"], writes=["t3"])
                        P.op("vector", lambda e: e.tensor_tensor(O1, Tt[0], Tt[1], ALU.subtract), reads=["t0", "t1"], writes=["qr"])
                        P.op("vector", lambda e: e.tensor_tensor(O2, Tt[2], Tt[3], ALU.add), reads=["t2", "t3"], writes=["qr"])
                        P.op("scalar", lambda e: e.activation(qb, A.v(o_qr, [[1, 1280]]), AF.Copy), reads=["qr"], writes=["qb"])
                    else:
                        P.op("scalar", lambda e: e.activation(qb, A.v(o_z, [[1, 1280]]), AF.Copy), reads=["zq"], writes=["qb"])
                    P.op("gpsimd", lambda e: e.tensor_copy(vb, A.v(o_z + 1280, [[1, 256]])), reads=["zq"], writes=["vb"])
                    h0 = 0
                    krow = seq * Lk + tin
                else:
                    c0 = (ti - T // 128) * 128
                    P.dma(lambda e, c0=c0: e.dma_start(out=A.v(o_z + 1024, [[1, 256]]), in_=ck.ap()[c0:c0 + 128, :]), writes=["zq"])
                    P.dma(lambda e, c0=c0: e.dma_start(out=A.v(o_z + 1280, [[1, 256]]), in_=cv.ap()[c0:c0 + 128, :]), writes=["zq"])
                    P.op("scalar", lambda e: e.activation(qb[:, 1024:1280], A.v(o_z + 1024, [[1, 256]]), AF.Copy), reads=["zq"], writes=["qb"])
                    P.op("gpsimd", lambda e: e.tensor_copy(vb, A.v(o_z + 1280, [[1, 256]])), reads=["zq"], writes=["vb"])
                    h0 = 8
                    krow = L + c0
                for h in range(h0, 10):
                    P.op("tensor", lambda e, h=h: e.transpose(psT[:, h * 128:(h + 1) * 128], qb[:, h * 128:(h + 1) * 128], identb),
                         reads=["qb", "identb"], writes=["psT"])
                P.op("vector", lambda e, h0=h0: e.tensor_copy(qT[:, h0:10, :], psT[:, h0 * 128:1280].rearrange("p (h t) -> p h t", h=10 - h0)),
                     reads=["psT"], writes=["qT"])
                if not cache:
                    P.dma(lambda e, t0=t0: e.dma_start(out=qT_d[g].ap()[:, :, t0:t0 + 128].rearrange("h p t -> p h t"), in_=qT[:, 0:8, :]),
                          reads=["qT"], writes=["qT_d"])
                P.dma(lambda e, krow=krow: e.dma_start(out=kT_d[g].ap()[:, :, krow:krow + 128].rearrange("h p t -> p h t"), in_=qT[:, 8:10, :]),
                      reads=["qT"], writes=["kT_d"])
                P.dma(lambda e, krow=krow: e.dma_start(out=V_d[g].ap()[krow:krow + 128, :], in_=vb), reads=["vb"], writes=["V_d"])
            P.barrier()

        def stage_attn(g, T, seq, L, Lk):
            A.reset()
            nkt = Lk // 128
            NQ = min(512, L)
            o_kT = A.f32(2 * Lk // 2); o_V = A.f32(nkt * 256 // 2)
            o_q = [A.f32(NQ // 2), A.f32(NQ // 2)]
            o_E = [A.f32(NQ // 2), A.f32(NQ // 2)]
            o_rec = A.f32(NQ); o_ob = [A.f32(NQ // 2), A.f32(NQ // 2)]
            o_one = A.f32(64)
            KT = A.bf(o_kT, 2 * Lk // 2).rearrange("p (h t) -> p h t", h=2)
            VV = A.bf(o_V, nkt * 256 // 2).rearrange("p (k n) -> p k n", k=nkt)
            ones = A.bf(o_one, 64)
            P.op("vector", lambda e: e.memset(ones, 1.0), writes=["ones"])
            P.dma(lambda e: e.dma_start(out=KT, in_=kT_d[g].ap()[:, :, seq * Lk:(seq + 1) * Lk].rearrange("h p t -> p h t")),
                  reads=["kT_d"], writes=["KT"])
            P.dma(lambda e: e.dma_start(out=VV, in_=V_d[g].ap()[seq * Lk:(seq + 1) * Lk, :].rearrange("(k p) n -> p k n", p=128)),
                  reads=["V_d"], writes=["VV"])
            scale = 128 ** -0.5
            it = 0
            for h in range(8):
                kv = h // 4
                for qb_ in range(L // NQ):
                    q0 = seq * L + qb_ * NQ
                    kq = (h * (L // NQ) + qb_) % 2
                    qt = A.bf(o_q[kq], NQ // 2)
                    P.dma(lambda e, qt=qt, h=h, q0=q0: e.dma_start(out=qt, in_=qT_d[g].ap()[h, :, q0:q0 + NQ]), reads=["qT_d"], writes=["q%d" % kq])
                    for kt in range(nkt):
                        ke = it % 2
                        it += 1
                        Et = A.bf(o_E[ke], NQ // 2)
                        P.op("tensor", lambda e, kt=kt, kv=kv, qt=qt, ke=ke: e.matmul(psb[ke][:, 0:NQ], KT[:, kv, kt * 128:(kt + 1) * 128], qt, start=True, stop=True),
                             reads=["KT", "q%d" % kq], writes=["pS%d" % ke])
                        P.op("scalar", lambda e, Et=Et, ke=ke: e.activation(Et, psb[ke][:, 0:NQ], AF.Exp, scale=scale),
                             reads=["pS%d" % ke], writes=["E%d" % ke])
                        P.op("tensor", lambda e, kt=kt, kv=kv, Et=Et: e.matmul(psb[2][:, 0:NQ], VV[:, kt, kv * 128:(kv + 1) * 128], Et,
                                                                               start=(kt == 0), stop=(kt == nkt - 1)),
                             reads=["VV", "E%d" % ke], writes=["pO"])
                        P.op("tensor", lambda e, kt=kt, Et=Et: e.matmul(psb[3][:, 0:NQ], ones, Et, start=(kt == 0), stop=(kt == nkt - 1)),
                             reads=["ones", "E%d" % ke], writes=["pD"])
                    rec = A.v(o_rec, [[1, NQ]])
                    ob = A.bf(o_ob[kq], NQ // 2)
                    P.op("vector", lambda e, rec=rec: e.reciprocal(rec, psb[3][:, 0:NQ]), reads=["pD"], writes=["rec"])
                    P.op("vector", lambda e, rec=rec, ob=ob: e.tensor_tensor(ob, psb[2][:, 0:NQ], rec, ALU.mult), reads=["pO", "rec"], writes=["ob%d" % kq])
                    P.dma(lambda e, ob=ob, h=h, q0=q0: e.dma_start(out=ycatT[g].ap()[8 + h, :, q0:q0 + NQ], in_=ob),
                          reads=["ob%d" % kq], writes=[ycatT[g].name])
            P.barrier()

        def stage_peer(g, T, layer, q_t, hm_t, xres_ap, out_ap, out_key, is_output=False):
            A.reset()
            NB = 8
            o_kn = A.f32(2048); o_kT = A.f32(2048); o_q = A.f32(2048); o_qT = A.f32(2048); o_S = A.f32(2048)
            o_X = A.f32(2048); o_ACC = A.f32(2048); o_jk = A.f32(2048); o_GT = A.f32(2048); o_xr = A.f32(2048)
            o_R = [A.f32(1024) for _ in range(NB)]
            o_V1 = A.f32(256); o_IDX = A.f32(256); o_IDF = A.f32(256); o_CAND = A.f32(256); o_EID = A.f32(256); o_CW = A.f32(256)
            o_W = A.f32(128); o_T = A.f32(128); o_E = A.f32(128); o_EI = A.f32(128); o_GATE = A.f32(128); o_DOT = A.f32(128); o_WG = A.f32(128)
            o_sm = A.f32(16)
            o_MK = A.f32(256); o_FF = A.f32(256); o_CD = A.f32(256)
            u32v = lambda off, dims: bass.AP(A.t, off, [[A.size, 128]] + [list(d_) for d_ in dims]).bitcast(U32)
            P.dma(lambda e: e.dma_start(out=u32v(o_MK, [[1, 256]]), in_=kconst.ap()[0]), writes=["MK"])
            P.dma(lambda e: e.dma_start(out=u32v(o_FF, [[1, 256]]), in_=kconst.ap()[1]), writes=["FF"])
            P.dma(lambda e: e.dma_start(out=u32v(o_CD, [[1, 256]]), in_=kconst.ap()[2]), writes=["CD"])
            ut = peer_ub[layer]; vt_ = peer_vb[layer]
            GT = A.v(o_GT, [[1, 2048]])
            P.dma(lambda e: e.dma_start(out=GT, in_=bc_rows(modv, (layer * 2 + g) * 6 * D + 5 * D, D)), reads=["modv"], writes=["GT"])
            P.dma(lambda e: e.dma_start(out=A.v(o_kn, [[128, 16], [1, 128]]), in_=peer_keys.ap()[layer].rearrange("h c n d -> n (h c) d")), writes=["kn"])
            for hc in range(16):
                P.op("tensor", lambda e, hc=hc: e.transpose(psb[hc // 4][:, (hc % 4) * 128:(hc % 4 + 1) * 128], A.v(o_kn + hc * 128, [[1, 128]]), ident),
                     reads=["kn", "ident"], writes=["pk%d" % (hc // 4)])
            for b4 in range(4):
                P.op("scalar", lambda e, b4=b4: e.activation(A.v(o_kT + b4 * 512, [[1, 512]]), psb[b4][:, :], AF.Copy), reads=["pk%d" % b4], writes=["kT"])
            EIu = bass.AP(A.t, o_EI, [[A.size, 128], [1, 128]]).bitcast(I32)
            IDXu = bass.AP(A.t, o_IDX, [[A.size, 128], [1, 256]]).bitcast(U32)
            for ti in range(T // 128):
                t0 = ti * 128
                P.dma(lambda e, t0=t0: e.dma_start(out=A.v(o_q, [[1, 2048]]), in_=q_t.ap()[t0:t0 + 128, :]), reads=[q_t.name], writes=["q"])
                P.dma(lambda e, t0=t0: e.dma_start(out=A.v(o_X, [[1, 2048]]), in_=hm_t.ap()[t0:t0 + 128, :]), reads=[hm_t.name], writes=["X"])
                P.dma(lambda e, t0=t0: e.dma_start(out=A.v(o_xr, [[1, 2048]]), in_=xres_ap[t0:t0 + 128, :]), writes=["xr"])
                for hc in range(16):
                    P.op("tensor", lambda e, hc=hc: e.transpose(psb[hc // 4][:, (hc % 4) * 128:(hc % 4 + 1) * 128], A.v(o_q + hc * 128, [[1, 128]]), ident),
                         reads=["q", "ident"], writes=["pk%d" % (hc // 4)])
                for b4 in range(4):
                    P.op("scalar", lambda e, b4=b4: e.activation(A.v(o_qT + b4 * 512, [[1, 512]]), psb[b4][:, :], AF.Copy), reads=["pk%d" % b4], writes=["qT"])
                for hc in range(16):
                    P.op("tensor", lambda e, hc=hc: e.matmul(psb[hc // 4][:, (hc % 4) * 128:(hc % 4 + 1) * 128], A.v(o_qT + hc * 128, [[1, 128]]),
                                                             A.v(o_kT + hc * 128, [[1, 128]]), start=True, stop=True),
                         reads=["qT", "kT"], writes=["pk%d" % (hc // 4)])
                for b4 in range(4):
                    P.op("scalar", lambda e, b4=b4: e.activation(A.v(o_S + b4 * 512, [[1, 512]]), psb[b4][:, :], AF.Copy), reads=["pk%d" % b4], writes=["S"])
                P.op("vector", lambda e: e.tensor_scalar_add(A.v(o_S, [[1, 2048]]), A.v(o_S, [[1, 2048]]), 64.0), reads=["S"], writes=["S"])
                P.op("vector", lambda e: e.tensor_tensor(u32v(o_S, [[128, 16], [1, 128]]), u32v(o_S, [[128, 16], [1, 128]]), u32v(o_MK, [[0, 16], [1, 128]]), ALU.bitwise_and),
                     reads=["S", "MK"], writes=["S"])
                P.op("vector", lambda e: e.tensor_tensor(u32v(o_S, [[128, 16], [1, 128]]), u32v(o_S, [[128, 16], [1, 128]]), u32v(o_CD, [[0, 16], [1, 128]]), ALU.bitwise_or),
                     reads=["S", "CD"], writes=["S"])
                Wk = A.v(o_W, [[1, 128]])
                for hc in range(16):
                    Sh = A.v(o_S + hc * 128, [[1, 128]])
                    va = A.v(o_V1 + hc * 16, [[1, 8]]); vb_ = A.v(o_V1 + hc * 16 + 8, [[1, 8]])
                    P.op("vector", lambda e, Sh=Sh, va=va: e.max(out=va, in_=Sh), reads=["S"], writes=["V1"])
                    P.op("vector", lambda e, Sh=Sh, va=va: e.match_replace(out=Wk, in_to_replace=va, in_values=Sh, imm_value=-1e30), reads=["S", "V1"], writes=["Wk"])
                    P.op("vector", lambda e, vb_=vb_: e.max(out=vb_, in_=Wk), reads=["Wk"], writes=["V1"])
                P.op("vector", lambda e: e.tensor_tensor(u32v(o_IDX, [[1, 256]]), u32v(o_V1, [[1, 256]]), u32v(o_FF, [[1, 256]]), ALU.bitwise_and), reads=["V1", "FF"], writes=["IDX"])
                P.op("vector", lambda e: e.tensor_copy(A.v(o_IDF, [[1, 256]]), u32v(o_IDX, [[1, 256]])), reads=["IDX"], writes=["IDF"])
                P.op("vector", lambda e: e.tensor_scalar(A.v(o_IDF, [[1, 256]]), A.v(o_IDF, [[1, 256]]), -1.0, 255.0, ALU.mult, ALU.add), reads=["IDF"], writes=["IDF"])
                P.op("vector", lambda e: e.memset(A.v(o_E, [[1, 128]]), 0.0), writes=["E"])
                CAND = A.v(o_CAND, [[1, 256]]); EID = A.v(o_EID, [[1, 256]]); CW = A.v(o_CW, [[1, 256]])
                for h in range(8):
                    c3 = A.v(o_CAND, [[16, 16], [1, 16]]); e3 = A.v(o_EID, [[16, 16], [1, 16]])
                    v1a = A.v(o_V1 + (2 * h) * 16, [[1, 16], [0, 16]]); v2b = A.v(o_V1 + (2 * h + 1) * 16, [[0, 16], [1, 16]])
                    i1a = A.v(o_IDF + (2 * h) * 16, [[1, 16], [0, 16]]); i2b = A.v(o_IDF + (2 * h + 1) * 16, [[0, 16], [1, 16]])
                    P.op("vector", lambda e, c3=c3, v1a=v1a, v2b=v2b: e.tensor_tensor(c3, v1a, v2b, ALU.add), reads=["V1"], writes=["CAND"])
                    P.op("vector", lambda e: e.tensor_tensor(u32v(o_CAND, [[1, 256]]), u32v(o_CAND, [[1, 256]]), u32v(o_MK, [[1, 256]]), ALU.bitwise_and), reads=["CAND", "MK"], writes=["CAND"])
                    P.op("vector", lambda e: e.tensor_tensor(u32v(o_CAND, [[1, 256]]), u32v(o_CAND, [[1, 256]]), u32v(o_CD, [[1, 256]]), ALU.bitwise_or), reads=["CAND", "CD"], writes=["CAND"])
                    P.op("vector", lambda e, e3=e3, i1a=i1a, i2b=i2b: e.scalar_tensor_tensor(e3, i1a, 128.0, i2b, ALU.mult, ALU.add), reads=["IDF"], writes=["EID"])
                    ta = A.v(o_T + h * 16, [[1, 8]]); tb = A.v(o_T + h * 16 + 8, [[1, 8]])
                    P.op("vector", lambda e, ta=ta: e.max(out=ta, in_=CAND), reads=["CAND"], writes=["T"])
                    P.op("vector", lambda e, ta=ta: e.match_replace(out=CW, in_to_replace=ta, in_values=CAND, imm_value=-1e30), reads=["CAND", "T"], writes=["CW"])
                    P.op("vector", lambda e, tb=tb: e.max(out=tb, in_=CW), reads=["CW"], writes=["T"])
                    for k in range(16):
                        P.op("vector", lambda e, h=h, k=k: e.scalar_tensor_tensor(CW, CAND, A.v(o_T + h * 16 + k, [[1, 1]]), EID, ALU.is_equal, ALU.mult,
                                                                                   accum_out=A.v(o_E + h * 16 + k, [[1, 1]])),
                             reads=["CAND", "T", "EID", "E"], writes=["CW", "E"])
                P.op("vector", lambda e: e.tensor_copy(EIu, A.v(o_E, [[1, 128]])), reads=["E"], writes=["EI"])
                P.op("vector", lambda e: e.tensor_tensor(A.v(o_GATE, [[16, 8], [1, 16]]), A.v(o_T, [[16, 8], [1, 16]]), A.v(o_T, [[16, 8], [0, 16]]), ALU.subtract),
                     reads=["T"], writes=["GATE"])
                P.op("scalar", lambda e: e.activation(A.v(o_GATE, [[1, 128]]), A.v(o_GATE, [[1, 128]]), AF.Exp), reads=["GATE"], writes=["GATE"])
                P.op("vector", lambda e: e.tensor_reduce(A.v(o_sm, [[1, 8]]), A.v(o_GATE, [[16, 8], [1, 16]]), AX.X, ALU.add), reads=["GATE"], writes=["sm"])
                P.op("vector", lambda e: e.reciprocal(A.v(o_sm, [[1, 8]]), A.v(o_sm, [[1, 8]])), reads=["sm"], writes=["sm"])
                P.op("vector", lambda e: e.tensor_tensor(A.v(o_GATE, [[16, 8], [1, 16]]), A.v(o_GATE, [[16, 8], [1, 16]]), A.v(o_sm, [[1, 8], [0, 16]]), ALU.mult),
                     reads=["GATE", "sm"], writes=["GATE"])
                P.op("vector", lambda e: e.memset(A.v(o_DOT, [[1, 128]]), 0.0), writes=["DOT"])
                gi = 0
                for hk in range(128):
                    kb = gi % NB; gi += 1
                    Rr = A.bf(o_R[kb], 1024)
                    P.dma(lambda e, Rr=Rr, hk=hk: e.indirect_dma_start(out=Rr, out_offset=None, in_=ut.ap(),
                                                                     in_offset=bass.IndirectOffsetOnAxis(ap=EIu[:, hk:hk + 1], axis=0)),
                          reads=["EI"], writes=["R%d" % kb], q="gpsimd")
                    P.op("vector", lambda e, Rr=Rr, hk=hk: e.scalar_tensor_tensor(A.v(o_jk, [[1, 2048]]), Rr, 1.0, A.v(o_X, [[1, 2048]]), ALU.mult, ALU.mult,
                                                                                 accum_out=A.v(o_DOT + hk, [[1, 1]])),
                         reads=["R%d" % kb, "X", "DOT"], writes=["jk", "DOT"])
                P.op("scalar", lambda e: e.activation(A.v(o_WG, [[1, 128]]), A.v(o_DOT, [[1, 128]]), AF.Gelu), reads=["DOT"], writes=["WG"])
                P.op("vector", lambda e: e.tensor_tensor(A.v(o_WG, [[1, 128]]), A.v(o_WG, [[1, 128]]), A.v(o_GATE, [[1, 128]]), ALU.mult), reads=["WG", "GATE"], writes=["WG"])
                P.op("gpsimd", lambda e: e.memset(A.v(o_ACC, [[1, 2048]]), 0.0), writes=["ACC"])
                for hk in range(128):
                    kb = gi % NB; gi += 1
                    Rr = A.bf(o_R[kb], 1024)
                    P.dma(lambda e, Rr=Rr, hk=hk: e.indirect_dma_start(out=Rr, out_offset=None, in_=vt_.ap(),
                                                                     in_offset=bass.IndirectOffsetOnAxis(ap=EIu[:, hk:hk + 1], axis=0)),
                          reads=["EI"], writes=["R%d" % kb], q="gpsimd")
                    P.op("vector", lambda e, Rr=Rr, hk=hk: e.scalar_tensor_tensor(A.v(o_ACC, [[1, 2048]]), Rr, A.v(o_WG + hk, [[1, 1]]), A.v(o_ACC, [[1, 2048]]),
                                                                                 ALU.mult, ALU.add),
                         reads=["R%d" % kb, "WG", "ACC"], writes=["ACC"])
                P.op("vector", lambda e: e.tensor_tensor(A.v(o_ACC, [[1, 2048]]), A.v(o_ACC, [[1, 2048]]), GT, ALU.mult), reads=["ACC", "GT"], writes=["ACC"])
                P.op("gpsimd", lambda e: e.tensor_tensor(A.v(o_ACC, [[1, 2048]]), A.v(o_ACC, [[1, 2048]]), A.v(o_xr, [[1, 2048]]), ALU.add), reads=["ACC", "xr"], writes=["ACC"])
                P.dma(lambda e, t0=t0: e.dma_start(out=out_ap[t0:t0 + 128, :], in_=A.v(o_ACC, [[1, 2048]])), reads=["ACC"], writes=[out_key], is_output=is_output)
            P.no_self.discard("vector")
            P.barrier()

        def stage_hy_conv3(g, T, L):
            A.reset()
            o_cw = A.f32(3 * 2048); o_cb = A.f32(2048); o_acc = A.f32(2048)
            o_ld = [A.f32(2048), A.f32(2048)]
            o_vb = A.f32(1024)
            vb = A.bf(o_vb, 1024)
            tps = L // 128
            for j in range(3):
                P.dma(lambda e, j=j: e.dma_start(out=A.v(o_cw, [[2048, 3], [1, 2048]]),
                                                 in_=bass.AP(c_conv, j * 2048, [[0, 128], [6144, 3], [1, 2048]])), writes=["CW"])
                P.dma(lambda e, j=j: e.dma_start(out=A.v(o_cb, [[1, 2048]]), in_=bc_rows(c_convb, j * 2048, 2048)), writes=["CB"])
                for ti in range(T // 128):
                    t0 = ti * 128
                    first = (ti % tps == 0); last = (ti % tps == tps - 1)
                    ACC = A.v(o_acc, [[1, 2048]])
                    c0 = j * 2048
                    ld = A.v(o_ld[0], [[1, 2048]])
                    P.dma(lambda e, ld=ld, t0=t0, c0=c0: e.dma_start(out=ld, in_=uz[g].ap()[t0:t0 + 128, c0:c0 + 2048]), reads=[uz[g].name], writes=["ld0"])
                    P.op("vector", lambda e, ld=ld: e.tensor_tensor(ACC, ld, A.v(o_cw + 2048, [[1, 2048]]), ALU.mult), reads=["ld0", "CW"], writes=["ACC"])
                    ld = A.v(o_ld[1], [[1, 2048]])
                    if first:
                        P.op("gpsimd", lambda e, ld=ld: e.memset(ld, 0.0), writes=["ld1"])
                        P.dma(lambda e, t0=t0, c0=c0: e.dma_start(out=A.v(o_ld[1], [[1, 2048]], p0=1, np_=127), in_=uz[g].ap()[t0:t0 + 127, c0:c0 + 2048]),
                              reads=[uz[g].name], writes=["ld1"])
                    else:
                        P.dma(lambda e, ld=ld, t0=t0, c0=c0: e.dma_start(out=ld, in_=uz[g].ap()[t0 - 1:t0 + 127, c0:c0 + 2048]), reads=[uz[g].name], writes=["ld1"])
                    P.op("gpsimd", lambda e, ld=ld: e.tensor_tensor(ld, ld, A.v(o_cw, [[1, 2048]]), ALU.mult), reads=["ld1", "CW"], writes=["ld1"])
                    P.op("vector", lambda e, ld=ld: e.tensor_tensor(ACC, ACC, ld, ALU.add), reads=["ld1", "ACC"], writes=["ACC"])
                    ld = A.v(o_ld[0], [[1, 2048]])
                    if last:
                        P.op("gpsimd", lambda e, ld=ld: e.memset(ld, 0.0), writes=["ld0"])
                        P.dma(lambda e, t0=t0, c0=c0: e.dma_start(out=A.v(o_ld[0], [[1, 2048]], p0=0, np_=127), in_=uz[g].ap()[t0 + 1:t0 + 128, c0:c0 + 2048]),
                              reads=[uz[g].name], writes=["ld0"])
                    else:
                        P.dma(lambda e, ld=ld, t0=t0, c0=c0: e.dma_start(out=ld, in_=uz[g].ap()[t0 + 1:t0 + 129, c0:c0 + 2048]), reads=[uz[g].name], writes=["ld0"])
                    P.op("gpsimd", lambda e, ld=ld: e.tensor_tensor(ld, ld, A.v(o_cw + 4096, [[1, 2048]]), ALU.mult), reads=["ld0", "CW"], writes=["ld0"])
                    P.op("vector", lambda e, ld=ld: e.tensor_tensor(ACC, ACC, ld, ALU.add), reads=["ld0", "ACC"], writes=["ACC"])
                    if j < 2:
                        P.op("vector", lambda e: e.tensor_tensor(ACC, ACC, A.v(o_cb, [[1, 2048]]), ALU.add), reads=["ACC", "CB"], writes=["ACC"])
                        P.dma(lambda e, t0=t0, c0=c0: e.dma_start(out=ug[g].ap()[t0:t0 + 128, c0:c0 + 2048], in_=ACC), reads=["ACC"], writes=[ug[g].name])
                    else:
                        P.op("vector", lambda e: e.tensor_tensor(vb, ACC, A.v(o_cb, [[1, 2048]]), ALU.add), reads=["ACC", "CB"], writes=["vb"])
                        P.dma(lambda e, t0=t0: e.dma_start(out=zb16[g][0].ap()[t0:t0 + 128, :], in_=vb), reads=["vb"], writes=[zb16[g][0].name])
            P.barrier()

        def sin_wrapped(dst, arg, m1, m2, key, okey):
            PI = math.pi
            for _ in range(2):
                P.op("vector", lambda e: e.tensor_scalar(m1, arg, -PI, 2 * PI, ALU.is_lt, ALU.mult), reads=[key], writes=[key + "m1"])
                P.op("vector", lambda e: e.tensor_scalar(m2, arg, PI, 2 * PI, ALU.is_gt, ALU.mult), reads=[key], writes=[key + "m2"])
                P.op("vector", lambda e: e.tensor_tensor(arg, arg, m1, ALU.add), reads=[key, key + "m1"], writes=[key])
                P.op("vector", lambda e: e.tensor_tensor(arg, arg, m2, ALU.subtract), reads=[key, key + "m2"], writes=[key])
            P.op("scalar", lambda e: e.activation(dst, arg, AF.Sin), reads=[key], writes=[okey])

        def stage_hy_filters(li, L):
            A.reset()
            NT = L // 128
            o_zT = A.f32(L); o_fw3 = A.f32(8192); o_h1 = A.f32(L); o_h2 = A.f32(L)
            o_FT = [A.f32(2048) for _ in range(4)]; o_SS = [A.f32(2048), A.f32(2048)]
            o_WIN = A.f32(2048); o_DEL = A.f32(2048); o_sq = A.f32(512)
            o_fw1 = A.f32(64); o_fw2 = A.f32(64); o_col = A.f32(8); o_arg = A.f32(512); o_m1 = A.f32(512); o_m2 = A.f32(512)
            o_tn = A.f32(NT); o_one = A.f32(1); o_row = A.f32(4096); o_hs = A.f32(1024)
            hsb = A.bf(o_hs, 1024)
            zT = A.v(o_zT, [[1, L]], np_=33)
            P.dma(lambda e: e.dma_start(out=zT, in_=zposT[li].ap()), writes=["zT"])
            P.dma(lambda e: e.dma_start(out=A.v(o_fw1, [[1, 64]], np_=33), in_=c_fw1.ap()), writes=["fw1"])
            P.dma(lambda e: e.dma_start(out=A.v(o_fw2, [[1, 64]], np_=64), in_=c_fw2.ap()), writes=["fw2"])
            P.dma(lambda e: e.dma_start(out=A.v(o_fw3, [[1, 8192]], np_=64), in_=c_fw3.ap()), writes=["fw3"])
            for i, t_ in enumerate((c_fb1, c_freq, c_fb2)):
                P.dma(lambda e, i=i, t_=t_: e.dma_start(out=A.v(o_col + i, [[1, 1]], np_=64), in_=bass.AP(t_, 0, [[1, 64], [1, 1]])), writes=["col"])
            P.dma(lambda e: e.dma_start(out=A.v(o_DEL, [[1, 2048]]), in_=bc_rows(deltas, 0, 2048)), writes=["DEL"])
            P.dma(lambda e: e.dma_start(out=A.v(o_tn, [[1, NT]]), in_=tneg[li].ap().rearrange("(n p) -> p n", p=128)), writes=["tn"])
            P.op("vector", lambda e: e.memset(A.v(o_one, [[1, 1]]), 1.0), writes=["one"])
            for n in range(2):
                P.op("vector", lambda e, n=n: e.memset(A.v(o_SS[n], [[1, 2048]]), 0.0), writes=["SS%d" % n])
            nb = max(1, L // 512)
            bw = min(512, L)
            for layer_i, (o_src, o_dst, o_w, kk, bcol, skey, wkey, okey) in enumerate(
                    ((o_zT, o_h1, o_fw1, 33, 0, "zT", "fw1", "h1"), (o_h1, o_h2, o_fw2, 64, 2, "h1", "fw2", "h2"))):
                for b in range(nb):
                    P.op("tensor", lambda e, b=b, o_src=o_src, o_w=o_w, kk=kk: e.matmul(psb[b % 2][0:64, 0:bw], A.v(o_w, [[1, 64]], np_=kk),
                                                                                      A.v(o_src + b * bw, [[1, bw]], np_=kk), start=True, stop=True),
                         reads=[skey, wkey], writes=["pf%d" % (b % 2)])
                    arg = A.v(o_arg, [[1, bw]], np_=64)
                    P.op("vector", lambda e, b=b, arg=arg, bcol=bcol: e.tensor_scalar(arg, psb[b % 2][0:64, 0:bw], A.v(o_col + bcol, [[1, 1]], np_=64),
                                                                                     A.v(o_col + 1, [[1, 1]], np_=64), ALU.add, ALU.mult),
                         reads=["pf%d" % (b % 2), "col"], writes=["arg"])
                    sin_wrapped(A.v(o_dst + b * bw, [[1, bw]], np_=64), arg, A.v(o_m1, [[1, bw]], np_=64), A.v(o_m2, [[1, bw]], np_=64), "arg", okey)
            for lt in range(NT):
                P.op("scalar", lambda e, lt=lt: e.activation(A.v(o_WIN, [[1, 2048]]), A.v(o_DEL, [[1, 2048]]), AF.Exp, scale=A.v(o_tn + lt, [[1, 1]])),
                     reads=["DEL", "tn"], writes=["WIN"])
                for cc in range(16):
                    nd = cc // 4; cch = cc % 4
                    P.op("tensor", lambda e, lt=lt, cc=cc: e.matmul(psb[cc % 4][:, :], A.v(o_h2 + lt * 128, [[1, 128]], np_=64),
                                                                    A.v(o_fw3 + cc * 512, [[1, 512]], np_=64), start=True, stop=True),
                         reads=["h2", "fw3"], writes=["pf%d" % (cc % 4)])
                    ft = A.v(o_FT[nd] + cch * 512, [[1, 512]])
                    P.op("vector", lambda e, cc=cc, ft=ft, cch=cch: e.tensor_tensor(ft, psb[cc % 4][:, :], A.v(o_WIN + cch * 512, [[1, 512]]), ALU.mult),
                         reads=["pf%d" % (cc % 4), "WIN"], writes=["FT%d" % nd])
                    P.op("gpsimd", lambda e, ft=ft: e.tensor_tensor(A.v(o_sq, [[1, 512]]), ft, ft, ALU.mult), reads=["FT%d" % nd], writes=["sqf"])
                    ssv = A.v(o_SS[nd // 2] + cch * 512, [[1, 512]])
                    P.op("vector", lambda e, ssv=ssv: e.tensor_tensor(ssv, ssv, A.v(o_sq, [[1, 512]]), ALU.add), reads=["sqf", "SS%d" % (nd // 2)], writes=["SS%d" % (nd // 2)])
                for n in range(2):
                    hf = A.v(o_FT[2 * n], [[1, 2048]]); hbk = A.v(o_FT[2 * n + 1], [[1, 2048]])
                    if lt == 0:
                        P.op("vector", lambda e, n=n: e.memset(A.v(o_FT[2 * n + 1], [[1, 2048]], np_=1), 0.0), reads=["FT%d" % (2 * n + 1)], writes=["FT%d" % (2 * n + 1)])
                    P.op("vector", lambda e, hf=hf, hbk=hbk: e.tensor_tensor(hsb[:, 0:2048], hf, hbk, ALU.add), reads=["FT%d" % (2 * n), "FT%d" % (2 * n + 1)], writes=["hs"])
                    P.dma(lambda e, lt=lt, n=n: e.dma_start(out=HS[li].ap()[lt * 128:(lt + 1) * 128, 2 * n, :], in_=hsb[:, 0:2048]), reads=["hs"], writes=["HS"])
                    P.op("vector", lambda e, hf=hf, hbk=hbk: e.tensor_tensor(hsb[:, 0:2048], hbk, hf, ALU.subtract), reads=["FT%d" % (2 * n), "FT%d" % (2 * n + 1)], writes=["hs"])
                    P.dma(lambda e, lt=lt, n=n: e.dma_start(out=HS[li].ap()[lt * 128:(lt + 1) * 128, 2 * n + 1, :], in_=hsb[:, 0:2048]), reads=["hs"], writes=["HS"])
            for n in range(2):
                for cch in range(4):
                    P.op("tensor", lambda e, n=n, cch=cch: e.matmul(psb[cch][0:1, :], A.v(o_one, [[1, 1]]), A.v(o_SS[n] + cch * 512, [[1, 512]]), start=True, stop=True),
                         reads=["one", "SS%d" % n], writes=["pf%d" % cch])
                    P.op("vector", lambda e, n=n, cch=cch: e.tensor_copy(A.v(o_row + n * 2048 + cch * 512, [[1, 512]], np_=1), psb[cch][0:1, :]),
                         reads=["pf%d" % cch], writes=["row"])
            rsqrt_ops(P, A.v(o_row, [[1, 4096]], np_=1), 1.0, 1e-12, "row")
            P.dma(lambda e: e.dma_start(out=bass.AP(nsc[li], 0, [[4096, 1], [1, 4096]]), in_=A.v(o_row, [[1, 4096]], np_=1)),
                  reads=["row"], writes=["nsc"])
            P.barrier()

        def hy_forward(li, L, Zc, o_mat, kt, data_keys, tag):
            NT = L // 128
            kb = kt % 2
            Cm = A.bf(o_mat[kb][0], NT * 64).rearrange("p (t k) -> p t k", t=NT)
            Sm = A.bf(o_mat[kb][1], NT * 64).rearrange("p (t k) -> p t k", t=NT)
            P.dma(lambda e: e.dma_start(out=Cm, in_=dftc[li].ap()[:, kt * 128:(kt + 1) * 128].rearrange("(t p) k -> p t k", p=128)), writes=["Cm%d" % kb])
            P.dma(lambda e: e.dma_start(out=Sm, in_=dfts[li].ap()[:, kt * 128:(kt + 1) * 128].rearrange("(t p) k -> p t k", p=128)), writes=["Sm%d" % kb])
            Zcos, Zsin = Zc if isinstance(Zc, tuple) else (Zc, Zc)
            pa = psb[2 * kb]; pbk = psb[2 * kb + 1]
            for tt in range(NT):
                P.op("tensor", lambda e, tt=tt: e.matmul(pa[:, :], Cm[:, tt, :], Zcos[:, tt, :], start=(tt == 0), stop=(tt == NT - 1)),
                     reads=["Cm%d" % kb] + data_keys, writes=["pA%d" % kb])
            for tt in range(NT):
                P.op("tensor", lambda e, tt=tt: e.matmul(pbk[:, :], Sm[:, tt, :], Zsin[:, tt, :], start=(tt == 0), stop=(tt == NT - 1)),
                     reads=["Sm%d" % kb] + data_keys, writes=["pB%d" % kb])
            return pa, pbk, kb

        def stage_hy_fspec(li, L):
            A.reset()
            NT = L // 128
            o_hs = A.f32(NT * 256); o_hd = A.f32(NT * 256)
            o_mat = [[A.f32(NT * 64), A.f32(NT * 64)], [A.f32(NT * 64), A.f32(NT * 64)]]
            o_wk = A.f32(NT); o_o = [A.f32(512), A.f32(512)]
            P.dma(lambda e: e.dma_start(out=A.v(o_wk, [[1, NT]]), in_=wkv[li].ap().rearrange("(n p) -> p n", p=128)), writes=["wk"])
            for n in range(2):
                for cch in range(4):
                    hsT = A.bf(o_hs, NT * 256).rearrange("p (t c) -> p t c", t=NT)
                    hdT = A.bf(o_hd, NT * 256).rearrange("p (t c) -> p t c", t=NT)
                    P.dma(lambda e, n=n, cch=cch, hsT=hsT: e.dma_start(out=hsT, in_=HS[li].ap()[:, 2 * n, cch * 512:(cch + 1) * 512].rearrange("(t p) c -> p t c", p=128)),
                          reads=["HS"], writes=["hsT"])
                    P.dma(lambda e, n=n, cch=cch, hdT=hdT: e.dma_start(out=hdT, in_=HS[li].ap()[:, 2 * n + 1, cch * 512:(cch + 1) * 512].rearrange("(t p) c -> p t c", p=128)),
                          reads=["HS"], writes=["hdT"])
                    for kt in range(NT):
                        pa, pbk, kb = hy_forward(li, L, (hsT, hdT), o_mat, kt, ["hsT", "hdT"], "f")
                        for j, (pp, key) in enumerate(((pa, "pA%d" % kb), (pbk, "pB%d" % kb))):
                            ot = A.v(o_o[j], [[1, 512]])
                            P.op("vector", lambda e, ot=ot, pp=pp, kt=kt: e.tensor_scalar(ot, pp[:, :], A.v(o_wk + kt, [[1, 1]]), None, ALU.mult),
                                 reads=[key, "wk"], writes=["o%d" % j])
                            P.dma(lambda e, ot=ot, n=n, j=j, kt=kt, cch=cch: e.dma_start(out=FS[li].ap()[n, j, kt * 128:(kt + 1) * 128, cch * 512:(cch + 1) * 512], in_=ot),
                                  reads=["o%d" % j], writes=["FS"])
            P.barrier()

        def stage_hy_conv(g, T, li, L, seq, n, final):
            A.reset()
            NT = L // 128
            base = seq * L
            o_Z = A.f32(NT * 256); o_YC = A.f32(NT * 256); o_YS = A.f32(NT * 256)
            o_mat = [[A.f32(NT * 64), A.f32(NT * 64)], [A.f32(NT * 64), A.f32(NT * 64)]]
            o_F = [[A.f32(512), A.f32(512)], [A.f32(512), A.f32(512)]]
            o_t = [A.f32(512) for _ in range(4)]
            o_gt = [A.f32(512), A.f32(512)]; o_ns = A.f32(512); o_bs = A.f32(512)
            o_zn = [A.f32(256), A.f32(256)]; o_zT = A.f32(256)
            zin = zb16[g][n]; zout = zb16[g][n + 1]
            for cch in range(4):
                c0 = cch * 512
                Z = A.bf(o_Z, NT * 256).rearrange("p (t c) -> p t c", t=NT)
                YC = A.bf(o_YC, NT * 256).rearrange("p (t c) -> p t c", t=NT)
                YS = A.bf(o_YS, NT * 256).rearrange("p (t c) -> p t c", t=NT)
                P.dma(lambda e, c0=c0, Z=Z: e.dma_start(out=Z, in_=zin.ap()[base:base + L, c0:c0 + 512].rearrange("(t p) c -> p t c", p=128)),
                      reads=[zin.name], writes=["Z"])
                P.dma(lambda e, c0=c0: e.dma_start(out=A.v(o_ns, [[1, 512]]), in_=bc_rows(nsc[li], n * 2048 + c0, 512)), reads=["nsc"], writes=["NS"])
                P.dma(lambda e, c0=c0: e.dma_start(out=A.v(o_bs, [[1, 512]]), in_=bc_rows(c_bias, n * 2048 + c0, 512)), writes=["BS"])
                for kt in range(NT):
                    pa, pbk, kb = hy_forward(li, L, Z, o_mat, kt, ["Z"], "c")
                    Fc = A.v(o_F[kb][0], [[1, 512]]); Fs = A.v(o_F[kb][1], [[1, 512]])
                    P.dma(lambda e, Fc=Fc, kt=kt, c0=c0: e.dma_start(out=Fc, in_=FS[li].ap()[n, 0, kt * 128:(kt + 1) * 128, c0:c0 + 512]), reads=["FS"], writes=["Fc%d" % kb])
                    P.dma(lambda e, Fs=Fs, kt=kt, c0=c0: e.dma_start(out=Fs, in_=FS[li].ap()[n, 1, kt * 128:(kt + 1) * 128, c0:c0 + 512]), reads=["FS"], writes=["Fs%d" % kb])
                    t = [A.v(o, [[1, 512]]) for o in o_t]
                    P.op("vector", lambda e, t=t, pa=pa, Fc=Fc: e.tensor_tensor(t[0], pa[:, :], Fc, ALU.mult), reads=["pA%d" % kb, "Fc%d" % kb], writes=["t0"])
                    P.op("vector", lambda e, t=t, pbk=pbk, Fs=Fs: e.tensor_tensor(t[1], pbk[:, :], Fs, ALU.mult), reads=["pB%d" % kb, "Fs%d" % kb], writes=["t1"])
                    P.op("vector", lambda e, t=t, pbk=pbk, Fc=Fc: e.tensor_tensor(t[2], pbk[:, :], Fc, ALU.mult), reads=["pB%d" % kb, "Fc%d" % kb], writes=["t2"])
                    P.op("vector", lambda e, t=t, pa=pa, Fs=Fs: e.tensor_tensor(t[3], pa[:, :], Fs, ALU.mult), reads=["pA%d" % kb, "Fs%d" % kb], writes=["t3"])
                    P.op("gpsimd", lambda e, t=t, kt=kt, YC=YC: e.tensor_tensor(YC[:, kt, :], t[0], t[1], ALU.add), reads=["t0", "t1"], writes=["YC"])
                    P.op("gpsimd", lambda e, t=t, kt=kt, YS=YS: e.tensor_tensor(YS[:, kt, :], t[2], t[3], ALU.subtract), reads=["t2", "t3"], writes=["YS"])
                for tt in range(NT):
                    kb = tt % 2
                    Cm = A.bf(o_mat[kb][0], NT * 64).rearrange("p (t k) -> p t k", t=NT)
                    Sm = A.bf(o_mat[kb][1], NT * 64).rearrange("p (t k) -> p t k", t=NT)
                    P.dma(lambda e, Cm=Cm, tt=tt: e.dma_start(out=Cm, in_=dftc[li].ap()[:, tt * 128:(tt + 1) * 128].rearrange("(t p) k -> p t k", p=128)), writes=["Cm%d" % kb])
                    P.dma(lambda e, Sm=Sm, tt=tt: e.dma_start(out=Sm, in_=dfts[li].ap()[:, tt * 128:(tt + 1) * 128].rearrange("(t p) k -> p t k", p=128)), writes=["Sm%d" % kb])
                    py = psb[4 + kb]
                    for kt in range(NT):
                        P.op("tensor", lambda e, kt=kt, Cm=Cm, YC=YC, py=py: e.matmul(py[:, :], Cm[:, kt, :], YC[:, kt, :], start=(kt == 0), stop=False),
                             reads=["Cm%d" % kb, "YC"], writes=["pY%d" % kb])
                    for kt in range(NT):
                        P.op("tensor", lambda e, kt=kt, Sm=Sm, YS=YS, py=py: e.matmul(py[:, :], Sm[:, kt, :], YS[:, kt, :], start=False, stop=(kt == NT - 1)),
                             reads=["Sm%d" % kb, "YS"], writes=["pY%d" % kb])
                    gt_ = A.v(o_gt[kb], [[1, 512]])
                    t0 = base + tt * 128
                    P.dma(lambda e, gt_=gt_, t0=t0, c0=c0: e.dma_start(out=gt_, in_=ug[g].ap()[t0:t0 + 128, n * 2048 + c0:n * 2048 + c0 + 512]), reads=[ug[g].name], writes=["gt%d" % kb])
                    ta = A.v(o_t[0], [[1, 512]]); tb = A.v(o_t[1], [[1, 512]])
                    P.op("vector", lambda e, ta=ta, py=py: e.tensor_tensor(ta, py[:, :], A.v(o_ns, [[1, 512]]), ALU.mult), reads=["pY%d" % kb, "NS"], writes=["t0"])
                    P.op("gpsimd", lambda e, tb=tb, tt=tt, Z=Z: e.tensor_tensor(tb, Z[:, tt, :], A.v(o_bs, [[1, 512]]), ALU.mult), reads=["Z", "BS"], writes=["t1"])
                    P.op("vector", lambda e, ta=ta, tb=tb: e.tensor_tensor(ta, ta, tb, ALU.add), reads=["t0", "t1"], writes=["t0"])
                    zn = A.bf(o_zn[kb], 256)
                    P.op("vector", lambda e, ta=ta, gt_=gt_, zn=zn: e.tensor_tensor(zn, ta, gt_, ALU.mult), reads=["t0", "gt%d" % kb], writes=["zn%d" % kb])
                    if not final:
                        P.dma(lambda e, zn=zn, t0=t0, c0=c0: e.dma_start(out=zout.ap()[t0:t0 + 128, c0:c0 + 512], in_=zn), reads=["zn%d" % kb], writes=[zout.name])
                    else:
                        zT4 = A.bf(o_zT, 256).rearrange("p (c t) -> p c t", c=4)
                        for c4 in range(4):
                            P.op("tensor", lambda e, c4=c4, zn=zn: e.transpose(psT[:, c4 * 128:(c4 + 1) * 128], zn[:, c4 * 128:(c4 + 1) * 128], identb),
                                 reads=["zn%d" % kb, "identb"], writes=["psT"])
                        P.op("scalar", lambda e, zT4=zT4: e.activation(zT4, psT[:, 0:512].rearrange("p (c t) -> p c t", c=4), AF.Copy), reads=["psT"], writes=["zT4"])
                        P.dma(lambda e, zT4=zT4, t0=t0, cch=cch: e.dma_start(out=hzT[g].ap()[cch * 4:(cch + 1) * 4, :, t0:t0 + 128].rearrange("c p t -> p c t"), in_=zT4),
                              reads=["zT4"], writes=[hzT[g].name])
            P.barrier()

        def stage_cast_tables():
            for l in range(2):
                for src_t, dst_t in ((peer_u[l], peer_ub[l]), (peer_v[l], peer_vb[l])):
                    for c in range(16):
                        P.dma(lambda e, src_t=src_t, dst_t=dst_t, c=c: e.dma_start(out=dst_t.ap()[c * 1024:(c + 1) * 1024, :], in_=src_t.ap()[c * 1024:(c + 1) * 1024, :]),
                              writes=[dst_t.name], q="gpsimd")

        def stage_gather_rows(src_t, dst_t, n):
            A.reset()
            o_i = A.f32(8); o_r = [A.f32(2048), A.f32(2048)]
            idx = bass.AP(A.t, o_i, [[A.size, 128], [1, 8]]).bitcast(I32)
            P.dma(lambda e: e.dma_start(out=idx, in_=qidx.ap().rearrange("(n p) o -> p (n o)", p=128)), writes=["qi"])
            for ti in range(n // 128):
                k = ti % 2
                Rr = A.v(o_r[k], [[1, 2048]])
                P.dma(lambda e, Rr=Rr, ti=ti: e.indirect_dma_start(out=Rr, out_offset=None, in_=src_t.ap(),
                                                                 in_offset=bass.IndirectOffsetOnAxis(ap=idx[:, ti:ti + 1], axis=0)),
                      reads=["qi"], writes=["gr%d" % k], q="gpsimd")
                P.dma(lambda e, Rr=Rr, ti=ti: e.dma_start(out=dst_t.ap()[ti * 128:(ti + 1) * 128, :], in_=Rr), reads=["gr%d" % k], writes=[dst_t.name])
            P.barrier()

        def stage_kv_out():
            A.reset()
            o_z = A.f32(512); o_t = A.f32(256); o_s = A.f32(8); o_kn = A.f32(128)
            KN = A.v(o_kn, [[1, 128]])
            P.dma(lambda e: e.dma_start(out=KN, in_=bc_rows(b_kn, 0, 128)), writes=["KN"])
            for ti in range(TP // 128):
                t0 = ti * 128
                zt = A.v(o_z, [[1, 512]])
                P.dma(lambda e, t0=t0: e.dma_start(out=zt, in_=zg[0].ap()[t0:t0 + 128, 4480:4992]), reads=["zp"], writes=["zt"])
                P.dma(lambda e, t0=t0: e.dma_start(out=nv.ap()[t0:t0 + 128, :], in_=A.v(o_z + 256, [[1, 256]])), reads=["zt"], writes=["nv"], is_output=True)
                P.op("vector", lambda e: e.tensor_tensor(A.v(o_t, [[1, 256]]), A.v(o_z, [[1, 256]]), A.v(o_z, [[1, 256]]), ALU.mult), reads=["zt"], writes=["t"])
                P.op("vector", lambda e: e.tensor_reduce(A.v(o_s, [[1, 2]]), A.v(o_t, [[128, 2], [1, 128]]), AX.X, ALU.add), reads=["t"], writes=["s"])
                rsqrt_ops(P, A.v(o_s, [[1, 2]]), 1.0 / 128, 1e-6, "s")
                P.op("vector", lambda e: e.tensor_tensor(A.v(o_t, [[128, 2], [1, 128]]), A.v(o_z, [[128, 2], [1, 128]]), A.v(o_s, [[1, 2], [0, 128]]), ALU.mult),
                     reads=["s", "zt"], writes=["t"])
                P.op("vector", lambda e: e.tensor_tensor(A.v(o_t, [[128, 2], [1, 128]]), A.v(o_t, [[128, 2], [1, 128]]), A.v(o_kn, [[0, 2], [1, 128]]), ALU.mult),
                     reads=["t", "KN"], writes=["t"])
                P.dma(lambda e, t0=t0: e.dma_start(out=nk.ap()[t0:t0 + 128, :], in_=A.v(o_t, [[1, 256]])), reads=["t"], writes=["nk"], is_output=True)
            P.barrier()

        GROUPS = ((0, TP, 256), (1, TS, 4096))

        def layer1(g, T, L, out_ap, out_key):
            li = 0 if L == 256 else 1
            stage_norm_gemm(g, T, x2[g].ap(), 1, 1, norm1, odd_w_in.ap(), 6144, uz[g])
            stage_hy_conv3(g, T, L)
            for n in range(2):
                for seq in range(T // L):
                    stage_hy_conv(g, T, li, L, seq, n, n == 1)
            stage_norm_gemm(g, T, None, 1, 1, None, odd_w_out.ap(), D, x3[g], srcT=hzT[g], resid=(x2[g].ap(), 2 * D))
            if mode == "hytest":
                return
            if g == 1:
                stage_gather_rows(x3[1], x3q, 1024)
                stage_norm_gemm(g, 1024, x3q.ap(), 1, 2, norm2, peer_wq.ap()[1], D, pq[g], hm_out=hm2[g])
                stage_peer(g, 1024, 1, pq[g], hm2[g], x3q.ap(), out_ap, out_key, is_output=True)
                return
            stage_norm_gemm(g, T, x3[g].ap(), 1, 2, norm2, peer_wq.ap()[1], D, pq[g], hm_out=hm2[g])
            stage_peer(g, T, 1, pq[g], hm2[g], x3[g].ap(), out_ap, out_key, is_output=True)

        if mode == "bench_scan":
            stage_scan(0, 0, 256, 0, None, ns)
            stage_scan(0, 0, 256, 1, None, ns)
            P.emit()
            return nc
        if mode == "bench_peer":
            stage_peer(0, 256, 0, pq[0], hm2[0], xg[0].ap(), x2[0].ap(), "x2_0")
            P.emit()
            return nc
        if mode == "bench_gemm":
            stage_norm_gemm(0, TP, xg[0].ap(), 0, 1, norm1, even_w_in.ap(), 4992, zg[0])
            P.emit()
            return nc
        if mode == "full":
            stage_cast_tables()
        stage_mod()
        for li, L_ in enumerate((256, 4096)):
            stage_hy_filters(li, L_)
            stage_hy_fspec(li, L_)
        if mode == "full":
            for g, T, L in GROUPS:
                latent = (g == 1)
                stage_norm_gemm(g, T, xg[g].ap(), 0, 1, norm1, even_w_in.ap(), 4992, zg[g])
                if g == 0:
                    stage_kv_out()
                stage_even_prep(g, T, L)
                for seq in range(T // L):
                    for d in range(2):
                        stage_scan(g, seq, L, d, st_in if latent else None, None if latent else ns)
                stage_rwkv_post(g, T)
                stage_attn_prep(g, T, L, latent)
                for seq in range(T // L):
                    stage_attn(g, T, seq, L, L + (512 if latent else 0))
                stage_norm_gemm(g, T, None, 0, 1, None, even_w_out.ap(), D, x1[g], srcT=ycatT[g], resid=(xg[g].ap(), 2 * D))
                stage_norm_gemm(g, T, x1[g].ap(), 0, 2, norm2, peer_wq.ap()[0], D, pq[g], hm_out=hm2[g])
                stage_peer(g, T, 0, pq[g], hm2[g], x1[g].ap(), x2[g].ap(), x2[g].name)
        layer1(0, TP, 256, yp.ap(), "yp")
        layer1(1, TS, 4096, ys.ap(), "ys")
        P.emit()
    return nc


_CACHE = {}


def _rope_tables():
    pos = np.arange(4096)
    row = (pos // 64).astype(np.float32); col = (pos % 64).astype(np.float32)
    inv = (np.float32(10000.0) ** (-np.arange(32, dtype=np.float32) / np.float32(32))).astype(np.float32)
    ar = (row[:, None] * inv[None, :]).astype(np.float32); ac = (col[:, None] * inv[None, :]).astype(np.float32)
    cs = np.concatenate([np.cos(ar), np.cos(ac)], 1).astype(np.float32)
    sn = np.concatenate([np.sin(ar), np.sin(ac)], 1).astype(np.float32)
    return cs, sn


def _hyena_consts():
    import ml_dtypes
    out = {}
    for li, L in enumerate((256, 4096)):
        t = np.linspace(0.0, 1.0, L, dtype=np.float32)[:, None]
        wpos = (np.float32(2.0 * math.pi) * np.arange(L, dtype=np.float32)[:, None] / np.float32(L)).astype(np.float32)
        fb = np.linspace(1e-4, 15, 16, dtype=np.float32)[None, :]
        z = np.concatenate([t, np.cos(fb * wpos), -np.sin(fb * wpos)], axis=-1).astype(np.float32)
        out["zposT_%d" % li] = np.ascontiguousarray(z.T)
        out["tneg_%d" % li] = np.ascontiguousarray(-t[:, 0])
        Nf = 2 * L - 1
        wk = np.full((L,), 2.0 / Nf, np.float32); wk[0] = 1.0 / Nf
        out["wk_%d" % li] = wk
        idx = np.arange(L, dtype=np.int64)
        ang = (2.0 * np.pi / Nf) * ((idx[:, None] * idx[None, :]) % Nf).astype(np.float64)
        out["dftc_%d" % li] = np.cos(ang).astype(ml_dtypes.bfloat16)
        out["dfts_%d" % li] = np.sin(ang).astype(ml_dtypes.bfloat16)
    out["deltas"] = np.abs(np.linspace(math.log(1e-2) / 1.5, math.log(1e-2) / 0.3, 2048, dtype=np.float32)).astype(np.float32)
    return out


def _shared_inputs(inputs):
    f = lambda a: np.ascontiguousarray(np.asarray(a, dtype=np.float32))
    sh = {
        "mod_w": f(inputs["mod_w"]), "mod_b": f(inputs["mod_b"]),
        "norm1": f(inputs["norm1"]), "norm2": f(inputs["norm2"]),
        "even_w_in": f(inputs["even_w_in"][0]),
        "even_a_conv": f(inputs["even_a_conv"][0]),
        "even_a_w0": f(inputs["even_a_w0"][0]), "even_a_wu": f(inputs["even_a_wu"][0]).reshape(128, 1024),
        "even_a_a0": f(inputs["even_a_a0"][0]), "even_a_au": f(inputs["even_a_au"][0]).reshape(128, 1024),
        "even_a_gu": f(inputs["even_a_gu"][0]),
        "even_a_kk": f(inputs["even_a_kk"][0]), "even_a_ka": f(inputs["even_a_ka"][0]),
        "even_a_rk": f(inputs["even_a_rk"][0]).reshape(1024),
        "even_a_ln_w": f(inputs["even_a_ln_w"][0]), "even_a_ln_b": f(inputs["even_a_ln_b"][0]),
        "even_b_qnorm": f(inputs["even_b_qnorm"][0]), "even_b_knorm": f(inputs["even_b_knorm"][0]),
        "ident": np.eye(128, dtype=np.float32),
        "ropec": _rope_tables()[0], "ropes": _rope_tables()[1],
        "even_w_out": f(inputs["even_w_out"][0]),
        "odd_w_in": f(inputs["odd_w_in"][0]), "odd_w_out": f(inputs["odd_w_out"][0]),
        "odd_c_conv": f(inputs["odd_c_conv"][0]), "odd_c_conv_b": f(inputs["odd_c_conv_b"][0]),
        "odd_c_fw1": f(inputs["odd_c_fw1"][0]), "odd_c_fb1": f(inputs["odd_c_fb1"][0]), "odd_c_freq": f(inputs["odd_c_freq"][0]),
        "odd_c_fw2": f(inputs["odd_c_fw2"][0]), "odd_c_fb2": f(inputs["odd_c_fb2"][0]), "odd_c_fw3": f(inputs["odd_c_fw3"][0]),
        "odd_c_bias": f(inputs["odd_c_bias"][0]),
        "peer_wq": f(inputs["peer_wq"]), "peer_keys": f(inputs["peer_keys"]),
        "peer_u0": f(inputs["peer_u"][0]), "peer_u1": f(inputs["peer_u"][1]),
        "peer_v0": f(inputs["peer_v"][0]), "peer_v1": f(inputs["peer_v"][1]),
    }
    sh.update(_hyena_consts())
    kc = np.zeros((3, 128, 256), np.uint32)
    kc[0] = 0xFFFFFF00
    kc[1] = 0xFF
    kc[2] = (255 - (np.arange(256) % 256)).astype(np.uint32)[None, :]
    sh["kconst"] = kc
    return sh


def kernel(**inputs):
    f = lambda a: np.ascontiguousarray(np.asarray(a, dtype=np.float32))
    if "nc" not in _CACHE:
        _CACHE["nc"] = build_program()
    nc = _CACHE["nc"]
    x_prompt = f(inputs["x_prompt"]); x_sample = f(inputs["x_sample"])
    shared = _shared_inputs(inputs)
    in_maps = []
    for c in range(NC):
        b = c // 4
        m = dict(shared)
        m["xp"] = x_prompt[4 * c:4 * c + 4].reshape(1024, D)
        m["xs"] = x_sample[b]
        m["ck"] = f(inputs["cache_b_k"][b, 0]).reshape(512, 256)
        m["cv"] = f(inputs["cache_b_v"][b, 0]).reshape(512, 256)
        m["st"] = f(inputs["state_a"][b, 0])
        m["cond"] = np.stack([f(inputs["c_ctx"]), f(inputs["c"][b])], 0)
        m["qidx"] = (np.arange(1024, dtype=np.int32) + 1024 * (c % 4)).reshape(1024, 1)
        in_maps.append(m)
    res = run_bass_kernel_spmd(nc, in_maps, core_ids=list(range(NC)))
    R = res.results
    if DEBUG_OUT:
        DEBUG_RES['R'] = R
    y_prompt = np.concatenate([R[c]["yp"].reshape(4, 256, D) for c in range(NC)], 0)
    y_sample = np.stack([np.concatenate([R[4 * b + q]["ys"] for q in range(4)], 0) for b in range(2)], 0)
    new_k = np.concatenate([R[c]["nk"].reshape(4, 1, 256, 2, 128) for c in range(NC)], 0)
    new_v = np.concatenate([R[c]["nv"].reshape(4, 1, 256, 2, 128) for c in range(NC)], 0)
    new_s = np.concatenate([R[c]["ns"].reshape(4, 1, 2, 16, 64, 64) for c in range(NC)], 0)
    return (y_prompt.astype(np.float32), y_sample.astype(np.float32), new_k.astype(np.float32),
            new_v.astype(np.float32), new_s.astype(np.float32))
```

```python
from contextlib import ExitStack
import math
import numpy as np
import concourse.bass as bass
import concourse.mybir as mybir
from concourse.bass_utils import run_bass_kernel_spmd

F32 = mybir.dt.float32
BF16 = mybir.dt.bfloat16
I32 = mybir.dt.int32
U32 = mybir.dt.uint32
AF = mybir.ActivationFunctionType
ALU = mybir.AluOpType
AX = mybir.AxisListType

D = 2048
NC = 8
COMPUTE = ("tensor", "vector", "scalar", "gpsimd")
NDMA_SLOTS = 12
SEM_LIMIT = 30000
DEBUG_OUT = set()
UNTRACKED = {"sq", "sq_r", "sq_kk", "sq_w", "sq_kt", "sq_b", "gsc", "bon", "vT", "yT", "nv", "nk", "qT_d", "kT_d", "V_d",
             "HS", "FS", "ns", "yp", "ys", "modv", "nsc", "zero_d"}
SCAN_DVE_INORDER = True
DEBUG_RES = {}


class Prog:
    def __init__(self, nc, stack):
        self.nc = nc
        self.stack = stack
        self.ops = {e: [] for e in COMPUTE + ("sync",)}
        self.nsem = 0
        self.csem = {e: self._newsem("c_" + e) for e in COMPUTE}
        self.ccnt = {e: 0 for e in COMPUTE}
        self.dslots = {}
        self.dnext = {}
        for q in ("sync", "gpsimd"):
            self.dslots[q] = [[self._newsem("d_%s%d" % (q, i)), 0] for i in range(NDMA_SLOTS)]
            self.dnext[q] = 0
        self.last_w = {}
        self.readers = {}
        self.waited = {e: {} for e in self.ops}
        self.out_events = []
        self.all_events = {}
        self.no_self = {"tensor"}
        self.own = {e: {id(self.csem[e])} for e in COMPUTE}

    def _newsem(self, name):
        self.nsem += 1
        return self.stack.enter_context(self.nc.semaphore("%s_%d" % (name, self.nsem)))

    def _deps(self, reads, writes):
        reads = [k for k in reads if k not in UNTRACKED]
        writes = [k for k in writes if k not in UNTRACKED]
        deps = []
        for k in reads:
            if k in self.last_w:
                deps.append(self.last_w[k])
        for k in writes:
            if k in self.last_w:
                deps.append(self.last_w[k])
            deps.extend(self.readers.get(k, ()))
        return deps

    def _emit_waits(self, eng, deps):
        need = {}
        for (s, v) in deps:
            if v > need.get(id(s), (s, 0))[1]:
                need[id(s)] = (s, v)
        w = self.waited[eng]
        skip = self.own.get(eng, ()) if eng in self.no_self else ()
        for sid, (s, v) in need.items():
            if w.get(sid, 0) >= v or sid in skip:
                continue
            w[sid] = v
            self.ops[eng].append(("wait", s, v))

    def _commit(self, ev, reads, writes):
        self.all_events[id(ev[0])] = ev
        reads = [k for k in reads if k not in UNTRACKED]
        writes = [k for k in writes if k not in UNTRACKED]
        for k in reads:
            self.readers.setdefault(k, []).append(ev)
        for k in writes:
            self.last_w[k] = ev
            self.readers[k] = []

    def op(self, eng, fn, reads=(), writes=(), inorder=False):
        deps = self._deps(reads, writes)
        if inorder and eng not in self.no_self:
            self.no_self.add(eng)
            self._emit_waits(eng, deps)
            self.no_self.discard(eng)
        else:
            self._emit_waits(eng, deps)
        if self.ccnt[eng] >= SEM_LIMIT:
            self.csem[eng] = self._newsem("c_" + eng)
            self.own[eng].add(id(self.csem[eng]))
            self.ccnt[eng] = 0
        self.ccnt[eng] += 1
        ev = (self.csem[eng], self.ccnt[eng])
        self.ops[eng].append(("op", fn, ev[0], 1))
        self._commit(ev, reads, writes)
        return ev

    def dma(self, fn, reads=(), writes=(), q="sync", is_output=False):
        deps = self._deps(reads, writes)
        slots = self.dslots[q]
        i = self.dnext[q]
        self.dnext[q] = (i + 1) % len(slots)
        slot = slots[i]
        if slot[1] > 0:
            deps.append((slot[0], slot[1]))
        self._emit_waits(q, deps)
        if slot[1] + 16 > SEM_LIMIT:
            slot[0] = self._newsem("d_" + q)
            slot[1] = 0
        slot[1] += 16
        ev = (slot[0], slot[1])
        self.ops[q].append(("op", fn, ev[0], 16))
        self._commit(ev, reads, writes)
        if is_output:
            self.out_events.append(ev)
        return ev

    def barrier(self):
        evs = list(self.all_events.values())
        saved = self.no_self
        self.no_self = set()
        for e in self.ops:
            self._emit_waits(e, evs)
        self.no_self = saved
        self.last_w = {}
        self.readers = {}

    def emit(self):
        nc = self.nc
        self._emit_waits("sync", list(self.all_events.values()))
        with nc.Block() as block:
            def run(engname):
                def body(eng):
                    for item in self.ops[engname]:
                        if item[0] == "wait":
                            eng.wait_ge(item[1], item[2])
                        else:
                            item[1](eng).then_inc(item[2], item[3])
                return body
            block.sync(run("sync"))
            block.tensor(run("tensor"))
            block.vector(run("vector"))
            block.scalar(run("scalar"))
            block.gpsimd(run("gpsimd"))


class Arena:
    def __init__(self, t, size):
        self.t = t
        self.size = size
        self.off = 0

    def reset(self):
        self.off = 0

    def f32(self, n):
        o = self.off
        self.off += n
        assert self.off <= self.size, ("arena overflow", self.off, self.size)
        return o

    def v(self, off, dims, p0=0, np_=128):
        return bass.AP(self.t, p0 * self.size + off, [[self.size, np_]] + [list(d) for d in dims])

    def bf(self, off, n):
        return self.t[:, off:off + n].bitcast(BF16)


def bc_rows(dram_t, elem_off, n, nparts=128):
    return bass.AP(dram_t, elem_off, [[0, nparts], [1, n]])


def rsqrt_ops(P, ap, mul, add, key):
    P.op("vector", lambda e: e.tensor_scalar(ap, ap, mul, add, ALU.mult, ALU.add), reads=[key], writes=[key])
    P.op("scalar", lambda e: e.activation(ap, ap, AF.Sqrt), reads=[key], writes=[key])
    P.op("vector", lambda e: e.reciprocal(ap, ap), reads=[key], writes=[key])

def build_program(mode="full"):
    nc = bass.Bass("TRN2", target_bir_lowering=False)
    TP, TS = 1024, 4096
    dt_in = {}

    def inp(name, shape, dt=F32):
        dt_in[name] = nc.dram_tensor(name, list(shape), dt, kind="ExternalInput")
        return dt_in[name]

    def outp(name, shape, dt=F32):
        return nc.dram_tensor(name, list(shape), dt, kind="ExternalOutput")

    def scr(name, shape, dt=F32):
        UNTRACKED.add(name)
        if name in DEBUG_OUT:
            return nc.dram_tensor(name, list(shape), dt, kind="ExternalOutput")
        return nc.dram_tensor(name, list(shape), dt)

    xg = [inp("xp", [TP, D]), inp("xs", [TS, D])]
    ck = inp("ck", [512, 256]); cv = inp("cv", [512, 256]); st_in = inp("st", [2, 16, 64, 64])
    cond = inp("cond", [2, D])
    mod_w = inp("mod_w", [2, D, 6 * D]); mod_b = inp("mod_b", [2, 6 * D])
    norm1 = inp("norm1", [2, D]); norm2 = inp("norm2", [2, D])
    even_w_in = inp("even_w_in", [D, 4992])
    a_conv = inp("even_a_conv", [3, 3456])
    a_w0 = inp("even_a_w0", [2, 1024]); a_wu = inp("even_a_wu", [128, 1024])
    a_a0 = inp("even_a_a0", [2, 1024]); a_au = inp("even_a_au", [128, 1024])
    a_gu = inp("even_a_gu", [128, 1024])
    a_kk = inp("even_a_kk", [1024]); a_ka = inp("even_a_ka", [1024]); a_rk = inp("even_a_rk", [1024])
    a_lnw = inp("even_a_ln_w", [1024]); a_lnb = inp("even_a_ln_b", [1024])
    b_qn = inp("even_b_qnorm", [128]); b_kn = inp("even_b_knorm", [128])
    ident_d = inp("ident", [128, 128])
    ropec = inp("ropec", [4096, 64]); ropes = inp("ropes", [4096, 64])
    even_w_out = inp("even_w_out", [D, D])
    odd_w_in = inp("odd_w_in", [D, 6144]); odd_w_out = inp("odd_w_out", [D, D])
    c_conv = inp("odd_c_conv", [3, 6144]); c_convb = inp("odd_c_conv_b", [6144])
    c_fw1 = inp("odd_c_fw1", [33, 64]); c_fb1 = inp("odd_c_fb1", [64]); c_freq = inp("odd_c_freq", [64])
    c_fw2 = inp("odd_c_fw2", [64, 64]); c_fb2 = inp("odd_c_fb2", [64]); c_fw3 = inp("odd_c_fw3", [64, 8192])
    c_bias = inp("odd_c_bias", [2, 2048])
    zposT = [inp("zposT_%d" % li, [33, L_]) for li, L_ in enumerate((256, 4096))]
    tneg = [inp("tneg_%d" % li, [L_]) for li, L_ in enumerate((256, 4096))]
    wkv = [inp("wk_%d" % li, [L_]) for li, L_ in enumerate((256, 4096))]
    deltas = inp("deltas", [2048])
    kconst = inp("kconst", [3, 128, 256], U32)
    qidx = inp("qidx", [1024, 1], I32)
    dftc = [inp("dftc_%d" % li, [L_, L_], BF16) for li, L_ in enumerate((256, 4096))]
    dfts = [inp("dfts_%d" % li, [L_, L_], BF16) for li, L_ in enumerate((256, 4096))]
    peer_wq = inp("peer_wq", [2, D, D]); peer_keys = inp("peer_keys", [2, 8, 2, 128, 128])
    peer_u = [inp("peer_u0", [16384, D]), inp("peer_u1", [16384, D])]
    peer_v = [inp("peer_v0", [16384, D]), inp("peer_v1", [16384, D])]

    yp = outp("yp", [TP, D]); ys = outp("ys", [1024, D])
    nk = outp("nk", [TP, 256]); nv = outp("nv", [TP, 256]); ns = outp("ns", [4, 2, 16, 64, 64])

    modv = scr("modv", [2, 2, 6 * D])
    zg = [scr("zp", [TP, 4992]), scr("zs", [TS, 4992])]
    SQ = ["kk", "w0", "w1", "b0", "b1", "kt0", "kt1", "r"]
    sq = [{n: scr("sq_%s_%d" % (n, g), [T, 1024]) for n in SQ} for g, T in enumerate((TP, TS))]
    vT = [scr("vT_%d" % g, [8, 128, T]) for g, T in enumerate((TP, TS))]
    yT = [[scr("yT_%d_%d" % (g, d), [8, 128, T]) for d in range(2)] for g, T in enumerate((TP, TS))]
    gsc = [scr("g_%d" % g, [T, 1024]) for g, T in enumerate((TP, TS))]
    bon = [scr("bon_%d" % g, [T, 1024]) for g, T in enumerate((TP, TS))]
    zero_d = scr("zero_d", [128, 64])
    ycatT = [scr("ycatT_%d" % g, [16, 128, T], BF16) for g, T in enumerate((TP, TS))]
    qT_d = [scr("qT_%d" % g, [8, 128, T], BF16) for g, T in enumerate((TP, TS))]
    kT_d = [scr("kT_0", [2, 128, TP], BF16), scr("kT_1", [2, 128, TS + 512], BF16)]
    V_d = [scr("V_0", [TP, 256], BF16), scr("V_1", [TS + 512, 256], BF16)]
    x1 = [scr("x1_%d" % g, [T, D]) for g, T in enumerate((TP, TS))]
    if mode == "hytest":
        x2 = [inp("x2_%d" % g, [T, D]) for g, T in enumerate((TP, TS))]
    else:
        x2 = [scr("x2_%d" % g, [T, D]) for g, T in enumerate((TP, TS))]
    x3 = [scr("x3_%d" % g, [T, D]) for g, T in enumerate((TP, TS))]
    x3q = scr("x3q", [1024, D])
    peer_ub = [scr("peer_ub%d" % l, [16384, D], BF16) for l in range(2)]
    peer_vb = [scr("peer_vb%d" % l, [16384, D], BF16) for l in range(2)]
    uz = [scr("uz_%d" % g, [T, 6144]) for g, T in enumerate((TP, TS))]
    ug = [scr("ug_%d" % g, [T, 4096]) for g, T in enumerate((TP, TS))]
    zb16 = [[scr("zb_%d_%d" % (g, i), [T, 2048], BF16) for i in range(3)] for g, T in enumerate((TP, TS))]
    hzT = [scr("hzT_%d" % g, [16, 128, T], BF16) for g, T in enumerate((TP, TS))]
    LL = (256, 4096)
    HS = [scr("HS_%d" % li, [L_, 4, 2048], BF16) for li, L_ in enumerate(LL)]
    FS = [scr("FS_%d" % li, [2, 2, L_, 2048]) for li, L_ in enumerate(LL)]
    nsc = [scr("nsc_%d" % li, [2, 2048]) for li in range(2)]
    hm2 = [scr("hm2_%d" % g, [T, D]) for g, T in enumerate((TP, TS))]
    pq = [scr("pq_%d" % g, [T, D]) for g, T in enumerate((TP, TS))]

    with ExitStack() as st:
        st.enter_context(nc.allow_non_contiguous_dma(reason="layout"))
        P = Prog(nc, st)
        ASZ = 47616
        A = Arena(st.enter_context(nc.sbuf_tensor("arena", [128, ASZ], F32)), ASZ)
        psb = [st.enter_context(nc.psum_tensor("ps%d" % i, [128, 512], F32)) for i in range(6)]
        psT = st.enter_context(nc.psum_tensor("psT", [128, 2048], BF16))
        cons = st.enter_context(nc.sbuf_tensor("cons", [128, 128 + 64], F32))
        consb = st.enter_context(nc.sbuf_tensor("consb", [128, 128], BF16))
        ident = cons[:, 0:128]
        identb = consb[:, 0:128]
        P.dma(lambda e: e.dma_start(out=ident, in_=ident_d.ap()), writes=["ident"])
        P.op("vector", lambda e: e.tensor_copy(identb, ident), reads=["ident"], writes=["identb"])
        P.op("vector", lambda e: e.memset(cons[:, 128:192], 0.0), writes=["zeros"])
        P.dma(lambda e: e.dma_start(out=zero_d.ap(), in_=cons[:, 128:192]), reads=["zeros"], writes=["zero_d"])

        def stage_mod():
            A.reset()
            o_c = A.f32(32); o_s = A.f32(32); o_mb = A.f32(6 * D)
            o_w = [A.f32(16 * 512), A.f32(16 * 512)]
            cT = A.v(o_c, [[1, 32]]); sT = A.v(o_s, [[1, 32]])
            for gg in range(2):
                P.dma(lambda e, gg=gg: e.dma_start(out=A.v(o_c + gg * 16, [[1, 16]]),
                                                   in_=cond.ap()[gg].rearrange("(kc p) -> p kc", p=128)), writes=["cT"])
            P.op("scalar", lambda e: e.activation(sT, cT, AF.Silu), reads=["cT"], writes=["sT"])
            mb = A.v(o_mb, [[1, 6 * D]], np_=2)
            for layer in range(2):
                P.dma(lambda e, layer=layer: e.dma_start(out=mb, in_=bc_rows(mod_b, layer * 6 * D, 6 * D, 2)), writes=["mb"])
                for ch in range(24):
                    k = ch % 2
                    wt = A.v(o_w[k], [[512, 16], [1, 512]])
                    P.dma(lambda e, layer=layer, ch=ch, wt=wt: e.dma_start(
                        out=wt, in_=mod_w.ap()[layer, :, ch * 512:(ch + 1) * 512].rearrange("(kc p) n -> p kc n", p=128)),
                        writes=["mw%d" % k])
                    pb = psb[ch % 2]
                    for kc in range(16):
                        P.op("tensor", lambda e, kc=kc, k=k, pb=pb: e.matmul(
                            pb[0:2, :], A.v(o_s + kc, [[16, 2]]), A.v(o_w[k] + kc * 512, [[1, 512]]),
                            start=(kc == 0), stop=(kc == 15)),
                            reads=["sT", "mw%d" % k], writes=["psm%d" % (ch % 2)])
                    P.op("vector", lambda e, ch=ch, pb=pb: e.tensor_tensor(
                        A.v(o_mb + ch * 512, [[1, 512]], np_=2), A.v(o_mb + ch * 512, [[1, 512]], np_=2), pb[0:2, :], ALU.add),
                        reads=["psm%d" % (ch % 2), "mb"], writes=["mb"])
                for so in (D, 4 * D):
                    P.op("vector", lambda e, so=so: e.tensor_scalar_add(
                        A.v(o_mb + so, [[1, D]], np_=2), A.v(o_mb + so, [[1, D]], np_=2), 1.0), reads=["mb"], writes=["mb"])
                P.dma(lambda e, layer=layer: e.dma_start(out=modv.ap()[layer], in_=mb), reads=["mb"], writes=["modv"])
            P.barrier()

        def stage_norm_gemm(g, T, x_ap, layer, which, normw, W_ap, N, out_t, hm_out=None, srcT=None, resid=None):
            A.reset()
            sc_off = (1 if which == 1 else 4) * D
            sh_off = (0 if which == 1 else 3) * D
            o_G = A.f32(D); o_SH = A.f32(D); o_h = A.f32(D)
            o_x = [A.f32(D), A.f32(D)]
            o_w = A.f32(16 * 512)
            o_ot = [A.f32(512), A.f32(512)]
            o_xr = [A.f32(512), A.f32(512)]
            o_ss = A.f32(8)
            o_hb = A.f32(D // 2)
            o_hT = A.f32(16 * 1024 // 2)
            o_wb = [A.f32(16 * 512 // 2), A.f32(16 * 512 // 2)]
            Gb = A.v(o_G, [[1, D]]); SHb = A.v(o_SH, [[1, D]]); hh = A.v(o_h, [[1, D]])
            hb = A.bf(o_hb, D // 2)
            hT = A.bf(o_hT, 16 * 1024 // 2).rearrange("p (kc t) -> p kc t", kc=16)
            if srcT is None:
                P.dma(lambda e: e.dma_start(out=Gb, in_=bc_rows(normw, layer * D, D)), writes=["Gb"])
                P.dma(lambda e: e.dma_start(out=hh, in_=bc_rows(modv, (layer * 2 + g) * 6 * D + sc_off, D)), reads=["modv"], writes=["hh"])
                P.dma(lambda e: e.dma_start(out=SHb, in_=bc_rows(modv, (layer * 2 + g) * 6 * D + sh_off, D)), reads=["modv"], writes=["SHb"])
                P.op("vector", lambda e: e.tensor_tensor(Gb, Gb, hh, ALU.mult), reads=["Gb", "hh"], writes=["Gb"])
            if resid is not None:
                P.dma(lambda e: e.dma_start(out=SHb, in_=bc_rows(modv, (layer * 2 + g) * 6 * D + resid[1], D)), reads=["modv"], writes=["SHb"])
            nchunks = (N + 511) // 512
            for blk in range(T // 1024):
                if srcT is not None:
                    P.dma(lambda e, blk=blk: e.dma_start(out=hT, in_=srcT.ap()[:, :, blk * 1024:(blk + 1) * 1024].rearrange("kc p t -> p kc t")),
                          reads=[srcT.name], writes=["hT"])
                for tt in range(8 if srcT is None else 0):
                    t0 = blk * 1024 + tt * 128
                    k = tt % 2
                    xt = A.v(o_x[k], [[1, D]])
                    P.dma(lambda e, xt=xt, t0=t0: e.dma_start(out=xt, in_=x_ap[t0:t0 + 128, :]), writes=["x%d" % k])
                    ss = A.v(o_ss + k, [[1, 1]]); rs = A.v(o_ss + 2 + k, [[1, 1]])
                    P.op("scalar", lambda e, xt=xt, ss=ss: e.activation(hh, xt, AF.Square, accum_out=ss),
                         reads=["x%d" % k], writes=["hh", "ss%d" % k])
                    P.op("vector", lambda e, ss=ss, rs=rs: e.tensor_copy(rs, ss), reads=["ss%d" % k], writes=["rs%d" % k])
                    rsqrt_ops(P, rs, 1.0 / D, 1e-6, "rs%d" % k)
                    P.op("vector", lambda e, xt=xt, rs=rs: e.scalar_tensor_tensor(hh, xt, rs, Gb, ALU.mult, ALU.mult),
                         reads=["x%d" % k, "rs%d" % k, "Gb"], writes=["hh"])
                    if hm_out is not None:
                        P.op("gpsimd", lambda e: e.tensor_tensor(hh, hh, SHb, ALU.add), reads=["hh", "SHb"], writes=["hh"])
                        P.dma(lambda e, t0=t0: e.dma_start(out=hm_out.ap()[t0:t0 + 128, :], in_=hh), reads=["hh"], writes=[hm_out.name])
                        P.op("scalar", lambda e: e.activation(hb, hh, AF.Copy), reads=["hh"], writes=["hb"])
                    else:
                        P.op("gpsimd", lambda e: e.tensor_tensor(hb, hh, SHb, ALU.add), reads=["hh", "SHb"], writes=["hb"])
                    for kc in range(16):
                        P.op("tensor", lambda e, kc=kc: e.transpose(psT[:, kc * 128:(kc + 1) * 128], hb[:, kc * 128:(kc + 1) * 128], identb),
                             reads=["hb", "identb"], writes=["psT"])
                    P.op("scalar", lambda e, tt=tt: e.activation(hT[:, :, tt * 128:(tt + 1) * 128],
                                                                 psT[:, :].rearrange("p (kc t) -> p kc t", kc=16), AF.Copy),
                         reads=["psT"], writes=["hT"])
                for ch in range(nchunks):
                    n0 = ch * 512
                    nw = min(512, N - n0)
                    kb = ch % 2
                    wt = A.v(o_w, [[512, 16], [1, nw]])
                    wb = A.bf(o_wb[kb], 16 * 512 // 2).rearrange("p (kc n) -> p kc n", kc=16)
                    P.dma(lambda e, wt=wt, n0=n0, nw=nw: e.dma_start(
                        out=wt, in_=W_ap[:, n0:n0 + nw].rearrange("(kc p) n -> p kc n", p=128)), writes=["wt"])
                    P.op("gpsimd" if ch % 2 else "scalar",
                         (lambda e, wt=wt, wb=wb, nw=nw: e.tensor_copy(wb[:, :, 0:nw], wt)) if ch % 2 else
                         (lambda e, wt=wt, wb=wb, nw=nw: e.activation(wb[:, :, 0:nw], wt, AF.Copy)),
                         reads=["wt"], writes=["wb%d" % kb])
                    for tt in range(8):
                        t0 = blk * 1024 + tt * 128
                        pi = (ch * 8 + tt) % 4
                        pb = psb[pi]
                        for kc in range(16):
                            P.op("tensor", lambda e, kc=kc, tt=tt, pb=pb, wb=wb, nw=nw: e.matmul(
                                pb[:, 0:nw], hT[:, kc, tt * 128:(tt + 1) * 128], wb[:, kc, 0:nw],
                                start=(kc == 0), stop=(kc == 15)),
                                reads=["hT", "wb%d" % kb], writes=["pg%d" % pi])
                        ko = tt % 2
                        ot = A.v(o_ot[ko], [[1, nw]])
                        if resid is None:
                            P.op("vector", lambda e, ot=ot, pb=pb, nw=nw: e.tensor_copy(ot, pb[:, 0:nw]),
                                 reads=["pg%d" % pi], writes=["ot%d" % ko])
                        else:
                            xr = A.v(o_xr[ko], [[1, nw]])
                            P.dma(lambda e, xr=xr, t0=t0, n0=n0, nw=nw: e.dma_start(out=xr, in_=resid[0][t0:t0 + 128, n0:n0 + nw]),
                                  writes=["xr%d" % ko])
                            P.op("vector", lambda e, ot=ot, pb=pb, nw=nw, n0=n0: e.tensor_tensor(ot, pb[:, 0:nw], A.v(o_SH + n0, [[1, nw]]), ALU.mult),
                                 reads=["pg%d" % pi, "SHb"], writes=["ot%d" % ko])
                            P.op("gpsimd", lambda e, ot=ot, xr=xr: e.tensor_tensor(ot, ot, xr, ALU.add),
                                 reads=["ot%d" % ko, "xr%d" % ko], writes=["ot%d" % ko])
                        P.dma(lambda e, ot=ot, t0=t0, n0=n0, nw=nw: e.dma_start(out=out_t.ap()[t0:t0 + 128, n0:n0 + nw], in_=ot),
                              reads=["ot%d" % ko], writes=[out_t.name])
            P.barrier()

        def sq_store(g, nm, t0, tile_ap, rd, wr):
            off = tile_ap.offset
            for hh in range(2):
                src = bass.AP(A.t, off + hh * 64, [[A.size, 128], [128, 8], [1, 64]])
                dst = bass.AP(sq[g][nm], t0 * 1024 + hh * 512, [[1024, 128], [64, 8], [1, 64]])
                P.dma(lambda e, src=src, dst=dst: e.dma_start(out=dst, in_=src), reads=rd, writes=wr)

        def stage_even_prep(g, T, L):
            A.reset()
            z = zg[g]
            o_cw = A.f32(3 * 3456)
            o_vec = A.f32(9 * 1024)
            o_za = A.f32(3456)
            o_ld = [A.f32(3456), A.f32(3456)]
            o_d = [A.f32(1024) for _ in range(4)]
            o_kk = A.f32(1024); o_tmp = A.f32(1024); o_g = A.f32(1024); o_bon = A.f32(1024)
            o_sm = A.f32(64)
            o_lr = A.f32(3 * 128 // 2); o_lrT = A.f32(3 * 128 // 2)
            o_lw = A.f32(3 * 1024 // 2); o_lwf = A.f32(1024)
            o_vt = A.f32(1024)
            CW = A.v(o_cw, [[3456, 3], [1, 3456]])
            P.dma(lambda e: e.dma_start(out=CW, in_=bass.AP(a_conv, 0, [[0, 128], [3456, 3], [1, 3456]])), writes=["CW"])
            vecsrc = [(a_w0, 0), (a_w0, 1024), (a_a0, 0), (a_a0, 1024), (a_kk, 0), (a_ka, 0), (a_rk, 0)]
            for i, (t_, off) in enumerate(vecsrc):
                P.dma(lambda e, i=i, t_=t_, off=off: e.dma_start(out=A.v(o_vec + i * 1024, [[1, 1024]]), in_=bc_rows(t_, off, 1024)),
                      writes=["vec"])
            vec = lambda i: A.v(o_vec + i * 1024, [[1, 1024]])
            lw = A.bf(o_lw, 3 * 1024 // 2).rearrange("p (j n) -> p j n", j=3)
            for j, t_ in enumerate((a_wu, a_au, a_gu)):
                lwf = A.v(o_lwf, [[1, 1024]])
                P.dma(lambda e, t_=t_, lwf=lwf: e.dma_start(out=lwf, in_=t_.ap()), writes=["lwf"])
                P.op("vector", lambda e, j=j, lwf=lwf: e.tensor_copy(lw[:, j, :], lwf), reads=["lwf"], writes=["lw"])
            lr = A.bf(o_lr, 3 * 128 // 2).rearrange("p (j n) -> p j n", j=3)
            lrT = A.bf(o_lrT, 3 * 128 // 2).rearrange("p (j n) -> p j n", j=3)
            ZA = A.v(o_za, [[1, 3456]])
            r_ = A.v(o_za, [[1, 1024]]); k_ = A.v(o_za + 1024, [[1, 1024]]); v_ = A.v(o_za + 2048, [[1, 1024]])
            KK = A.v(o_kk, [[1, 1024]]); TMP = A.v(o_tmp, [[1, 1024]]); G = A.v(o_g, [[1, 1024]]); BON = A.v(o_bon, [[1, 1024]])
            h3 = lambda o: A.v(o, [[64, 16], [1, 64]])
            hb3 = lambda o: A.v(o, [[1, 16], [0, 64]])
            tiles_per_seq = L // 128
            for ti in range(T // 128):
                t0 = ti * 128
                first = (ti % tiles_per_seq == 0)
                last = (ti % tiles_per_seq == tiles_per_seq - 1)
                ld = A.v(o_ld[0], [[1, 3456]])
                P.dma(lambda e, ld=ld, t0=t0: e.dma_start(out=ld, in_=z.ap()[t0:t0 + 128, 0:3456]), reads=[z.name], writes=["ld0"])
                P.op("vector", lambda e, ld=ld: e.tensor_tensor(ZA, ld, A.v(o_cw + 3456, [[1, 3456]]), ALU.mult),
                     reads=["ld0", "CW"], writes=["ZA"])
                ld = A.v(o_ld[1], [[1, 3456]])
                if first:
                    P.op("gpsimd", lambda e, ld=ld: e.memset(ld, 0.0), writes=["ld1"])
                    P.dma(lambda e, t0=t0: e.dma_start(out=A.v(o_ld[1], [[1, 3456]], p0=1, np_=127), in_=z.ap()[t0:t0 + 127, 0:3456]),
                          reads=[z.name], writes=["ld1"])
                else:
                    P.dma(lambda e, ld=ld, t0=t0: e.dma_start(out=ld, in_=z.ap()[t0 - 1:t0 + 127, 0:3456]), reads=[z.name], writes=["ld1"])
                P.op("gpsimd", lambda e, ld=ld: e.tensor_tensor(ld, ld, A.v(o_cw, [[1, 3456]]), ALU.mult), reads=["ld1", "CW"], writes=["ld1"])
                P.op("vector", lambda e, ld=ld: e.tensor_tensor(ZA, ZA, ld, ALU.add), reads=["ld1", "ZA"], writes=["ZA"])
                ld = A.v(o_ld[0], [[1, 3456]])
                if last:
                    P.op("gpsimd", lambda e, ld=ld: e.memset(ld, 0.0), reads=[], writes=["ld0"])
                    P.dma(lambda e, t0=t0: e.dma_start(out=A.v(o_ld[0], [[1, 3456]], p0=0, np_=127), in_=z.ap()[t0 + 1:t0 + 128, 0:3456]),
                          reads=[z.name], writes=["ld0"])
                else:
                    P.dma(lambda e, ld=ld, t0=t0: e.dma_start(out=ld, in_=z.ap()[t0 + 1:t0 + 129, 0:3456]), reads=[z.name], writes=["ld0"])
                P.op("gpsimd", lambda e, ld=ld: e.tensor_tensor(ld, ld, A.v(o_cw + 2 * 3456, [[1, 3456]]), ALU.mult), reads=["ld0", "CW"], writes=["ld0"])
                P.op("vector", lambda e, ld=ld: e.tensor_tensor(ZA, ZA, ld, ALU.add), reads=["ld0", "ZA"], writes=["ZA"])
                sq_store(g, "r", t0, r_, ["ZA"], ["sq_r"])
                P.op("scalar", lambda e: e.activation(lr[:, 0, :], A.v(o_za + 3072, [[1, 128]]), AF.Tanh), reads=["ZA"], writes=["lr"])
                P.op("scalar", lambda e: e.activation(lr[:, 1, :], A.v(o_za + 3200, [[1, 128]]), AF.Copy), reads=["ZA"], writes=["lr"])
                P.op("scalar", lambda e: e.activation(lr[:, 2, :], A.v(o_za + 3328, [[1, 128]]), AF.Sigmoid), reads=["ZA"], writes=["lr"])
                for j in range(3):
                    P.op("tensor", lambda e, j=j: e.transpose(psT[:, j * 128:(j + 1) * 128], lr[:, j, :], identb), reads=["lr", "identb"], writes=["psT"])
                P.op("vector", lambda e: e.tensor_copy(lrT, psT[:, 0:384].rearrange("p (j n) -> p j n", j=3)), reads=["psT"], writes=["lrT"])
                for hf in range(2):
                    P.op("tensor", lambda e, hf=hf: e.matmul(psb[hf][:, :], lrT[:, 2, :], lw[:, 2, hf * 512:(hf + 1) * 512], start=True, stop=True),
                         reads=["lrT", "lw"], writes=["pe%d" % hf])
                    P.op("scalar", lambda e, hf=hf: e.activation(A.v(o_g + hf * 512, [[1, 512]]), psb[hf][:, :], AF.Copy),
                         reads=["pe%d" % hf], writes=["G"])
                P.dma(lambda e, t0=t0: e.dma_start(out=gsc[g].ap()[t0:t0 + 128, :], in_=G), reads=["G"], writes=["gsc"])
                P.op("vector", lambda e: e.tensor_tensor(KK, k_, vec(4), ALU.mult), reads=["ZA", "vec"], writes=["KK"])
                P.op("gpsimd", lambda e: e.tensor_tensor(TMP, KK, KK, ALU.mult), reads=["KK"], writes=["TMP"])
                P.op("vector", lambda e: e.tensor_reduce(A.v(o_sm, [[1, 16]]), h3(o_tmp), AX.X, ALU.add), reads=["TMP"], writes=["sm"])
                rsqrt_ops(P, A.v(o_sm, [[1, 16]]), 1.0, 1e-12, "sm")
                P.op("vector", lambda e: e.tensor_tensor(h3(o_kk), h3(o_kk), hb3(o_sm), ALU.mult), reads=["sm", "KK"], writes=["KK"])
                sq_store(g, "kk", t0, KK, ["KK"], ["sq_kk"])
                P.op("gpsimd", lambda e: e.tensor_tensor(TMP, r_, k_, ALU.mult), reads=["ZA"], writes=["TMP"])
                P.op("gpsimd", lambda e: e.tensor_tensor(TMP, TMP, vec(6), ALU.mult), reads=["TMP", "vec"], writes=["TMP"])
                P.op("vector", lambda e: e.tensor_reduce(A.v(o_sm + 16, [[1, 16]]), h3(o_tmp), AX.X, ALU.add), reads=["TMP"], writes=["sm2"])
                P.op("vector", lambda e: e.tensor_tensor(h3(o_bon), h3(o_za + 2048), hb3(o_sm + 16), ALU.mult), reads=["sm2", "ZA"], writes=["BON"])
                P.dma(lambda e, t0=t0: e.dma_start(out=bon[g].ap()[t0:t0 + 128, :], in_=BON), reads=["BON"], writes=["bon"])
                vt = A.v(o_vt, [[128, 8], [1, 128]])
                for hf in range(2):
                    for j in range(4):
                        c = hf * 4 + j
                        P.op("tensor", lambda e, c=c, hf=hf, j=j: e.transpose(psb[2 + hf][:, j * 128:(j + 1) * 128],
                                                                             A.v(o_za + 2048 + c * 128, [[1, 128]]), ident),
                             reads=["ZA", "ident"], writes=["pv%d" % hf])
                    P.op("scalar", lambda e, hf=hf: e.activation(A.v(o_vt + hf * 512, [[1, 512]]), psb[2 + hf][:, :], AF.Copy),
                         reads=["pv%d" % hf], writes=["vt"])
                P.dma(lambda e, t0=t0, vt=vt: e.dma_start(out=vT[g].ap()[:, :, t0:t0 + 128].rearrange("c p t -> p c t"), in_=vt),
                      reads=["vt"], writes=["vT"])
                for d in range(2):
                    Wd, Ad, KTd, Bd = (A.v(o, [[1, 1024]]) for o in o_d)
                    for hf in range(2):
                        P.op("tensor", lambda e, d=d, hf=hf: e.matmul(psb[hf][:, :], lrT[64 * d:64 * d + 64, 0, :],
                                                                      lw[64 * d:64 * d + 64, 0, hf * 512:(hf + 1) * 512], start=True, stop=True),
                             reads=["lrT", "lw"], writes=["pe%d" % hf])
                        P.op("vector", lambda e, d=d, hf=hf: e.tensor_tensor(A.v(o_d[0] + hf * 512, [[1, 512]]), psb[hf][:, :],
                                                                            A.v(o_vec + d * 1024 + hf * 512, [[1, 512]]), ALU.add),
                             reads=["pe%d" % hf, "vec"], writes=["Wd"])
                    P.op("scalar", lambda e, Wd=Wd: e.activation(Wd, Wd, AF.Sigmoid), reads=["Wd"], writes=["Wd"])
                    P.op("scalar", lambda e, Wd=Wd: e.activation(Wd, Wd, AF.Exp, scale=-math.exp(-0.5)), reads=["Wd"], writes=["Wd"])
                    sq_store(g, "w%d" % d, t0, Wd, ["Wd"], ["sq_w"])
                    for hf in range(2):
                        P.op("tensor", lambda e, d=d, hf=hf: e.matmul(psb[hf][:, :], lrT[64 * d:64 * d + 64, 1, :],
                                                                      lw[64 * d:64 * d + 64, 1, hf * 512:(hf + 1) * 512], start=True, stop=True),
                             reads=["lrT", "lw"], writes=["pe%d" % hf])
                        P.op("vector", lambda e, d=d, hf=hf: e.tensor_tensor(A.v(o_d[1] + hf * 512, [[1, 512]]), psb[hf][:, :],
                                                                            A.v(o_vec + (2 + d) * 1024 + hf * 512, [[1, 512]]), ALU.add),
                             reads=["pe%d" % hf, "vec"], writes=["Ad"])
                    P.op("scalar", lambda e, Ad=Ad: e.activation(Ad, Ad, AF.Sigmoid), reads=["Ad"], writes=["Ad"])
                    P.op("vector", lambda e, Ad=Ad, KTd=KTd: e.scalar_tensor_tensor(KTd, Ad, -1.0, vec(5), ALU.add, ALU.mult),
                         reads=["Ad", "vec"], writes=["KTd"])
                    P.op("vector", lambda e, KTd=KTd: e.scalar_tensor_tensor(KTd, KTd, 1.0, k_, ALU.add, ALU.mult),
                         reads=["KTd", "ZA"], writes=["KTd"])
                    sq_store(g, "kt%d" % d, t0, KTd, ["KTd"], ["sq_kt"])
                    P.op("gpsimd", lambda e, Ad=Ad, Bd=Bd: e.tensor_tensor(Bd, KK, Ad, ALU.mult), reads=["KK", "Ad"], writes=["Bd"])
                    sq_store(g, "b%d" % d, t0, Bd, ["Bd"], ["sq_b"])
            P.barrier()

        def stage_scan(g, seq, L, d, s0_ap=None, sfin_ap=None):
            A.reset()
            if SCAN_DVE_INORDER:
                P.no_self.add("vector")
            SB = 4
            o_S = A.f32(512); o_tmp = A.f32(512); o_sa = A.f32(8)
            o_X = [A.f32(5 * SB * 512), A.f32(5 * SB * 512)]
            VB = 128
            o_V = [A.f32(8 * VB), A.f32(8 * VB)]
            o_Y = [A.f32(8 * VB), A.f32(8 * VB)]
            S = A.v(o_S, [[1, 512]]); TMP = A.v(o_tmp, [[1, 512]])
            S3 = A.v(o_S, [[64, 8], [1, 64]]); TMP3 = A.v(o_tmp, [[64, 8], [1, 64]])
            sa = A.v(o_sa, [[1, 8]]); sab = A.v(o_sa, [[1, 8], [0, 64]])
            base = seq * L
            for hh in range(2):
                dst = A.v(o_S, [[64, 8], [1, 64]], p0=64 * hh, np_=64)
                if s0_ap is not None:
                    src = bass.AP(s0_ap, (d * 16 + hh) * 4096, [[64, 64], [2 * 4096, 8], [1, 64]])
                    P.dma(lambda e, dst=dst, src=src: e.dma_start(out=dst, in_=src), writes=["S"])
                else:
                    src = bass.AP(zero_d, 0, [[64, 64], [0, 8], [1, 64]])
                    P.dma(lambda e, dst=dst, src=src: e.dma_start(out=dst, in_=src), reads=["zero_d"], writes=["S"])
            names = ["kk", "w%d" % d, "b%d" % d, "kt%d" % d, "r"]
            for blk in range(L // SB):
                kx = blk % 2
                if d == 0:
                    tok0 = blk * SB
                else:
                    tok0 = L - (blk + 1) * SB
                for qi, nm in enumerate(names):
                    for hh in range(2):
                        dst = A.v(o_X[kx] + qi * SB * 512, [[512, SB], [1, 512]], p0=64 * hh, np_=64)
                        src = bass.AP(sq[g][nm], (base + tok0) * 1024 + hh * 512, [[0, 64], [1024, SB], [1, 512]])
                        P.dma(lambda e, dst=dst, src=src: e.dma_start(out=dst, in_=src), reads=["sq"], writes=["X%d_%d_%d" % (kx, qi, hh)],
                              q="gpsimd" if (qi % 2) else "sync")
                if (blk * SB) % VB == 0:
                    vb = (blk * SB) // VB
                    kv = vb % 2
                    vtok0 = vb * VB if d == 0 else L - (vb + 1) * VB
                    dst = A.v(o_V[kv], [[VB, 8], [1, VB]])
                    src = bass.AP(vT[g], base + vtok0, [[vT[g].shape[2], 128], [128 * vT[g].shape[2], 8], [1, VB]])
                    P.dma(lambda e, dst=dst, src=src: e.dma_start(out=dst, in_=src), reads=["vT"], writes=["V%d" % kv])
                for j in range(SB):
                    step = blk * SB + j
                    jj = j if d == 0 else SB - 1 - j
                    vb = step // VB
                    kv = vb % 2
                    sv = step % VB
                    vcol = sv if d == 0 else VB - 1 - sv
                    X = (lambda xs: (lambda qi: xs[qi]))([A.v(o_X[kx] + qi * SB * 512 + jj * 512, [[64, 8], [1, 64]]) for qi in range(5)])
                    vbc = A.v(o_V[kv] + vcol, [[VB, 8], [0, 64]])
                    ycol = A.v(o_Y[kv] + vcol, [[VB, 8]])
                    rk = lambda qi: ["X%d_%d_0" % (kx, qi), "X%d_%d_1" % (kx, qi)]
                    P.op("vector", lambda e, X=X: e.tensor_tensor(TMP3, S3, X(0), ALU.mult), reads=["S"] + rk(0), writes=["TMP"])
                    P.op("vector", lambda e: e.tensor_reduce(sa, TMP3, AX.X, ALU.add), reads=["TMP"], writes=["sa"])
                    P.op("vector", lambda e, X=X: e.tensor_tensor(S3, S3, X(1), ALU.mult), reads=["S"] + rk(1), writes=["S"])
                    P.op("vector", lambda e, X=X: e.tensor_tensor(TMP3, X(2), sab, ALU.mult), reads=["sa"] + rk(2), writes=["TMP"])
                    P.op("vector", lambda e: e.tensor_tensor(S3, S3, TMP3, ALU.subtract), reads=["S", "TMP"], writes=["S"])
                    P.op("vector", lambda e, X=X, vbc=vbc: e.tensor_tensor(TMP3, X(3), vbc, ALU.mult), reads=["V%d" % kv] + rk(3), writes=["TMP"])
                    P.op("vector", lambda e: e.tensor_tensor(S3, S3, TMP3, ALU.add), reads=["S", "TMP"], writes=["S"])
                    P.op("vector", lambda e, X=X: e.tensor_tensor(TMP3, S3, X(4), ALU.mult), reads=["S"] + rk(4), writes=["TMP"])
                    P.op("vector", lambda e, ycol=ycol: e.tensor_reduce(ycol, TMP3, AX.X, ALU.add), reads=["TMP"], writes=["Y%d" % kv])
                    if sv == VB - 1:
                        ytok0 = vb * VB if d == 0 else L - (vb + 1) * VB
                        srcy = A.v(o_Y[kv], [[VB, 8], [1, VB]])
                        dsty = bass.AP(yT[g][d], base + ytok0, [[yT[g][d].shape[2], 128], [128 * yT[g][d].shape[2], 8], [1, VB]])
                        P.dma(lambda e, srcy=srcy, dsty=dsty: e.dma_start(out=dsty, in_=srcy), reads=["Y%d" % kv], writes=["yT"])
            if sfin_ap is not None:
                for hh in range(2):
                    srcS = A.v(o_S, [[64, 8], [1, 64]], p0=64 * hh, np_=64)
                    dstS = bass.AP(sfin_ap, ((seq * 2 + d) * 16 + hh) * 4096, [[64, 64], [2 * 4096, 8], [1, 64]])
                    P.dma(lambda e, srcS=srcS, dstS=dstS: e.dma_start(out=dstS, in_=srcS), reads=["S"], writes=["ns"], is_output=True)
            P.no_self.discard("vector")
            P.barrier()

        def stage_rwkv_post(g, T):
            A.reset()
            o_y0 = A.f32(1024); o_y1 = A.f32(1024); o_Y = A.f32(1024); o_YC = A.f32(1024); o_SQ = A.f32(1024)
            o_bon = A.f32(1024); o_g = A.f32(1024); o_lnw = A.f32(1024); o_lnb = A.f32(1024); o_sm = A.f32(64)
            o_yb = A.f32(512); o_yT = A.f32(512)
            LNW = A.v(o_lnw, [[1, 1024]]); LNB = A.v(o_lnb, [[1, 1024]])
            P.dma(lambda e: e.dma_start(out=LNW, in_=bc_rows(a_lnw, 0, 1024)), writes=["LNW"])
            P.dma(lambda e: e.dma_start(out=LNB, in_=bc_rows(a_lnb, 0, 1024)), writes=["LNB"])
            Y = A.v(o_Y, [[1, 1024]]); YC = A.v(o_YC, [[1, 1024]]); SQ = A.v(o_SQ, [[1, 1024]])
            BON = A.v(o_bon, [[1, 1024]]); G = A.v(o_g, [[1, 1024]])
            h3 = lambda o: A.v(o, [[64, 16], [1, 64]])
            hb3 = lambda o: A.v(o, [[1, 16], [0, 64]])
            yb = A.bf(o_yb, 512)
            yT8 = A.bf(o_yT, 512).rearrange("p (c t) -> p c t", c=8)
            for ti in range(T // 128):
                t0 = ti * 128
                y0 = A.v(o_y0, [[128, 8], [1, 128]]); y1 = A.v(o_y1, [[128, 8], [1, 128]])
                for d, yy in ((0, y0), (1, y1)):
                    src = bass.AP(yT[g][d], t0, [[T, 128], [128 * T, 8], [1, 128]])
                    P.dma(lambda e, yy=yy, src=src: e.dma_start(out=yy, in_=src), reads=["yT"], writes=["y%d" % d])
                P.op("vector", lambda e: e.tensor_tensor(A.v(o_y0, [[1, 1024]]), A.v(o_y0, [[1, 1024]]), A.v(o_y1, [[1, 1024]]), ALU.add),
                     reads=["y0", "y1"], writes=["y0"])
                for c in range(8):
                    P.op("tensor", lambda e, c=c: e.transpose(psb[c // 4][:, (c % 4) * 128:(c % 4 + 1) * 128], A.v(o_y0 + c * 128, [[1, 128]]), ident),
                         reads=["y0", "ident"], writes=["pp%d" % (c // 4)])
                for hf in range(2):
                    P.op("scalar", lambda e, hf=hf: e.activation(A.v(o_Y + hf * 512, [[1, 512]]), psb[hf][:, :], AF.Copy), reads=["pp%d" % hf], writes=["Y"])
                P.op("vector", lambda e: e.tensor_reduce(A.v(o_sm, [[1, 16]]), h3(o_Y), AX.X, ALU.add), reads=["Y"], writes=["mu"])
                P.op("vector", lambda e: e.tensor_scalar_mul(A.v(o_sm, [[1, 16]]), A.v(o_sm, [[1, 16]]), 1.0 / 64), reads=["mu"], writes=["mu"])
                P.op("vector", lambda e: e.tensor_tensor(h3(o_YC), h3(o_Y), hb3(o_sm), ALU.subtract), reads=["Y", "mu"], writes=["YC"])
                P.op("gpsimd", lambda e: e.tensor_tensor(SQ, YC, YC, ALU.mult), reads=["YC"], writes=["SQ"])
                P.op("vector", lambda e: e.tensor_reduce(A.v(o_sm + 16, [[1, 16]]), h3(o_SQ), AX.X, ALU.add), reads=["SQ"], writes=["var"])
                rsqrt_ops(P, A.v(o_sm + 16, [[1, 16]]), 1.0 / 64, 64e-5, "var")
                P.op("vector", lambda e: e.tensor_tensor(h3(o_YC), h3(o_YC), hb3(o_sm + 16), ALU.mult), reads=["YC", "var"], writes=["YC"])
                P.op("gpsimd", lambda e: e.tensor_tensor(YC, YC, LNW, ALU.mult), reads=["YC", "LNW"], writes=["YC"])
                P.op("vector", lambda e: e.tensor_tensor(YC, YC, LNB, ALU.add), reads=["YC", "LNB"], writes=["YC"])
                P.dma(lambda e, t0=t0: e.dma_start(out=BON, in_=bon[g].ap()[t0:t0 + 128, :]), reads=["bon"], writes=["BON"])
                P.dma(lambda e, t0=t0: e.dma_start(out=G, in_=gsc[g].ap()[t0:t0 + 128, :]), reads=["gsc"], writes=["G"])
                P.op("gpsimd", lambda e: e.tensor_tensor(YC, YC, BON, ALU.add), reads=["YC", "BON"], writes=["YC"])
                P.op("vector", lambda e: e.tensor_tensor(yb, YC, G, ALU.mult), reads=["YC", "G"], writes=["yb"])
                for c in range(8):
                    P.op("tensor", lambda e, c=c: e.transpose(psT[:, c * 128:(c + 1) * 128], yb[:, c * 128:(c + 1) * 128], identb),
                         reads=["yb", "identb"], writes=["psT"])
                P.op("scalar", lambda e: e.activation(yT8, psT[:, 0:1024].rearrange("p (c t) -> p c t", c=8), AF.Copy), reads=["psT"], writes=["yT8"])
                P.dma(lambda e, t0=t0: e.dma_start(out=ycatT[g].ap()[0:8, :, t0:t0 + 128].rearrange("c p t -> p c t"), in_=yT8),
                      reads=["yT8"], writes=[ycatT[g].name])
            P.barrier()

        def stage_attn_prep(g, T, L, latent):
            A.reset()
            o_z = A.f32(1536); o_sq = A.f32(1280); o_qr = A.f32(1280); o_sm = A.f32(16)
            o_qn = A.f32(128); o_kn = A.f32(128); o_cs = A.f32(128); o_t = [A.f32(640) for _ in range(4)]
            o_qb = A.f32(640); o_qT = A.f32(640); o_vb = A.f32(128)
            QN = A.v(o_qn, [[1, 128]]); KN = A.v(o_kn, [[1, 128]])
            P.dma(lambda e: e.dma_start(out=QN, in_=bc_rows(b_qn, 0, 128)), writes=["QN"])
            P.dma(lambda e: e.dma_start(out=KN, in_=bc_rows(b_kn, 0, 128)), writes=["KN"])
            qb = A.bf(o_qb, 640)
            qT = A.bf(o_qT, 640).rearrange("p (h t) -> p h t", h=10)
            vb = A.bf(o_vb, 128)
            Lk = kT_d[g].shape[2] // (T // L)
            ntile = T // 128 + (4 if latent else 0)
            for ti in range(ntile):
                t0 = ti * 128
                cache = ti >= T // 128
                seq = t0 // L
                tin = t0 % L
                if not cache:
                    P.dma(lambda e, t0=t0: e.dma_start(out=A.v(o_z, [[1, 1536]]), in_=zg[g].ap()[t0:t0 + 128, 3456:4992]), reads=[zg[g].name], writes=["zq"])
                    P.op("gpsimd", lambda e: e.tensor_tensor(A.v(o_sq, [[1, 1280]]), A.v(o_z, [[1, 1280]]), A.v(o_z, [[1, 1280]]), ALU.mult), reads=["zq"], writes=["sqq"])
                    P.op("vector", lambda e: e.tensor_reduce(A.v(o_sm, [[1, 10]]), A.v(o_sq, [[128, 10], [1, 128]]), AX.X, ALU.add), reads=["sqq"], writes=["sm"])
                    rsqrt_ops(P, A.v(o_sm, [[1, 10]]), 1.0 / 128, 1e-6, "sm")
                    P.op("vector", lambda e: e.tensor_tensor(A.v(o_z, [[128, 10], [1, 128]]), A.v(o_z, [[128, 10], [1, 128]]), A.v(o_sm, [[1, 10], [0, 128]]), ALU.mult),
                         reads=["zq", "sm"], writes=["zq"])
                    P.op("vector", lambda e: e.tensor_tensor(A.v(o_z, [[128, 8], [1, 128]]), A.v(o_z, [[128, 8], [1, 128]]), A.v(o_qn, [[0, 8], [1, 128]]), ALU.mult),
                         reads=["zq", "QN"], writes=["zq"])
                    P.op("vector", lambda e: e.tensor_tensor(A.v(o_z + 1024, [[128, 2], [1, 128]]), A.v(o_z + 1024, [[128, 2], [1, 128]]), A.v(o_kn, [[0, 2], [1, 128]]), ALU.mult),
                         reads=["zq", "KN"], writes=["zq"])
                    if latent:
                        P.dma(lambda e, tin=tin: e.dma_start(out=A.v(o_cs, [[1, 64]]), in_=ropec.ap()[tin:tin + 128, :]), writes=["cs"])
                        P.dma(lambda e, tin=tin: e.dma_start(out=A.v(o_cs + 64, [[1, 64]]), in_=ropes.ap()[tin:tin + 128, :]), writes=["cs"])
                        X1 = A.v(o_z, [[128, 10], [64, 2], [1, 32]]); X2 = A.v(o_z + 32, [[128, 10], [64, 2], [1, 32]])
                        O1 = A.v(o_qr, [[128, 10], [64, 2], [1, 32]]); O2 = A.v(o_qr + 32, [[128, 10], [64, 2], [1, 32]])
                        Cb = A.v(o_cs, [[0, 10], [32, 2], [1, 32]]); Sb = A.v(o_cs + 64, [[0, 10], [32, 2], [1, 32]])
                        Tt = [A.v(o, [[64, 10], [32, 2], [1, 32]]) for o in o_t]
                        P.op("vector", lambda e: e.tensor_tensor(Tt[0], X1, Cb, ALU.mult), reads=["zq", "cs"], writes=["t0"])
                        P.op("gpsimd", lambda e: e.tensor_tensor(Tt[1], X2, Sb, ALU.mult), reads=["zq", "cs"], writes=["t1"])
                        P.op("vector", lambda e: e.tensor_tensor(Tt[2], X1, Sb, ALU.mult), reads=["zq", "cs"], writes=["t2"])
                        P.op("gpsimd", lambda e: e.tensor_tensor(Tt[3], X2, Cb, ALU.mult), reads=["zq", "cs"], writes=["t3"])
                        P.op("vector", lambda e: e.tensor_tensor(O1, Tt[0], Tt[1], ALU.subtract), reads=["t0", "t1"], writes=["qr"])
                        P.op("vector", lambda e: e.tensor_tensor(O2, Tt[2], Tt[3], ALU.add), reads=["t2", "t3"], writes=["qr"])
                        P.op("scalar", lambda e: e.activation(qb, A.v(o_qr, [[1, 1280]]), AF.Copy), reads=["qr"], writes=["qb"])
                    else:
                        P.op("scalar", lambda e: e.activation(qb, A.v(o_z, [[1, 1280]]), AF.Copy), reads=["zq"], writes=["qb"])
                    P.op("gpsimd", lambda e: e.tensor_copy(vb, A.v(o_z + 1280, [[1, 256]])), reads=["zq"], writes=["vb"])
                    h0 = 0
                    krow = seq * Lk + tin
                else:
                    c0 = (ti - T // 128) * 128
                    P.dma(lambda e, c0=c0: e.dma_start(out=A.v(o_z + 1024, [[1, 256]]), in_=ck.ap()[c0:c0 + 128, :]), writes=["zq"])
                    P.dma(lambda e, c0=c0: e.dma_start(out=A.v(o_z + 1280, [[1, 256]]), in_=cv.ap()[c0:c0 + 128, :]), writes=["zq"])
                    P.op("scalar", lambda e: e.activation(qb[:, 1024:1280], A.v(o_z + 1024, [[1, 256]]), AF.Copy), reads=["zq"], writes=["qb"])
                    P.op("gpsimd", lambda e: e.tensor_copy(vb, A.v(o_z + 1280, [[1, 256]])), reads=["zq"], writes=["vb"])
                    h0 = 8
                    krow = L + c0
                for h in range(h0, 10):
                    P.op("tensor", lambda e, h=h: e.transpose(psT[:, h * 128:(h + 1) * 128], qb[:, h * 128:(h + 1) * 128], identb),
                         reads=["qb", "identb"], writes=["psT"])
                P.op("vector", lambda e, h0=h0: e.tensor_copy(qT[:, h0:10, :], psT[:, h0 * 128:1280].rearrange("p (h t) -> p h t", h=10 - h0)),
                     reads=["psT"], writes=["qT"])
                if not cache:
                    P.dma(lambda e, t0=t0: e.dma_start(out=qT_d[g].ap()[:, :, t0:t0 + 128].rearrange("h p t -> p h t"), in_=qT[:, 0:8, :]),
                          reads=["qT"], writes=["qT_d"])
                P.dma(lambda e, krow=krow: e.dma_start(out=kT_d[g].ap()[:, :, krow:krow + 128].rearrange("h p t -> p h t"), in_=qT[:, 8:10, :]),
                      reads=["qT"], writes=["kT_d"])
                P.dma(lambda e, krow=krow: e.dma_start(out=V_d[g].ap()[krow:krow + 128, :], in_=vb), reads=["vb"], writes=["V_d"])
            P.barrier()

        def stage_attn(g, T, seq, L, Lk):
            A.reset()
            nkt = Lk // 128
            NQ = min(512, L)
            o_kT = A.f32(2 * Lk // 2); o_V = A.f32(nkt * 256 // 2)
            o_q = [A.f32(NQ // 2), A.f32(NQ // 2)]
            o_E = [A.f32(NQ // 2), A.f32(NQ // 2)]
            o_rec = A.f32(NQ); o_ob = [A.f32(NQ // 2), A.f32(NQ // 2)]
            o_one = A.f32(64)
            KT = A.bf(o_kT, 2 * Lk // 2).rearrange("p (h t) -> p h t", h=2)
            VV = A.bf(o_V, nkt * 256 // 2).rearrange("p (k n) -> p k n", k=nkt)
            ones = A.bf(o_one, 64)
            P.op("vector", lambda e: e.memset(ones, 1.0), writes=["ones"])
            P.dma(lambda e: e.dma_start(out=KT, in_=kT_d[g].ap()[:, :, seq * Lk:(seq + 1) * Lk].rearrange("h p t -> p h t")),
                  reads=["kT_d"], writes=["KT"])
            P.dma(lambda e: e.dma_start(out=VV, in_=V_d[g].ap()[seq * Lk:(seq + 1) * Lk, :].rearrange("(k p) n -> p k n", p=128)),
                  reads=["V_d"], writes=["VV"])
            scale = 128 ** -0.5
            it = 0
            for h in range(8):
                kv = h // 4
                for qb_ in range(L // NQ):
                    q0 = seq * L + qb_ * NQ
                    kq = (h * (L // NQ) + qb_) % 2
                    qt = A.bf(o_q[kq], NQ // 2)
                    P.dma(lambda e, qt=qt, h=h, q0=q0: e.dma_start(out=qt, in_=qT_d[g].ap()[h, :, q0:q0 + NQ]), reads=["qT_d"], writes=["q%d" % kq])
                    for kt in range(nkt):
                        ke = it % 2
                        it += 1
                        Et = A.bf(o_E[ke], NQ // 2)
                        P.op("tensor", lambda e, kt=kt, kv=kv, qt=qt, ke=ke: e.matmul(psb[ke][:, 0:NQ], KT[:, kv, kt * 128:(kt + 1) * 128], qt, start=True, stop=True),
                             reads=["KT", "q%d" % kq], writes=["pS%d" % ke])
                        P.op("scalar", lambda e, Et=Et, ke=ke: e.activation(Et, psb[ke][:, 0:NQ], AF.Exp, scale=scale),
                             reads=["pS%d" % ke], writes=["E%d" % ke])
                        P.op("tensor", lambda e, kt=kt, kv=kv, Et=Et: e.matmul(psb[2][:, 0:NQ], VV[:, kt, kv * 128:(kv + 1) * 128], Et,
                                                                               start=(kt == 0), stop=(kt == nkt - 1)),
                             reads=["VV", "E%d" % ke], writes=["pO"])
                        P.op("tensor", lambda e, kt=kt, Et=Et: e.matmul(psb[3][:, 0:NQ], ones, Et, start=(kt == 0), stop=(kt == nkt - 1)),
                             reads=["ones", "E%d" % ke], writes=["pD"])
                    rec = A.v(o_rec, [[1, NQ]])
                    ob = A.bf(o_ob[kq], NQ // 2)
                    P.op("vector", lambda e, rec=rec: e.reciprocal(rec, psb[3][:, 0:NQ]), reads=["pD"], writes=["rec"])
                    P.op("vector", lambda e, rec=rec, ob=ob: e.tensor_tensor(ob, psb[2][:, 0:NQ], rec, ALU.mult), reads=["pO", "rec"], writes=["ob%d" % kq])
                    P.dma(lambda e, ob=ob, h=h, q0=q0: e.dma_start(out=ycatT[g].ap()[8 + h, :, q0:q0 + NQ], in_=ob),
                          reads=["ob%d" % kq], writes=[ycatT[g].name])
            P.barrier()

        def stage_peer(g, T, layer, q_t, hm_t, xres_ap, out_ap, out_key, is_output=False):
            A.reset()
            NB = 8
            o_kn = A.f32(2048); o_kT = A.f32(2048); o_q = A.f32(2048); o_qT = A.f32(2048); o_S = A.f32(2048)
            o_X = A.f32(2048); o_ACC = A.f32(2048); o_jk = A.f32(2048); o_GT = A.f32(2048); o_xr = A.f32(2048)
            o_R = [A.f32(1024) for _ in range(NB)]
            o_V1 = A.f32(256); o_IDX = A.f32(256); o_IDF = A.f32(256); o_CAND = A.f32(256); o_EID = A.f32(256); o_CW = A.f32(256)
            o_W = A.f32(128); o_T = A.f32(128); o_E = A.f32(128); o_EI = A.f32(128); o_GATE = A.f32(128); o_DOT = A.f32(128); o_WG = A.f32(128)
            o_sm = A.f32(16)
            o_MK = A.f32(256); o_FF = A.f32(256); o_CD = A.f32(256)
            u32v = lambda off, dims: bass.AP(A.t, off, [[A.size, 128]] + [list(d_) for d_ in dims]).bitcast(U32)
            P.dma(lambda e: e.dma_start(out=u32v(o_MK, [[1, 256]]), in_=kconst.ap()[0]), writes=["MK"])
            P.dma(lambda e: e.dma_start(out=u32v(o_FF, [[1, 256]]), in_=kconst.ap()[1]), writes=["FF"])
            P.dma(lambda e: e.dma_start(out=u32v(o_CD, [[1, 256]]), in_=kconst.ap()[2]), writes=["CD"])
            ut = peer_ub[layer]; vt_ = peer_vb[layer]
            GT = A.v(o_GT, [[1, 2048]])
            P.dma(lambda e: e.dma_start(out=GT, in_=bc_rows(modv, (layer * 2 + g) * 6 * D + 5 * D, D)), reads=["modv"], writes=["GT"])
            P.dma(lambda e: e.dma_start(out=A.v(o_kn, [[128, 16], [1, 128]]), in_=peer_keys.ap()[layer].rearrange("h c n d -> n (h c) d")), writes=["kn"])
            for hc in range(16):
                P.op("tensor", lambda e, hc=hc: e.transpose(psb[hc // 4][:, (hc % 4) * 128:(hc % 4 + 1) * 128], A.v(o_kn + hc * 128, [[1, 128]]), ident),
                     reads=["kn", "ident"], writes=["pk%d" % (hc // 4)])
            for b4 in range(4):
                P.op("scalar", lambda e, b4=b4: e.activation(A.v(o_kT + b4 * 512, [[1, 512]]), psb[b4][:, :], AF.Copy), reads=["pk%d" % b4], writes=["kT"])
            EIu = bass.AP(A.t, o_EI, [[A.size, 128], [1, 128]]).bitcast(I32)
            IDXu = bass.AP(A.t, o_IDX, [[A.size, 128], [1, 256]]).bitcast(U32)
            for ti in range(T // 128):
                t0 = ti * 128
                P.dma(lambda e, t0=t0: e.dma_start(out=A.v(o_q, [[1, 2048]]), in_=q_t.ap()[t0:t0 + 128, :]), reads=[q_t.name], writes=["q"])
                P.dma(lambda e, t0=t0: e.dma_start(out=A.v(o_X, [[1, 2048]]), in_=hm_t.ap()[t0:t0 + 128, :]), reads=[hm_t.name], writes=["X"])
                P.dma(lambda e, t0=t0: e.dma_start(out=A.v(o_xr, [[1, 2048]]), in_=xres_ap[t0:t0 + 128, :]), writes=["xr"])
                for hc in range(16):
                    P.op("tensor", lambda e, hc=hc: e.transpose(psb[hc // 4][:, (hc % 4) * 128:(hc % 4 + 1) * 128], A.v(o_q + hc * 128, [[1, 128]]), ident),
                         reads=["q", "ident"], writes=["pk%d" % (hc // 4)])
                for b4 in range(4):
                    P.op("scalar", lambda e, b4=b4: e.activation(A.v(o_qT + b4 * 512, [[1, 512]]), psb[b4][:, :], AF.Copy), reads=["pk%d" % b4], writes=["qT"])
                for hc in range(16):
                    P.op("tensor", lambda e, hc=hc: e.matmul(psb[hc // 4][:, (hc % 4) * 128:(hc % 4 + 1) * 128], A.v(o_qT + hc * 128, [[1, 128]]),
                                                             A.v(o_kT + hc * 128, [[1, 128]]), start=True, stop=True),
                         reads=["qT", "kT"], writes=["pk%d" % (hc // 4)])
                for b4 in range(4):
                    P.op("scalar", lambda e, b4=b4: e.activation(A.v(o_S + b4 * 512, [[1, 512]]), psb[b4][:, :], AF.Copy), reads=["pk%d" % b4], writes=["S"])
                P.op("vector", lambda e: e.tensor_scalar_add(A.v(o_S, [[1, 2048]]), A.v(o_S, [[1, 2048]]), 64.0), reads=["S"], writes=["S"])
                P.op("vector", lambda e: e.tensor_tensor(u32v(o_S, [[128, 16], [1, 128]]), u32v(o_S, [[128, 16], [1, 128]]), u32v(o_MK, [[0, 16], [1, 128]]), ALU.bitwise_and),
                     reads=["S", "MK"], writes=["S"])
                P.op("vector", lambda e: e.tensor_tensor(u32v(o_S, [[128, 16], [1, 128]]), u32v(o_S, [[128, 16], [1, 128]]), u32v(o_CD, [[0, 16], [1, 128]]), ALU.bitwise_or),
                     reads=["S", "CD"], writes=["S"])
                Wk = A.v(o_W, [[1, 128]])
                for hc in range(16):
                    Sh = A.v(o_S + hc * 128, [[1, 128]])
                    va = A.v(o_V1 + hc * 16, [[1, 8]]); vb_ = A.v(o_V1 + hc * 16 + 8, [[1, 8]])
                    P.op("vector", lambda e, Sh=Sh, va=va: e.max(out=va, in_=Sh), reads=["S"], writes=["V1"])
                    P.op("vector", lambda e, Sh=Sh, va=va: e.match_replace(out=Wk, in_to_replace=va, in_values=Sh, imm_value=-1e30), reads=["S", "V1"], writes=["Wk"])
                    P.op("vector", lambda e, vb_=vb_: e.max(out=vb_, in_=Wk), reads=["Wk"], writes=["V1"])
                P.op("vector", lambda e: e.tensor_tensor(u32v(o_IDX, [[1, 256]]), u32v(o_V1, [[1, 256]]), u32v(o_FF, [[1, 256]]), ALU.bitwise_and), reads=["V1", "FF"], writes=["IDX"])
                P.op("vector", lambda e: e.tensor_copy(A.v(o_IDF, [[1, 256]]), u32v(o_IDX, [[1, 256]])), reads=["IDX"], writes=["IDF"])
                P.op("vector", lambda e: e.tensor_scalar(A.v(o_IDF, [[1, 256]]), A.v(o_IDF, [[1, 256]]), -1.0, 255.0, ALU.mult, ALU.add), reads=["IDF"], writes=["IDF"])
                P.op("vector", lambda e: e.memset(A.v(o_E, [[1, 128]]), 0.0), writes=["E"])
                CAND = A.v(o_CAND, [[1, 256]]); EID = A.v(o_EID, [[1, 256]]); CW = A.v(o_CW, [[1, 256]])
                for h in range(8):
                    c3 = A.v(o_CAND, [[16, 16], [1, 16]]); e3 = A.v(o_EID, [[16, 16], [1, 16]])
                    v1a = A.v(o_V1 + (2 * h) * 16, [[1, 16], [0, 16]]); v2b = A.v(o_V1 + (2 * h + 1) * 16, [[0, 16], [1, 16]])
                    i1a = A.v(o_IDF + (2 * h) * 16, [[1, 16], [0, 16]]); i2b = A.v(o_IDF + (2 * h + 1) * 16, [[0, 16], [1, 16]])
                    P.op("vector", lambda e, c3=c3, v1a=v1a, v2b=v2b: e.tensor_tensor(c3, v1a, v2b, ALU.add), reads=["V1"], writes=["CAND"])
                    P.op("vector", lambda e: e.tensor_tensor(u32v(o_CAND, [[1, 256]]), u32v(o_CAND, [[1, 256]]), u32v(o_MK, [[1, 256]]), ALU.bitwise_and), reads=["CAND", "MK"], writes=["CAND"])
                    P.op("vector", lambda e: e.tensor_tensor(u32v(o_CAND, [[1, 256]]), u32v(o_CAND, [[1, 256]]), u32v(o_CD, [[1, 256]]), ALU.bitwise_or), reads=["CAND", "CD"], writes=["CAND"])
                    P.op("vector", lambda e, e3=e3, i1a=i1a, i2b=i2b: e.scalar_tensor_tensor(e3, i1a, 128.0, i2b, ALU.mult, ALU.add), reads=["IDF"], writes=["EID"])
                    ta = A.v(o_T + h * 16, [[1, 8]]); tb = A.v(o_T + h * 16 + 8, [[1, 8]])
                    P.op("vector", lambda e, ta=ta: e.max(out=ta, in_=CAND), reads=["CAND"], writes=["T"])
                    P.op("vector", lambda e, ta=ta: e.match_replace(out=CW, in_to_replace=ta, in_values=CAND, imm_value=-1e30), reads=["CAND", "T"], writes=["CW"])
                    P.op("vector", lambda e, tb=tb: e.max(out=tb, in_=CW), reads=["CW"], writes=["T"])
                    for k in range(16):
                        P.op("vector", lambda e, h=h, k=k: e.scalar_tensor_tensor(CW, CAND, A.v(o_T + h * 16 + k, [[1, 1]]), EID, ALU.is_equal, ALU.mult,
                                                                                   accum_out=A.v(o_E + h * 16 + k, [[1, 1]])),
                             reads=["CAND", "T", "EID", "E"], writes=["CW", "E"])
                P.op("vector", lambda e: e.tensor_copy(EIu, A.v(o_E, [[1, 128]])), reads=["E"], writes=["EI"])
                P.op("vector", lambda e: e.tensor_tensor(A.v(o_GATE, [[16, 8], [1, 16]]), A.v(o_T, [[16, 8], [1, 16]]), A.v(o_T, [[16, 8], [0, 16]]), ALU.subtract),
                     reads=["T"], writes=["GATE"])
                P.op("scalar", lambda e: e.activation(A.v(o_GATE, [[1, 128]]), A.v(o_GATE, [[1, 128]]), AF.Exp), reads=["GATE"], writes=["GATE"])
                P.op("vector", lambda e: e.tensor_reduce(A.v(o_sm, [[1, 8]]), A.v(o_GATE, [[16, 8], [1, 16]]), AX.X, ALU.add), reads=["GATE"], writes=["sm"])
                P.op("vector", lambda e: e.reciprocal(A.v(o_sm, [[1, 8]]), A.v(o_sm, [[1, 8]])), reads=["sm"], writes=["sm"])
                P.op("vector", lambda e: e.tensor_tensor(A.v(o_GATE, [[16, 8], [1, 16]]), A.v(o_GATE, [[16, 8], [1, 16]]), A.v(o_sm, [[1, 8], [0, 16]]), ALU.mult),
                     reads=["GATE", "sm"], writes=["GATE"])
                P.op("vector", lambda e: e.memset(A.v(o_DOT, [[1, 128]]), 0.0), writes=["DOT"])
                gi = 0
                for hk in range(128):
                    kb = gi % NB; gi += 1
                    Rr = A.bf(o_R[kb], 1024)
                    P.dma(lambda e, Rr=Rr, hk=hk: e.indirect_dma_start(out=Rr, out_offset=None, in_=ut.ap(),
                                                                     in_offset=bass.IndirectOffsetOnAxis(ap=EIu[:, hk:hk + 1], axis=0)),
                          reads=["EI"], writes=["R%d" % kb], q="gpsimd")
                    P.op("vector", lambda e, Rr=Rr, hk=hk: e.scalar_tensor_tensor(A.v(o_jk, [[1, 2048]]), Rr, 1.0, A.v(o_X, [[1, 2048]]), ALU.mult, ALU.mult,
                                                                                 accum_out=A.v(o_DOT + hk, [[1, 1]])),
                         reads=["R%d" % kb, "X", "DOT"], writes=["jk", "DOT"], inorder=(hk > 0))
                P.op("scalar", lambda e: e.activation(A.v(o_WG, [[1, 128]]), A.v(o_DOT, [[1, 128]]), AF.Gelu), reads=["DOT"], writes=["WG"])
                P.op("vector", lambda e: e.tensor_tensor(A.v(o_WG, [[1, 128]]), A.v(o_WG, [[1, 128]]), A.v(o_GATE, [[1, 128]]), ALU.mult), reads=["WG", "GATE"], writes=["WG"])
                P.op("gpsimd", lambda e: e.memset(A.v(o_ACC, [[1, 2048]]), 0.0), writes=["ACC"])
                for hk in range(128):
                    kb = gi % NB; gi += 1
                    Rr = A.bf(o_R[kb], 1024)
                    P.dma(lambda e, Rr=Rr, hk=hk: e.indirect_dma_start(out=Rr, out_offset=None, in_=vt_.ap(),
                                                                     in_offset=bass.IndirectOffsetOnAxis(ap=EIu[:, hk:hk + 1], axis=0)),
                          reads=["EI"], writes=["R%d" % kb], q="gpsimd")
                    P.op("vector", lambda e, Rr=Rr, hk=hk: e.scalar_tensor_tensor(A.v(o_ACC, [[1, 2048]]), Rr, A.v(o_WG + hk, [[1, 1]]), A.v(o_ACC, [[1, 2048]]),
                                                                                 ALU.mult, ALU.add),
                         reads=["R%d" % kb, "WG", "ACC"], writes=["ACC"], inorder=(hk > 0))
                P.op("vector", lambda e: e.tensor_tensor(A.v(o_ACC, [[1, 2048]]), A.v(o_ACC, [[1, 2048]]), GT, ALU.mult), reads=["ACC", "GT"], writes=["ACC"])
                P.op("gpsimd", lambda e: e.tensor_tensor(A.v(o_ACC, [[1, 2048]]), A.v(o_ACC, [[1, 2048]]), A.v(o_xr, [[1, 2048]]), ALU.add), reads=["ACC", "xr"], writes=["ACC"])
                P.dma(lambda e, t0=t0: e.dma_start(out=out_ap[t0:t0 + 128, :], in_=A.v(o_ACC, [[1, 2048]])), reads=["ACC"], writes=[out_key], is_output=is_output)
            P.no_self.discard("vector")
            P.barrier()

        def stage_hy_conv3(g, T, L):
            A.reset()
            o_cw = A.f32(3 * 2048); o_cb = A.f32(2048); o_acc = A.f32(2048)
            o_ld = [A.f32(2048), A.f32(2048)]
            o_vb = A.f32(1024)
            vb = A.bf(o_vb, 1024)
            tps = L // 128
            for j in range(3):
                P.dma(lambda e, j=j: e.dma_start(out=A.v(o_cw, [[2048, 3], [1, 2048]]),
                                                 in_=bass.AP(c_conv, j * 2048, [[0, 128], [6144, 3], [1, 2048]])), writes=["CW"])
                P.dma(lambda e, j=j: e.dma_start(out=A.v(o_cb, [[1, 2048]]), in_=bc_rows(c_convb, j * 2048, 2048)), writes=["CB"])
                for ti in range(T // 128):
                    t0 = ti * 128
                    first = (ti % tps == 0); last = (ti % tps == tps - 1)
                    ACC = A.v(o_acc, [[1, 2048]])
                    c0 = j * 2048
                    ld = A.v(o_ld[0], [[1, 2048]])
                    P.dma(lambda e, ld=ld, t0=t0, c0=c0: e.dma_start(out=ld, in_=uz[g].ap()[t0:t0 + 128, c0:c0 + 2048]), reads=[uz[g].name], writes=["ld0"])
                    P.op("vector", lambda e, ld=ld: e.tensor_tensor(ACC, ld, A.v(o_cw + 2048, [[1, 2048]]), ALU.mult), reads=["ld0", "CW"], writes=["ACC"])
                    ld = A.v(o_ld[1], [[1, 2048]])
                    if first:
                        P.op("gpsimd", lambda e, ld=ld: e.memset(ld, 0.0), writes=["ld1"])
                        P.dma(lambda e, t0=t0, c0=c0: e.dma_start(out=A.v(o_ld[1], [[1, 2048]], p0=1, np_=127), in_=uz[g].ap()[t0:t0 + 127, c0:c0 + 2048]),
                              reads=[uz[g].name], writes=["ld1"])
                    else:
                        P.dma(lambda e, ld=ld, t0=t0, c0=c0: e.dma_start(out=ld, in_=uz[g].ap()[t0 - 1:t0 + 127, c0:c0 + 2048]), reads=[uz[g].name], writes=["ld1"])
                    P.op("gpsimd", lambda e, ld=ld: e.tensor_tensor(ld, ld, A.v(o_cw, [[1, 2048]]), ALU.mult), reads=["ld1", "CW"], writes=["ld1"])
                    P.op("vector", lambda e, ld=ld: e.tensor_tensor(ACC, ACC, ld, ALU.add), reads=["ld1", "ACC"], writes=["ACC"])
                    ld = A.v(o_ld[0], [[1, 2048]])
                    if last:
                        P.op("gpsimd", lambda e, ld=ld: e.memset(ld, 0.0), writes=["ld0"])
                        P.dma(lambda e, t0=t0, c0=c0: e.dma_start(out=A.v(o_ld[0], [[1, 2048]], p0=0, np_=127), in_=uz[g].ap()[t0 + 1:t0 + 128, c0:c0 + 2048]),
                              reads=[uz[g].name], writes=["ld0"])
                    else:
                        P.dma(lambda e, ld=ld, t0=t0, c0=c0: e.dma_start(out=ld, in_=uz[g].ap()[t0 + 1:t0 + 129, c0:c0 + 2048]), reads=[uz[g].name], writes=["ld0"])
                    P.op("gpsimd", lambda e, ld=ld: e.tensor_tensor(ld, ld, A.v(o_cw + 4096, [[1, 2048]]), ALU.mult), reads=["ld0", "CW"], writes=["ld0"])
                    P.op("vector", lambda e, ld=ld: e.tensor_tensor(ACC, ACC, ld, ALU.add), reads=["ld0", "ACC"], writes=["ACC"])
                    if j < 2:
                        P.op("vector", lambda e: e.tensor_tensor(ACC, ACC, A.v(o_cb, [[1, 2048]]), ALU.add), reads=["ACC", "CB"], writes=["ACC"])
                        P.dma(lambda e, t0=t0, c0=c0: e.dma_start(out=ug[g].ap()[t0:t0 + 128, c0:c0 + 2048], in_=ACC), reads=["ACC"], writes=[ug[g].name])
                    else:
                        P.op("vector", lambda e: e.tensor_tensor(vb, ACC, A.v(o_cb, [[1, 2048]]), ALU.add), reads=["ACC", "CB"], writes=["vb"])
                        P.dma(lambda e, t0=t0: e.dma_start(out=zb16[g][0].ap()[t0:t0 + 128, :], in_=vb), reads=["vb"], writes=[zb16[g][0].name])
            P.barrier()

        def sin_wrapped(dst, arg, m1, m2, key, okey):
            PI = math.pi
            for _ in range(2):
                P.op("vector", lambda e: e.tensor_scalar(m1, arg, -PI, 2 * PI, ALU.is_lt, ALU.mult), reads=[key], writes=[key + "m1"])
                P.op("vector", lambda e: e.tensor_scalar(m2, arg, PI, 2 * PI, ALU.is_gt, ALU.mult), reads=[key], writes=[key + "m2"])
                P.op("vector", lambda e: e.tensor_tensor(arg, arg, m1, ALU.add), reads=[key, key + "m1"], writes=[key])
                P.op("vector", lambda e: e.tensor_tensor(arg, arg, m2, ALU.subtract), reads=[key, key + "m2"], writes=[key])
            P.op("scalar", lambda e: e.activation(dst, arg, AF.Sin), reads=[key], writes=[okey])

        def stage_hy_filters(li, L):
            A.reset()
            NT = L // 128
            o_zT = A.f32(L); o_fw3 = A.f32(8192); o_h1 = A.f32(L); o_h2 = A.f32(L)
            o_FT = [A.f32(2048) for _ in range(4)]; o_SS = [A.f32(2048), A.f32(2048)]
            o_WIN = A.f32(2048); o_DEL = A.f32(2048); o_sq = A.f32(512)
            o_fw1 = A.f32(64); o_fw2 = A.f32(64); o_col = A.f32(8); o_arg = A.f32(512); o_m1 = A.f32(512); o_m2 = A.f32(512)
            o_tn = A.f32(NT); o_one = A.f32(1); o_row = A.f32(4096); o_hs = A.f32(1024)
            hsb = A.bf(o_hs, 1024)
            zT = A.v(o_zT, [[1, L]], np_=33)
            P.dma(lambda e: e.dma_start(out=zT, in_=zposT[li].ap()), writes=["zT"])
            P.dma(lambda e: e.dma_start(out=A.v(o_fw1, [[1, 64]], np_=33), in_=c_fw1.ap()), writes=["fw1"])
            P.dma(lambda e: e.dma_start(out=A.v(o_fw2, [[1, 64]], np_=64), in_=c_fw2.ap()), writes=["fw2"])
            P.dma(lambda e: e.dma_start(out=A.v(o_fw3, [[1, 8192]], np_=64), in_=c_fw3.ap()), writes=["fw3"])
            for i, t_ in enumerate((c_fb1, c_freq, c_fb2)):
                P.dma(lambda e, i=i, t_=t_: e.dma_start(out=A.v(o_col + i, [[1, 1]], np_=64), in_=bass.AP(t_, 0, [[1, 64], [1, 1]])), writes=["col"])
            P.dma(lambda e: e.dma_start(out=A.v(o_DEL, [[1, 2048]]), in_=bc_rows(deltas, 0, 2048)), writes=["DEL"])
            P.dma(lambda e: e.dma_start(out=A.v(o_tn, [[1, NT]]), in_=tneg[li].ap().rearrange("(n p) -> p n", p=128)), writes=["tn"])
            P.op("vector", lambda e: e.memset(A.v(o_one, [[1, 1]]), 1.0), writes=["one"])
            for n in range(2):
                P.op("vector", lambda e, n=n: e.memset(A.v(o_SS[n], [[1, 2048]]), 0.0), writes=["SS%d" % n])
            nb = max(1, L // 512)
            bw = min(512, L)
            for layer_i, (o_src, o_dst, o_w, kk, bcol, skey, wkey, okey) in enumerate(
                    ((o_zT, o_h1, o_fw1, 33, 0, "zT", "fw1", "h1"), (o_h1, o_h2, o_fw2, 64, 2, "h1", "fw2", "h2"))):
                for b in range(nb):
                    P.op("tensor", lambda e, b=b, o_src=o_src, o_w=o_w, kk=kk: e.matmul(psb[b % 2][0:64, 0:bw], A.v(o_w, [[1, 64]], np_=kk),
                                                                                      A.v(o_src + b * bw, [[1, bw]], np_=kk), start=True, stop=True),
                         reads=[skey, wkey], writes=["pf%d" % (b % 2)])
                    arg = A.v(o_arg, [[1, bw]], np_=64)
                    P.op("vector", lambda e, b=b, arg=arg, bcol=bcol: e.tensor_scalar(arg, psb[b % 2][0:64, 0:bw], A.v(o_col + bcol, [[1, 1]], np_=64),
                                                                                     A.v(o_col + 1, [[1, 1]], np_=64), ALU.add, ALU.mult),
                         reads=["pf%d" % (b % 2), "col"], writes=["arg"])
                    sin_wrapped(A.v(o_dst + b * bw, [[1, bw]], np_=64), arg, A.v(o_m1, [[1, bw]], np_=64), A.v(o_m2, [[1, bw]], np_=64), "arg", okey)
            for lt in range(NT):
                P.op("scalar", lambda e, lt=lt: e.activation(A.v(o_WIN, [[1, 2048]]), A.v(o_DEL, [[1, 2048]]), AF.Exp, scale=A.v(o_tn + lt, [[1, 1]])),
                     reads=["DEL", "tn"], writes=["WIN"])
                for cc in range(16):
                    nd = cc // 4; cch = cc % 4
                    P.op("tensor", lambda e, lt=lt, cc=cc: e.matmul(psb[cc % 4][:, :], A.v(o_h2 + lt * 128, [[1, 128]], np_=64),
                                                                    A.v(o_fw3 + cc * 512, [[1, 512]], np_=64), start=True, stop=True),
                         reads=["h2", "fw3"], writes=["pf%d" % (cc % 4)])
                    ft = A.v(o_FT[nd] + cch * 512, [[1, 512]])
                    P.op("vector", lambda e, cc=cc, ft=ft, cch=cch: e.tensor_tensor(ft, psb[cc % 4][:, :], A.v(o_WIN + cch * 512, [[1, 512]]), ALU.mult),
                         reads=["pf%d" % (cc % 4), "WIN"], writes=["FT%d" % nd])
                    P.op("gpsimd", lambda e, ft=ft: e.tensor_tensor(A.v(o_sq, [[1, 512]]), ft, ft, ALU.mult), reads=["FT%d" % nd], writes=["sqf"])
                    ssv = A.v(o_SS[nd // 2] + cch * 512, [[1, 512]])
                    P.op("vector", lambda e, ssv=ssv: e.tensor_tensor(ssv, ssv, A.v(o_sq, [[1, 512]]), ALU.add), reads=["sqf", "SS%d" % (nd // 2)], writes=["SS%d" % (nd // 2)])
                for n in range(2):
                    hf = A.v(o_FT[2 * n], [[1, 2048]]); hbk = A.v(o_FT[2 * n + 1], [[1, 2048]])
                    if lt == 0:
                        P.op("vector", lambda e, n=n: e.memset(A.v(o_FT[2 * n + 1], [[1, 2048]], np_=1), 0.0), reads=["FT%d" % (2 * n + 1)], writes=["FT%d" % (2 * n + 1)])
                    P.op("vector", lambda e, hf=hf, hbk=hbk: e.tensor_tensor(hsb[:, 0:2048], hf, hbk, ALU.add), reads=["FT%d" % (2 * n), "FT%d" % (2 * n + 1)], writes=["hs"])
                    P.dma(lambda e, lt=lt, n=n: e.dma_start(out=HS[li].ap()[lt * 128:(lt + 1) * 128, 2 * n, :], in_=hsb[:, 0:2048]), reads=["hs"], writes=["HS"])
                    P.op("vector", lambda e, hf=hf, hbk=hbk: e.tensor_tensor(hsb[:, 0:2048], hbk, hf, ALU.subtract), reads=["FT%d" % (2 * n), "FT%d" % (2 * n + 1)], writes=["hs"])
                    P.dma(lambda e, lt=lt, n=n: e.dma_start(out=HS[li].ap()[lt * 128:(lt + 1) * 128, 2 * n + 1, :], in_=hsb[:, 0:2048]), reads=["hs"], writes=["HS"])
            for n in range(2):
                for cch in range(4):
                    P.op("tensor", lambda e, n=n, cch=cch: e.matmul(psb[cch][0:1, :], A.v(o_one, [[1, 1]]), A.v(o_SS[n] + cch * 512, [[1, 512]]), start=True, stop=True),
                         reads=["one", "SS%d" % n], writes=["pf%d" % cch])
                    P.op("vector", lambda e, n=n, cch=cch: e.tensor_copy(A.v(o_row + n * 2048 + cch * 512, [[1, 512]], np_=1), psb[cch][0:1, :]),
                         reads=["pf%d" % cch], writes=["row"])
            rsqrt_ops(P, A.v(o_row, [[1, 4096]], np_=1), 1.0, 1e-12, "row")
            P.dma(lambda e: e.dma_start(out=bass.AP(nsc[li], 0, [[4096, 1], [1, 4096]]), in_=A.v(o_row, [[1, 4096]], np_=1)),
                  reads=["row"], writes=["nsc"])
            P.barrier()

        def hy_forward(li, L, Zc, o_mat, kt, data_keys, tag):
            NT = L // 128
            kb = kt % 2
            Cm = A.bf(o_mat[kb][0], NT * 64).rearrange("p (t k) -> p t k", t=NT)
            Sm = A.bf(o_mat[kb][1], NT * 64).rearrange("p (t k) -> p t k", t=NT)
            P.dma(lambda e: e.dma_start(out=Cm, in_=dftc[li].ap()[:, kt * 128:(kt + 1) * 128].rearrange("(t p) k -> p t k", p=128)), writes=["Cm%d" % kb])
            P.dma(lambda e: e.dma_start(out=Sm, in_=dfts[li].ap()[:, kt * 128:(kt + 1) * 128].rearrange("(t p) k -> p t k", p=128)), writes=["Sm%d" % kb])
            Zcos, Zsin = Zc if isinstance(Zc, tuple) else (Zc, Zc)
            pa = psb[2 * kb]; pbk = psb[2 * kb + 1]
            for tt in range(NT):
                P.op("tensor", lambda e, tt=tt: e.matmul(pa[:, :], Cm[:, tt, :], Zcos[:, tt, :], start=(tt == 0), stop=(tt == NT - 1)),
                     reads=["Cm%d" % kb] + data_keys, writes=["pA%d" % kb])
            for tt in range(NT):
                P.op("tensor", lambda e, tt=tt: e.matmul(pbk[:, :], Sm[:, tt, :], Zsin[:, tt, :], start=(tt == 0), stop=(tt == NT - 1)),
                     reads=["Sm%d" % kb] + data_keys, writes=["pB%d" % kb])
            return pa, pbk, kb

        def stage_hy_fspec(li, L):
            A.reset()
            NT = L // 128
            o_hs = A.f32(NT * 256); o_hd = A.f32(NT * 256)
            o_mat = [[A.f32(NT * 64), A.f32(NT * 64)], [A.f32(NT * 64), A.f32(NT * 64)]]
            o_wk = A.f32(NT); o_o = [A.f32(512), A.f32(512)]
            P.dma(lambda e: e.dma_start(out=A.v(o_wk, [[1, NT]]), in_=wkv[li].ap().rearrange("(n p) -> p n", p=128)), writes=["wk"])
            for n in range(2):
                for cch in range(4):
                    hsT = A.bf(o_hs, NT * 256).rearrange("p (t c) -> p t c", t=NT)
                    hdT = A.bf(o_hd, NT * 256).rearrange("p (t c) -> p t c", t=NT)
                    P.dma(lambda e, n=n, cch=cch, hsT=hsT: e.dma_start(out=hsT, in_=HS[li].ap()[:, 2 * n, cch * 512:(cch + 1) * 512].rearrange("(t p) c -> p t c", p=128)),
                          reads=["HS"], writes=["hsT"])
                    P.dma(lambda e, n=n, cch=cch, hdT=hdT: e.dma_start(out=hdT, in_=HS[li].ap()[:, 2 * n + 1, cch * 512:(cch + 1) * 512].rearrange("(t p) c -> p t c", p=128)),
                          reads=["HS"], writes=["hdT"])
                    for kt in range(NT):
                        pa, pbk, kb = hy_forward(li, L, (hsT, hdT), o_mat, kt, ["hsT", "hdT"], "f")
                        for j, (pp, key) in enumerate(((pa, "pA%d" % kb), (pbk, "pB%d" % kb))):
                            ot = A.v(o_o[j], [[1, 512]])
                            P.op("vector", lambda e, ot=ot, pp=pp, kt=kt: e.tensor_scalar(ot, pp[:, :], A.v(o_wk + kt, [[1, 1]]), None, ALU.mult),
                                 reads=[key, "wk"], writes=["o%d" % j])
                            P.dma(lambda e, ot=ot, n=n, j=j, kt=kt, cch=cch: e.dma_start(out=FS[li].ap()[n, j, kt * 128:(kt + 1) * 128, cch * 512:(cch + 1) * 512], in_=ot),
                                  reads=["o%d" % j], writes=["FS"])
            P.barrier()

        def stage_hy_conv(g, T, li, L, seq, n, final):
            A.reset()
            NT = L // 128
            base = seq * L
            o_Z = A.f32(NT * 256); o_YC = A.f32(NT * 256); o_YS = A.f32(NT * 256)
            o_mat = [[A.f32(NT * 64), A.f32(NT * 64)], [A.f32(NT * 64), A.f32(NT * 64)]]
            o_F = [[A.f32(512), A.f32(512)], [A.f32(512), A.f32(512)]]
            o_t = [A.f32(512) for _ in range(4)]
            o_gt = [A.f32(512), A.f32(512)]; o_ns = A.f32(512); o_bs = A.f32(512)
            o_zn = [A.f32(256), A.f32(256)]; o_zT = A.f32(256)
            zin = zb16[g][n]; zout = zb16[g][n + 1]
            for cch in range(4):
                c0 = cch * 512
                Z = A.bf(o_Z, NT * 256).rearrange("p (t c) -> p t c", t=NT)
                YC = A.bf(o_YC, NT * 256).rearrange("p (t c) -> p t c", t=NT)
                YS = A.bf(o_YS, NT * 256).rearrange("p (t c) -> p t c", t=NT)
                P.dma(lambda e, c0=c0, Z=Z: e.dma_start(out=Z, in_=zin.ap()[base:base + L, c0:c0 + 512].rearrange("(t p) c -> p t c", p=128)),
                      reads=[zin.name], writes=["Z"])
                P.dma(lambda e, c0=c0: e.dma_start(out=A.v(o_ns, [[1, 512]]), in_=bc_rows(nsc[li], n * 2048 + c0, 512)), reads=["nsc"], writes=["NS"])
                P.dma(lambda e, c0=c0: e.dma_start(out=A.v(o_bs, [[1, 512]]), in_=bc_rows(c_bias, n * 2048 + c0, 512)), writes=["BS"])
                for kt in range(NT):
                    pa, pbk, kb = hy_forward(li, L, Z, o_mat, kt, ["Z"], "c")
                    Fc = A.v(o_F[kb][0], [[1, 512]]); Fs = A.v(o_F[kb][1], [[1, 512]])
                    P.dma(lambda e, Fc=Fc, kt=kt, c0=c0: e.dma_start(out=Fc, in_=FS[li].ap()[n, 0, kt * 128:(kt + 1) * 128, c0:c0 + 512]), reads=["FS"], writes=["Fc%d" % kb])
                    P.dma(lambda e, Fs=Fs, kt=kt, c0=c0: e.dma_start(out=Fs, in_=FS[li].ap()[n, 1, kt * 128:(kt + 1) * 128, c0:c0 + 512]), reads=["FS"], writes=["Fs%d" % kb])
                    t = [A.v(o, [[1, 512]]) for o in o_t]
                    P.op("vector", lambda e, t=t, pa=pa, Fc=Fc: e.tensor_tensor(t[0], pa[:, :], Fc, ALU.mult), reads=["pA%d" % kb, "Fc%d" % kb], writes=["t0"])
                    P.op("vector", lambda e, t=t, pbk=pbk, Fs=Fs: e.tensor_tensor(t[1], pbk[:, :], Fs, ALU.mult), reads=["pB%d" % kb, "Fs%d" % kb], writes=["t1"])
                    P.op("vector", lambda e, t=t, pbk=pbk, Fc=Fc: e.tensor_tensor(t[2], pbk[:, :], Fc, ALU.mult), reads=["pB%d" % kb, "Fc%d" % kb], writes=["t2"])
                    P.op("vector", lambda e, t=t, pa=pa, Fs=Fs: e.tensor_tensor(t[3], pa[:, :], Fs, ALU.mult), reads=["pA%d" % kb, "Fs%d" % kb], writes=["t3"])
                    P.op("gpsimd", lambda e, t=t, kt=kt, YC=YC: e.tensor_tensor(YC[:, kt, :], t[0], t[1], ALU.add), reads=["t0", "t1"], writes=["YC"])
                    P.op("gpsimd", lambda e, t=t, kt=kt, YS=YS: e.tensor_tensor(YS[:, kt, :], t[2], t[3], ALU.subtract), reads=["t2", "t3"], writes=["YS"])
                for tt in range(NT):
                    kb = tt % 2
                    Cm = A.bf(o_mat[kb][0], NT * 64).rearrange("p (t k) -> p t k", t=NT)
                    Sm = A.bf(o_mat[kb][1], NT * 64).rearrange("p (t k) -> p t k", t=NT)
                    P.dma(lambda e, Cm=Cm, tt=tt: e.dma_start(out=Cm, in_=dftc[li].ap()[:, tt * 128:(tt + 1) * 128].rearrange("(t p) k -> p t k", p=128)), writes=["Cm%d" % kb])
                    P.dma(lambda e, Sm=Sm, tt=tt: e.dma_start(out=Sm, in_=dfts[li].ap()[:, tt * 128:(tt + 1) * 128].rearrange("(t p) k -> p t k", p=128)), writes=["Sm%d" % kb])
                    py = psb[4 + kb]
                    for kt in range(NT):
                        P.op("tensor", lambda e, kt=kt, Cm=Cm, YC=YC, py=py: e.matmul(py[:, :], Cm[:, kt, :], YC[:, kt, :], start=(kt == 0), stop=False),
                             reads=["Cm%d" % kb, "YC"], writes=["pY%d" % kb])
                    for kt in range(NT):
                        P.op("tensor", lambda e, kt=kt, Sm=Sm, YS=YS, py=py: e.matmul(py[:, :], Sm[:, kt, :], YS[:, kt, :], start=False, stop=(kt == NT - 1)),
                             reads=["Sm%d" % kb, "YS"], writes=["pY%d" % kb])
                    gt_ = A.v(o_gt[kb], [[1, 512]])
                    t0 = base + tt * 128
                    P.dma(lambda e, gt_=gt_, t0=t0, c0=c0: e.dma_start(out=gt_, in_=ug[g].ap()[t0:t0 + 128, n * 2048 + c0:n * 2048 + c0 + 512]), reads=[ug[g].name], writes=["gt%d" % kb])
                    ta = A.v(o_t[0], [[1, 512]]); tb = A.v(o_t[1], [[1, 512]])
                    P.op("vector", lambda e, ta=ta, py=py: e.tensor_tensor(ta, py[:, :], A.v(o_ns, [[1, 512]]), ALU.mult), reads=["pY%d" % kb, "NS"], writes=["t0"])
                    P.op("gpsimd", lambda e, tb=tb, tt=tt, Z=Z: e.tensor_tensor(tb, Z[:, tt, :], A.v(o_bs, [[1, 512]]), ALU.mult), reads=["Z", "BS"], writes=["t1"])
                    P.op("vector", lambda e, ta=ta, tb=tb: e.tensor_tensor(ta, ta, tb, ALU.add), reads=["t0", "t1"], writes=["t0"])
                    zn = A.bf(o_zn[kb], 256)
                    P.op("vector", lambda e, ta=ta, gt_=gt_, zn=zn: e.tensor_tensor(zn, ta, gt_, ALU.mult), reads=["t0", "gt%d" % kb], writes=["zn%d" % kb])
                    if not final:
                        P.dma(lambda e, zn=zn, t0=t0, c0=c0: e.dma_start(out=zout.ap()[t0:t0 + 128, c0:c0 + 512], in_=zn), reads=["zn%d" % kb], writes=[zout.name])
                    else:
                        zT4 = A.bf(o_zT, 256).rearrange("p (c t) -> p c t", c=4)
                        for c4 in range(4):
                            P.op("tensor", lambda e, c4=c4, zn=zn: e.transpose(psT[:, c4 * 128:(c4 + 1) * 128], zn[:, c4 * 128:(c4 + 1) * 128], identb),
                                 reads=["zn%d" % kb, "identb"], writes=["psT"])
                        P.op("scalar", lambda e, zT4=zT4: e.activation(zT4, psT[:, 0:512].rearrange("p (c t) -> p c t", c=4), AF.Copy), reads=["psT"], writes=["zT4"])
                        P.dma(lambda e, zT4=zT4, t0=t0, cch=cch: e.dma_start(out=hzT[g].ap()[cch * 4:(cch + 1) * 4, :, t0:t0 + 128].rearrange("c p t -> p c t"), in_=zT4),
                              reads=["zT4"], writes=[hzT[g].name])
            P.barrier()

        def stage_cast_tables():
            for l in range(2):
                for src_t, dst_t in ((peer_u[l], peer_ub[l]), (peer_v[l], peer_vb[l])):
                    for c in range(16):
                        P.dma(lambda e, src_t=src_t, dst_t=dst_t, c=c: e.dma_start(out=dst_t.ap()[c * 1024:(c + 1) * 1024, :], in_=src_t.ap()[c * 1024:(c + 1) * 1024, :]),
                              writes=[dst_t.name], q="gpsimd")

        def stage_gather_rows(src_t, dst_t, n):
            A.reset()
            o_i = A.f32(8); o_r = [A.f32(2048), A.f32(2048)]
            idx = bass.AP(A.t, o_i, [[A.size, 128], [1, 8]]).bitcast(I32)
            P.dma(lambda e: e.dma_start(out=idx, in_=qidx.ap().rearrange("(n p) o -> p (n o)", p=128)), writes=["qi"])
            for ti in range(n // 128):
                k = ti % 2
                Rr = A.v(o_r[k], [[1, 2048]])
                P.dma(lambda e, Rr=Rr, ti=ti: e.indirect_dma_start(out=Rr, out_offset=None, in_=src_t.ap(),
                                                                 in_offset=bass.IndirectOffsetOnAxis(ap=idx[:, ti:ti + 1], axis=0)),
                      reads=["qi"], writes=["gr%d" % k], q="gpsimd")
                P.dma(lambda e, Rr=Rr, ti=ti: e.dma_start(out=dst_t.ap()[ti * 128:(ti + 1) * 128, :], in_=Rr), reads=["gr%d" % k], writes=[dst_t.name])
            P.barrier()

        def stage_kv_out():
            A.reset()
            o_z = A.f32(512); o_t = A.f32(256); o_s = A.f32(8); o_kn = A.f32(128)
            KN = A.v(o_kn, [[1, 128]])
            P.dma(lambda e: e.dma_start(out=KN, in_=bc_rows(b_kn, 0, 128)), writes=["KN"])
            for ti in range(TP // 128):
                t0 = ti * 128
                zt = A.v(o_z, [[1, 512]])
                P.dma(lambda e, t0=t0: e.dma_start(out=zt, in_=zg[0].ap()[t0:t0 + 128, 4480:4992]), reads=["zp"], writes=["zt"])
                P.dma(lambda e, t0=t0: e.dma_start(out=nv.ap()[t0:t0 + 128, :], in_=A.v(o_z + 256, [[1, 256]])), reads=["zt"], writes=["nv"], is_output=True)
                P.op("vector", lambda e: e.tensor_tensor(A.v(o_t, [[1, 256]]), A.v(o_z, [[1, 256]]), A.v(o_z, [[1, 256]]), ALU.mult), reads=["zt"], writes=["t"])
                P.op("vector", lambda e: e.tensor_reduce(A.v(o_s, [[1, 2]]), A.v(o_t, [[128, 2], [1, 128]]), AX.X, ALU.add), reads=["t"], writes=["s"])
                rsqrt_ops(P, A.v(o_s, [[1, 2]]), 1.0 / 128, 1e-6, "s")
                P.op("vector", lambda e: e.tensor_tensor(A.v(o_t, [[128, 2], [1, 128]]), A.v(o_z, [[128, 2], [1, 128]]), A.v(o_s, [[1, 2], [0, 128]]), ALU.mult),
                     reads=["s", "zt"], writes=["t"])
                P.op("vector", lambda e: e.tensor_tensor(A.v(o_t, [[128, 2], [1, 128]]), A.v(o_t, [[128, 2], [1, 128]]), A.v(o_kn, [[0, 2], [1, 128]]), ALU.mult),
                     reads=["t", "KN"], writes=["t"])
                P.dma(lambda e, t0=t0: e.dma_start(out=nk.ap()[t0:t0 + 128, :], in_=A.v(o_t, [[1, 256]])), reads=["t"], writes=["nk"], is_output=True)
            P.barrier()

        GROUPS = ((0, TP, 256), (1, TS, 4096))

        def layer1(g, T, L, out_ap, out_key):
            li = 0 if L == 256 else 1
            stage_norm_gemm(g, T, x2[g].ap(), 1, 1, norm1, odd_w_in.ap(), 6144, uz[g])
            stage_hy_conv3(g, T, L)
            for n in range(2):
                for seq in range(T // L):
                    stage_hy_conv(g, T, li, L, seq, n, n == 1)
            stage_norm_gemm(g, T, None, 1, 1, None, odd_w_out.ap(), D, x3[g], srcT=hzT[g], resid=(x2[g].ap(), 2 * D))
            if mode == "hytest":
                return
            if g == 1:
                stage_gather_rows(x3[1], x3q, 1024)
                stage_norm_gemm(g, 1024, x3q.ap(), 1, 2, norm2, peer_wq.ap()[1], D, pq[g], hm_out=hm2[g])
                stage_peer(g, 1024, 1, pq[g], hm2[g], x3q.ap(), out_ap, out_key, is_output=True)
                return
            stage_norm_gemm(g, T, x3[g].ap(), 1, 2, norm2, peer_wq.ap()[1], D, pq[g], hm_out=hm2[g])
            stage_peer(g, T, 1, pq[g], hm2[g], x3[g].ap(), out_ap, out_key, is_output=True)

        if mode == "bench_scan":
            stage_scan(0, 0, 256, 0, None, ns)
            stage_scan(0, 0, 256, 1, None, ns)
            P.emit()
            return nc
        if mode == "bench_peer":
            stage_peer(0, 256, 0, pq[0], hm2[0], xg[0].ap(), x2[0].ap(), "x2_0")
            P.emit()
            return nc
        if mode == "bench_gemm":
            stage_norm_gemm(0, TP, xg[0].ap(), 0, 1, norm1, even_w_in.ap(), 4992, zg[0])
            P.emit()
            return nc
        if mode == "full":
            stage_cast_tables()
        stage_mod()
        for li, L_ in enumerate((256, 4096)):
            stage_hy_filters(li, L_)
            stage_hy_fspec(li, L_)
        if mode == "full":
            for g, T, L in GROUPS:
                latent = (g == 1)
                stage_norm_gemm(g, T, xg[g].ap(), 0, 1, norm1, even_w_in.ap(), 4992, zg[g])
                if g == 0:
                    stage_kv_out()
                stage_even_prep(g, T, L)
                for seq in range(T // L):
                    for d in range(2):
                        stage_scan(g, seq, L, d, st_in if latent else None, None if latent else ns)
                stage_rwkv_post(g, T)
                stage_attn_prep(g, T, L, latent)
                for seq in range(T // L):
                    stage_attn(g, T, seq, L, L + (512 if latent else 0))
                stage_norm_gemm(g, T, None, 0, 1, None, even_w_out.ap(), D, x1[g], srcT=ycatT[g], resid=(xg[g].ap(), 2 * D))
                stage_norm_gemm(g, T, x1[g].ap(), 0, 2, norm2, peer_wq.ap()[0], D, pq[g], hm_out=hm2[g])
                stage_peer(g, T, 0, pq[g], hm2[g], x1[g].ap(), x2[g].ap(), x2[g].name)
        layer1(0, TP, 256, yp.ap(), "yp")
        layer1(1, TS, 4096, ys.ap(), "ys")
        P.emit()
    return nc


_CACHE = {}


def _rope_tables():
    pos = np.arange(4096)
    row = (pos // 64).astype(np.float32); col = (pos % 64).astype(np.float32)
    inv = (np.float32(10000.0) ** (-np.arange(32, dtype=np.float32) / np.float32(32))).astype(np.float32)
    ar = (row[:, None] * inv[None, :]).astype(np.float32); ac = (col[:, None] * inv[None, :]).astype(np.float32)
    cs = np.concatenate([np.cos(ar), np.cos(ac)], 1).astype(np.float32)
    sn = np.concatenate([np.sin(ar), np.sin(ac)], 1).astype(np.float32)
    return cs, sn


def _hyena_consts():
    import ml_dtypes
    out = {}
    for li, L in enumerate((256, 4096)):
        t = np.linspace(0.0, 1.0, L, dtype=np.float32)[:, None]
        wpos = (np.float32(2.0 * math.pi) * np.arange(L, dtype=np.float32)[:, None] / np.float32(L)).astype(np.float32)
        fb = np.linspace(1e-4, 15, 16, dtype=np.float32)[None, :]
        z = np.concatenate([t, np.cos(fb * wpos), -np.sin(fb * wpos)], axis=-1).astype(np.float32)
        out["zposT_%d" % li] = np.ascontiguousarray(z.T)
        out["tneg_%d" % li] = np.ascontiguousarray(-t[:, 0])
        Nf = 2 * L - 1
        wk = np.full((L,), 2.0 / Nf, np.float32); wk[0] = 1.0 / Nf
        out["wk_%d" % li] = wk
        idx = np.arange(L, dtype=np.int64)
        ang = (2.0 * np.pi / Nf) * ((idx[:, None] * idx[None, :]) % Nf).astype(np.float64)
        out["dftc_%d" % li] = np.cos(ang).astype(ml_dtypes.bfloat16)
        out["dfts_%d" % li] = np.sin(ang).astype(ml_dtypes.bfloat16)
    out["deltas"] = np.abs(np.linspace(math.log(1e-2) / 1.5, math.log(1e-2) / 0.3, 2048, dtype=np.float32)).astype(np.float32)
    return out


def _shared_inputs(inputs):
    f = lambda a: np.ascontiguousarray(np.asarray(a, dtype=np.float32))
    sh = {
        "mod_w": f(inputs["mod_w"]), "mod_b": f(inputs["mod_b"]),
        "norm1": f(inputs["norm1"]), "norm2": f(inputs["norm2"]),
        "even_w_in": f(inputs["even_w_in"][0]),
        "even_a_conv": f(inputs["even_a_conv"][0]),
        "even_a_w0": f(inputs["even_a_w0"][0]), "even_a_wu": f(inputs["even_a_wu"][0]).reshape(128, 1024),
        "even_a_a0": f(inputs["even_a_a0"][0]), "even_a_au": f(inputs["even_a_au"][0]).reshape(128, 1024),
        "even_a_gu": f(inputs["even_a_gu"][0]),
        "even_a_kk": f(inputs["even_a_kk"][0]), "even_a_ka": f(inputs["even_a_ka"][0]),
        "even_a_rk": f(inputs["even_a_rk"][0]).reshape(1024),
        "even_a_ln_w": f(inputs["even_a_ln_w"][0]), "even_a_ln_b": f(inputs["even_a_ln_b"][0]),
        "even_b_qnorm": f(inputs["even_b_qnorm"][0]), "even_b_knorm": f(inputs["even_b_knorm"][0]),
        "ident": np.eye(128, dtype=np.float32),
        "ropec": _rope_tables()[0], "ropes": _rope_tables()[1],
        "even_w_out": f(inputs["even_w_out"][0]),
        "odd_w_in": f(inputs["odd_w_in"][0]), "odd_w_out": f(inputs["odd_w_out"][0]),
        "odd_c_conv": f(inputs["odd_c_conv"][0]), "odd_c_conv_b": f(inputs["odd_c_conv_b"][0]),
        "odd_c_fw1": f(inputs["odd_c_fw1"][0]), "odd_c_fb1": f(inputs["odd_c_fb1"][0]), "odd_c_freq": f(inputs["odd_c_freq"][0]),
        "odd_c_fw2": f(inputs["odd_c_fw2"][0]), "odd_c_fb2": f(inputs["odd_c_fb2"][0]), "odd_c_fw3": f(inputs["odd_c_fw3"][0]),
        "odd_c_bias": f(inputs["odd_c_bias"][0]),
        "peer_wq": f(inputs["peer_wq"]), "peer_keys": f(inputs["peer_keys"]),
        "peer_u0": f(inputs["peer_u"][0]), "peer_u1": f(inputs["peer_u"][1]),
        "peer_v0": f(inputs["peer_v"][0]), "peer_v1": f(inputs["peer_v"][1]),
    }
    sh.update(_hyena_consts())
    kc = np.zeros((3, 128, 256), np.uint32)
    kc[0] = 0xFFFFFF00
    kc[1] = 0xFF
    kc[2] = (255 - (np.arange(256) % 256)).astype(np.uint32)[None, :]
    sh["kconst"] = kc
    return sh


def kernel(**inputs):
    f = lambda a: np.ascontiguousarray(np.asarray(a, dtype=np.float32))
    if "nc" not in _CACHE:
        _CACHE["nc"] = build_program()
    nc = _CACHE["nc"]
    x_prompt = f(inputs["x_prompt"]); x_sample = f(inputs["x_sample"])
    shared = _shared_inputs(inputs)
    in_maps = []
    for c in range(NC):
        b = c // 4
        m = dict(shared)
        m["xp"] = x_prompt[4 * c:4 * c + 4].reshape(1024, D)
        m["xs"] = x_sample[b]
        m["ck"] = f(inputs["cache_b_k"][b, 0]).reshape(512, 256)
        m["cv"] = f(inputs["cache_b_v"][b, 0]).reshape(512, 256)
        m["st"] = f(inputs["state_a"][b, 0])
        m["cond"] = np.stack([f(inputs["c_ctx"]), f(inputs["c"][b])], 0)
        m["qidx"] = (np.arange(1024, dtype=np.int32) + 1024 * (c % 4)).reshape(1024, 1)
        in_maps.append(m)
    res = run_bass_kernel_spmd(nc, in_maps, core_ids=list(range(NC)))
    R = res.results
    if DEBUG_OUT:
        DEBUG_RES['R'] = R
    y_prompt = np.concatenate([R[c]["yp"].reshape(4, 256, D) for c in range(NC)], 0)
    y_sample = np.stack([np.concatenate([R[4 * b + q]["ys"] for q in range(4)], 0) for b in range(2)], 0)
    new_k = np.concatenate([R[c]["nk"].reshape(4, 1, 256, 2, 128) for c in range(NC)], 0)
    new_v = np.concatenate([R[c]["nv"].reshape(4, 1, 256, 2, 128) for c in range(NC)], 0)
    new_s = np.concatenate([R[c]["ns"].reshape(4, 1, 2, 16, 64, 64) for c in range(NC)], 0)
    return (y_prompt.astype(np.float32), y_sample.astype(np.float32), new_k.astype(np.float32),
            new_v.astype(np.float32), new_s.astype(np.float32))
```

```python
from contextlib import ExitStack
import math
import numpy as np
import concourse.bass as bass
import concourse.mybir as mybir
from concourse.bass_utils import run_bass_kernel_spmd

F32 = mybir.dt.float32
BF16 = mybir.dt.bfloat16
I32 = mybir.dt.int32
U32 = mybir.dt.uint32
AF = mybir.ActivationFunctionType
ALU = mybir.AluOpType
AX = mybir.AxisListType

D = 2048
NC = 8
COMPUTE = ("tensor", "vector", "scalar", "gpsimd")
NDMA_SLOTS = 12
SEM_LIMIT = 30000
DEBUG_OUT = set()
UNTRACKED = {"sq", "sq_r", "sq_kk", "sq_w", "sq_kt", "sq_b", "gsc", "bon", "vT", "yT", "nv", "nk", "qT_d", "kT_d", "V_d",
             "HS", "FS", "ns", "yp", "ys", "modv", "nsc", "zero_d"}
SCAN_DVE_INORDER = True
DEBUG_RES = {}


class Prog:
    def __init__(self, nc, stack):
        self.nc = nc
        self.stack = stack
        self.ops = {e: [] for e in COMPUTE + ("sync",)}
        self.nsem = 0
        self.csem = {e: self._newsem("c_" + e) for e in COMPUTE}
        self.ccnt = {e: 0 for e in COMPUTE}
        self.dslots = {}
        self.dnext = {}
        for q in ("sync", "gpsimd"):
            self.dslots[q] = [[self._newsem("d_%s%d" % (q, i)), 0] for i in range(NDMA_SLOTS)]
            self.dnext[q] = 0
        self.last_w = {}
        self.readers = {}
        self.waited = {e: {} for e in self.ops}
        self.out_events = []
        self.all_events = {}
        self.no_self = {"tensor"}
        self.own = {e: {id(self.csem[e])} for e in COMPUTE}

    def _newsem(self, name):
        self.nsem += 1
        return self.stack.enter_context(self.nc.semaphore("%s_%d" % (name, self.nsem)))

    def _deps(self, reads, writes):
        reads = [k for k in reads if k not in UNTRACKED]
        writes = [k for k in writes if k not in UNTRACKED]
        deps = []
        for k in reads:
            if k in self.last_w:
                deps.append(self.last_w[k])
        for k in writes:
            if k in self.last_w:
                deps.append(self.last_w[k])
            deps.extend(self.readers.get(k, ()))
        return deps

    def _emit_waits(self, eng, deps):
        need = {}
        for (s, v) in deps:
            if v > need.get(id(s), (s, 0))[1]:
                need[id(s)] = (s, v)
        w = self.waited[eng]
        skip = self.own.get(eng, ()) if eng in self.no_self else ()
        for sid, (s, v) in need.items():
            if w.get(sid, 0) >= v or sid in skip:
                continue
            w[sid] = v
            self.ops[eng].append(("wait", s, v))

    def _commit(self, ev, reads, writes):
        self.all_events[id(ev[0])] = ev
        reads = [k for k in reads if k not in UNTRACKED]
        writes = [k for k in writes if k not in UNTRACKED]
        for k in reads:
            self.readers.setdefault(k, []).append(ev)
        for k in writes:
            self.last_w[k] = ev
            self.readers[k] = []

    def op(self, eng, fn, reads=(), writes=(), inorder=False):
        deps = self._deps(reads, writes)
        if inorder and eng not in self.no_self:
            self.no_self.add(eng)
            self._emit_waits(eng, deps)
            self.no_self.discard(eng)
        else:
            self._emit_waits(eng, deps)
        if self.ccnt[eng] >= SEM_LIMIT:
            self.csem[eng] = self._newsem("c_" + eng)
            self.own[eng].add(id(self.csem[eng]))
            self.ccnt[eng] = 0
        self.ccnt[eng] += 1
        ev = (self.csem[eng], self.ccnt[eng])
        self.ops[eng].append(("op", fn, ev[0], 1))
        self._commit(ev, reads, writes)
        return ev

    def dma(self, fn, reads=(), writes=(), q="sync", is_output=False):
        deps = self._deps(reads, writes)
        slots = self.dslots[q]
        i = self.dnext[q]
        self.dnext[q] = (i + 1) % len(slots)
        slot = slots[i]
        if slot[1] > 0:
            deps.append((slot[0], slot[1]))
        self._emit_waits(q, deps)
        if slot[1] + 16 > SEM_LIMIT:
            slot[0] = self._newsem("d_" + q)
            slot[1] = 0
        slot[1] += 16
        ev = (slot[0], slot[1])
        self.ops[q].append(("op", fn, ev[0], 16))
        self._commit(ev, reads, writes)
        if is_output:
            self.out_events.append(ev)
        return ev

    def barrier(self):
        evs = list(self.all_events.values())
        saved = self.no_self
        self.no_self = set()
        for e in self.ops:
            self._emit_waits(e, evs)
        self.no_self = saved
        self.last_w = {}
        self.readers = {}

    def emit(self):
        nc = self.nc
        self._emit_waits("sync", list(self.all_events.values()))
        with nc.Block() as block:
            def run(engname):
                def body(eng):
                    for item in self.ops[engname]:
                        if item[0] == "wait":
                            eng.wait_ge(item[1], item[2])
                        else:
                            item[1](eng).then_inc(item[2], item[3])
                return body
            block.sync(run("sync"))
            block.tensor(run("tensor"))
            block.vector(run("vector"))
            block.scalar(run("scalar"))
            block.gpsimd(run("gpsimd"))


class Arena:
    def __init__(self, t, size):
        self.t = t
        self.size = size
        self.off = 0

    def reset(self):
        self.off = 0

    def f32(self, n):
        o = self.off
        self.off += n
        assert self.off <= self.size, ("arena overflow", self.off, self.size)
        return o

    def v(self, off, dims, p0=0, np_=128):
        return bass.AP(self.t, p0 * self.size + off, [[self.size, np_]] + [list(d) for d in dims])

    def bf(self, off, n):
        return self.t[:, off:off + n].bitcast(BF16)


def bc_rows(dram_t, elem_off, n, nparts=128):
    return bass.AP(dram_t, elem_off, [[0, nparts], [1, n]])


def rsqrt_ops(P, ap, mul, add, key):
    P.op("vector", lambda e: e.tensor_scalar(ap, ap, mul, add, ALU.mult, ALU.add), reads=[key], writes=[key])
    P.op("scalar", lambda e: e.activation(ap, ap, AF.Sqrt), reads=[key], writes=[key])
    P.op("vector", lambda e: e.reciprocal(ap, ap), reads=[key], writes=[key])

def build_program(mode="full"):
    nc = bass.Bass("TRN2", target_bir_lowering=False)
    TP, TS = 1024, 4096
    dt_in = {}

    def inp(name, shape, dt=F32):
        dt_in[name] = nc.dram_tensor(name, list(shape), dt, kind="ExternalInput")
        return dt_in[name]

    def outp(name, shape, dt=F32):
        return nc.dram_tensor(name, list(shape), dt, kind="ExternalOutput")

    def scr(name, shape, dt=F32):
        UNTRACKED.add(name)
        if name in DEBUG_OUT:
            return nc.dram_tensor(name, list(shape), dt, kind="ExternalOutput")
        return nc.dram_tensor(name, list(shape), dt)

    xg = [inp("xp", [TP, D]), inp("xs", [TS, D])]
    ck = inp("ck", [512, 256]); cv = inp("cv", [512, 256]); st_in = inp("st", [2, 16, 64, 64])
    cond = inp("cond", [2, D])
    mod_w = inp("mod_w", [2, D, 6 * D]); mod_b = inp("mod_b", [2, 6 * D])
    norm1 = inp("norm1", [2, D]); norm2 = inp("norm2", [2, D])
    even_w_in = inp("even_w_in", [D, 4992])
    a_conv = inp("even_a_conv", [3, 3456])
    a_w0 = inp("even_a_w0", [2, 1024]); a_wu = inp("even_a_wu", [128, 1024])
    a_a0 = inp("even_a_a0", [2, 1024]); a_au = inp("even_a_au", [128, 1024])
    a_gu = inp("even_a_gu", [128, 1024])
    a_kk = inp("even_a_kk", [1024]); a_ka = inp("even_a_ka", [1024]); a_rk = inp("even_a_rk", [1024])
    a_lnw = inp("even_a_ln_w", [1024]); a_lnb = inp("even_a_ln_b", [1024])
    b_qn = inp("even_b_qnorm", [128]); b_kn = inp("even_b_knorm", [128])
    ident_d = inp("ident", [128, 128])
    ropec = inp("ropec", [4096, 64]); ropes = inp("ropes", [4096, 64])
    even_w_out = inp("even_w_out", [D, D])
    odd_w_in = inp("odd_w_in", [D, 6144]); odd_w_out = inp("odd_w_out", [D, D])
    c_conv = inp("odd_c_conv", [3, 6144]); c_convb = inp("odd_c_conv_b", [6144])
    c_fw1 = inp("odd_c_fw1", [33, 64]); c_fb1 = inp("odd_c_fb1", [64]); c_freq = inp("odd_c_freq", [64])
    c_fw2 = inp("odd_c_fw2", [64, 64]); c_fb2 = inp("odd_c_fb2", [64]); c_fw3 = inp("odd_c_fw3", [64, 8192])
    c_bias = inp("odd_c_bias", [2, 2048])
    zposT = [inp("zposT_%d" % li, [33, L_]) for li, L_ in enumerate((256, 4096))]
    tneg = [inp("tneg_%d" % li, [L_]) for li, L_ in enumerate((256, 4096))]
    wkv = [inp("wk_%d" % li, [L_]) for li, L_ in enumerate((256, 4096))]
    deltas = inp("deltas", [2048])
    kconst = inp("kconst", [3, 128, 256], U32)
    qidx = inp("qidx", [1024, 1], I32)
    dftc = [inp("dftc_%d" % li, [L_, L_], BF16) for li, L_ in enumerate((256, 4096))]
    dfts = [inp("dfts_%d" % li, [L_, L_], BF16) for li, L_ in enumerate((256, 4096))]
    peer_wq = inp("peer_wq", [2, D, D]); peer_keys = inp("peer_keys", [2, 8, 2, 128, 128])
    peer_u = [inp("peer_u0", [16384, D]), inp("peer_u1", [16384, D])]
    peer_v = [inp("peer_v0", [16384, D]), inp("peer_v1", [16384, D])]

    yp = outp("yp", [TP, D]); ys = outp("ys", [1024, D])
    nk = outp("nk", [TP, 256]); nv = outp("nv", [TP, 256]); ns = outp("ns", [4, 2, 16, 64, 64])

    modv = scr("modv", [2, 2, 6 * D])
    zg = [scr("zp", [TP, 4992]), scr("zs", [TS, 4992])]
    SQ = ["kk", "w0", "w1", "b0", "b1", "kt0", "kt1", "r"]
    sq = [{n: scr("sq_%s_%d" % (n, g), [T, 1024]) for n in SQ} for g, T in enumerate((TP, TS))]
    vT = [scr("vT_%d" % g, [8, 128, T]) for g, T in enumerate((TP, TS))]
    yT = [[scr("yT_%d_%d" % (g, d), [8, 128, T]) for d in range(2)] for g, T in enumerate((TP, TS))]
    gsc = [scr("g_%d" % g, [T, 1024]) for g, T in enumerate((TP, TS))]
    bon = [scr("bon_%d" % g, [T, 1024]) for g, T in enumerate((TP, TS))]
    zero_d = scr("zero_d", [128, 64])
    ycatT = [scr("ycatT_%d" % g, [16, 128, T], BF16) for g, T in enumerate((TP, TS))]
    qT_d = [scr("qT_%d" % g, [8, 128, T], BF16) for g, T in enumerate((TP, TS))]
    kT_d = [scr("kT_0", [2, 128, TP], BF16), scr("kT_1", [2, 128, TS + 512], BF16)]
    V_d = [scr("V_0", [TP, 256], BF16), scr("V_1", [TS + 512, 256], BF16)]
    x1 = [scr("x1_%d" % g, [T, D]) for g, T in enumerate((TP, TS))]
    if mode == "hytest":
        x2 = [inp("x2_%d" % g, [T, D]) for g, T in enumerate((TP, TS))]
    else:
        x2 = [scr("x2_%d" % g, [T, D]) for g, T in enumerate((TP, TS))]
    x3 = [scr("x3_%d" % g, [T, D]) for g, T in enumerate((TP, TS))]
    x3q = scr("x3q", [1024, D])
    peer_ub = [scr("peer_ub%d" % l, [16384, D], BF16) for l in range(2)]
    peer_vb = [scr("peer_vb%d" % l, [16384, D], BF16) for l in range(2)]
    uz = [scr("uz_%d" % g, [T, 6144]) for g, T in enumerate((TP, TS))]
    ug = [scr("ug_%d" % g, [T, 4096]) for g, T in enumerate((TP, TS))]
    zb16 = [[scr("zb_%d_%d" % (g, i), [T, 2048], BF16) for i in range(3)] for g, T in enumerate((TP, TS))]
    hzT = [scr("hzT_%d" % g, [16, 128, T], BF16) for g, T in enumerate((TP, TS))]
    LL = (256, 4096)
    HS = [scr("HS_%d" % li, [L_, 4, 2048], BF16) for li, L_ in enumerate(LL)]
    FS = [scr("FS_%d" % li, [2, 2, L_, 2048]) for li, L_ in enumerate(LL)]
    nsc = [scr("nsc_%d" % li, [2, 2048]) for li in range(2)]
    hm2 = [scr("hm2_%d" % g, [T, D]) for g, T in enumerate((TP, TS))]
    pq = [scr("pq_%d" % g, [T, D]) for g, T in enumerate((TP, TS))]

    with ExitStack() as st:
        st.enter_context(nc.allow_non_contiguous_dma(reason="layout"))
        P = Prog(nc, st)
        ASZ = 47616
        A = Arena(st.enter_context(nc.sbuf_tensor("arena", [128, ASZ], F32)), ASZ)
        psb = [st.enter_context(nc.psum_tensor("ps%d" % i, [128, 512], F32)) for i in range(6)]
        psT = st.enter_context(nc.psum_tensor("psT", [128, 2048], BF16))
        cons = st.enter_context(nc.sbuf_tensor("cons", [128, 128 + 64], F32))
        consb = st.enter_context(nc.sbuf_tensor("consb", [128, 128], BF16))
        ident = cons[:, 0:128]
        identb = consb[:, 0:128]
        P.dma(lambda e: e.dma_start(out=ident, in_=ident_d.ap()), writes=["ident"])
        P.op("vector", lambda e: e.tensor_copy(identb, ident), reads=["ident"], writes=["identb"])
        P.op("vector", lambda e: e.memset(cons[:, 128:192], 0.0), writes=["zeros"])
        P.dma(lambda e: e.dma_start(out=zero_d.ap(), in_=cons[:, 128:192]), reads=["zeros"], writes=["zero_d"])

        def stage_mod():
            A.reset()
            o_c = A.f32(32); o_s = A.f32(32); o_mb = A.f32(6 * D)
            o_w = [A.f32(16 * 512), A.f32(16 * 512)]
            cT = A.v(o_c, [[1, 32]]); sT = A.v(o_s, [[1, 32]])
            for gg in range(2):
                P.dma(lambda e, gg=gg: e.dma_start(out=A.v(o_c + gg * 16, [[1, 16]]),
                                                   in_=cond.ap()[gg].rearrange("(kc p) -> p kc", p=128)), writes=["cT"])
            P.op("scalar", lambda e: e.activation(sT, cT, AF.Silu), reads=["cT"], writes=["sT"])
            mb = A.v(o_mb, [[1, 6 * D]], np_=2)
            for layer in range(2):
                P.dma(lambda e, layer=layer: e.dma_start(out=mb, in_=bc_rows(mod_b, layer * 6 * D, 6 * D, 2)), writes=["mb"])
                for ch in range(24):
                    k = ch % 2
                    wt = A.v(o_w[k], [[512, 16], [1, 512]])
                    P.dma(lambda e, layer=layer, ch=ch, wt=wt: e.dma_start(
                        out=wt, in_=mod_w.ap()[layer, :, ch * 512:(ch + 1) * 512].rearrange("(kc p) n -> p kc n", p=128)),
                        writes=["mw%d" % k])
                    pb = psb[ch % 2]
                    for kc in range(16):
                        P.op("tensor", lambda e, kc=kc, k=k, pb=pb: e.matmul(
                            pb[0:2, :], A.v(o_s + kc, [[16, 2]]), A.v(o_w[k] + kc * 512, [[1, 512]]),
                            start=(kc == 0), stop=(kc == 15)),
                            reads=["sT", "mw%d" % k], writes=["psm%d" % (ch % 2)])
                    P.op("vector", lambda e, ch=ch, pb=pb: e.tensor_tensor(
                        A.v(o_mb + ch * 512, [[1, 512]], np_=2), A.v(o_mb + ch * 512, [[1, 512]], np_=2), pb[0:2, :], ALU.add),
                        reads=["psm%d" % (ch % 2), "mb"], writes=["mb"])
                for so in (D, 4 * D):
                    P.op("vector", lambda e, so=so: e.tensor_scalar_add(
                        A.v(o_mb + so, [[1, D]], np_=2), A.v(o_mb + so, [[1, D]], np_=2), 1.0), reads=["mb"], writes=["mb"])
                P.dma(lambda e, layer=layer: e.dma_start(out=modv.ap()[layer], in_=mb), reads=["mb"], writes=["modv"])
            P.barrier()

        def stage_norm_gemm(g, T, x_ap, layer, which, normw, W_ap, N, out_t, hm_out=None, srcT=None, resid=None):
            A.reset()
            sc_off = (1 if which == 1 else 4) * D
            sh_off = (0 if which == 1 else 3) * D
            o_G = A.f32(D); o_SH = A.f32(D); o_h = A.f32(D)
            o_x = [A.f32(D), A.f32(D)]
            o_w = A.f32(16 * 512)
            o_ot = [A.f32(512), A.f32(512)]
            o_xr = [A.f32(512), A.f32(512)]
            o_ss = A.f32(8)
            o_hb = A.f32(D // 2)
            o_hT = A.f32(16 * 1024 // 2)
            o_wb = [A.f32(16 * 512 // 2), A.f32(16 * 512 // 2)]
            Gb = A.v(o_G, [[1, D]]); SHb = A.v(o_SH, [[1, D]]); hh = A.v(o_h, [[1, D]])
            hb = A.bf(o_hb, D // 2)
            hT = A.bf(o_hT, 16 * 1024 // 2).rearrange("p (kc t) -> p kc t", kc=16)
            if srcT is None:
                P.dma(lambda e: e.dma_start(out=Gb, in_=bc_rows(normw, layer * D, D)), writes=["Gb"])
                P.dma(lambda e: e.dma_start(out=hh, in_=bc_rows(modv, (layer * 2 + g) * 6 * D + sc_off, D)), reads=["modv"], writes=["hh"])
                P.dma(lambda e: e.dma_start(out=SHb, in_=bc_rows(modv, (layer * 2 + g) * 6 * D + sh_off, D)), reads=["modv"], writes=["SHb"])
                P.op("vector", lambda e: e.tensor_tensor(Gb, Gb, hh, ALU.mult), reads=["Gb", "hh"], writes=["Gb"])
            if resid is not None:
                P.dma(lambda e: e.dma_start(out=SHb, in_=bc_rows(modv, (layer * 2 + g) * 6 * D + resid[1], D)), reads=["modv"], writes=["SHb"])
            nchunks = (N + 511) // 512
            for blk in range(T // 1024):
                if srcT is not None:
                    P.dma(lambda e, blk=blk: e.dma_start(out=hT, in_=srcT.ap()[:, :, blk * 1024:(blk + 1) * 1024].rearrange("kc p t -> p kc t")),
                          reads=[srcT.name], writes=["hT"])
                for tt in range(8 if srcT is None else 0):
                    t0 = blk * 1024 + tt * 128
                    k = tt % 2
                    xt = A.v(o_x[k], [[1, D]])
                    P.dma(lambda e, xt=xt, t0=t0: e.dma_start(out=xt, in_=x_ap[t0:t0 + 128, :]), writes=["x%d" % k])
                    ss = A.v(o_ss + k, [[1, 1]]); rs = A.v(o_ss + 2 + k, [[1, 1]])
                    P.op("scalar", lambda e, xt=xt, ss=ss: e.activation(hh, xt, AF.Square, accum_out=ss),
                         reads=["x%d" % k], writes=["hh", "ss%d" % k])
                    P.op("vector", lambda e, ss=ss, rs=rs: e.tensor_copy(rs, ss), reads=["ss%d" % k], writes=["rs%d" % k])
                    rsqrt_ops(P, rs, 1.0 / D, 1e-6, "rs%d" % k)
                    P.op("vector", lambda e, xt=xt, rs=rs: e.scalar_tensor_tensor(hh, xt, rs, Gb, ALU.mult, ALU.mult),
                         reads=["x%d" % k, "rs%d" % k, "Gb"], writes=["hh"])
                    if hm_out is not None:
                        P.op("gpsimd", lambda e: e.tensor_tensor(hh, hh, SHb, ALU.add), reads=["hh", "SHb"], writes=["hh"])
                        P.dma(lambda e, t0=t0: e.dma_start(out=hm_out.ap()[t0:t0 + 128, :], in_=hh), reads=["hh"], writes=[hm_out.name])
                        P.op("scalar", lambda e: e.activation(hb, hh, AF.Copy), reads=["hh"], writes=["hb"])
                    else:
                        P.op("gpsimd", lambda e: e.tensor_tensor(hb, hh, SHb, ALU.add), reads=["hh", "SHb"], writes=["hb"])
                    for kc in range(16):
                        P.op("tensor", lambda e, kc=kc: e.transpose(psT[:, kc * 128:(kc + 1) * 128], hb[:, kc * 128:(kc + 1) * 128], identb),
                             reads=["hb", "identb"], writes=["psT"])
                    P.op("scalar", lambda e, tt=tt: e.activation(hT[:, :, tt * 128:(tt + 1) * 128],
                                                                 psT[:, :].rearrange("p (kc t) -> p kc t", kc=16), AF.Copy),
                         reads=["psT"], writes=["hT"])
                for ch in range(nchunks):
                    n0 = ch * 512
                    nw = min(512, N - n0)
                    kb = ch % 2
                    wt = A.v(o_w, [[512, 16], [1, nw]])
                    wb = A.bf(o_wb[kb], 16 * 512 // 2).rearrange("p (kc n) -> p kc n", kc=16)
                    P.dma(lambda e, wt=wt, n0=n0, nw=nw: e.dma_start(
                        out=wt, in_=W_ap[:, n0:n0 + nw].rearrange("(kc p) n -> p kc n", p=128)), writes=["wt"])
                    P.op("gpsimd" if ch % 2 else "scalar",
                         (lambda e, wt=wt, wb=wb, nw=nw: e.tensor_copy(wb[:, :, 0:nw], wt)) if ch % 2 else
                         (lambda e, wt=wt, wb=wb, nw=nw: e.activation(wb[:, :, 0:nw], wt, AF.Copy)),
                         reads=["wt"], writes=["wb%d" % kb])
                    for tt in range(8):
                        t0 = blk * 1024 + tt * 128
                        pi = (ch * 8 + tt) % 4
                        pb = psb[pi]
                        for kc in range(16):
                            P.op("tensor", lambda e, kc=kc, tt=tt, pb=pb, wb=wb, nw=nw: e.matmul(
                                pb[:, 0:nw], hT[:, kc, tt * 128:(tt + 1) * 128], wb[:, kc, 0:nw],
                                start=(kc == 0), stop=(kc == 15)),
                                reads=["hT", "wb%d" % kb], writes=["pg%d" % pi])
                        ko = tt % 2
                        ot = A.v(o_ot[ko], [[1, nw]])
                        if resid is None:
                            P.op("vector", lambda e, ot=ot, pb=pb, nw=nw: e.tensor_copy(ot, pb[:, 0:nw]),
                                 reads=["pg%d" % pi], writes=["ot%d" % ko])
                        else:
                            xr = A.v(o_xr[ko], [[1, nw]])
                            P.dma(lambda e, xr=xr, t0=t0, n0=n0, nw=nw: e.dma_start(out=xr, in_=resid[0][t0:t0 + 128, n0:n0 + nw]),
                                  writes=["xr%d" % ko])
                            P.op("vector", lambda e, ot=ot, pb=pb, nw=nw, n0=n0: e.tensor_tensor(ot, pb[:, 0:nw], A.v(o_SH + n0, [[1, nw]]), ALU.mult),
                                 reads=["pg%d" % pi, "SHb"], writes=["ot%d" % ko])
                            P.op("gpsimd", lambda e, ot=ot, xr=xr: e.tensor_tensor(ot, ot, xr, ALU.add),
                                 reads=["ot%d" % ko, "xr%d" % ko], writes=["ot%d" % ko])
                        P.dma(lambda e, ot=ot, t0=t0, n0=n0, nw=nw: e.dma_start(out=out_t.ap()[t0:t0 + 128, n0:n0 + nw], in_=ot),
                              reads=["ot%d" % ko], writes=[out_t.name])
            P.barrier()

        def sq_store(g, nm, t0, tile_ap, rd, wr):
            off = tile_ap.offset
            for hh in range(2):
                src = bass.AP(A.t, off + hh * 64, [[A.size, 128], [128, 8], [1, 64]])
                dst = bass.AP(sq[g][nm], t0 * 1024 + hh * 512, [[1024, 128], [64, 8], [1, 64]])
                P.dma(lambda e, src=src, dst=dst: e.dma_start(out=dst, in_=src), reads=rd, writes=wr)

        def stage_even_prep(g, T, L):
            A.reset()
            z = zg[g]
            o_cw = A.f32(3 * 3456)
            o_vec = A.f32(9 * 1024)
            o_za = A.f32(3456)
            o_ld = [A.f32(3456), A.f32(3456)]
            o_d = [A.f32(1024) for _ in range(4)]
            o_kk = A.f32(1024); o_tmp = A.f32(1024); o_g = A.f32(1024); o_bon = A.f32(1024)
            o_sm = A.f32(64)
            o_lr = A.f32(3 * 128 // 2); o_lrT = A.f32(3 * 128 // 2)
            o_lw = A.f32(3 * 1024 // 2); o_lwf = A.f32(1024)
            o_vt = A.f32(1024)
            CW = A.v(o_cw, [[3456, 3], [1, 3456]])
            P.dma(lambda e: e.dma_start(out=CW, in_=bass.AP(a_conv, 0, [[0, 128], [3456, 3], [1, 3456]])), writes=["CW"])
            vecsrc = [(a_w0, 0), (a_w0, 1024), (a_a0, 0), (a_a0, 1024), (a_kk, 0), (a_ka, 0), (a_rk, 0)]
            for i, (t_, off) in enumerate(vecsrc):
                P.dma(lambda e, i=i, t_=t_, off=off: e.dma_start(out=A.v(o_vec + i * 1024, [[1, 1024]]), in_=bc_rows(t_, off, 1024)),
                      writes=["vec"])
            vec = lambda i: A.v(o_vec + i * 1024, [[1, 1024]])
            lw = A.bf(o_lw, 3 * 1024 // 2).rearrange("p (j n) -> p j n", j=3)
            for j, t_ in enumerate((a_wu, a_au, a_gu)):
                lwf = A.v(o_lwf, [[1, 1024]])
                P.dma(lambda e, t_=t_, lwf=lwf: e.dma_start(out=lwf, in_=t_.ap()), writes=["lwf"])
                P.op("vector", lambda e, j=j, lwf=lwf: e.tensor_copy(lw[:, j, :], lwf), reads=["lwf"], writes=["lw"])
            lr = A.bf(o_lr, 3 * 128 // 2).rearrange("p (j n) -> p j n", j=3)
            lrT = A.bf(o_lrT, 3 * 128 // 2).rearrange("p (j n) -> p j n", j=3)
            ZA = A.v(o_za, [[1, 3456]])
            r_ = A.v(o_za, [[1, 1024]]); k_ = A.v(o_za + 1024, [[1, 1024]]); v_ = A.v(o_za + 2048, [[1, 1024]])
            KK = A.v(o_kk, [[1, 1024]]); TMP = A.v(o_tmp, [[1, 1024]]); G = A.v(o_g, [[1, 1024]]); BON = A.v(o_bon, [[1, 1024]])
            h3 = lambda o: A.v(o, [[64, 16], [1, 64]])
            hb3 = lambda o: A.v(o, [[1, 16], [0, 64]])
            tiles_per_seq = L // 128
            for ti in range(T // 128):
                t0 = ti * 128
                first = (ti % tiles_per_seq == 0)
                last = (ti % tiles_per_seq == tiles_per_seq - 1)
                ld = A.v(o_ld[0], [[1, 3456]])
                P.dma(lambda e, ld=ld, t0=t0: e.dma_start(out=ld, in_=z.ap()[t0:t0 + 128, 0:3456]), reads=[z.name], writes=["ld0"])
                P.op("vector", lambda e, ld=ld: e.tensor_tensor(ZA, ld, A.v(o_cw + 3456, [[1, 3456]]), ALU.mult),
                     reads=["ld0", "CW"], writes=["ZA"])
                ld = A.v(o_ld[1], [[1, 3456]])
                if first:
                    P.op("gpsimd", lambda e, ld=ld: e.memset(ld, 0.0), writes=["ld1"])
                    P.dma(lambda e, t0=t0: e.dma_start(out=A.v(o_ld[1], [[1, 3456]], p0=1, np_=127), in_=z.ap()[t0:t0 + 127, 0:3456]),
                          reads=[z.name], writes=["ld1"])
                else:
                    P.dma(lambda e, ld=ld, t0=t0: e.dma_start(out=ld, in_=z.ap()[t0 - 1:t0 + 127, 0:3456]), reads=[z.name], writes=["ld1"])
                P.op("gpsimd", lambda e, ld=ld: e.tensor_tensor(ld, ld, A.v(o_cw, [[1, 3456]]), ALU.mult), reads=["ld1", "CW"], writes=["ld1"])
                P.op("vector", lambda e, ld=ld: e.tensor_tensor(ZA, ZA, ld, ALU.add), reads=["ld1", "ZA"], writes=["ZA"])
                ld = A.v(o_ld[0], [[1, 3456]])
                if last:
                    P.op("gpsimd", lambda e, ld=ld: e.memset(ld, 0.0), reads=[], writes=["ld0"])
                    P.dma(lambda e, t0=t0: e.dma_start(out=A.v(o_ld[0], [[1, 3456]], p0=0, np_=127), in_=z.ap()[t0 + 1:t0 + 128, 0:3456]),
                          reads=[z.name], writes=["ld0"])
                else:
                    P.dma(lambda e, ld=ld, t0=t0: e.dma_start(out=ld, in_=z.ap()[t0 + 1:t0 + 129, 0:3456]), reads=[z.name], writes=["ld0"])
                P.op("gpsimd", lambda e, ld=ld: e.tensor_tensor(ld, ld, A.v(o_cw + 2 * 3456, [[1, 3456]]), ALU.mult), reads=["ld0", "CW"], writes=["ld0"])
                P.op("vector", lambda e, ld=ld: e.tensor_tensor(ZA, ZA, ld, ALU.add), reads=["ld0", "ZA"], writes=["ZA"])
                sq_store(g, "r", t0, r_, ["ZA"], ["sq_r"])
                P.op("scalar", lambda e: e.activation(lr[:, 0, :], A.v(o_za + 3072, [[1, 128]]), AF.Tanh), reads=["ZA"], writes=["lr"])
                P.op("scalar", lambda e: e.activation(lr[:, 1, :], A.v(o_za + 3200, [[1, 128]]), AF.Copy), reads=["ZA"], writes=["lr"])
                P.op("scalar", lambda e: e.activation(lr[:, 2, :], A.v(o_za + 3328, [[1, 128]]), AF.Sigmoid), reads=["ZA"], writes=["lr"])
                for j in range(3):
                    P.op("tensor", lambda e, j=j: e.transpose(psT[:, j * 128:(j + 1) * 128], lr[:, j, :], identb), reads=["lr", "identb"], writes=["psT"])
                P.op("vector", lambda e: e.tensor_copy(lrT, psT[:, 0:384].rearrange("p (j n) -> p j n", j=3)), reads=["psT"], writes=["lrT"])
                for hf in range(2):
                    P.op("tensor", lambda e, hf=hf: e.matmul(psb[hf][:, :], lrT[:, 2, :], lw[:, 2, hf * 512:(hf + 1) * 512], start=True, stop=True),
                         reads=["lrT", "lw"], writes=["pe%d" % hf])
                    P.op("scalar", lambda e, hf=hf: e.activation(A.v(o_g + hf * 512, [[1, 512]]), psb[hf][:, :], AF.Copy),
                         reads=["pe%d" % hf], writes=["G"])
                P.dma(lambda e, t0=t0: e.dma_start(out=gsc[g].ap()[t0:t0 + 128, :], in_=G), reads=["G"], writes=["gsc"])
                P.op("vector", lambda e: e.tensor_tensor(KK, k_, vec(4), ALU.mult), reads=["ZA", "vec"], writes=["KK"])
                P.op("gpsimd", lambda e: e.tensor_tensor(TMP, KK, KK, ALU.mult), reads=["KK"], writes=["TMP"])
                P.op("vector", lambda e: e.tensor_reduce(A.v(o_sm, [[1, 16]]), h3(o_tmp), AX.X, ALU.add), reads=["TMP"], writes=["sm"])
                rsqrt_ops(P, A.v(o_sm, [[1, 16]]), 1.0, 1e-12, "sm")
                P.op("vector", lambda e: e.tensor_tensor(h3(o_kk), h3(o_kk), hb3(o_sm), ALU.mult), reads=["sm", "KK"], writes=["KK"])
                sq_store(g, "kk", t0, KK, ["KK"], ["sq_kk"])
                P.op("gpsimd", lambda e: e.tensor_tensor(TMP, r_, k_, ALU.mult), reads=["ZA"], writes=["TMP"])
                P.op("gpsimd", lambda e: e.tensor_tensor(TMP, TMP, vec(6), ALU.mult), reads=["TMP", "vec"], writes=["TMP"])
                P.op("vector", lambda e: e.tensor_reduce(A.v(o_sm + 16, [[1, 16]]), h3(o_tmp), AX.X, ALU.add), reads=["TMP"], writes=["sm2"])
                P.op("vector", lambda e: e.tensor_tensor(h3(o_bon), h3(o_za + 2048), hb3(o_sm + 16), ALU.mult), reads=["sm2", "ZA"], writes=["BON"])
                P.dma(lambda e, t0=t0: e.dma_start(out=bon[g].ap()[t0:t0 + 128, :], in_=BON), reads=["BON"], writes=["bon"])
                vt = A.v(o_vt, [[128, 8], [1, 128]])
                for hf in range(2):
                    for j in range(4):
                        c = hf * 4 + j
                        P.op("tensor", lambda e, c=c, hf=hf, j=j: e.transpose(psb[2 + hf][:, j * 128:(j + 1) * 128],
                                                                             A.v(o_za + 2048 + c * 128, [[1, 128]]), ident),
                             reads=["ZA", "ident"], writes=["pv%d" % hf])
                    P.op("scalar", lambda e, hf=hf: e.activation(A.v(o_vt + hf * 512, [[1, 512]]), psb[2 + hf][:, :], AF.Copy),
                         reads=["pv%d" % hf], writes=["vt"])
                P.dma(lambda e, t0=t0, vt=vt: e.dma_start(out=vT[g].ap()[:, :, t0:t0 + 128].rearrange("c p t -> p c t"), in_=vt),
                      reads=["vt"], writes=["vT"])
                for d in range(2):
                    Wd, Ad, KTd, Bd = (A.v(o, [[1, 1024]]) for o in o_d)
                    for hf in range(2):
                        P.op("tensor", lambda e, d=d, hf=hf: e.matmul(psb[hf][:, :], lrT[64 * d:64 * d + 64, 0, :],
                                                                      lw[64 * d:64 * d + 64, 0, hf * 512:(hf + 1) * 512], start=True, stop=True),
                             reads=["lrT", "lw"], writes=["pe%d" % hf])
                        P.op("vector", lambda e, d=d, hf=hf: e.tensor_tensor(A.v(o_d[0] + hf * 512, [[1, 512]]), psb[hf][:, :],
                                                                            A.v(o_vec + d * 1024 + hf * 512, [[1, 512]]), ALU.add),
                             reads=["pe%d" % hf, "vec"], writes=["Wd"])
                    P.op("scalar", lambda e, Wd=Wd: e.activation(Wd, Wd, AF.Sigmoid), reads=["Wd"], writes=["Wd"])
                    P.op("scalar", lambda e, Wd=Wd: e.activation(Wd, Wd, AF.Exp, scale=-math.exp(-0.5)), reads=["Wd"], writes=["Wd"])
                    sq_store(g, "w%d" % d, t0, Wd, ["Wd"], ["sq_w"])
                    for hf in range(2):
                        P.op("tensor", lambda e, d=d, hf=hf: e.matmul(psb[hf][:, :], lrT[64 * d:64 * d + 64, 1, :],
                                                                      lw[64 * d:64 * d + 64, 1, hf * 512:(hf + 1) * 512], start=True, stop=True),
                             reads=["lrT", "lw"], writes=["pe%d" % hf])
                        P.op("vector", lambda e, d=d, hf=hf: e.tensor_tensor(A.v(o_d[1] + hf * 512, [[1, 512]]), psb[hf][:, :],
                                                                            A.v(o_vec + (2 + d) * 1024 + hf * 512, [[1, 512]]), ALU.add),
                             reads=["pe%d" % hf, "vec"], writes=["Ad"])
                    P.op("scalar", lambda e, Ad=Ad: e.activation(Ad, Ad, AF.Sigmoid), reads=["Ad"], writes=["Ad"])
                    P.op("vector", lambda e, Ad=Ad, KTd=KTd: e.scalar_tensor_tensor(KTd, Ad, -1.0, vec(5), ALU.add, ALU.mult),
                         reads=["Ad", "vec"], writes=["KTd"])
                    P.op("vector", lambda e, KTd=KTd: e.scalar_tensor_tensor(KTd, KTd, 1.0, k_, ALU.add, ALU.mult),
                         reads=["KTd", "ZA"], writes=["KTd"])
                    sq_store(g, "kt%d" % d, t0, KTd, ["KTd"], ["sq_kt"])
                    P.op("gpsimd", lambda e, Ad=Ad, Bd=Bd: e.tensor_tensor(Bd, KK, Ad, ALU.mult), reads=["KK", "Ad"], writes=["Bd"])
                    sq_store(g, "b%d" % d, t0, Bd, ["Bd"], ["sq_b"])
            P.barrier()

        def stage_scan(g, seq, L, d, s0_ap=None, sfin_ap=None):
            A.reset()
            if SCAN_DVE_INORDER:
                P.no_self.add("vector")
            SB = 4
            o_S = A.f32(512); o_tmp = A.f32(512); o_sa = A.f32(8)
            o_KV = [A.f32(512), A.f32(512)]
            o_X = [A.f32(5 * SB * 512), A.f32(5 * SB * 512)]
            VB = 128
            o_V = [A.f32(8 * VB), A.f32(8 * VB)]
            o_Y = [A.f32(8 * VB), A.f32(8 * VB)]
            S = A.v(o_S, [[1, 512]]); TMP = A.v(o_tmp, [[1, 512]])
            S3 = A.v(o_S, [[64, 8], [1, 64]]); TMP3 = A.v(o_tmp, [[64, 8], [1, 64]])
            sa = A.v(o_sa, [[1, 8]]); sab = A.v(o_sa, [[1, 8], [0, 64]])
            base = seq * L
            for hh in range(2):
                dst = A.v(o_S, [[64, 8], [1, 64]], p0=64 * hh, np_=64)
                if s0_ap is not None:
                    src = bass.AP(s0_ap, (d * 16 + hh) * 4096, [[64, 64], [2 * 4096, 8], [1, 64]])
                    P.dma(lambda e, dst=dst, src=src: e.dma_start(out=dst, in_=src), writes=["S"])
                else:
                    src = bass.AP(zero_d, 0, [[64, 64], [0, 8], [1, 64]])
                    P.dma(lambda e, dst=dst, src=src: e.dma_start(out=dst, in_=src), reads=["zero_d"], writes=["S"])
            names = ["kk", "w%d" % d, "b%d" % d, "kt%d" % d, "r"]
            for blk in range(L // SB):
                kx = blk % 2
                if d == 0:
                    tok0 = blk * SB
                else:
                    tok0 = L - (blk + 1) * SB
                for qi, nm in enumerate(names):
                    for hh in range(2):
                        dst = A.v(o_X[kx] + qi * SB * 512, [[512, SB], [1, 512]], p0=64 * hh, np_=64)
                        src = bass.AP(sq[g][nm], (base + tok0) * 1024 + hh * 512, [[0, 64], [1024, SB], [1, 512]])
                        P.dma(lambda e, dst=dst, src=src: e.dma_start(out=dst, in_=src), reads=["sq"], writes=["X%d_%d_%d" % (kx, qi, hh)],
                              q="gpsimd" if (qi % 2) else "sync")
                if (blk * SB) % VB == 0:
                    vb = (blk * SB) // VB
                    kv = vb % 2
                    vtok0 = vb * VB if d == 0 else L - (vb + 1) * VB
                    dst = A.v(o_V[kv], [[VB, 8], [1, VB]])
                    src = bass.AP(vT[g], base + vtok0, [[vT[g].shape[2], 128], [128 * vT[g].shape[2], 8], [1, VB]])
                    P.dma(lambda e, dst=dst, src=src: e.dma_start(out=dst, in_=src), reads=["vT"], writes=["V%d" % kv])
                for j in range(SB):
                    step = blk * SB + j
                    jj = j if d == 0 else SB - 1 - j
                    vb = step // VB
                    kv = vb % 2
                    sv = step % VB
                    vcol = sv if d == 0 else VB - 1 - sv
                    X = (lambda xs: (lambda qi: xs[qi]))([A.v(o_X[kx] + qi * SB * 512 + jj * 512, [[64, 8], [1, 64]]) for qi in range(5)])
                    vbc = A.v(o_V[kv] + vcol, [[VB, 8], [0, 64]])
                    ycol = A.v(o_Y[kv] + vcol, [[VB, 8]])
                    rk = lambda qi: ["X%d_%d_0" % (kx, qi), "X%d_%d_1" % (kx, qi)]
                    P.op("vector", lambda e, X=X: e.tensor_tensor(TMP3, S3, X(0), ALU.mult), reads=["S"] + rk(0), writes=["TMP"])
                    P.op("vector", lambda e: e.tensor_reduce(sa, TMP3, AX.X, ALU.add), reads=["TMP"], writes=["sa"])
                    P.op("vector", lambda e, X=X: e.tensor_tensor(S3, S3, X(1), ALU.mult), reads=["S"] + rk(1), writes=["S"])
                    P.op("vector", lambda e, X=X: e.tensor_tensor(TMP3, X(2), sab, ALU.mult), reads=["sa"] + rk(2), writes=["TMP"])
                    P.op("vector", lambda e: e.tensor_tensor(S3, S3, TMP3, ALU.subtract), reads=["S", "TMP"], writes=["S"])
                    kvk = step % 2
                    KV3 = A.v(o_KV[kvk], [[64, 8], [1, 64]])
                    P.op("gpsimd", lambda e, X=X, vbc=vbc, KV3=KV3: e.tensor_tensor(KV3, X(3), vbc, ALU.mult), reads=["V%d" % kv] + rk(3), writes=["KV%d" % kvk])
                    P.op("vector", lambda e, KV3=KV3: e.tensor_tensor(S3, S3, KV3, ALU.add), reads=["S", "KV%d" % kvk], writes=["S"])
                    P.op("vector", lambda e, X=X: e.tensor_tensor(TMP3, S3, X(4), ALU.mult), reads=["S"] + rk(4), writes=["TMP"])
                    P.op("vector", lambda e, ycol=ycol: e.tensor_reduce(ycol, TMP3, AX.X, ALU.add), reads=["TMP"], writes=["Y%d" % kv])
                    if sv == VB - 1:
                        ytok0 = vb * VB if d == 0 else L - (vb + 1) * VB
                        srcy = A.v(o_Y[kv], [[VB, 8], [1, VB]])
                        dsty = bass.AP(yT[g][d], base + ytok0, [[yT[g][d].shape[2], 128], [128 * yT[g][d].shape[2], 8], [1, VB]])
                        P.dma(lambda e, srcy=srcy, dsty=dsty: e.dma_start(out=dsty, in_=srcy), reads=["Y%d" % kv], writes=["yT"])
            if sfin_ap is not None:
                for hh in range(2):
                    srcS = A.v(o_S, [[64, 8], [1, 64]], p0=64 * hh, np_=64)
                    dstS = bass.AP(sfin_ap, ((seq * 2 + d) * 16 + hh) * 4096, [[64, 64], [2 * 4096, 8], [1, 64]])
                    P.dma(lambda e, srcS=srcS, dstS=dstS: e.dma_start(out=dstS, in_=srcS), reads=["S"], writes=["ns"], is_output=True)
            P.no_self.discard("vector")
            P.barrier()

        def stage_rwkv_post(g, T):
            A.reset()
            o_y0 = A.f32(1024); o_y1 = A.f32(1024); o_Y = A.f32(1024); o_YC = A.f32(1024); o_SQ = A.f32(1024)
            o_bon = A.f32(1024); o_g = A.f32(1024); o_lnw = A.f32(1024); o_lnb = A.f32(1024); o_sm = A.f32(64)
            o_yb = A.f32(512); o_yT = A.f32(512)
            LNW = A.v(o_lnw, [[1, 1024]]); LNB = A.v(o_lnb, [[1, 1024]])
            P.dma(lambda e: e.dma_start(out=LNW, in_=bc_rows(a_lnw, 0, 1024)), writes=["LNW"])
            P.dma(lambda e: e.dma_start(out=LNB, in_=bc_rows(a_lnb, 0, 1024)), writes=["LNB"])
            Y = A.v(o_Y, [[1, 1024]]); YC = A.v(o_YC, [[1, 1024]]); SQ = A.v(o_SQ, [[1, 1024]])
            BON = A.v(o_bon, [[1, 1024]]); G = A.v(o_g, [[1, 1024]])
            h3 = lambda o: A.v(o, [[64, 16], [1, 64]])
            hb3 = lambda o: A.v(o, [[1, 16], [0, 64]])
            yb = A.bf(o_yb, 512)
            yT8 = A.bf(o_yT, 512).rearrange("p (c t) -> p c t", c=8)
            for ti in range(T // 128):
                t0 = ti * 128
                y0 = A.v(o_y0, [[128, 8], [1, 128]]); y1 = A.v(o_y1, [[128, 8], [1, 128]])
                for d, yy in ((0, y0), (1, y1)):
                    src = bass.AP(yT[g][d], t0, [[T, 128], [128 * T, 8], [1, 128]])
                    P.dma(lambda e, yy=yy, src=src: e.dma_start(out=yy, in_=src), reads=["yT"], writes=["y%d" % d])
                P.op("vector", lambda e: e.tensor_tensor(A.v(o_y0, [[1, 1024]]), A.v(o_y0, [[1, 1024]]), A.v(o_y1, [[1, 1024]]), ALU.add),
                     reads=["y0", "y1"], writes=["y0"])
                for c in range(8):
                    P.op("tensor", lambda e, c=c: e.transpose(psb[c // 4][:, (c % 4) * 128:(c % 4 + 1) * 128], A.v(o_y0 + c * 128, [[1, 128]]), ident),
                         reads=["y0", "ident"], writes=["pp%d" % (c // 4)])
                for hf in range(2):
                    P.op("scalar", lambda e, hf=hf: e.activation(A.v(o_Y + hf * 512, [[1, 512]]), psb[hf][:, :], AF.Copy), reads=["pp%d" % hf], writes=["Y"])
                P.op("vector", lambda e: e.tensor_reduce(A.v(o_sm, [[1, 16]]), h3(o_Y), AX.X, ALU.add), reads=["Y"], writes=["mu"])
                P.op("vector", lambda e: e.tensor_scalar_mul(A.v(o_sm, [[1, 16]]), A.v(o_sm, [[1, 16]]), 1.0 / 64), reads=["mu"], writes=["mu"])
                P.op("vector", lambda e: e.tensor_tensor(h3(o_YC), h3(o_Y), hb3(o_sm), ALU.subtract), reads=["Y", "mu"], writes=["YC"])
                P.op("gpsimd", lambda e: e.tensor_tensor(SQ, YC, YC, ALU.mult), reads=["YC"], writes=["SQ"])
                P.op("vector", lambda e: e.tensor_reduce(A.v(o_sm + 16, [[1, 16]]), h3(o_SQ), AX.X, ALU.add), reads=["SQ"], writes=["var"])
                rsqrt_ops(P, A.v(o_sm + 16, [[1, 16]]), 1.0 / 64, 64e-5, "var")
                P.op("vector", lambda e: e.tensor_tensor(h3(o_YC), h3(o_YC), hb3(o_sm + 16), ALU.mult), reads=["YC", "var"], writes=["YC"])
                P.op("gpsimd", lambda e: e.tensor_tensor(YC, YC, LNW, ALU.mult), reads=["YC", "LNW"], writes=["YC"])
                P.op("vector", lambda e: e.tensor_tensor(YC, YC, LNB, ALU.add), reads=["YC", "LNB"], writes=["YC"])
                P.dma(lambda e, t0=t0: e.dma_start(out=BON, in_=bon[g].ap()[t0:t0 + 128, :]), reads=["bon"], writes=["BON"])
                P.dma(lambda e, t0=t0: e.dma_start(out=G, in_=gsc[g].ap()[t0:t0 + 128, :]), reads=["gsc"], writes=["G"])
                P.op("gpsimd", lambda e: e.tensor_tensor(YC, YC, BON, ALU.add), reads=["YC", "BON"], writes=["YC"])
                P.op("vector", lambda e: e.tensor_tensor(yb, YC, G, ALU.mult), reads=["YC", "G"], writes=["yb"])
                for c in range(8):
                    P.op("tensor", lambda e, c=c: e.transpose(psT[:, c * 128:(c + 1) * 128], yb[:, c * 128:(c + 1) * 128], identb),
                         reads=["yb", "identb"], writes=["psT"])
                P.op("scalar", lambda e: e.activation(yT8, psT[:, 0:1024].rearrange("p (c t) -> p c t", c=8), AF.Copy), reads=["psT"], writes=["yT8"])
                P.dma(lambda e, t0=t0: e.dma_start(out=ycatT[g].ap()[0:8, :, t0:t0 + 128].rearrange("c p t -> p c t"), in_=yT8),
                      reads=["yT8"], writes=[ycatT[g].name])
            P.barrier()

        def stage_attn_prep(g, T, L, latent):
            A.reset()
            o_z = A.f32(1536); o_sq = A.f32(1280); o_qr = A.f32(1280); o_sm = A.f32(16)
            o_qn = A.f32(128); o_kn = A.f32(128); o_cs = A.f32(128); o_t = [A.f32(640) for _ in range(4)]
            o_qb = A.f32(640); o_qT = A.f32(640); o_vb = A.f32(128)
            QN = A.v(o_qn, [[1, 128]]); KN = A.v(o_kn, [[1, 128]])
            P.dma(lambda e: e.dma_start(out=QN, in_=bc_rows(b_qn, 0, 128)), writes=["QN"])
            P.dma(lambda e: e.dma_start(out=KN, in_=bc_rows(b_kn, 0, 128)), writes=["KN"])
            qb = A.bf(o_qb, 640)
            qT = A.bf(o_qT, 640).rearrange("p (h t) -> p h t", h=10)
            vb = A.bf(o_vb, 128)
            Lk = kT_d[g].shape[2] // (T // L)
            ntile = T // 128 + (4 if latent else 0)
            for ti in range(ntile):
                t0 = ti * 128
                cache = ti >= T // 128
                seq = t0 // L
                tin = t0 % L
                if not cache:
                    P.dma(lambda e, t0=t0: e.dma_start(out=A.v(o_z, [[1, 1536]]), in_=zg[g].ap()[t0:t0 + 128, 3456:4992]), reads=[zg[g].name], writes=["zq"])
                    P.op("gpsimd", lambda e: e.tensor_tensor(A.v(o_sq, [[1, 1280]]), A.v(o_z, [[1, 1280]]), A.v(o_z, [[1, 1280]]), ALU.mult), reads=["zq"], writes=["sqq"])
                    P.op("vector", lambda e: e.tensor_reduce(A.v(o_sm, [[1, 10]]), A.v(o_sq, [[128, 10], [1, 128]]), AX.X, ALU.add), reads=["sqq"], writes=["sm"])
                    rsqrt_ops(P, A.v(o_sm, [[1, 10]]), 1.0 / 128, 1e-6, "sm")
                    P.op("vector", lambda e: e.tensor_tensor(A.v(o_z, [[128, 10], [1, 128]]), A.v(o_z, [[128, 10], [1, 128]]), A.v(o_sm, [[1, 10], [0, 128]]), ALU.mult),
                         reads=["zq", "sm"], writes=["zq"])
                    P.op("vector", lambda e: e.tensor_tensor(A.v(o_z, [[128, 8], [1, 128]]), A.v(o_z, [[128, 8], [1, 128]]), A.v(o_qn, [[0, 8], [1, 128]]), ALU.mult),
                         reads=["zq", "QN"], writes=["zq"])
                    P.op("vector", lambda e: e.tensor_tensor(A.v(o_z + 1024, [[128, 2], [1, 128]]), A.v(o_z + 1024, [[128, 2], [1, 128]]), A.v(o_kn, [[0, 2], [1, 128]]), ALU.mult),
                         reads=["zq", "KN"], writes=["zq"])
                    if latent:
                        P.dma(lambda e, tin=tin: e.dma_start(out=A.v(o_cs, [[1, 64]]), in_=ropec.ap()[tin:tin + 128, :]), writes=["cs"])
                        P.dma(lambda e, tin=tin: e.dma_start(out=A.v(o_cs + 64, [[1, 64]]), in_=ropes.ap()[tin:tin + 128, :]), writes=["cs"])
                        X1 = A.v(o_z, [[128, 10], [64, 2], [1, 32]]); X2 = A.v(o_z + 32, [[128, 10], [64, 2], [1, 32]])
                        O1 = A.v(o_qr, [[128, 10], [64, 2], [1, 32]]); O2 = A.v(o_qr + 32, [[128, 10], [64, 2], [1, 32]])
                        Cb = A.v(o_cs, [[0, 10], [32, 2], [1, 32]]); Sb = A.v(o_cs + 64, [[0, 10], [32, 2], [1, 32]])
                        Tt = [A.v(o, [[64, 10], [32, 2], [1, 32]]) for o in o_t]
                        P.op("vector", lambda e: e.tensor_tensor(Tt[0], X1, Cb, ALU.mult), reads=["zq", "cs"], writes=["t0"])
                        P.op("gpsimd", lambda e: e.tensor_tensor(Tt[1], X2, Sb, ALU.mult), reads=["zq", "cs"], writes=["t1"])
                        P.op("vector", lambda e: e.tensor_tensor(Tt[2], X1, Sb, ALU.mult), reads=["zq", "cs"], writes=["t2"])
                        P.op("gpsimd", lambda e: e.tensor_tensor(Tt[3], X2, Cb, ALU.mult), reads=["zq", "cs"], writes=["t3"])
                        P.op("vector", lambda e: e.tensor_tensor(O1, Tt[0], Tt[1], ALU.subtract), reads=["t0", "t1"], writes=["qr"])
                        P.op("vector", lambda e: e.tensor_tensor(O2, Tt[2], Tt[3], ALU.add), reads=["t2", "t3"], writes=["qr"])
                        P.op("scalar", lambda e: e.activation(qb, A.v(o_qr, [[1, 1280]]), AF.Copy), reads=["qr"], writes=["qb"])
                    else:
                        P.op("scalar", lambda e: e.activation(qb, A.v(o_z, [[1, 1280]]), AF.Copy), reads=["zq"], writes=["qb"])
                    P.op("gpsimd", lambda e: e.tensor_copy(vb, A.v(o_z + 1280, [[1, 256]])), reads=["zq"], writes=["vb"])
                    h0 = 0
                    krow = seq * Lk + tin
                else:
                    c0 = (ti - T // 128) * 128
                    P.dma(lambda e, c0=c0: e.dma_start(out=A.v(o_z + 1024, [[1, 256]]), in_=ck.ap()[c0:c0 + 128, :]), writes=["zq"])
                    P.dma(lambda e, c0=c0: e.dma_start(out=A.v(o_z + 1280, [[1, 256]]), in_=cv.ap()[c0:c0 + 128, :]), writes=["zq"])
                    P.op("scalar", lambda e: e.activation(qb[:, 1024:1280], A.v(o_z + 1024, [[1, 256]]), AF.Copy), reads=["zq"], writes=["qb"])
                    P.op("gpsimd", lambda e: e.tensor_copy(vb, A.v(o_z + 1280, [[1, 256]])), reads=["zq"], writes=["vb"])
                    h0 = 8
                    krow = L + c0
                for h in range(h0, 10):
                    P.op("tensor", lambda e, h=h: e.transpose(psT[:, h * 128:(h + 1) * 128], qb[:, h * 128:(h + 1) * 128], identb),
                         reads=["qb", "identb"], writes=["psT"])
                P.op("vector", lambda e, h0=h0: e.tensor_copy(qT[:, h0:10, :], psT[:, h0 * 128:1280].rearrange("p (h t) -> p h t", h=10 - h0)),
                     reads=["psT"], writes=["qT"])
                if not cache:
                    P.dma(lambda e, t0=t0: e.dma_start(out=qT_d[g].ap()[:, :, t0:t0 + 128].rearrange("h p t -> p h t"), in_=qT[:, 0:8, :]),
                          reads=["qT"], writes=["qT_d"])
                P.dma(lambda e, krow=krow: e.dma_start(out=kT_d[g].ap()[:, :, krow:krow + 128].rearrange("h p t -> p h t"), in_=qT[:, 8:10, :]),
                      reads=["qT"], writes=["kT_d"])
                P.dma(lambda e, krow=krow: e.dma_start(out=V_d[g].ap()[krow:krow + 128, :], in_=vb), reads=["vb"], writes=["V_d"])
            P.barrier()

        def stage_attn(g, T, seq, L, Lk):
            A.reset()
            nkt = Lk // 128
            NQ = min(512, L)
            o_kT = A.f32(2 * Lk // 2); o_V = A.f32(nkt * 256 // 2)
            o_q = [A.f32(NQ // 2), A.f32(NQ // 2)]
            o_E = [A.f32(NQ // 2), A.f32(NQ // 2)]
            o_rec = A.f32(NQ); o_ob = [A.f32(NQ // 2), A.f32(NQ // 2)]
            o_one = A.f32(64)
            KT = A.bf(o_kT, 2 * Lk // 2).rearrange("p (h t) -> p h t", h=2)
            VV = A.bf(o_V, nkt * 256 // 2).rearrange("p (k n) -> p k n", k=nkt)
            ones = A.bf(o_one, 64)
            P.op("vector", lambda e: e.memset(ones, 1.0), writes=["ones"])
            P.dma(lambda e: e.dma_start(out=KT, in_=kT_d[g].ap()[:, :, seq * Lk:(seq + 1) * Lk].rearrange("h p t -> p h t")),
                  reads=["kT_d"], writes=["KT"])
            P.dma(lambda e: e.dma_start(out=VV, in_=V_d[g].ap()[seq * Lk:(seq + 1) * Lk, :].rearrange("(k p) n -> p k n", p=128)),
                  reads=["V_d"], writes=["VV"])
            scale = 128 ** -0.5
            it = 0
            for h in range(8):
                kv = h // 4
                for qb_ in range(L // NQ):
                    q0 = seq * L + qb_ * NQ
                    kq = (h * (L // NQ) + qb_) % 2
                    qt = A.bf(o_q[kq], NQ // 2)
                    P.dma(lambda e, qt=qt, h=h, q0=q0: e.dma_start(out=qt, in_=qT_d[g].ap()[h, :, q0:q0 + NQ]), reads=["qT_d"], writes=["q%d" % kq])
                    for kt in range(nkt):
                        ke = it % 2
                        it += 1
                        Et = A.bf(o_E[ke], NQ // 2)
                        P.op("tensor", lambda e, kt=kt, kv=kv, qt=qt, ke=ke: e.matmul(psb[ke][:, 0:NQ], KT[:, kv, kt * 128:(kt + 1) * 128], qt, start=True, stop=True),
                             reads=["KT", "q%d" % kq], writes=["pS%d" % ke])
                        P.op("scalar", lambda e, Et=Et, ke=ke: e.activation(Et, psb[ke][:, 0:NQ], AF.Exp, scale=scale),
                             reads=["pS%d" % ke], writes=["E%d" % ke])
                        P.op("tensor", lambda e, kt=kt, kv=kv, Et=Et: e.matmul(psb[2][:, 0:NQ], VV[:, kt, kv * 128:(kv + 1) * 128], Et,
                                                                               start=(kt == 0), stop=(kt == nkt - 1)),
                             reads=["VV", "E%d" % ke], writes=["pO"])
                        P.op("tensor", lambda e, kt=kt, Et=Et: e.matmul(psb[3][:, 0:NQ], ones, Et, start=(kt == 0), stop=(kt == nkt - 1)),
                             reads=["ones", "E%d" % ke], writes=["pD"])
                    rec = A.v(o_rec, [[1, NQ]])
                    ob = A.bf(o_ob[kq], NQ // 2)
                    P.op("vector", lambda e, rec=rec: e.reciprocal(rec, psb[3][:, 0:NQ]), reads=["pD"], writes=["rec"])
                    P.op("vector", lambda e, rec=rec, ob=ob: e.tensor_tensor(ob, psb[2][:, 0:NQ], rec, ALU.mult), reads=["pO", "rec"], writes=["ob%d" % kq])
                    P.dma(lambda e, ob=ob, h=h, q0=q0: e.dma_start(out=ycatT[g].ap()[8 + h, :, q0:q0 + NQ], in_=ob),
                          reads=["ob%d" % kq], writes=[ycatT[g].name])
            P.barrier()

        def stage_peer(g, T, layer, q_t, hm_t, xres_ap, out_ap, out_key, is_output=False):
            A.reset()
            NB = 8
            o_kn = A.f32(2048); o_kT = A.f32(2048); o_q = A.f32(2048); o_qT = A.f32(2048); o_S = A.f32(2048)
            o_X = A.f32(2048); o_ACC = A.f32(2048); o_jk = A.f32(2048); o_GT = A.f32(2048); o_xr = A.f32(2048)
            o_R = [A.f32(1024) for _ in range(NB)]
            o_V1 = A.f32(256); o_IDX = A.f32(256); o_IDF = A.f32(256); o_CAND = A.f32(256); o_EID = A.f32(256); o_CW = A.f32(256)
            o_W = A.f32(128); o_T = A.f32(128); o_E = A.f32(128); o_EI = A.f32(128); o_GATE = A.f32(128); o_DOT = A.f32(128); o_WG = A.f32(128)
            o_sm = A.f32(16)
            o_MK = A.f32(256); o_FF = A.f32(256); o_CD = A.f32(256)
            u32v = lambda off, dims: bass.AP(A.t, off, [[A.size, 128]] + [list(d_) for d_ in dims]).bitcast(U32)
            P.dma(lambda e: e.dma_start(out=u32v(o_MK, [[1, 256]]), in_=kconst.ap()[0]), writes=["MK"])
            P.dma(lambda e: e.dma_start(out=u32v(o_FF, [[1, 256]]), in_=kconst.ap()[1]), writes=["FF"])
            P.dma(lambda e: e.dma_start(out=u32v(o_CD, [[1, 256]]), in_=kconst.ap()[2]), writes=["CD"])
            ut = peer_ub[layer]; vt_ = peer_vb[layer]
            GT = A.v(o_GT, [[1, 2048]])
            P.dma(lambda e: e.dma_start(out=GT, in_=bc_rows(modv, (layer * 2 + g) * 6 * D + 5 * D, D)), reads=["modv"], writes=["GT"])
            P.dma(lambda e: e.dma_start(out=A.v(o_kn, [[128, 16], [1, 128]]), in_=peer_keys.ap()[layer].rearrange("h c n d -> n (h c) d")), writes=["kn"])
            for hc in range(16):
                P.op("tensor", lambda e, hc=hc: e.transpose(psb[hc // 4][:, (hc % 4) * 128:(hc % 4 + 1) * 128], A.v(o_kn + hc * 128, [[1, 128]]), ident),
                     reads=["kn", "ident"], writes=["pk%d" % (hc // 4)])
            for b4 in range(4):
                P.op("scalar", lambda e, b4=b4: e.activation(A.v(o_kT + b4 * 512, [[1, 512]]), psb[b4][:, :], AF.Copy), reads=["pk%d" % b4], writes=["kT"])
            EIu = bass.AP(A.t, o_EI, [[A.size, 128], [1, 128]]).bitcast(I32)
            IDXu = bass.AP(A.t, o_IDX, [[A.size, 128], [1, 256]]).bitcast(U32)
            for ti in range(T // 128):
                t0 = ti * 128
                P.dma(lambda e, t0=t0: e.dma_start(out=A.v(o_q, [[1, 2048]]), in_=q_t.ap()[t0:t0 + 128, :]), reads=[q_t.name], writes=["q"])
                P.dma(lambda e, t0=t0: e.dma_start(out=A.v(o_X, [[1, 2048]]), in_=hm_t.ap()[t0:t0 + 128, :]), reads=[hm_t.name], writes=["X"])
                P.dma(lambda e, t0=t0: e.dma_start(out=A.v(o_xr, [[1, 2048]]), in_=xres_ap[t0:t0 + 128, :]), writes=["xr"])
                for hc in range(16):
                    P.op("tensor", lambda e, hc=hc: e.transpose(psb[hc // 4][:, (hc % 4) * 128:(hc % 4 + 1) * 128], A.v(o_q + hc * 128, [[1, 128]]), ident),
                         reads=["q", "ident"], writes=["pk%d" % (hc // 4)])
                for b4 in range(4):
                    P.op("scalar", lambda e, b4=b4: e.activation(A.v(o_qT + b4 * 512, [[1, 512]]), psb[b4][:, :], AF.Copy), reads=["pk%d" % b4], writes=["qT"])
                for hc in range(16):
                    P.op("tensor", lambda e, hc=hc: e.matmul(psb[hc // 4][:, (hc % 4) * 128:(hc % 4 + 1) * 128], A.v(o_qT + hc * 128, [[1, 128]]),
                                                             A.v(o_kT + hc * 128, [[1, 128]]), start=True, stop=True),
                         reads=["qT", "kT"], writes=["pk%d" % (hc // 4)])
                for b4 in range(4):
                    P.op("scalar", lambda e, b4=b4: e.activation(A.v(o_S + b4 * 512, [[1, 512]]), psb[b4][:, :], AF.Copy), reads=["pk%d" % b4], writes=["S"])
                P.op("vector", lambda e: e.tensor_scalar_add(A.v(o_S, [[1, 2048]]), A.v(o_S, [[1, 2048]]), 64.0), reads=["S"], writes=["S"])
                P.op("vector", lambda e: e.tensor_tensor(u32v(o_S, [[128, 16], [1, 128]]), u32v(o_S, [[128, 16], [1, 128]]), u32v(o_MK, [[0, 16], [1, 128]]), ALU.bitwise_and),
                     reads=["S", "MK"], writes=["S"])
                P.op("vector", lambda e: e.tensor_tensor(u32v(o_S, [[128, 16], [1, 128]]), u32v(o_S, [[128, 16], [1, 128]]), u32v(o_CD, [[0, 16], [1, 128]]), ALU.bitwise_or),
                     reads=["S", "CD"], writes=["S"])
                Wk = A.v(o_W, [[1, 128]])
                for hc in range(16):
                    Sh = A.v(o_S + hc * 128, [[1, 128]])
                    va = A.v(o_V1 + hc * 16, [[1, 8]]); vb_ = A.v(o_V1 + hc * 16 + 8, [[1, 8]])
                    P.op("vector", lambda e, Sh=Sh, va=va: e.max(out=va, in_=Sh), reads=["S"], writes=["V1"])
                    P.op("vector", lambda e, Sh=Sh, va=va: e.match_replace(out=Wk, in_to_replace=va, in_values=Sh, imm_value=-1e30), reads=["S", "V1"], writes=["Wk"])
                    P.op("vector", lambda e, vb_=vb_: e.max(out=vb_, in_=Wk), reads=["Wk"], writes=["V1"])
                P.op("vector", lambda e: e.tensor_tensor(u32v(o_IDX, [[1, 256]]), u32v(o_V1, [[1, 256]]), u32v(o_FF, [[1, 256]]), ALU.bitwise_and), reads=["V1", "FF"], writes=["IDX"])
                P.op("vector", lambda e: e.tensor_copy(A.v(o_IDF, [[1, 256]]), u32v(o_IDX, [[1, 256]])), reads=["IDX"], writes=["IDF"])
                P.op("vector", lambda e: e.tensor_scalar(A.v(o_IDF, [[1, 256]]), A.v(o_IDF, [[1, 256]]), -1.0, 255.0, ALU.mult, ALU.add), reads=["IDF"], writes=["IDF"])
                P.op("vector", lambda e: e.memset(A.v(o_E, [[1, 128]]), 0.0), writes=["E"])
                CAND = A.v(o_CAND, [[1, 256]]); EID = A.v(o_EID, [[1, 256]]); CW = A.v(o_CW, [[1, 256]])
                for h in range(8):
                    c3 = A.v(o_CAND, [[16, 16], [1, 16]]); e3 = A.v(o_EID, [[16, 16], [1, 16]])
                    v1a = A.v(o_V1 + (2 * h) * 16, [[1, 16], [0, 16]]); v2b = A.v(o_V1 + (2 * h + 1) * 16, [[0, 16], [1, 16]])
                    i1a = A.v(o_IDF + (2 * h) * 16, [[1, 16], [0, 16]]); i2b = A.v(o_IDF + (2 * h + 1) * 16, [[0, 16], [1, 16]])
                    P.op("vector", lambda e, c3=c3, v1a=v1a, v2b=v2b: e.tensor_tensor(c3, v1a, v2b, ALU.add), reads=["V1"], writes=["CAND"])
                    P.op("vector", lambda e: e.tensor_tensor(u32v(o_CAND, [[1, 256]]), u32v(o_CAND, [[1, 256]]), u32v(o_MK, [[1, 256]]), ALU.bitwise_and), reads=["CAND", "MK"], writes=["CAND"])
                    P.op("vector", lambda e: e.tensor_tensor(u32v(o_CAND, [[1, 256]]), u32v(o_CAND, [[1, 256]]), u32v(o_CD, [[1, 256]]), ALU.bitwise_or), reads=["CAND", "CD"], writes=["CAND"])
                    P.op("vector", lambda e, e3=e3, i1a=i1a, i2b=i2b: e.scalar_tensor_tensor(e3, i1a, 128.0, i2b, ALU.mult, ALU.add), reads=["IDF"], writes=["EID"])
                    ta = A.v(o_T + h * 16, [[1, 8]]); tb = A.v(o_T + h * 16 + 8, [[1, 8]])
                    P.op("vector", lambda e, ta=ta: e.max(out=ta, in_=CAND), reads=["CAND"], writes=["T"])
                    P.op("vector", lambda e, ta=ta: e.match_replace(out=CW, in_to_replace=ta, in_values=CAND, imm_value=-1e30), reads=["CAND", "T"], writes=["CW"])
                    P.op("vector", lambda e, tb=tb: e.max(out=tb, in_=CW), reads=["CW"], writes=["T"])
                    for k in range(16):
                        P.op("vector", lambda e, h=h, k=k: e.scalar_tensor_tensor(CW, CAND, A.v(o_T + h * 16 + k, [[1, 1]]), EID, ALU.is_equal, ALU.mult,
                                                                                   accum_out=A.v(o_E + h * 16 + k, [[1, 1]])),
                             reads=["CAND", "T", "EID", "E"], writes=["CW", "E"])
                P.op("vector", lambda e: e.tensor_copy(EIu, A.v(o_E, [[1, 128]])), reads=["E"], writes=["EI"])
                P.op("vector", lambda e: e.tensor_tensor(A.v(o_GATE, [[16, 8], [1, 16]]), A.v(o_T, [[16, 8], [1, 16]]), A.v(o_T, [[16, 8], [0, 16]]), ALU.subtract),
                     reads=["T"], writes=["GATE"])
                P.op("scalar", lambda e: e.activation(A.v(o_GATE, [[1, 128]]), A.v(o_GATE, [[1, 128]]), AF.Exp), reads=["GATE"], writes=["GATE"])
                P.op("vector", lambda e: e.tensor_reduce(A.v(o_sm, [[1, 8]]), A.v(o_GATE, [[16, 8], [1, 16]]), AX.X, ALU.add), reads=["GATE"], writes=["sm"])
                P.op("vector", lambda e: e.reciprocal(A.v(o_sm, [[1, 8]]), A.v(o_sm, [[1, 8]])), reads=["sm"], writes=["sm"])
                P.op("vector", lambda e: e.tensor_tensor(A.v(o_GATE, [[16, 8], [1, 16]]), A.v(o_GATE, [[16, 8], [1, 16]]), A.v(o_sm, [[1, 8], [0, 16]]), ALU.mult),
                     reads=["GATE", "sm"], writes=["GATE"])
                P.op("vector", lambda e: e.memset(A.v(o_DOT, [[1, 128]]), 0.0), writes=["DOT"])
                gi = 0
                for hk in range(128):
                    kb = gi % NB; gi += 1
                    Rr = A.bf(o_R[kb], 1024)
                    P.dma(lambda e, Rr=Rr, hk=hk: e.indirect_dma_start(out=Rr, out_offset=None, in_=ut.ap(),
                                                                     in_offset=bass.IndirectOffsetOnAxis(ap=EIu[:, hk:hk + 1], axis=0)),
                          reads=["EI"], writes=["R%d" % kb], q="gpsimd")
                    P.op("vector", lambda e, Rr=Rr, hk=hk: e.scalar_tensor_tensor(A.v(o_jk, [[1, 2048]]), Rr, 1.0, A.v(o_X, [[1, 2048]]), ALU.mult, ALU.mult,
                                                                                 accum_out=A.v(o_DOT + hk, [[1, 1]])),
                         reads=["R%d" % kb, "X", "DOT"], writes=["jk", "DOT"], inorder=(hk > 0))
                P.op("scalar", lambda e: e.activation(A.v(o_WG, [[1, 128]]), A.v(o_DOT, [[1, 128]]), AF.Gelu), reads=["DOT"], writes=["WG"])
                P.op("vector", lambda e: e.tensor_tensor(A.v(o_WG, [[1, 128]]), A.v(o_WG, [[1, 128]]), A.v(o_GATE, [[1, 128]]), ALU.mult), reads=["WG", "GATE"], writes=["WG"])
                P.op("gpsimd", lambda e: e.memset(A.v(o_ACC, [[1, 2048]]), 0.0), writes=["ACC"])
                for hk in range(128):
                    kb = gi % NB; gi += 1
                    Rr = A.bf(o_R[kb], 1024)
                    P.dma(lambda e, Rr=Rr, hk=hk: e.indirect_dma_start(out=Rr, out_offset=None, in_=vt_.ap(),
                                                                     in_offset=bass.IndirectOffsetOnAxis(ap=EIu[:, hk:hk + 1], axis=0)),
                          reads=["EI"], writes=["R%d" % kb], q="gpsimd")
                    P.op("vector", lambda e, Rr=Rr, hk=hk: e.scalar_tensor_tensor(A.v(o_ACC, [[1, 2048]]), Rr, A.v(o_WG + hk, [[1, 1]]), A.v(o_ACC, [[1, 2048]]),
                                                                                 ALU.mult, ALU.add),
                         reads=["R%d" % kb, "WG", "ACC"], writes=["ACC"], inorder=(hk > 0))
                P.op("vector", lambda e: e.tensor_tensor(A.v(o_ACC, [[1, 2048]]), A.v(o_ACC, [[1, 2048]]), GT, ALU.mult), reads=["ACC", "GT"], writes=["ACC"])
                P.op("gpsimd", lambda e: e.tensor_tensor(A.v(o_ACC, [[1, 2048]]), A.v(o_ACC, [[1, 2048]]), A.v(o_xr, [[1, 2048]]), ALU.add), reads=["ACC", "xr"], writes=["ACC"])
                P.dma(lambda e, t0=t0: e.dma_start(out=out_ap[t0:t0 + 128, :], in_=A.v(o_ACC, [[1, 2048]])), reads=["ACC"], writes=[out_key], is_output=is_output)
            P.no_self.discard("vector")
            P.barrier()

        def stage_hy_conv3(g, T, L):
            A.reset()
            o_cw = A.f32(3 * 2048); o_cb = A.f32(2048); o_acc = A.f32(2048)
            o_ld = [A.f32(2048), A.f32(2048)]
            o_vb = A.f32(1024)
            vb = A.bf(o_vb, 1024)
            tps = L // 128
            for j in range(3):
                P.dma(lambda e, j=j: e.dma_start(out=A.v(o_cw, [[2048, 3], [1, 2048]]),
                                                 in_=bass.AP(c_conv, j * 2048, [[0, 128], [6144, 3], [1, 2048]])), writes=["CW"])
                P.dma(lambda e, j=j: e.dma_start(out=A.v(o_cb, [[1, 2048]]), in_=bc_rows(c_convb, j * 2048, 2048)), writes=["CB"])
                for ti in range(T // 128):
                    t0 = ti * 128
                    first = (ti % tps == 0); last = (ti % tps == tps - 1)
                    ACC = A.v(o_acc, [[1, 2048]])
                    c0 = j * 2048
                    ld = A.v(o_ld[0], [[1, 2048]])
                    P.dma(lambda e, ld=ld, t0=t0, c0=c0: e.dma_start(out=ld, in_=uz[g].ap()[t0:t0 + 128, c0:c0 + 2048]), reads=[uz[g].name], writes=["ld0"])
                    P.op("vector", lambda e, ld=ld: e.tensor_tensor(ACC, ld, A.v(o_cw + 2048, [[1, 2048]]), ALU.mult), reads=["ld0", "CW"], writes=["ACC"])
                    ld = A.v(o_ld[1], [[1, 2048]])
                    if first:
                        P.op("gpsimd", lambda e, ld=ld: e.memset(ld, 0.0), writes=["ld1"])
                        P.dma(lambda e, t0=t0, c0=c0: e.dma_start(out=A.v(o_ld[1], [[1, 2048]], p0=1, np_=127), in_=uz[g].ap()[t0:t0 + 127, c0:c0 + 2048]),
                              reads=[uz[g].name], writes=["ld1"])
                    else:
                        P.dma(lambda e, ld=ld, t0=t0, c0=c0: e.dma_start(out=ld, in_=uz[g].ap()[t0 - 1:t0 + 127, c0:c0 + 2048]), reads=[uz[g].name], writes=["ld1"])
                    P.op("gpsimd", lambda e, ld=ld: e.tensor_tensor(ld, ld, A.v(o_cw, [[1, 2048]]), ALU.mult), reads=["ld1", "CW"], writes=["ld1"])
                    P.op("vector", lambda e, ld=ld: e.tensor_tensor(ACC, ACC, ld, ALU.add), reads=["ld1", "ACC"], writes=["ACC"])
                    ld = A.v(o_ld[0], [[1, 2048]])
                    if last:
                        P.op("gpsimd", lambda e, ld=ld: e.memset(ld, 0.0), writes=["ld0"])
                        P.dma(lambda e, t0=t0, c0=c0: e.dma_start(out=A.v(o_ld[0], [[1, 2048]], p0=0, np_=127), in_=uz[g].ap()[t0 + 1:t0 + 128, c0:c0 + 2048]),
                              reads=[uz[g].name], writes=["ld0"])
                    else:
                        P.dma(lambda e, ld=ld, t0=t0, c0=c0: e.dma_start(out=ld, in_=uz[g].ap()[t0 + 1:t0 + 129, c0:c0 + 2048]), reads=[uz[g].name], writes=["ld0"])
                    P.op("gpsimd", lambda e, ld=ld: e.tensor_tensor(ld, ld, A.v(o_cw + 4096, [[1, 2048]]), ALU.mult), reads=["ld0", "CW"], writes=["ld0"])
                    P.op("vector", lambda e, ld=ld: e.tensor_tensor(ACC, ACC, ld, ALU.add), reads=["ld0", "ACC"], writes=["ACC"])
                    if j < 2:
                        P.op("vector", lambda e: e.tensor_tensor(ACC, ACC, A.v(o_cb, [[1, 2048]]), ALU.add), reads=["ACC", "CB"], writes=["ACC"])
                        P.dma(lambda e, t0=t0, c0=c0: e.dma_start(out=ug[g].ap()[t0:t0 + 128, c0:c0 + 2048], in_=ACC), reads=["ACC"], writes=[ug[g].name])
                    else:
                        P.op("vector", lambda e: e.tensor_tensor(vb, ACC, A.v(o_cb, [[1, 2048]]), ALU.add), reads=["ACC", "CB"], writes=["vb"])
                        P.dma(lambda e, t0=t0: e.dma_start(out=zb16[g][0].ap()[t0:t0 + 128, :], in_=vb), reads=["vb"], writes=[zb16[g][0].name])
            P.barrier()

        def sin_wrapped(dst, arg, m1, m2, key, okey):
            PI = math.pi
            for _ in range(2):
                P.op("vector", lambda e: e.tensor_scalar(m1, arg, -PI, 2 * PI, ALU.is_lt, ALU.mult), reads=[key], writes=[key + "m1"])
                P.op("vector", lambda e: e.tensor_scalar(m2, arg, PI, 2 * PI, ALU.is_gt, ALU.mult), reads=[key], writes=[key + "m2"])
                P.op("vector", lambda e: e.tensor_tensor(arg, arg, m1, ALU.add), reads=[key, key + "m1"], writes=[key])
                P.op("vector", lambda e: e.tensor_tensor(arg, arg, m2, ALU.subtract), reads=[key, key + "m2"], writes=[key])
            P.op("scalar", lambda e: e.activation(dst, arg, AF.Sin), reads=[key], writes=[okey])

        def stage_hy_filters(li, L):
            A.reset()
            NT = L // 128
            o_zT = A.f32(L); o_fw3 = A.f32(8192); o_h1 = A.f32(L); o_h2 = A.f32(L)
            o_FT = [A.f32(2048) for _ in range(4)]; o_SS = [A.f32(2048), A.f32(2048)]
            o_WIN = A.f32(2048); o_DEL = A.f32(2048); o_sq = A.f32(512)
            o_fw1 = A.f32(64); o_fw2 = A.f32(64); o_col = A.f32(8); o_arg = A.f32(512); o_m1 = A.f32(512); o_m2 = A.f32(512)
            o_tn = A.f32(NT); o_one = A.f32(1); o_row = A.f32(4096); o_hs = A.f32(1024)
            hsb = A.bf(o_hs, 1024)
            zT = A.v(o_zT, [[1, L]], np_=33)
            P.dma(lambda e: e.dma_start(out=zT, in_=zposT[li].ap()), writes=["zT"])
            P.dma(lambda e: e.dma_start(out=A.v(o_fw1, [[1, 64]], np_=33), in_=c_fw1.ap()), writes=["fw1"])
            P.dma(lambda e: e.dma_start(out=A.v(o_fw2, [[1, 64]], np_=64), in_=c_fw2.ap()), writes=["fw2"])
            P.dma(lambda e: e.dma_start(out=A.v(o_fw3, [[1, 8192]], np_=64), in_=c_fw3.ap()), writes=["fw3"])
            for i, t_ in enumerate((c_fb1, c_freq, c_fb2)):
                P.dma(lambda e, i=i, t_=t_: e.dma_start(out=A.v(o_col + i, [[1, 1]], np_=64), in_=bass.AP(t_, 0, [[1, 64], [1, 1]])), writes=["col"])
            P.dma(lambda e: e.dma_start(out=A.v(o_DEL, [[1, 2048]]), in_=bc_rows(deltas, 0, 2048)), writes=["DEL"])
            P.dma(lambda e: e.dma_start(out=A.v(o_tn, [[1, NT]]), in_=tneg[li].ap().rearrange("(n p) -> p n", p=128)), writes=["tn"])
            P.op("vector", lambda e: e.memset(A.v(o_one, [[1, 1]]), 1.0), writes=["one"])
            for n in range(2):
                P.op("vector", lambda e, n=n: e.memset(A.v(o_SS[n], [[1, 2048]]), 0.0), writes=["SS%d" % n])
            nb = max(1, L // 512)
            bw = min(512, L)
            for layer_i, (o_src, o_dst, o_w, kk, bcol, skey, wkey, okey) in enumerate(
                    ((o_zT, o_h1, o_fw1, 33, 0, "zT", "fw1", "h1"), (o_h1, o_h2, o_fw2, 64, 2, "h1", "fw2", "h2"))):
                for b in range(nb):
                    P.op("tensor", lambda e, b=b, o_src=o_src, o_w=o_w, kk=kk: e.matmul(psb[b % 2][0:64, 0:bw], A.v(o_w, [[1, 64]], np_=kk),
                                                                                      A.v(o_src + b * bw, [[1, bw]], np_=kk), start=True, stop=True),
                         reads=[skey, wkey], writes=["pf%d" % (b % 2)])
                    arg = A.v(o_arg, [[1, bw]], np_=64)
                    P.op("vector", lambda e, b=b, arg=arg, bcol=bcol: e.tensor_scalar(arg, psb[b % 2][0:64, 0:bw], A.v(o_col + bcol, [[1, 1]], np_=64),
                                                                                     A.v(o_col + 1, [[1, 1]], np_=64), ALU.add, ALU.mult),
                         reads=["pf%d" % (b % 2), "col"], writes=["arg"])
                    sin_wrapped(A.v(o_dst + b * bw, [[1, bw]], np_=64), arg, A.v(o_m1, [[1, bw]], np_=64), A.v(o_m2, [[1, bw]], np_=64), "arg", okey)
            for lt in range(NT):
                P.op("scalar", lambda e, lt=lt: e.activation(A.v(o_WIN, [[1, 2048]]), A.v(o_DEL, [[1, 2048]]), AF.Exp, scale=A.v(o_tn + lt, [[1, 1]])),
                     reads=["DEL", "tn"], writes=["WIN"])
                for cc in range(16):
                    nd = cc // 4; cch = cc % 4
                    P.op("tensor", lambda e, lt=lt, cc=cc: e.matmul(psb[cc % 4][:, :], A.v(o_h2 + lt * 128, [[1, 128]], np_=64),
                                                                    A.v(o_fw3 + cc * 512, [[1, 512]], np_=64), start=True, stop=True),
                         reads=["h2", "fw3"], writes=["pf%d" % (cc % 4)])
                    ft = A.v(o_FT[nd] + cch * 512, [[1, 512]])
                    P.op("vector", lambda e, cc=cc, ft=ft, cch=cch: e.tensor_tensor(ft, psb[cc % 4][:, :], A.v(o_WIN + cch * 512, [[1, 512]]), ALU.mult),
                         reads=["pf%d" % (cc % 4), "WIN"], writes=["FT%d" % nd])
                    P.op("gpsimd", lambda e, ft=ft: e.tensor_tensor(A.v(o_sq, [[1, 512]]), ft, ft, ALU.mult), reads=["FT%d" % nd], writes=["sqf"])
                    ssv = A.v(o_SS[nd // 2] + cch * 512, [[1, 512]])
                    P.op("vector", lambda e, ssv=ssv: e.tensor_tensor(ssv, ssv, A.v(o_sq, [[1, 512]]), ALU.add), reads=["sqf", "SS%d" % (nd // 2)], writes=["SS%d" % (nd // 2)])
                for n in range(2):
                    hf = A.v(o_FT[2 * n], [[1, 2048]]); hbk = A.v(o_FT[2 * n + 1], [[1, 2048]])
                    if lt == 0:
                        P.op("vector", lambda e, n=n: e.memset(A.v(o_FT[2 * n + 1], [[1, 2048]], np_=1), 0.0), reads=["FT%d" % (2 * n + 1)], writes=["FT%d" % (2 * n + 1)])
                    P.op("vector", lambda e, hf=hf, hbk=hbk: e.tensor_tensor(hsb[:, 0:2048], hf, hbk, ALU.add), reads=["FT%d" % (2 * n), "FT%d" % (2 * n + 1)], writes=["hs"])
                    P.dma(lambda e, lt=lt, n=n: e.dma_start(out=HS[li].ap()[lt * 128:(lt + 1) * 128, 2 * n, :], in_=hsb[:, 0:2048]), reads=["hs"], writes=["HS"])
                    P.op("vector", lambda e, hf=hf, hbk=hbk: e.tensor_tensor(hsb[:, 0:2048], hbk, hf, ALU.subtract), reads=["FT%d" % (2 * n), "FT%d" % (2 * n + 1)], writes=["hs"])
                    P.dma(lambda e, lt=lt, n=n: e.dma_start(out=HS[li].ap()[lt * 128:(lt + 1) * 128, 2 * n + 1, :], in_=hsb[:, 0:2048]), reads=["hs"], writes=["HS"])
            for n in range(2):
                for cch in range(4):
                    P.op("tensor", lambda e, n=n, cch=cch: e.matmul(psb[cch][0:1, :], A.v(o_one, [[1, 1]]), A.v(o_SS[n] + cch * 512, [[1, 512]]), start=True, stop=True),
                         reads=["one", "SS%d" % n], writes=["pf%d" % cch])
                    P.op("vector", lambda e, n=n, cch=cch: e.tensor_copy(A.v(o_row + n * 2048 + cch * 512, [[1, 512]], np_=1), psb[cch][0:1, :]),
                         reads=["pf%d" % cch], writes=["row"])
            rsqrt_ops(P, A.v(o_row, [[1, 4096]], np_=1), 1.0, 1e-12, "row")
            P.dma(lambda e: e.dma_start(out=bass.AP(nsc[li], 0, [[4096, 1], [1, 4096]]), in_=A.v(o_row, [[1, 4096]], np_=1)),
                  reads=["row"], writes=["nsc"])
            P.barrier()

        def hy_forward(li, L, Zc, o_mat, kt, data_keys, tag):
            NT = L // 128
            kb = kt % 2
            Cm = A.bf(o_mat[kb][0], NT * 64).rearrange("p (t k) -> p t k", t=NT)
            Sm = A.bf(o_mat[kb][1], NT * 64).rearrange("p (t k) -> p t k", t=NT)
            P.dma(lambda e: e.dma_start(out=Cm, in_=dftc[li].ap()[:, kt * 128:(kt + 1) * 128].rearrange("(t p) k -> p t k", p=128)), writes=["Cm%d" % kb])
            P.dma(lambda e: e.dma_start(out=Sm, in_=dfts[li].ap()[:, kt * 128:(kt + 1) * 128].rearrange("(t p) k -> p t k", p=128)), writes=["Sm%d" % kb])
            Zcos, Zsin = Zc if isinstance(Zc, tuple) else (Zc, Zc)
            pa = psb[2 * kb]; pbk = psb[2 * kb + 1]
            for tt in range(NT):
                P.op("tensor", lambda e, tt=tt: e.matmul(pa[:, :], Cm[:, tt, :], Zcos[:, tt, :], start=(tt == 0), stop=(tt == NT - 1)),
                     reads=["Cm%d" % kb] + data_keys, writes=["pA%d" % kb])
            for tt in range(NT):
                P.op("tensor", lambda e, tt=tt: e.matmul(pbk[:, :], Sm[:, tt, :], Zsin[:, tt, :], start=(tt == 0), stop=(tt == NT - 1)),
                     reads=["Sm%d" % kb] + data_keys, writes=["pB%d" % kb])
            return pa, pbk, kb

        def stage_hy_fspec(li, L):
            A.reset()
            NT = L // 128
            o_hs = A.f32(NT * 256); o_hd = A.f32(NT * 256)
            o_mat = [[A.f32(NT * 64), A.f32(NT * 64)], [A.f32(NT * 64), A.f32(NT * 64)]]
            o_wk = A.f32(NT); o_o = [A.f32(512), A.f32(512)]
            P.dma(lambda e: e.dma_start(out=A.v(o_wk, [[1, NT]]), in_=wkv[li].ap().rearrange("(n p) -> p n", p=128)), writes=["wk"])
            for n in range(2):
                for cch in range(4):
                    hsT = A.bf(o_hs, NT * 256).rearrange("p (t c) -> p t c", t=NT)
                    hdT = A.bf(o_hd, NT * 256).rearrange("p (t c) -> p t c", t=NT)
                    P.dma(lambda e, n=n, cch=cch, hsT=hsT: e.dma_start(out=hsT, in_=HS[li].ap()[:, 2 * n, cch * 512:(cch + 1) * 512].rearrange("(t p) c -> p t c", p=128)),
                          reads=["HS"], writes=["hsT"])
                    P.dma(lambda e, n=n, cch=cch, hdT=hdT: e.dma_start(out=hdT, in_=HS[li].ap()[:, 2 * n + 1, cch * 512:(cch + 1) * 512].rearrange("(t p) c -> p t c", p=128)),
                          reads=["HS"], writes=["hdT"])
                    for kt in range(NT):
                        pa, pbk, kb = hy_forward(li, L, (hsT, hdT), o_mat, kt, ["hsT", "hdT"], "f")
                        for j, (pp, key) in enumerate(((pa, "pA%d" % kb), (pbk, "pB%d" % kb))):
                            ot = A.v(o_o[j], [[1, 512]])
                            P.op("vector", lambda e, ot=ot, pp=pp, kt=kt: e.tensor_scalar(ot, pp[:, :], A.v(o_wk + kt, [[1, 1]]), None, ALU.mult),
                                 reads=[key, "wk"], writes=["o%d" % j])
                            P.dma(lambda e, ot=ot, n=n, j=j, kt=kt, cch=cch: e.dma_start(out=FS[li].ap()[n, j, kt * 128:(kt + 1) * 128, cch * 512:(cch + 1) * 512], in_=ot),
                                  reads=["o%d" % j], writes=["FS"])
            P.barrier()

        def stage_hy_conv(g, T, li, L, seq, n, final):
            A.reset()
            NT = L // 128
            base = seq * L
            o_Z = A.f32(NT * 256); o_YC = A.f32(NT * 256); o_YS = A.f32(NT * 256)
            o_mat = [[A.f32(NT * 64), A.f32(NT * 64)], [A.f32(NT * 64), A.f32(NT * 64)]]
            o_F = [[A.f32(512), A.f32(512)], [A.f32(512), A.f32(512)]]
            o_t = [A.f32(512) for _ in range(4)]
            o_gt = [A.f32(512), A.f32(512)]; o_ns = A.f32(512); o_bs = A.f32(512)
            o_zn = [A.f32(256), A.f32(256)]; o_zT = A.f32(256)
            zin = zb16[g][n]; zout = zb16[g][n + 1]
            for cch in range(4):
                c0 = cch * 512
                Z = A.bf(o_Z, NT * 256).rearrange("p (t c) -> p t c", t=NT)
                YC = A.bf(o_YC, NT * 256).rearrange("p (t c) -> p t c", t=NT)
                YS = A.bf(o_YS, NT * 256).rearrange("p (t c) -> p t c", t=NT)
                P.dma(lambda e, c0=c0, Z=Z: e.dma_start(out=Z, in_=zin.ap()[base:base + L, c0:c0 + 512].rearrange("(t p) c -> p t c", p=128)),
                      reads=[zin.name], writes=["Z"])
                P.dma(lambda e, c0=c0: e.dma_start(out=A.v(o_ns, [[1, 512]]), in_=bc_rows(nsc[li], n * 2048 + c0, 512)), reads=["nsc"], writes=["NS"])
                P.dma(lambda e, c0=c0: e.dma_start(out=A.v(o_bs, [[1, 512]]), in_=bc_rows(c_bias, n * 2048 + c0, 512)), writes=["BS"])
                for kt in range(NT):
                    pa, pbk, kb = hy_forward(li, L, Z, o_mat, kt, ["Z"], "c")
                    Fc = A.v(o_F[kb][0], [[1, 512]]); Fs = A.v(o_F[kb][1], [[1, 512]])
                    P.dma(lambda e, Fc=Fc, kt=kt, c0=c0: e.dma_start(out=Fc, in_=FS[li].ap()[n, 0, kt * 128:(kt + 1) * 128, c0:c0 + 512]), reads=["FS"], writes=["Fc%d" % kb])
                    P.dma(lambda e, Fs=Fs, kt=kt, c0=c0: e.dma_start(out=Fs, in_=FS[li].ap()[n, 1, kt * 128:(kt + 1) * 128, c0:c0 + 512]), reads=["FS"], writes=["Fs%d" % kb])
                    t = [A.v(o, [[1, 512]]) for o in o_t]
                    P.op("vector", lambda e, t=t, pa=pa, Fc=Fc: e.tensor_tensor(t[0], pa[:, :], Fc, ALU.mult), reads=["pA%d" % kb, "Fc%d" % kb], writes=["t0"])
                    P.op("vector", lambda e, t=t, pbk=pbk, Fs=Fs: e.tensor_tensor(t[1], pbk[:, :], Fs, ALU.mult), reads=["pB%d" % kb, "Fs%d" % kb], writes=["t1"])
                    P.op("vector", lambda e, t=t, pbk=pbk, Fc=Fc: e.tensor_tensor(t[2], pbk[:, :], Fc, ALU.mult), reads=["pB%d" % kb, "Fc%d" % kb], writes=["t2"])
                    P.op("vector", lambda e, t=t, pa=pa, Fs=Fs: e.tensor_tensor(t[3], pa[:, :], Fs, ALU.mult), reads=["pA%d" % kb, "Fs%d" % kb], writes=["t3"])
                    P.op("gpsimd", lambda e, t=t, kt=kt, YC=YC: e.tensor_tensor(YC[:, kt, :], t[0], t[1], ALU.add), reads=["t0", "t1"], writes=["YC"])
                    P.op("gpsimd", lambda e, t=t, kt=kt, YS=YS: e.tensor_tensor(YS[:, kt, :], t[2], t[3], ALU.subtract), reads=["t2", "t3"], writes=["YS"])
                for tt in range(NT):
                    kb = tt % 2
                    Cm = A.bf(o_mat[kb][0], NT * 64).rearrange("p (t k) -> p t k", t=NT)
                    Sm = A.bf(o_mat[kb][1], NT * 64).rearrange("p (t k) -> p t k", t=NT)
                    P.dma(lambda e, Cm=Cm, tt=tt: e.dma_start(out=Cm, in_=dftc[li].ap()[:, tt * 128:(tt + 1) * 128].rearrange("(t p) k -> p t k", p=128)), writes=["Cm%d" % kb])
                    P.dma(lambda e, Sm=Sm, tt=tt: e.dma_start(out=Sm, in_=dfts[li].ap()[:, tt * 128:(tt + 1) * 128].rearrange("(t p) k -> p t k", p=128)), writes=["Sm%d" % kb])
                    py = psb[4 + kb]
                    for kt in range(NT):
                        P.op("tensor", lambda e, kt=kt, Cm=Cm, YC=YC, py=py: e.matmul(py[:, :], Cm[:, kt, :], YC[:, kt, :], start=(kt == 0), stop=False),
                             reads=["Cm%d" % kb, "YC"], writes=["pY%d" % kb])
                    for kt in range(NT):
                        P.op("tensor", lambda e, kt=kt, Sm=Sm, YS=YS, py=py: e.matmul(py[:, :], Sm[:, kt, :], YS[:, kt, :], start=False, stop=(kt == NT - 1)),
                             reads=["Sm%d" % kb, "YS"], writes=["pY%d" % kb])
                    gt_ = A.v(o_gt[kb], [[1, 512]])
                    t0 = base + tt * 128
                    P.dma(lambda e, gt_=gt_, t0=t0, c0=c0: e.dma_start(out=gt_, in_=ug[g].ap()[t0:t0 + 128, n * 2048 + c0:n * 2048 + c0 + 512]), reads=[ug[g].name], writes=["gt%d" % kb])
                    ta = A.v(o_t[0], [[1, 512]]); tb = A.v(o_t[1], [[1, 512]])
                    P.op("vector", lambda e, ta=ta, py=py: e.tensor_tensor(ta, py[:, :], A.v(o_ns, [[1, 512]]), ALU.mult), reads=["pY%d" % kb, "NS"], writes=["t0"])
                    P.op("gpsimd", lambda e, tb=tb, tt=tt, Z=Z: e.tensor_tensor(tb, Z[:, tt, :], A.v(o_bs, [[1, 512]]), ALU.mult), reads=["Z", "BS"], writes=["t1"])
                    P.op("vector", lambda e, ta=ta, tb=tb: e.tensor_tensor(ta, ta, tb, ALU.add), reads=["t0", "t1"], writes=["t0"])
                    zn = A.bf(o_zn[kb], 256)
                    P.op("vector", lambda e, ta=ta, gt_=gt_, zn=zn: e.tensor_tensor(zn, ta, gt_, ALU.mult), reads=["t0", "gt%d" % kb], writes=["zn%d" % kb])
                    if not final:
                        P.dma(lambda e, zn=zn, t0=t0, c0=c0: e.dma_start(out=zout.ap()[t0:t0 + 128, c0:c0 + 512], in_=zn), reads=["zn%d" % kb], writes=[zout.name])
                    else:
                        zT4 = A.bf(o_zT, 256).rearrange("p (c t) -> p c t", c=4)
                        for c4 in range(4):
                            P.op("tensor", lambda e, c4=c4, zn=zn: e.transpose(psT[:, c4 * 128:(c4 + 1) * 128], zn[:, c4 * 128:(c4 + 1) * 128], identb),
                                 reads=["zn%d" % kb, "identb"], writes=["psT"])
                        P.op("scalar", lambda e, zT4=zT4: e.activation(zT4, psT[:, 0:512].rearrange("p (c t) -> p c t", c=4), AF.Copy), reads=["psT"], writes=["zT4"])
                        P.dma(lambda e, zT4=zT4, t0=t0, cch=cch: e.dma_start(out=hzT[g].ap()[cch * 4:(cch + 1) * 4, :, t0:t0 + 128].rearrange("c p t -> p c t"), in_=zT4),
                              reads=["zT4"], writes=[hzT[g].name])
            P.barrier()

        def stage_cast_tables():
            for l in range(2):
                for src_t, dst_t in ((peer_u[l], peer_ub[l]), (peer_v[l], peer_vb[l])):
                    for c in range(16):
                        P.dma(lambda e, src_t=src_t, dst_t=dst_t, c=c: e.dma_start(out=dst_t.ap()[c * 1024:(c + 1) * 1024, :], in_=src_t.ap()[c * 1024:(c + 1) * 1024, :]),
                              writes=[dst_t.name], q="gpsimd")

        def stage_gather_rows(src_t, dst_t, n):
            A.reset()
            o_i = A.f32(8); o_r = [A.f32(2048), A.f32(2048)]
            idx = bass.AP(A.t, o_i, [[A.size, 128], [1, 8]]).bitcast(I32)
            P.dma(lambda e: e.dma_start(out=idx, in_=qidx.ap().rearrange("(n p) o -> p (n o)", p=128)), writes=["qi"])
            for ti in range(n // 128):
                k = ti % 2
                Rr = A.v(o_r[k], [[1, 2048]])
                P.dma(lambda e, Rr=Rr, ti=ti: e.indirect_dma_start(out=Rr, out_offset=None, in_=src_t.ap(),
                                                                 in_offset=bass.IndirectOffsetOnAxis(ap=idx[:, ti:ti + 1], axis=0)),
                      reads=["qi"], writes=["gr%d" % k], q="gpsimd")
                P.dma(lambda e, Rr=Rr, ti=ti: e.dma_start(out=dst_t.ap()[ti * 128:(ti + 1) * 128, :], in_=Rr), reads=["gr%d" % k], writes=[dst_t.name])
            P.barrier()

        def stage_kv_out():
            A.reset()
            o_z = A.f32(512); o_t = A.f32(256); o_s = A.f32(8); o_kn = A.f32(128)
            KN = A.v(o_kn, [[1, 128]])
            P.dma(lambda e: e.dma_start(out=KN, in_=bc_rows(b_kn, 0, 128)), writes=["KN"])
            for ti in range(TP // 128):
                t0 = ti * 128
                zt = A.v(o_z, [[1, 512]])
                P.dma(lambda e, t0=t0: e.dma_start(out=zt, in_=zg[0].ap()[t0:t0 + 128, 4480:4992]), reads=["zp"], writes=["zt"])
                P.dma(lambda e, t0=t0: e.dma_start(out=nv.ap()[t0:t0 + 128, :], in_=A.v(o_z + 256, [[1, 256]])), reads=["zt"], writes=["nv"], is_output=True)
                P.op("vector", lambda e: e.tensor_tensor(A.v(o_t, [[1, 256]]), A.v(o_z, [[1, 256]]), A.v(o_z, [[1, 256]]), ALU.mult), reads=["zt"], writes=["t"])
                P.op("vector", lambda e: e.tensor_reduce(A.v(o_s, [[1, 2]]), A.v(o_t, [[128, 2], [1, 128]]), AX.X, ALU.add), reads=["t"], writes=["s"])
                rsqrt_ops(P, A.v(o_s, [[1, 2]]), 1.0 / 128, 1e-6, "s")
                P.op("vector", lambda e: e.tensor_tensor(A.v(o_t, [[128, 2], [1, 128]]), A.v(o_z, [[128, 2], [1, 128]]), A.v(o_s, [[1, 2], [0, 128]]), ALU.mult),
                     reads=["s", "zt"], writes=["t"])
                P.op("vector", lambda e: e.tensor_tensor(A.v(o_t, [[128, 2], [1, 128]]), A.v(o_t, [[128, 2], [1, 128]]), A.v(o_kn, [[0, 2], [1, 128]]), ALU.mult),
                     reads=["t", "KN"], writes=["t"])
                P.dma(lambda e, t0=t0: e.dma_start(out=nk.ap()[t0:t0 + 128, :], in_=A.v(o_t, [[1, 256]])), reads=["t"], writes=["nk"], is_output=True)
            P.barrier()

        GROUPS = ((0, TP, 256), (1, TS, 4096))

        def layer1(g, T, L, out_ap, out_key):
            li = 0 if L == 256 else 1
            stage_norm_gemm(g, T, x2[g].ap(), 1, 1, norm1, odd_w_in.ap(), 6144, uz[g])
            stage_hy_conv3(g, T, L)
            for n in range(2):
                for seq in range(T // L):
                    stage_hy_conv(g, T, li, L, seq, n, n == 1)
            stage_norm_gemm(g, T, None, 1, 1, None, odd_w_out.ap(), D, x3[g], srcT=hzT[g], resid=(x2[g].ap(), 2 * D))
            if mode == "hytest":
                return
            if g == 1:
                stage_gather_rows(x3[1], x3q, 1024)
                stage_norm_gemm(g, 1024, x3q.ap(), 1, 2, norm2, peer_wq.ap()[1], D, pq[g], hm_out=hm2[g])
                stage_peer(g, 1024, 1, pq[g], hm2[g], x3q.ap(), out_ap, out_key, is_output=True)
                return
            stage_norm_gemm(g, T, x3[g].ap(), 1, 2, norm2, peer_wq.ap()[1], D, pq[g], hm_out=hm2[g])
            stage_peer(g, T, 1, pq[g], hm2[g], x3[g].ap(), out_ap, out_key, is_output=True)

        if mode == "bench_scan":
            stage_scan(0, 0, 256, 0, None, ns)
            stage_scan(0, 0, 256, 1, None, ns)
            P.emit()
            return nc
        if mode == "bench_peer":
            stage_peer(0, 256, 0, pq[0], hm2[0], xg[0].ap(), x2[0].ap(), "x2_0")
            P.emit()
            return nc
        if mode == "bench_gemm":
            stage_norm_gemm(0, TP, xg[0].ap(), 0, 1, norm1, even_w_in.ap(), 4992, zg[0])
            P.emit()
            return nc
        if mode == "full":
            stage_cast_tables()
        stage_mod()
        for li, L_ in enumerate((256, 4096)):
            stage_hy_filters(li, L_)
            stage_hy_fspec(li, L_)
        if mode == "full":
            for g, T, L in GROUPS:
                latent = (g == 1)
                stage_norm_gemm(g, T, xg[g].ap(), 0, 1, norm1, even_w_in.ap(), 4992, zg[g])
                if g == 0:
                    stage_kv_out()
                stage_even_prep(g, T, L)
                for seq in range(T // L):
                    for d in range(2):
                        stage_scan(g, seq, L, d, st_in if latent else None, None if latent else ns)
                stage_rwkv_post(g, T)
                stage_attn_prep(g, T, L, latent)
                for seq in range(T // L):
                    stage_attn(g, T, seq, L, L + (512 if latent else 0))
                stage_norm_gemm(g, T, None, 0, 1, None, even_w_out.ap(), D, x1[g], srcT=ycatT[g], resid=(xg[g].ap(), 2 * D))
                stage_norm_gemm(g, T, x1[g].ap(), 0, 2, norm2, peer_wq.ap()[0], D, pq[g], hm_out=hm2[g])
                stage_peer(g, T, 0, pq[g], hm2[g], x1[g].ap(), x2[g].ap(), x2[g].name)
        layer1(0, TP, 256, yp.ap(), "yp")
        layer1(1, TS, 4096, ys.ap(), "ys")
        P.emit()
    return nc


_CACHE = {}


def _rope_tables():
    pos = np.arange(4096)
    row = (pos // 64).astype(np.float32); col = (pos % 64).astype(np.float32)
    inv = (np.float32(10000.0) ** (-np.arange(32, dtype=np.float32) / np.float32(32))).astype(np.float32)
    ar = (row[:, None] * inv[None, :]).astype(np.float32); ac = (col[:, None] * inv[None, :]).astype(np.float32)
    cs = np.concatenate([np.cos(ar), np.cos(ac)], 1).astype(np.float32)
    sn = np.concatenate([np.sin(ar), np.sin(ac)], 1).astype(np.float32)
    return cs, sn


def _hyena_consts():
    import ml_dtypes
    out = {}
    for li, L in enumerate((256, 4096)):
        t = np.linspace(0.0, 1.0, L, dtype=np.float32)[:, None]
        wpos = (np.float32(2.0 * math.pi) * np.arange(L, dtype=np.float32)[:, None] / np.float32(L)).astype(np.float32)
        fb = np.linspace(1e-4, 15, 16, dtype=np.float32)[None, :]
        z = np.concatenate([t, np.cos(fb * wpos), -np.sin(fb * wpos)], axis=-1).astype(np.float32)
        out["zposT_%d" % li] = np.ascontiguousarray(z.T)
        out["tneg_%d" % li] = np.ascontiguousarray(-t[:, 0])
        Nf = 2 * L - 1
        wk = np.full((L,), 2.0 / Nf, np.float32); wk[0] = 1.0 / Nf
        out["wk_%d" % li] = wk
        idx = np.arange(L, dtype=np.int64)
        ang = (2.0 * np.pi / Nf) * ((idx[:, None] * idx[None, :]) % Nf).astype(np.float64)
        out["dftc_%d" % li] = np.cos(ang).astype(ml_dtypes.bfloat16)
        out["dfts_%d" % li] = np.sin(ang).astype(ml_dtypes.bfloat16)
    out["deltas"] = np.abs(np.linspace(math.log(1e-2) / 1.5, math.log(1e-2) / 0.3, 2048, dtype=np.float32)).astype(np.float32)
    return out


def _shared_inputs(inputs):
    f = lambda a: np.ascontiguousarray(np.asarray(a, dtype=np.float32))
    sh = {
        "mod_w": f(inputs["mod_w"]), "mod_b": f(inputs["mod_b"]),
        "norm1": f(inputs["norm1"]), "norm2": f(inputs["norm2"]),
        "even_w_in": f(inputs["even_w_in"][0]),
        "even_a_conv": f(inputs["even_a_conv"][0]),
        "even_a_w0": f(inputs["even_a_w0"][0]), "even_a_wu": f(inputs["even_a_wu"][0]).reshape(128, 1024),
        "even_a_a0": f(inputs["even_a_a0"][0]), "even_a_au": f(inputs["even_a_au"][0]).reshape(128, 1024),
        "even_a_gu": f(inputs["even_a_gu"][0]),
        "even_a_kk": f(inputs["even_a_kk"][0]), "even_a_ka": f(inputs["even_a_ka"][0]),
        "even_a_rk": f(inputs["even_a_rk"][0]).reshape(1024),
        "even_a_ln_w": f(inputs["even_a_ln_w"][0]), "even_a_ln_b": f(inputs["even_a_ln_b"][0]),
        "even_b_qnorm": f(inputs["even_b_qnorm"][0]), "even_b_knorm": f(inputs["even_b_knorm"][0]),
        "ident": np.eye(128, dtype=np.float32),
        "ropec": _rope_tables()[0], "ropes": _rope_tables()[1],
        "even_w_out": f(inputs["even_w_out"][0]),
        "odd_w_in": f(inputs["odd_w_in"][0]), "odd_w_out": f(inputs["odd_w_out"][0]),
        "odd_c_conv": f(inputs["odd_c_conv"][0]), "odd_c_conv_b": f(inputs["odd_c_conv_b"][0]),
        "odd_c_fw1": f(inputs["odd_c_fw1"][0]), "odd_c_fb1": f(inputs["odd_c_fb1"][0]), "odd_c_freq": f(inputs["odd_c_freq"][0]),
        "odd_c_fw2": f(inputs["odd_c_fw2"][0]), "odd_c_fb2": f(inputs["odd_c_fb2"][0]), "odd_c_fw3": f(inputs["odd_c_fw3"][0]),
        "odd_c_bias": f(inputs["odd_c_bias"][0]),
        "peer_wq": f(inputs["peer_wq"]), "peer_keys": f(inputs["peer_keys"]),
        "peer_u0": f(inputs["peer_u"][0]), "peer_u1": f(inputs["peer_u"][1]),
        "peer_v0": f(inputs["peer_v"][0]), "peer_v1": f(inputs["peer_v"][1]),
    }
    sh.update(_hyena_consts())
    kc = np.zeros((3, 128, 256), np.uint32)
    kc[0] = 0xFFFFFF00
    kc[1] = 0xFF
    kc[2] = (255 - (np.arange(256) % 256)).astype(np.uint32)[None, :]
    sh["kconst"] = kc
    return sh


def kernel(**inputs):
    f = lambda a: np.ascontiguousarray(np.asarray(a, dtype=np.float32))
    if "nc" not in _CACHE:
        _CACHE["nc"] = build_program()
    nc = _CACHE["nc"]
    x_prompt = f(inputs["x_prompt"]); x_sample = f(inputs["x_sample"])
    shared = _shared_inputs(inputs)
    in_maps = []
    for c in range(NC):
        b = c // 4
        m = dict(shared)
        m["xp"] = x_prompt[4 * c:4 * c + 4].reshape(1024, D)
        m["xs"] = x_sample[b]
        m["ck"] = f(inputs["cache_b_k"][b, 0]).reshape(512, 256)
        m["cv"] = f(inputs["cache_b_v"][b, 0]).reshape(512, 256)
        m["st"] = f(inputs["state_a"][b, 0])
        m["cond"] = np.stack([f(inputs["c_ctx"]), f(inputs["c"][b])], 0)
        m["qidx"] = (np.arange(1024, dtype=np.int32) + 1024 * (c % 4)).reshape(1024, 1)
        in_maps.append(m)
    res = run_bass_kernel_spmd(nc, in_maps, core_ids=list(range(NC)))
    R = res.results
    if DEBUG_OUT:
        DEBUG_RES['R'] = R
    y_prompt = np.concatenate([R[c]["yp"].reshape(4, 256, D) for c in range(NC)], 0)
    y_sample = np.stack([np.concatenate([R[4 * b + q]["ys"] for q in range(4)], 0) for b in range(2)], 0)
    new_k = np.concatenate([R[c]["nk"].reshape(4, 1, 256, 2, 128) for c in range(NC)], 0)
    new_v = np.concatenate([R[c]["nv"].reshape(4, 1, 256, 2, 128) for c in range(NC)], 0)
    new_s = np.concatenate([R[c]["ns"].reshape(4, 1, 2, 16, 64, 64) for c in range(NC)], 0)
    return (y_prompt.astype(np.float32), y_sample.astype(np.float32), new_k.astype(np.float32),
            new_v.astype(np.float32), new_s.astype(np.float32))
```
